# Optimizing a Trainium2 kernel written in Bass

```python
import math
import jax, jax.numpy as jnp
from jax import lax
import numpy as np

D_MODEL = 1024
BATCH = 8
SEQ = 4096
DEPTH = 2

CTX_LEN = 256
GRID_W = 64
EPS = 1e-6

A_HEADS = 4
A_DK = 128
A_DV = 128
A_CHUNK = 16
B_HEADS = 4
B_DK = 128
B_DV = 128
B_CONV = 4
B_CHUNK = 64
C_WIDTH = D_MODEL
C_HEADS = 4
C_BLOCK = C_WIDTH // C_HEADS
C_CONV = 4
C_GATE_C = 8.0
FFN_DIM = 2816
FFN_CONV = 3

N_EVEN = (DEPTH + 1) // 2
N_ODD = DEPTH // 2

AB_SIZES = (A_HEADS * A_DK, A_HEADS * A_DK, A_HEADS * A_DK, A_HEADS * A_DV, A_HEADS * A_DV,
            2 * B_HEADS * B_DK + B_HEADS * B_DV, B_HEADS * B_DV, 2 * B_HEADS, 2 * B_HEADS)
AB_IN = 3 * A_HEADS * A_DK + 2 * A_HEADS * A_DV + 2 * B_HEADS * B_DK + 2 * B_HEADS * B_DV + 4 * B_HEADS
AB_OUT = A_HEADS * A_DV + B_HEADS * B_DV

kernel_name = 'hybrid_bidir_hgrn2_gdn_rglru_block'


def rms_norm(x, g):
    xf = x.astype(jnp.float32)
    y = xf * lax.rsqrt(jnp.mean(xf * xf, axis=-1, keepdims=True) + EPS)
    return (y * g.astype(jnp.float32)).astype(x.dtype)


def head_rms_norm(o, g):
    return o * lax.rsqrt(jnp.mean(o * o, axis=-1, keepdims=True) + EPS) * g.astype(jnp.float32)


def l2norm(t):
    return t * lax.rsqrt(jnp.sum(t * t, axis=-1, keepdims=True) + EPS)


def to_heads(t, n):
    b, s, _ = t.shape
    return t.reshape(b, s, n, -1).transpose(0, 2, 1, 3)


def from_heads(t):
    b, n, s, d = t.shape
    return t.transpose(0, 2, 1, 3).reshape(b, s, n * d)


def _split(p, sizes):
    out, start = [], 0
    for n in sizes:
        out.append(p[..., start:start + n])
        start += n
    return out


def dwconv1d(x, w, width):
    left = width // 2
    return lax.conv_general_dilated(x, w[:, None, :].astype(x.dtype), (1,), [(left, width - 1 - left)],
                                    dimension_numbers=('NWC', 'WIO', 'NWC'), feature_group_count=x.shape[-1])


def hgrn2_chunk_scan(q, k, v, logf, s0):
    b, nh, s, dk = q.shape
    dv = v.shape[-1]
    n, cl = s // A_CHUNK, A_CHUNK
    q, k, logf = [t.reshape(b, nh, n, cl, dk) for t in (q, k, logf)]
    v = v.reshape(b, nh, n, cl, dv)
    bc = jnp.cumsum(logf, axis=-2)
    q_in = q * jnp.exp(bc)
    k_in = k * jnp.exp(-bc)
    incl = jnp.tril(jnp.ones((cl, cl), bool))
    attn = jnp.where(incl, jnp.einsum('bhnid,bhnjd->bhnij', q_in, k_in), 0.0)
    intra = jnp.einsum('bhnij,bhnjv->bhniv', attn, v)
    kd = k * jnp.exp(bc[..., -1:, :] - bc)
    gl = jnp.exp(bc[..., -1, :])
    xs = [jnp.moveaxis(t, 2, 0) for t in (intra, q_in, kd, v, gl)]

    def step(st, xc):
        intra_c, q_c, kd_c, v_c, gl_c = xc
        o_c = intra_c + jnp.einsum('bhck,bhkv->bhcv', q_c, st)
        st = gl_c[..., None] * st + jnp.einsum('bhck,bhcv->bhkv', kd_c, v_c)
        return st, o_c

    st, o = lax.scan(step, s0, xs)
    return jnp.moveaxis(o, 0, 2).reshape(b, nh, s, dv), st


def gdn_chunk_scan(q, k, v, g, beta, s0):
    b, nh, s, dk = q.shape
    dv = v.shape[-1]
    n, cl = s // B_CHUNK, B_CHUNK
    q, k = [t.reshape(b, nh, n, cl, dk) for t in (q, k)]
    v = v.reshape(b, nh, n, cl, dv)
    g, beta = [t.reshape(b, nh, n, cl) for t in (g, beta)]
    gc = jnp.cumsum(g, axis=-1)
    incl = jnp.tril(jnp.ones((cl, cl), bool))
    strict = jnp.tril(jnp.ones((cl, cl), bool), -1)
    diff = gc[..., :, None] - gc[..., None, :]
    decay = jnp.where(incl, jnp.exp(jnp.where(incl, diff, 0.0)), 0.0)
    kb = k * beta[..., None]
    a_mat = jnp.where(strict, jnp.einsum('bhnid,bhnjd->bhnij', kb, k) * decay, 0.0)
    rhs = jnp.concatenate([v * beta[..., None], kb * jnp.exp(gc)[..., None]], axis=-1)
    sol = lax.linalg.triangular_solve(a_mat, rhs, left_side=True, lower=True, unit_diagonal=True)
    u, w = sol[..., :dv], sol[..., dv:]
    attn = jnp.einsum('bhnid,bhnjd->bhnij', q, k) * decay
    qg = q * jnp.exp(gc)[..., None]
    kd = k * jnp.exp(gc[..., -1:] - gc)[..., None]
    gl = jnp.exp(gc[..., -1])
    xs = [jnp.moveaxis(t, 2, 0) for t in (u, w, attn, qg, kd, gl)]

    def step(st, xc):
        u_c, w_c, attn_c, qg_c, kd_c, gl_c = xc
        v_new = u_c - jnp.einsum('bhck,bhkv->bhcv', w_c, st)
        o_c = jnp.einsum('bhck,bhkv->bhcv', qg_c, st) + jnp.einsum('bhcj,bhjv->bhcv', attn_c, v_new)
        st = gl_c[..., None, None] * st + jnp.einsum('bhck,bhcv->bhkv', kd_c, v_new)
        return st, o_c

    st, o = lax.scan(step, s0, xs)
    return jnp.moveaxis(o, 0, 2).reshape(b, nh, s, dv), st


def linear_scan(a, u, h0):
    def combine(l, r):
        return l[0] * r[0], r[0] * l[1] + r[1]
    a_cum, u_cum = lax.associative_scan(combine, (a, u), axis=1)
    h = a_cum * h0[:, None, :] + u_cum
    return h, h[:, -1]


def bidirectional(scan_fn, ctx_args, lat_args, s0, time_axis):
    ys_x, ys_c = [], []
    for d in range(2):
        rev = (lambda t: jnp.flip(t, axis=time_axis)) if d == 1 else (lambda t: t)
        oc, sc = scan_fn(*[rev(t) for t in ctx_args[d]], s0)
        ox, _ = scan_fn(*[rev(t) for t in lat_args[d]], sc)
        ys_c.append(rev(oc))
        ys_x.append(rev(ox))
    return ys_x[0] + ys_x[1], ys_c[0] + ys_c[1]


def _ab_features(h, w_in, lb, conv_w, a_log, dt_bias):
    f32 = jnp.float32
    bsz, s, _ = h.shape
    aq, af_f, af_b, ai, ag, bqkv, bz, ba, bb = _split(h @ w_in, AB_SIZES)
    ka, lfa = [], []
    for f_pre, lb_d in ((af_f, lb[0]), (af_b, lb[1])):
        f_pre = f_pre.astype(f32)
        ka.append(to_heads((1.0 - lb_d) * jax.nn.sigmoid(-f_pre), A_HEADS))
        lfa.append(to_heads(jnp.log(lb_d + (1.0 - lb_d) * jax.nn.sigmoid(f_pre)), A_HEADS))
    qkv = jax.nn.silu(dwconv1d(bqkv, conv_w, B_CONV)).astype(f32)
    bq, bk, bv = _split(qkv, (B_HEADS * B_DK, B_HEADS * B_DK, B_HEADS * B_DV))
    ba = ba.astype(f32).reshape(bsz, s, 2, B_HEADS)
    bb = bb.astype(f32).reshape(bsz, s, 2, B_HEADS)
    g = (-jnp.exp(a_log.astype(f32)) * jax.nn.softplus(ba + dt_bias.astype(f32))).transpose(2, 0, 3, 1)
    beta = jax.nn.sigmoid(bb).transpose(2, 0, 3, 1)
    return dict(
        qa=to_heads(jax.nn.silu(aq.astype(f32)), A_HEADS) * A_DK ** -0.5,
        ka=ka, va=to_heads(ai.astype(f32), A_HEADS), lfa=lfa, ga=ag,
        qb=l2norm(to_heads(bq, B_HEADS)) * B_DK ** -0.5,
        kb=l2norm(to_heads(bk, B_HEADS)), vb=to_heads(bv, B_HEADS),
        gb=g, betab=beta, zb=bz)


def mixer_ab(h, hc, w_in, lb, hgrn_g, conv_w, a_log, dt_bias, gdn_g, w_out, need_ctx):
    f32 = jnp.float32
    fx = _ab_features(h, w_in, lb, conv_w, a_log, dt_bias)
    fc = _ab_features(hc, w_in, lb, conv_w, a_log, dt_bias)
    bsz = h.shape[0]

    def a_args(f):
        return [(f['qa'], f['ka'][d], f['va'], f['lfa'][d]) for d in range(2)]

    def b_args(f):
        return [(f['qb'], f['kb'], f['vb'], f['gb'][d], f['betab'][d]) for d in range(2)]

    oa_x, oa_c = bidirectional(hgrn2_chunk_scan, a_args(fc), a_args(fx),
                               jnp.zeros((bsz, A_HEADS, A_DK, A_DV), f32), 2)
    ob_x, ob_c = bidirectional(gdn_chunk_scan, b_args(fc), b_args(fx),
                               jnp.zeros((bsz, B_HEADS, B_DK, B_DV), f32), 2)

    def merge(oa, ob, f):
        ya = from_heads(head_rms_norm(oa, hgrn_g)) * jax.nn.silu(f['ga'].astype(f32))
        yb = from_heads(head_rms_norm(ob, gdn_g)) * jax.nn.silu(f['zb'].astype(f32))
        return jnp.concatenate([ya, yb], axis=-1).astype(h.dtype) @ w_out

    y = merge(oa_x, ob_x, fx)
    yc = merge(oa_c, ob_c, fc) if need_ctx else None
    return y, yc


def _c_features(h, w_in, conv_w, conv_b, gate_w, gate_b, lam):
    f32 = jnp.float32
    gate, xb = jnp.split(h @ w_in, 2, axis=-1)
    xb = (dwconv1d(xb, conv_w, C_CONV) + conv_b).astype(f32)
    bsz, s, _ = xb.shape
    gl = jnp.einsum('bshi,dhio->dbsho', xb.reshape(bsz, s, C_HEADS, C_BLOCK), gate_w.astype(f32))
    gl = jax.nn.sigmoid(gl + gate_b.astype(f32)[:, None, None])
    r, i = jnp.split(gl, 2, axis=-1)
    r = r.reshape(2, bsz, s, C_WIDTH)
    i = i.reshape(2, bsz, s, C_WIDTH)
    log_a = -C_GATE_C * r * jax.nn.softplus(-lam.astype(f32))[:, None, None, :]
    a = jnp.exp(log_a)
    u = jnp.sqrt(-jnp.expm1(2.0 * log_a)) * (i * xb[None])
    return gate, a, u


def mixer_c(h, hc, w_in, conv_w, conv_b, gate_w, gate_b, lam, w_out, need_ctx):
    f32 = jnp.float32
    gx, ax, ux = _c_features(h, w_in, conv_w, conv_b, gate_w, gate_b, lam)
    gc, ac, uc = _c_features(hc, w_in, conv_w, conv_b, gate_w, gate_b, lam)
    s0 = jnp.zeros((h.shape[0], C_WIDTH), f32)
    hx, hcx = bidirectional(linear_scan, [(ac[d], uc[d]) for d in range(2)],
                            [(ax[d], ux[d]) for d in range(2)], s0, 1)

    def out(g, r):
        return (jax.nn.gelu(g.astype(f32)) * r).astype(h.dtype) @ w_out

    return out(gx, hx), (out(gc, hcx) if need_ctx else None)


def conv_ffn(h, w_up, conv_w, conv_b, w_down, rows):
    a, v = jnp.split(h @ w_up, 2, axis=-1)
    if rows is None:
        a = dwconv1d(a, conv_w[FFN_CONV // 2], FFN_CONV)
    else:
        b, s, f = a.shape
        pad = FFN_CONV // 2
        a = lax.conv_general_dilated(a.reshape(b, rows, GRID_W, f), conv_w[:, :, None, :].astype(a.dtype), (1, 1),
                                     [(pad, pad), (pad, pad)], dimension_numbers=('NHWC', 'HWIO', 'NHWC'),
                                     feature_group_count=f).reshape(b, s, f)
    return (jax.nn.silu(a + conv_b) * v) @ w_down


def setup_inputs(seed: int = 0) -> dict:
    key = jax.random.key(seed)
    ks = jax.random.split(key, 28)
    cnt = [0]
    f32 = jnp.float32

    def nxt():
        k = ks[cnt[0]]
        cnt[0] += 1
        return k

    def nrm(shape, scale):
        return jax.random.normal(nxt(), shape, f32) * scale

    def unif(shape, lo, hi):
        return jax.random.uniform(nxt(), shape, f32, lo, hi)

    D = D_MODEL
    x = nrm((BATCH, SEQ, D), 1.0)
    c = nrm((BATCH, D), 1.0)
    ctx = nrm((BATCH, CTX_LEN, D), 1.0)
    c_ctx = nrm((D,), 1.0)
    mod_w = nrm((DEPTH, D, 6 * D), 0.5 * D ** -0.5)
    mod_b = nrm((DEPTH, 6 * D), 0.02)
    norm_mix_g = 1.0 + nrm((DEPTH, D), 0.02)
    norm_ffn_g = 1.0 + nrm((DEPTH, D), 0.02)
    final_norm_g = 1.0 + nrm((D,), 0.02)
    ab_w_in = nrm((N_EVEN, D, AB_IN), D ** -0.5)
    hgrn_lb = nrm((2, DEPTH + 1, A_HEADS * A_DK), 0.1)
    hgrn_norm_g = 1.0 + nrm((N_EVEN, A_DV), 0.02)
    gdn_conv_w = nrm((N_EVEN, B_CONV, 2 * B_HEADS * B_DK + B_HEADS * B_DV), B_CONV ** -0.5)
    gdn_a_log = jnp.log(unif((N_EVEN, 2, B_HEADS), 1.0, 16.0))
    dt = jnp.exp(unif((N_EVEN, 2, B_HEADS), math.log(1e-3), math.log(1e-1)))
    gdn_dt_bias = dt + jnp.log(-jnp.expm1(-dt))
    gdn_norm_g = 1.0 + nrm((N_EVEN, B_DV), 0.02)
    ab_w_out = nrm((N_EVEN, AB_OUT, D), AB_OUT ** -0.5)
    c_w_in = nrm((N_ODD, D, 2 * C_WIDTH), D ** -0.5)
    c_conv_w = nrm((N_ODD, C_CONV, C_WIDTH), C_CONV ** -0.5)
    c_conv_b = nrm((N_ODD, C_WIDTH), 0.02)
    c_gate_w = nrm((N_ODD, 2, C_HEADS, C_BLOCK, 2 * C_BLOCK), C_BLOCK ** -0.5)
    c_gate_b = nrm((N_ODD, 2, C_HEADS, 2 * C_BLOCK), 0.02)
    u = unif((N_ODD, 2, C_WIDTH), 0.9, 0.999)
    c_lambda = jnp.log(u) - jnp.log1p(-u)
    c_w_out = nrm((N_ODD, C_WIDTH, D), C_WIDTH ** -0.5)
    ffn_w_up = nrm((DEPTH, D, 2 * FFN_DIM), D ** -0.5)
    ffn_conv_w = nrm((DEPTH, FFN_CONV, FFN_CONV, FFN_DIM), 1.0 / FFN_CONV)
    ffn_conv_b = nrm((DEPTH, FFN_DIM), 0.02)
    ffn_w_down = nrm((DEPTH, FFN_DIM, D), FFN_DIM ** -0.5)
    return {'x': x, 'c': c, 'ctx': ctx, 'c_ctx': c_ctx, 'mod_w': mod_w, 'mod_b': mod_b,
            'norm_mix_g': norm_mix_g, 'norm_ffn_g': norm_ffn_g, 'final_norm_g': final_norm_g,
            'ab_w_in': ab_w_in, 'hgrn_lb': hgrn_lb, 'hgrn_norm_g': hgrn_norm_g, 'gdn_conv_w': gdn_conv_w,
            'gdn_a_log': gdn_a_log, 'gdn_dt_bias': gdn_dt_bias, 'gdn_norm_g': gdn_norm_g, 'ab_w_out': ab_w_out,
            'c_w_in': c_w_in, 'c_conv_w': c_conv_w, 'c_conv_b': c_conv_b, 'c_gate_w': c_gate_w,
            'c_gate_b': c_gate_b, 'c_lambda': c_lambda, 'c_w_out': c_w_out,
            'ffn_w_up': ffn_w_up, 'ffn_conv_w': ffn_conv_w, 'ffn_conv_b': ffn_conv_b, 'ffn_w_down': ffn_w_down}


def reference(x, c, ctx, c_ctx, mod_w, mod_b, norm_mix_g, norm_ffn_g, final_norm_g,
              ab_w_in, hgrn_lb, hgrn_norm_g, gdn_conv_w, gdn_a_log, gdn_dt_bias, gdn_norm_g, ab_w_out,
              c_w_in, c_conv_w, c_conv_b, c_gate_w, c_gate_b, c_lambda, c_w_out,
              ffn_w_up, ffn_conv_w, ffn_conv_b, ffn_w_down):
    rows = x.shape[1] // GRID_W
    lb_all = jnp.cumsum(jax.nn.softmax(hgrn_lb.astype(jnp.float32), axis=1), axis=1)
    xc = ctx
    for l in range(DEPTH):
        last = l == DEPTH - 1
        mod = jax.nn.silu(c) @ mod_w[l] + mod_b[l]
        modc = jax.nn.silu(c_ctx) @ mod_w[l] + mod_b[l]
        sh1, sc1, g1, sh2, sc2, g2 = jnp.split(mod[:, None, :], 6, axis=-1)
        sh1c, sc1c, g1c, sh2c, sc2c, g2c = jnp.split(modc, 6, axis=-1)
        h = rms_norm(x, norm_mix_g[l]) * (1.0 + sc1) + sh1
        hc = rms_norm(xc, norm_mix_g[l]) * (1.0 + sc1c) + sh1c
        if l % 2 == 0:
            e = l // 2
            y, yc = mixer_ab(h, hc, ab_w_in[e], lb_all[:, l], hgrn_norm_g[e], gdn_conv_w[e], gdn_a_log[e],
                             gdn_dt_bias[e], gdn_norm_g[e], ab_w_out[e], not last)
        else:
            o = l // 2
            y, yc = mixer_c(h, hc, c_w_in[o], c_conv_w[o], c_conv_b[o], c_gate_w[o], c_gate_b[o],
                            c_lambda[o], c_w_out[o], not last)
        x = x + g1 * y
        h = rms_norm(x, norm_ffn_g[l]) * (1.0 + sc2) + sh2
        x = x + g2 * conv_ffn(h, ffn_w_up[l], ffn_conv_w[l], ffn_conv_b[l], ffn_w_down[l], rows)
        if not last:
            xc = xc + g1c * yc
            hc = rms_norm(xc, norm_ffn_g[l]) * (1.0 + sc2c) + sh2c
            xc = xc + g2c * conv_ffn(hc, ffn_w_up[l], ffn_conv_w[l], ffn_conv_b[l], ffn_w_down[l], None)
    return rms_norm(x, final_norm_g)
```

```python
import numpy as np
from contextlib import ExitStack
import concourse.bass as bass
import concourse.mybir as mybir
from concourse.bass_utils import run_bass_kernel_spmd

F32 = mybir.dt.float32
BF16 = mybir.dt.bfloat16
F32R = mybir.dt.float32r
AF = mybir.ActivationFunctionType
ALU = mybir.AluOpType
AX = mybir.AxisListType

SAME_ENGINE_SYNC = True
USE_F32R = False
N_DMA_SEMS = 8


class Slot:
    __slots__ = ("w", "r")

    def __init__(self):
        self.w = None
        self.r = {}


class Em:
    ENG = ("pe", "dve", "act", "pool", "sp")

    def __init__(self, nc, stack):
        self.nc = nc
        self.eng = {"pe": nc.tensor, "dve": nc.vector, "act": nc.scalar, "pool": nc.gpsimd, "sp": nc.sync}
        self.prog = {e: [] for e in self.ENG}
        self.sem = {}
        self.cnt = {}
        for e in ("pe", "dve", "act", "pool"):
            self.sem[e] = stack.enter_context(nc.semaphore("s_" + e))
            self.cnt[e] = 0
        self.dq = {}
        for q in ("sp", "pool", "act"):
            lst = []
            for i in range(N_DMA_SEMS):
                k = "d_%s_%d" % (q, i)
                self.sem[k] = stack.enter_context(nc.semaphore(k))
                self.cnt[k] = 0
                lst.append(k)
            self.dq[q] = [lst, 0]
        self.seen = {e: {} for e in self.ENG}
        self.ninst = 0

    def slots(self, n):
        return [Slot() for _ in range(n)]

    def _need(self, e, reads, writes, is_pe_mm=False):
        need = {}

        def add(tok):
            if tok is None:
                return
            k, v = tok
            if k == e and not SAME_ENGINE_SYNC:
                return
            if need.get(k, 0) < v:
                need[k] = v
        for s in reads:
            add(s.w)
        for s in writes:
            if s.w is not None and s.w[0] != e:
                add(s.w)
            for k, v in s.r.items():
                if k == e:
                    continue
                add((k, v))
        return need

    def _emit_waits(self, e, need):
        seen = self.seen[e]
        for k, v in need.items():
            if seen.get(k, 0) >= v:
                continue
            seen[k] = v
            sem = self.sem[k]
            self.prog[e].append(lambda eng, sem=sem, v=v: eng.wait_ge(sem, v))

    def _mark(self, tok, reads, writes):
        for s in reads:
            if s.r.get(tok[0], 0) < tok[1]:
                s.r[tok[0]] = tok[1]
        for s in writes:
            s.w = tok
            s.r = {}

    def op(self, e, fn, reads=(), writes=(), mm=False):
        need = self._need(e, reads, writes, is_pe_mm=mm)
        self._emit_waits(e, need)
        self.cnt[e] += 1
        n = self.cnt[e]
        sem = self.sem[e]
        self.prog[e].append(lambda eng, fn=fn, sem=sem: fn().then_inc(sem, 1))
        self._mark((e, n), reads, writes)
        self.ninst += 1

    def dma(self, out, in_, reads=(), writes=(), q="sp", **kw):
        lst, idx = self.dq[q]
        k = lst[idx % len(lst)]
        self.dq[q][1] = idx + 1
        need = self._need(q, reads, writes)
        if self.cnt[k] > 0:
            need[k] = max(need.get(k, 0), self.cnt[k])
        self._emit_waits(q, need)
        self.cnt[k] += 16
        v = self.cnt[k]
        sem = self.sem[k]
        eng = self.eng[q]
        self.prog[q].append(lambda _e, eng=eng, out=out, in_=in_, sem=sem, kw=kw:
                            eng.dma_start(out=out, in_=in_, **kw).then_inc(sem, 16))
        self._mark((k, v), reads, writes)
        self.ninst += 1

    def wait_all(self, e, slots):
        need = {}
        for s in slots:
            if s.w is not None:
                need[s.w[0]] = max(need.get(s.w[0], 0), s.w[1])
        self._emit_waits(e, need)

    def finish(self, out_slots):
        self.wait_all("sp", out_slots)
        nc = self.nc
        with nc.Block() as block:
            @block.sync
            def _(eng):
                for f in self.prog["sp"]:
                    f(eng)

            @block.tensor
            def _(eng):
                for f in self.prog["pe"]:
                    f(eng)

            @block.vector
            def _(eng):
                for f in self.prog["dve"]:
                    f(eng)

            @block.scalar
            def _(eng):
                for f in self.prog["act"]:
                    f(eng)

            @block.gpsimd
            def _(eng):
                for f in self.prog["pool"]:
                    f(eng)


class T:
    __slots__ = ("ap", "s")

    def __init__(self, ap, slot=None):
        self.ap = ap
        self.s = slot if slot is not None else Slot()

    def __getitem__(self, key):
        return T(self.ap[key], self.s)


def _sl(lst):
    out = []
    for x in lst:
        s = getattr(x, "s", x)
        if isinstance(s, (list, tuple)):
            out.extend(s)
        else:
            out.append(s)
    return out


class DT:
    def __init__(self, ap, n=34):
        self.ap = ap
        self.slots = [Slot() for _ in range(n)]

    def t(self, ap, p0, n):
        a, b = p0 // 128, (p0 + n + 127) // 128
        return T(ap, tuple(self.slots[a:b]))


D = 1024
KT = 8
TC = 256
TX = 4096
P = TC + TX
NCH = P // 128
FF = 2816
FT = 22
ABN = 4624
EPS = 1e-6
TILES = [(0, 256, 1)] + [(TC + 512 * i, 512, 0) for i in range(8)]
LTILES = TILES[1:]


class Prog:
    def __init__(self, debug=()):
        self.debug = set(debug)
        nc = bass.Bass("TRN2", target_bir_lowering=False)
        self.nc = nc
        self.stack = ExitStack()
        self.em = Em(nc, self.stack)
        self.din = {}
        self.dscr = {}
        self.outs = []
        self.psi = 0

    def inp(self, name, shape, dt=F32):
        t = self.nc.dram_tensor(name, list(shape), dt, kind="ExternalInput").ap()
        self.din[name] = T(t)
        return self.din[name]

    def scr(self, name, shape, dt):
        kind = "ExternalOutput" if name in self.debug else "Internal"
        t = T(self.nc.dram_tensor(name, list(shape), dt, kind=kind).ap())
        self.dscr[name] = t
        if name in self.debug:
            self.outs.append(t)
        return t

    def sb(self, st, name, shape, dt):
        self.uid = getattr(self, "uid", 0) + 1
        return T(st.enter_context(self.nc.sbuf_tensor("s%d_%s" % (self.uid, name), list(shape), dt)).ap())

    def op(self, e, fn, r=(), w=(), mm=False):
        self.em.op(e, fn, _sl(r), _sl(w), mm=mm)

    def dma(self, out, in_, q="sp", **kw):
        self.em.dma(out.ap, in_.ap, reads=_sl([in_]), writes=_sl([out]), q=q, **kw)

    def mm(self, out, lhsT, rhs, start=True, stop=True):
        nc = self.nc
        self.op("pe", lambda: nc.tensor.matmul(out.ap, lhsT.ap, rhs.ap, start=start, stop=stop),
                r=[lhsT, rhs], w=[out], mm=True)

    def tr(self, out, in_, ident):
        nc = self.nc
        self.op("pe", lambda: nc.tensor.transpose(out.ap, in_.ap, ident.ap), r=[in_, ident], w=[out], mm=True)

    def act(self, out, in_, func, bias=None, scale=None, accum=None, extra_r=()):
        nc = self.nc
        kw = {}
        r = [in_] + list(extra_r)
        if bias is not None:
            kw["bias"] = bias.ap if isinstance(bias, T) else bias
            if isinstance(bias, T):
                r.append(bias)
        if scale is not None:
            kw["scale"] = scale.ap if isinstance(scale, T) else scale
            if isinstance(scale, T):
                r.append(scale)
        w = [out]
        if accum is not None:
            kw["accum_out"] = accum.ap
            w.append(accum)
        self.op("act", lambda: nc.scalar.activation(out.ap, in_.ap, func, **kw), r=r, w=w)

    def tt(self, out, a, b, op, e="dve"):
        eng = self.nc.vector if e == "dve" else self.nc.gpsimd
        self.op(e, lambda: eng.tensor_tensor(out.ap, a.ap, b.ap, op), r=[a, b], w=[out])

    def ts(self, out, a, s1, op0, s2=None, op1=None, e="dve"):
        eng = self.nc.vector if e == "dve" else self.nc.gpsimd
        r = [a]
        v1 = s1.ap if isinstance(s1, T) else s1
        v2 = s2.ap if isinstance(s2, T) else s2
        if isinstance(s1, T):
            r.append(s1)
        if isinstance(s2, T):
            r.append(s2)
        if op1 is None and e == "pool" and op0 in (ALU.mult, ALU.add):
            o1, c1 = (ALU.add, 0.0) if op0 == ALU.mult else (ALU.mult, 1.0)
            self.op(e, lambda: eng.tensor_scalar(out.ap, a.ap, v1, c1, op0, o1), r=r, w=[out])
        elif op1 is None:
            self.op(e, lambda: eng.tensor_scalar(out.ap, a.ap, v1, None, op0), r=r, w=[out])
        else:
            self.op(e, lambda: eng.tensor_scalar(out.ap, a.ap, v1, v2, op0, op1), r=r, w=[out])

    def stt(self, out, a, s, b, op0, op1):
        nc = self.nc
        r = [a, b]
        v = s.ap if isinstance(s, T) else s
        if isinstance(s, T):
            r.append(s)
        self.op("dve", lambda: nc.vector.scalar_tensor_tensor(out.ap, a.ap, v, b.ap, op0, op1), r=r, w=[out])

    def copy(self, out, in_, e="dve"):
        nc = self.nc
        if e == "act":
            self.op("act", lambda: nc.scalar.copy(out.ap, in_.ap), r=[in_], w=[out])
        elif e == "pool":
            self.op("pool", lambda: nc.gpsimd.tensor_copy(out.ap, in_.ap), r=[in_], w=[out])
        else:
            self.op("dve", lambda: nc.vector.tensor_copy(out.ap, in_.ap), r=[in_], w=[out])

    def memset(self, out, val, e="pool"):
        eng = self.nc.gpsimd if e == "pool" else self.nc.vector
        self.op(e, lambda: eng.memset(out.ap, val), w=[out])

    def recip(self, out, in_):
        nc = self.nc
        self.op("dve", lambda: nc.vector.reciprocal(out.ap, in_.ap), r=[in_], w=[out])

    def nextps(self, pool=None):
        if pool is None:
            t = self.PS[self.psi % len(self.PS)]
            self.psi += 1
            return t
        self.psp = getattr(self, "psp", {})
        k = self.psp.get(pool, 0)
        self.psp[pool] = k + 1
        return self.PS[4 * pool + (k % 4)]

    def run_pipe(self, gens, K, gap=1):
        it = iter(gens)
        active = []
        rnd = 0
        more = True
        while True:
            if more and len(active) < K and rnd % gap == 0:
                g = next(it, None)
                if g is None:
                    more = False
                else:
                    active.append(g)
            rnd += 1
            if not active:
                if not more:
                    break
                continue
            for g in list(active):
                try:
                    next(g)
                except StopIteration:
                    active.remove(g)

    def run_streams(self, gens):
        gens = list(gens)
        while gens:
            for g in list(gens):
                try:
                    next(g)
                except StopIteration:
                    gens.remove(g)

    def barrier(self):
        em = self.em
        need = {k: v for k, v in em.cnt.items() if v > 0}
        for e in Em.ENG:
            em._emit_waits(e, dict(need))

    def declare(self):
        i = self.inp
        i("x", [TX, D]); i("ctx", [TC, D]); i("cvec", [128, KT, 2])
        i("mod_w", [2, 128, KT, 6 * D]); i("mod_b", [2, 128, 48]); i("ng", [128, 4, KT]); i("fng", [128, D])
        i("ab_w_in", [128, KT, ABN]); i("hlb", [128, 2, 3, 512]); i("hng", [128, 512]); i("gng", [128, 512])
        i("gcw", [128, 12, 4]); i("galog", [128, 8]); i("gdtb", [128, 8]); i("ab_w_out", [128, KT, D])
        i("c_w_in", [128, KT, 2 * D]); i("ccw", [128, KT, 4]); i("ccb", [128, KT]); i("cgw", [128, 16, 512])
        i("cgb", [128, 2, 4, 4]); i("clam", [128, 2, KT]); i("c_w_out", [128, KT, D])
        i("ffn_w_up", [2, 128, KT, 2 * FF]); i("fcw", [2, 128, FT, 9]); i("fcb", [2, 128, FT])
        i("ffn_w_down", [2, 128, FT, D])
        i("cst", [128, 10, 128]); i("sel", [128, 2, 3]); i("m4", [128, 2, 512])
        nc = self.nc
        self.out = T(nc.dram_tensor("out", [TX, D], F32, kind="ExternalOutput").ap())
        self.outs.append(self.out)
        self.xT = DT(self.scr("xT", [KT, 128, P], F32).ap.rearrange("k p t -> p k t"))
        self.yT = DT(self.scr("yT", [KT, 128, P], BF16).ap.rearrange("k p t -> p k t"))
        self.uT = DT(self.scr("uT", [FT, 128, P], BF16).ap.rearrange("k p t -> p k t"))

    def setup(self):
        st = self.stack
        nc = self.nc
        self.PS = [T(st.enter_context(nc.psum_tensor("ps%d" % k, [128, 512], F32)).ap()) for k in range(8)]
        self.cst = self.sb(st, "cst", [128, 10, 128], F32)
        self.dma(self.cst, self.din["cst"])
        self.ident = self.cst[:, 0, :]
        self.ones = self.cst[:, 1, :]
        self.identb = self.sb(st, "identb", [128, 128], BF16)
        self.copy(self.identb, self.ident)
        self.onesb = self.sb(st, "onesb", [128, 128], BF16)
        self.copy(self.onesb, self.ones)
        self.ng = self.sb(st, "ng", [128, 4, KT], F32)
        self.dma(self.ng, self.din["ng"])
        self.modv = [self.sb(st, "modv%d" % l, [128, 48, 2], F32) for l in range(2)]
        self.gs = [[self.sb(st, "gs%d_%d" % (l, w), [128, KT, 2], F32) for w in range(2)] for l in range(2)]
        self.epsc = self.sb(st, "epsc", [128, 1], F32)
        self.memset(self.epsc, EPS)

    def mod_phase(self, l):
        nc = self.nc
        with ExitStack() as st:
            cv = self.sb(st, "cv", [128, KT, 2], F32)
            sv = self.sb(st, "sv", [128, KT, 2], F32)
            mb = self.sb(st, "mb", [128, 48], F32)
            tmp = self.sb(st, "mtmp", [128, KT, 2], F32)
            wb = [self.sb(st, "mw%d" % k, [128, KT, 128], F32) for k in range(2)]
            self.dma(cv, self.din["cvec"])
            self.dma(mb, self.din["mod_b"][l])
            self.act(sv, cv, AF.Silu)
            ps = self.nextps()
            for n in range(48):
                w = wb[n % 2]
                self.dma(w, self.din["mod_w"][l, :, :, n * 128:(n + 1) * 128])
                for kt in range(KT):
                    self.mm(ps[:, 2 * n:2 * n + 2], w[:, kt, :], sv[:, kt, :], start=(kt == 0), stop=(kt == KT - 1))
            pv = ps[:, 0:96]
            for j in range(2):
                self.tt(self.modv[l][:, :, j], T(pv.ap.rearrange("p (n j) -> p n j", j=2)[:, :, j], pv.s), mb, ALU.add)
            for w_, sec in ((0, 1), (1, 4)):
                self.ts(tmp, self.modv[l][:, sec * 8:(sec + 1) * 8, :], 1.0, ALU.add)
                for j in range(2):
                    self.tt(self.gs[l][w_][:, :, j], tmp[:, :, j], self.ng[:, 2 * l + w_, :], ALU.mult)
            self.barrier()

    def xinit_phase(self):
        with ExitStack() as st:
            xin = [self.sb(st, "xin%d" % k, [128, D], F32) for k in range(2)]
            xo = [self.sb(st, "xo%d" % k, [128, KT, 128], F32) for k in range(2)]
            for i in range(NCH):
                src = self.din["ctx"][i * 128:(i + 1) * 128, :] if i < 2 else self.din["x"][(i - 2) * 128:(i - 1) * 128, :]
                xi = xin[i % 2]
                self.dma(xi, src)
                for half in range(2):
                    ps = self.nextps()
                    for k4 in range(4):
                        kt = half * 4 + k4
                        self.tr(ps[:, k4 * 128:(k4 + 1) * 128], xi[:, kt * 128:(kt + 1) * 128], self.ident)
                    dst = xo[i % 2][:, half * 4:(half + 1) * 4, :]
                    src_ps = T(ps.ap.rearrange("p (k t) -> p k t", t=128), ps.s)
                    self.copy(dst, src_ps, e=("act" if half else "dve"))
                self.dma(self.xT.t(self.xT.ap[:, :, i * 128:(i + 1) * 128], i * 128, 128), xo[i % 2])
            self.barrier()

    def norm_phase(self, st, l, w_, hT, tiles):
        sec_sh = 0 if w_ == 0 else 3
        with ExitStack() as ph:
            xs = [self.sb(ph, "nxs%d" % k, [128, KT, 512], F32) for k in range(2)]
            sq = [self.sb(ph, "nsq%d" % k, [128, KT, 512], F32) for k in range(2)]
            rs = [self.sb(ph, "nrs%d" % k, [128, 512], F32) for k in range(2)]
            tm = [self.sb(ph, "ntm%d" % k, [128, 512], F32) for k in range(2)]
            for i, (p0, n, j) in enumerate(tiles):
                x_ = xs[i % 2]
                self.dma(x_[:, :, :n], self.xT.t(self.xT.ap[:, :, p0:p0 + n], p0, n))
                self.act(sq[i % 2][:, :, :n], x_[:, :, :n], AF.Square)
                ps = self.nextps()
                for kt in range(KT):
                    self.mm(ps[:, :n], self.ones, sq[i % 2][:, kt, :n], start=(kt == 0), stop=(kt == KT - 1))
                r_ = rs[i % 2]
                self.act(r_[:, :n], ps[:, :n], AF.Sqrt, bias=self.epsc, scale=1.0 / D)
                self.recip(r_[:, :n], r_[:, :n])
                for kt in range(KT):
                    t_ = tm[kt % 2]
                    self.stt(t_[:, :n], x_[:, kt, :n], self.gs[l][w_][:, kt, j:j + 1], r_[:, :n], ALU.mult, ALU.mult)
                    self.act(hT[:, kt, p0:p0 + n], t_[:, :n], AF.Identity,
                             bias=self.modv[l][:, sec_sh * 8 + kt, j:j + 1], scale=1.0)
            self.barrier()

    def outproj_phase(self, l, wname, widx, nkt, src, gsec, tiles):
        with ExitStack() as ph:
            wsb = self.sb(ph, "opw", [128, nkt, D], BF16)
            wd = self.din[wname] if widx is None else self.din[wname][widx]
            for kt in range(nkt):
                self.dma(wsb[:, kt, :], wd[:, kt, :], q="pool")
            ub = [self.sb(ph, "opu%d" % k, [128, nkt, 512], BF16) for k in range(2)]
            xs = [self.sb(ph, "opx%d" % k, [128, KT, 512], F32) for k in range(2)]
            for i, (p0, n, j) in enumerate(tiles):
                u = ub[i % 2]
                x_ = xs[i % 2]
                self.dma(u[:, :, :n], src.t(src.ap[:, :, p0:p0 + n], p0, n))
                self.dma(x_[:, :, :n], self.xT.t(self.xT.ap[:, :, p0:p0 + n], p0, n))
                for nt in range(KT):
                    ps = self.nextps()
                    for kt in range(nkt):
                        self.mm(ps[:, :n], wsb[:, kt, nt * 128:(nt + 1) * 128], u[:, kt, :n],
                                start=(kt == 0), stop=(kt == nkt - 1))
                    self.stt(x_[:, nt, :n], ps[:, :n], self.modv[l][:, gsec * 8 + nt, j:j + 1], x_[:, nt, :n],
                             ALU.mult, ALU.add)
                self.dma(self.xT.t(self.xT.ap[:, :, p0:p0 + n], p0, n), x_[:, :, :n])
            self.barrier()

    def ffn_phase(self, l, hT, tiles):
        nc = self.nc
        L0 = 323
        PADLEN = L0 + TX + 65
        with ExitStack() as ph:
            aT = self.sb(ph, "faT", [128, PADLEN], BF16)
            self.memset(aT, 0.0)
            cw = self.sb(ph, "fcw", [128, FT, 9], F32)
            cb = self.sb(ph, "fcb", [128, FT], F32)
            self.dma(cw, self.din["fcw"][l])
            self.dma(cb, self.din["fcb"][l])
            wa = [self.sb(ph, "fwa%d" % k, [128, KT, 128], BF16) for k in range(2)]
            wv = [self.sb(ph, "fwv%d" % k, [128, KT, 128], BF16) for k in range(2)]
            dw = [self.sb(ph, "fdw%d" % k, [128, 9, 128], BF16) for k in range(2)]
            sa = [self.sb(ph, "fsa%d" % k, [128, 512], F32) for k in range(2)]
            uo = [self.sb(ph, "fuo%d" % k, [128, 512], BF16) for k in range(2)]
            wup = self.din["ffn_w_up"][l]
            it = 0
            for c in range(FT):
                a_w, v_w, d_w = wa[c % 2], wv[c % 2], dw[c % 2]
                self.dma(a_w, wup[:, :, c * 128:(c + 1) * 128], q="pool")
                self.dma(v_w, wup[:, :, FF + c * 128:FF + (c + 1) * 128], q="pool")
                for tap in range(9):
                    self.ts(d_w[:, tap, :], self.identb, cw[:, c, tap:tap + 1], ALU.mult)
                for (p0, n, j) in tiles:
                    ps = self.nextps()
                    for kt in range(KT):
                        self.mm(ps[:, :n], a_w[:, kt, :], hT[:, kt, p0:p0 + n], start=(kt == 0), stop=(kt == KT - 1))
                    off = 1 + p0 if j == 1 else L0 + (p0 - TC)
                    self.copy(aT[:, off:off + n], ps[:, :n], e="act")
                for (p0, n, j) in tiles:
                    psv = self.nextps()
                    for kt in range(KT):
                        self.mm(psv[:, :n], v_w[:, kt, :], hT[:, kt, p0:p0 + n], start=(kt == 0), stop=(kt == KT - 1))
                    psc = self.nextps()
                    if j == 1:
                        for q_, kw in enumerate((1, 0, 2)):
                            o = 1 + p0 + (kw - 1)
                            self.mm(psc[:, :n], d_w[:, 3 + kw, :], aT[:, o:o + n], start=(q_ == 0), stop=(q_ == 2))
                    else:
                        t0 = L0 + (p0 - TC)
                        order = [(1, 1)] + [(kh, kw) for kh in range(3) for kw in range(3) if (kh, kw) != (1, 1)]
                        pv3 = T(psc.ap.rearrange("p (r c) -> p r c", c=64), psc.s)
                        for q_, (kh, kw) in enumerate(order):
                            o = t0 + 64 * (kh - 1)
                            src3 = T(aT.ap[:, o:o + 512].rearrange("p (r c) -> p r c", c=64), aT.s)
                            if kw == 1:
                                dst_, s_ = pv3, src3
                            elif kw == 2:
                                dst_, s_ = pv3[:, :, 0:63], src3[:, :, 1:64]
                            else:
                                dst_, s_ = pv3[:, :, 1:64], src3[:, :, 0:63]
                            self.mm(dst_, d_w[:, kh * 3 + kw, :], s_, start=(q_ == 0), stop=(q_ == 8))
                    s_a = sa[it % 2]
                    u_o = uo[it % 2]
                    it += 1
                    self.act(s_a[:, :n], psc[:, :n], AF.Silu, bias=cb[:, c:c + 1], scale=1.0)
                    self.tt(u_o[:, :n], s_a[:, :n], psv[:, :n], ALU.mult)
                    self.dma(self.uT.t(self.uT.ap[:, c, p0:p0 + n], p0, n), u_o[:, :n])
            self.barrier()

    def final_phase(self):
        with ExitStack() as ph:
            fg = self.sb(ph, "fng", [128, D], F32)
            self.dma(fg, self.din["fng"])
            xs = [self.sb(ph, "fx%d" % k, [128, KT, 128], F32) for k in range(2)]
            xt = [self.sb(ph, "fxt%d" % k, [128, D], F32) for k in range(2)]
            junk = self.sb(ph, "fjunk", [128, D], F32)
            ss = [self.sb(ph, "fss%d" % k, [128, 1], F32) for k in range(2)]
            for i in range(TX // 128):
                p0 = TC + i * 128
                x_ = xs[i % 2]
                self.dma(x_, self.xT.t(self.xT.ap[:, :, p0:p0 + 128], p0, 128))
                o_ = xt[i % 2]
                for half in range(2):
                    ps = self.nextps()
                    for k4 in range(4):
                        self.tr(ps[:, k4 * 128:(k4 + 1) * 128], x_[:, half * 4 + k4, :], self.ident)
                    self.copy(o_[:, half * 512:(half + 1) * 512], ps, e=("act" if half else "dve"))
                s_ = ss[i % 2]
                self.act(junk, o_, AF.Square, accum=s_)
                self.act(s_, s_, AF.Sqrt, bias=self.epsc, scale=1.0 / D)
                self.recip(s_, s_)
                self.stt(o_, o_, s_, fg, ALU.mult, ALU.mult)
                self.dma(self.out[i * 128:(i + 1) * 128, :], o_)
            self.barrier()


def _ktm(w, kt):
    return np.ascontiguousarray(w.reshape(kt, 128, -1).transpose(1, 0, 2))


def _col(v, kt):
    return np.ascontiguousarray(v.reshape(kt, 128).T)


def _consts():
    j = np.arange(128)[:, None]
    i = np.arange(128)[None, :]
    c = np.zeros((128, 10, 128), np.float32)
    c[:, 0] = (j == i)
    c[:, 1] = 1.0
    c[:, 2] = (j <= i)
    c[:, 3] = (j >= i)
    c[:, 4] = 1.0 * ((j >= 64) & (j <= i)) - 1.0 * ((j > i) & (j <= 63))
    c[:, 5] = 1.0 * ((j >= i) & (j <= 63)) - 1.0 * ((j >= 64) & (j < i))
    c[:, 6] = np.where(j <= i, 0.0, -30000.0)
    c[:, 7] = np.where(j >= i, 0.0, -30000.0)
    c[:, 8] = np.where(i < j, 0.0, 30000.0)
    c[:, 9] = np.where(i > j, 0.0, 30000.0)
    t = np.arange(128)
    sel = np.zeros((128, 2, 3), np.float32)
    sel[:, 0, 0] = (t <= 63); sel[:, 0, 1] = 1.0; sel[:, 0, 2] = (t > 63)
    sel[:, 1, 0] = (t >= 64); sel[:, 1, 1] = 1.0; sel[:, 1, 2] = (t < 64)
    m4 = np.zeros((128, 2, 512), np.float32)
    m4[:, 0] = np.tile(c[:, 2], (1, 4))
    m4[:, 1] = np.tile(c[:, 3], (1, 4))
    return c, sel, m4


def host_prep(inp, b):
    f = lambda a: np.ascontiguousarray(np.asarray(a, dtype=np.float32))
    m = {}
    m["x"] = f(inp["x"][b]); m["ctx"] = f(inp["ctx"][b])
    cv = np.stack([inp["c"][b], inp["c_ctx"]], axis=-1)
    m["cvec"] = f(cv.reshape(KT, 128, 2).transpose(1, 0, 2))
    m["mod_w"] = f(np.stack([_ktm(inp["mod_w"][l], KT) for l in range(2)]))
    m["mod_b"] = f(np.stack([_col(inp["mod_b"][l], 48) for l in range(2)]))
    m["ng"] = f(np.stack([_col(inp["norm_mix_g"][0], KT), _col(inp["norm_ffn_g"][0], KT),
                          _col(inp["norm_mix_g"][1], KT), _col(inp["norm_ffn_g"][1], KT)], axis=1))
    m["fng"] = f(np.tile(inp["final_norm_g"][None, :], (128, 1)))
    m["ab_w_in"] = f(_ktm(inp["ab_w_in"][0], KT))
    m["hlb"] = f(np.tile(inp["hgrn_lb"][None], (128, 1, 1, 1)))
    m["hng"] = f(np.tile(np.tile(inp["hgrn_norm_g"][0], 4)[None, :], (128, 1)))
    m["gng"] = f(np.tile(np.tile(inp["gdn_norm_g"][0], 4)[None, :], (128, 1)))
    m["gcw"] = f(inp["gdn_conv_w"][0].reshape(4, 12, 128).transpose(2, 1, 0))
    m["galog"] = f(np.tile(inp["gdn_a_log"][0].reshape(1, 8), (128, 1)))
    m["gdtb"] = f(np.tile(inp["gdn_dt_bias"][0].reshape(1, 8), (128, 1)))
    m["ab_w_out"] = f(_ktm(inp["ab_w_out"][0], KT))
    m["c_w_in"] = f(_ktm(inp["c_w_in"][0], KT))
    m["ccw"] = f(inp["c_conv_w"][0].reshape(4, KT, 128).transpose(2, 1, 0))
    m["ccb"] = f(_col(inp["c_conv_b"][0], KT))
    gw = inp["c_gate_w"][0]
    m["cgw"] = f(gw.reshape(2, 4, 2, 128, 512).transpose(3, 0, 1, 2, 4).reshape(128, 16, 512))
    m["cgb"] = f(inp["c_gate_b"][0].reshape(2, 4, 4, 128).transpose(3, 0, 1, 2))
    m["clam"] = f(inp["c_lambda"][0].reshape(2, KT, 128).transpose(2, 0, 1))
    m["c_w_out"] = f(_ktm(inp["c_w_out"][0], KT))
    m["ffn_w_up"] = f(np.stack([_ktm(inp["ffn_w_up"][l], KT) for l in range(2)]))
    m["fcw"] = f(np.stack([inp["ffn_conv_w"][l].reshape(9, FT, 128).transpose(2, 1, 0) for l in range(2)]))
    m["fcb"] = f(np.stack([_col(inp["ffn_conv_b"][l], FT) for l in range(2)]))
    m["ffn_w_down"] = f(np.stack([_ktm(inp["ffn_w_down"][l], FT) for l in range(2)]))
    c, sel, m4 = _consts()
    m["cst"] = c; m["sel"] = sel; m["m4"] = m4
    return m


GELU_C = 1.5957691216057308


def mixer_c_phases(self, hT):
    nc = self.nc
    CP0, CL0 = 2, 261
    CPAD = 4358
    xbT = DT(self.scr("xbT", [KT, 128, P], F32).ap.rearrange("k p t -> p k t"))
    gelT = DT(self.scr("gelT", [KT, 128, P], F32).ap.rearrange("k p t -> p k t"))
    win = self.din["c_w_in"]
    with ExitStack() as ph:
        pre = self.sb(ph, "cpre", [128, CPAD], BF16)
        self.memset(pre, 0.0)
        ccw = self.sb(ph, "ccw", [128, KT, 4], F32)
        ccb = self.sb(ph, "ccb", [128, KT], F32)
        self.dma(ccw, self.din["ccw"])
        self.dma(ccb, self.din["ccb"])
        wg = [self.sb(ph, "cwg%d" % k, [128, KT, 128], BF16) for k in range(2)]
        wx = [self.sb(ph, "cwx%d" % k, [128, KT, 128], BF16) for k in range(2)]
        dw = [self.sb(ph, "cdw%d" % k, [128, 4, 128], BF16) for k in range(2)]
        gx = [self.sb(ph, "cgx%d" % k, [128, 512], F32) for k in range(2)]
        gt = [self.sb(ph, "cgt%d" % k, [128, 512], F32) for k in range(2)]
        xo = [self.sb(ph, "cxo%d" % k, [128, 512], F32) for k in range(2)]
        it = 0
        for c in range(KT):
            g_w, x_w, d_w = wg[c % 2], wx[c % 2], dw[c % 2]
            self.dma(g_w, win[:, :, c * 128:(c + 1) * 128], q="pool")
            self.dma(x_w, win[:, :, D + c * 128:D + (c + 1) * 128], q="pool")
            for k in range(4):
                self.ts(d_w[:, k, :], self.identb, ccw[:, c, k:k + 1], ALU.mult)
            for (p0, n, j) in TILES:
                ps = self.nextps()
                for kt in range(KT):
                    self.mm(ps[:, :n], x_w[:, kt, :], hT[:, kt, p0:p0 + n], start=(kt == 0), stop=(kt == KT - 1))
                off = CP0 + p0 if j == 1 else CL0 + (p0 - TC)
                self.copy(pre[:, off:off + n], ps[:, :n], e="act")
            for (p0, n, j) in TILES:
                off = CP0 + p0 if j == 1 else CL0 + (p0 - TC)
                ps = self.nextps()
                for k in range(4):
                    self.mm(ps[:, :n], d_w[:, k, :], pre[:, off + k - 2:off + k - 2 + n], start=(k == 0), stop=(k == 3))
                x_o = xo[it % 2]
                self.act(x_o[:, :n], ps[:, :n], AF.Identity, bias=ccb[:, c:c + 1], scale=1.0)
                self.dma(xbT.t(xbT.ap[:, c, p0:p0 + n], p0, n), x_o[:, :n])
                if j == 0:
                    ps2 = self.nextps()
                    for kt in range(KT):
                        self.mm(ps2[:, :n], g_w[:, kt, :], hT[:, kt, p0:p0 + n], start=(kt == 0), stop=(kt == KT - 1))
                    g_x, g_t = gx[it % 2], gt[it % 2]
                    self.copy(g_x[:, :n], ps2[:, :n], e="act")
                    self.tt(g_t[:, :n], g_x[:, :n], g_x[:, :n], ALU.mult, e="pool")
                    self.ts(g_t[:, :n], g_t[:, :n], 0.044715, ALU.mult, 1.0, ALU.add)
                    self.tt(g_t[:, :n], g_t[:, :n], g_x[:, :n], ALU.mult, e="pool")
                    self.act(g_t[:, :n], g_t[:, :n], AF.Sigmoid, scale=GELU_C)
                    self.tt(g_x[:, :n], g_x[:, :n], g_t[:, :n], ALU.mult)
                    self.dma(gelT.t(gelT.ap[:, c, p0:p0 + n], p0, n), g_x[:, :n])
                it += 1
        self.barrier()
    return xbT, gelT


def mixer_c_scan(self, xbT, gelT):
    nc = self.nc
    with ExitStack() as ph:
        cgw = self.sb(ph, "cgw", [128, 16, 512], BF16)
        for k in range(16):
            self.dma(cgw[:, k, :], self.din["cgw"][:, k, :], q="pool")
        cgb = self.sb(ph, "cgb", [128, 2, 4, 4], F32)
        lam = self.sb(ph, "clam", [128, 2, KT], F32)
        scl = self.sb(ph, "cscl", [128, 2, KT], F32)
        scl2 = self.sb(ph, "cscl2", [128, 2, KT], F32)
        self.dma(cgb, self.din["cgb"])
        self.dma(lam, self.din["clam"])
        one = self.ones[:, 0:1]
        self.act(scl, lam, AF.Exp, scale=-1.0)
        self.act(scl, scl, AF.Ln, bias=one, scale=1.0)
        self.ts(scl2, scl, -16.0, ALU.mult)
        self.ts(scl, scl, -8.0, ALU.mult)
        xbf = self.sb(ph, "cxbf", [128, 2, P], F32)
        xbb = self.sb(ph, "cxbb", [128, 2, P], BF16)
        rb = self.sb(ph, "crb", [128, P], F32)
        ib = self.sb(ph, "cib", [128, P], F32)
        ab = self.sb(ph, "cab", [128, P], F32)
        hs = [self.sb(ph, "chs%d" % k, [128, P], F32) for k in range(2)]
        gl = [self.sb(ph, "cgl%d" % k, [128, 512], F32) for k in range(2)]
        yo = [self.sb(ph, "cyo%d" % k, [128, 512], BF16) for k in range(2)]
        it = 0
        for h in range(4):
            for half in range(2):
                self.dma(xbf[:, half, :], T(xbT.ap[:, 2 * h + half, :], tuple(xbT.slots)))
                self.copy(xbb[:, half, :], xbf[:, half, :], e="pool")
            for half in range(2):
                c = 2 * h + half
                for d in range(2):
                    for (p0, n, j) in TILES:
                        psr = self.nextps()
                        psi = self.nextps()
                        for k2 in range(2):
                            wbase = cgw[:, (d * 4 + h) * 2 + k2, :]
                            self.mm(psr[:, :n], wbase[:, half * 128:(half + 1) * 128], xbb[:, k2, p0:p0 + n],
                                    start=(k2 == 0), stop=(k2 == 1))
                        for k2 in range(2):
                            wbase = cgw[:, (d * 4 + h) * 2 + k2, :]
                            self.mm(psi[:, :n], wbase[:, 256 + half * 128:256 + (half + 1) * 128], xbb[:, k2, p0:p0 + n],
                                    start=(k2 == 0), stop=(k2 == 1))
                        self.act(rb[:, p0:p0 + n], psr[:, :n], AF.Sigmoid, bias=cgb[:, d, h, half:half + 1], scale=1.0)
                        self.act(ib[:, p0:p0 + n], psi[:, :n], AF.Sigmoid, bias=cgb[:, d, h, 2 + half:3 + half], scale=1.0)
                    self.act(ab, rb, AF.Exp, scale=scl[:, d, c:c + 1])
                    self.act(rb, rb, AF.Exp, scale=scl2[:, d, c:c + 1])
                    self.ts(rb, rb, -1.0, ALU.mult, 1.0, ALU.add)
                    self.act(rb, rb, AF.Sqrt)
                    self.tt(ib, ib, xbf[:, half, :], ALU.mult, e="pool")
                    self.tt(rb, rb, ib, ALU.mult)
                    h_ = hs[d]
                    if d == 0:
                        self.op("dve", lambda h_=h_: nc.vector.tensor_tensor_scan(h_.ap, ab.ap, rb.ap, 0.0, ALU.mult, ALU.add),
                                r=[ab, rb], w=[h_])
                    else:
                        self.op("dve", lambda h_=h_: nc.vector.tensor_tensor_scan(
                            h_.ap[:, TC - 1::-1], ab.ap[:, TC - 1::-1], rb.ap[:, TC - 1::-1], 0.0, ALU.mult, ALU.add),
                            r=[ab, rb], w=[h_])
                        self.op("dve", lambda h_=h_: nc.vector.tensor_tensor_scan(
                            h_.ap[:, P - 1:TC - 1:-1], ab.ap[:, P - 1:TC - 1:-1], rb.ap[:, P - 1:TC - 1:-1],
                            h_.ap[:, 0:1], ALU.mult, ALU.add), r=[ab, rb, h_], w=[h_])
                self.tt(hs[0], hs[0], hs[1], ALU.add, e="pool")
                for (p0, n, j) in LTILES:
                    g_ = gl[it % 2]
                    y_ = yo[it % 2]
                    it += 1
                    self.dma(g_[:, :n], gelT.t(gelT.ap[:, c, p0:p0 + n], p0, n))
                    self.tt(y_[:, :n], g_[:, :n], hs[0][:, p0:p0 + n], ALU.mult)
                    self.dma(self.yT.t(self.yT.ap[:, c, p0:p0 + n], p0, n), y_[:, :n])
        self.barrier()


Prog.mixer_c_phases = mixer_c_phases
Prog.mixer_c_scan = mixer_c_scan


NTM = 3088
CP0, CL0, CPAD = 2, 261, 4358


def psb(ps):
    return T(ps.ap.bitcast(BF16), ps.s)


def ab_proj_phase(self, hT):
    self.Atm = DT(self.scr("Atm", [P, NTM], F32).ap)
    self.preT = self.scr("preT", [12, 128, CPAD], BF16)
    win = self.din["ab_w_in"]
    with ExitStack() as ph:
        wb = [self.sb(ph, "abw%d" % k, [128, KT, 512], BF16) for k in range(2)]
        ob = [self.sb(ph, "abo%d" % k, [128, 512], F32) for k in range(3)]
        blocks = [(0, 0, 512, "silu"), (512, 512, 512, "c"), (1024, 1024, 512, "c"), (1536, 1536, 512, "c"),
                  (2048, 2048, 512, "silu"), (4096, 2560, 512, "silu"), (4608, 3072, 16, "c")]
        it = 0
        for bi, (wc, oc, ncol, kind) in enumerate(blocks):
            w = wb[bi % 2]
            self.dma(w[:, :, :ncol], win[:, :, wc:wc + ncol], q="pool")
            for i in range(NCH):
                ps = self.nextps()
                for kt in range(KT):
                    self.mm(ps[:, :ncol], hT[:, kt, i * 128:(i + 1) * 128], w[:, kt, :ncol],
                            start=(kt == 0), stop=(kt == KT - 1))
                o = ob[it % 3]
                it += 1
                if kind == "silu":
                    self.act(o[:, :ncol], ps[:, :ncol], AF.Silu)
                else:
                    self.copy(o[:, :ncol], ps[:, :ncol], e="dve")
                self.dma(self.Atm.t(self.Atm.ap[i * 128:(i + 1) * 128, oc:oc + ncol], i * 128, 128), o[:, :ncol])
        pre = [self.sb(ph, "abpre%d" % k, [128, CPAD], BF16) for k in range(2)]
        wq = [self.sb(ph, "abwq%d" % k, [128, KT, 128], BF16) for k in range(2)]
        for k in range(2):
            self.memset(pre[k], 0.0)
        for c in range(12):
            w = wq[c % 2]
            pr = pre[c % 2]
            self.dma(w, win[:, :, 2560 + c * 128:2560 + (c + 1) * 128], q="pool")
            for (p0, n, j) in TILES:
                ps = self.nextps()
                for kt in range(KT):
                    self.mm(ps[:, :n], w[:, kt, :], hT[:, kt, p0:p0 + n], start=(kt == 0), stop=(kt == KT - 1))
                off = CP0 + p0 if j == 1 else CL0 + (p0 - TC)
                self.copy(pr[:, off:off + n], ps[:, :n], e=("act" if (p0 // 512) % 2 else "dve"))
            self.dma(self.preT[c], pr)
        self.barrier()


def chunk_order(d):
    return list(range(NCH)) if d == 0 else [1, 0] + list(range(NCH - 1, 1, -1))


def hgrn_pass(self, d, st_p):
    nc = self.nc
    sc = 128.0 ** -0.5
    with ExitStack() as ph:
        S = self.sb(ph, "hS", [128, 4, 128], F32)
        self.memset(S, 0.0)
        Mrel = self.cst[:, 4 + d, :]
        sel = self.selc[:, d, :]
        m4 = self.m4c[:, d, :]
        lbB = self.lbB[:, d, :]
        omlb = self.omlb[:, d, :]
        inb = [self.sb(ph, "hin%d" % k, [128, 2048], F32) for k in range(2)]
        sg = self.sb(ph, "hsg", [128, 512], F32)
        lf = [self.sb(ph, "hlf%d" % k, [128, 512], F32) for k in range(2)]
        kk = self.sb(ph, "hkk", [128, 512], F32)
        E1 = self.sb(ph, "hE1", [128, 512], F32)
        E2 = self.sb(ph, "hE2", [128, 512], F32)
        Ee = [self.sb(ph, "hEe%d" % k, [128, 12], F32) for k in range(2)]
        qin = [self.sb(ph, "hqin%d" % k, [128, 512], BF16) for k in range(2)]
        kin = [self.sb(ph, "hkin%d" % k, [128, 512], BF16) for k in range(2)]
        vbf = [self.sb(ph, "hvbf%d" % k, [128, 512], BF16) for k in range(2)]
        qT = [self.sb(ph, "hqT%d" % k, [128, 4, 128], BF16) for k in range(2)]
        kT = [self.sb(ph, "hkT%d" % k, [128, 4, 128], BF16) for k in range(2)]
        at = [self.sb(ph, "hat%d" % k, [128, 512], BF16) for k in range(2)]
        St = [self.sb(ph, "hSt%d" % k, [128, 4, 128], BF16) for k in range(2)]
        tmpS = self.sb(ph, "htmpS", [128, 4, 128], F32)
        ob = [self.sb(ph, "hob%d" % k, [128, 512], F32) for k in range(2)]
        if d == 1:
            oa = [self.sb(ph, "hoa%d" % k, [128, 512], F32) for k in range(2)]
            ag = [self.sb(ph, "hag%d" % k, [128, 512], F32) for k in range(2)]
            hng = self.sb(ph, "hng", [128, 512], F32)
            self.dma(hng, self.din["hng"])
            junk = self.sb(ph, "hjunk", [128, 128], F32)
            ss = [self.sb(ph, "hss%d" % k, [128, 4], F32) for k in range(2)]
            ya = [self.sb(ph, "hya%d" % k, [128, 512], BF16) for k in range(2)]
            yaT = [self.sb(ph, "hyaT%d" % k, [128, 4, 128], BF16) for k in range(2)]
        for it, ci in enumerate(chunk_order(d)):
            b2 = it % 2
            p0 = ci * 128
            x_ = inb[b2]
            self.dma(x_, self.Atm.t(self.Atm.ap[p0:p0 + 128, 0:2048], p0, 128))
            af = x_[:, 512 * (1 + d):512 * (2 + d)]
            self.act(sg, af, AF.Sigmoid)
            self.tt(sg, sg, omlb, ALU.mult)
            self.tt(sg, sg, lbB, ALU.add, e="pool")
            l_ = lf[b2]
            self.act(l_, sg, AF.Ln)
            self.ts(kk, sg, -1.0, ALU.mult, 1.0, ALU.add, e="pool")
            psb_ = self.nextps()
            self.mm(psb_, Mrel, l_)
            pse = self.nextps()
            for h in range(4):
                self.mm(pse[:, 3 * h:3 * h + 3], l_[:, h * 128:(h + 1) * 128], sel)
            self.act(E1, psb_, AF.Exp)
            self.act(E2, psb_, AF.Exp, scale=-1.0)
            e_ = Ee[b2]
            self.act(e_, pse[:, 0:12], AF.Exp)
            q_, k_, v_ = qin[b2], kin[b2], vbf[b2]
            self.stt(q_, x_[:, 0:512], sc, E1, ALU.mult, ALU.mult)
            self.tt(k_, kk, E2, ALU.mult, e="pool")
            self.copy(v_, x_[:, 1536:2048], e="pool")
            psq = self.nextps()
            psk = self.nextps()
            for h in range(4):
                self.tr(psb(psq)[:, h * 128:(h + 1) * 128], q_[:, h * 128:(h + 1) * 128], self.identb)
            for h in range(4):
                self.tr(psb(psk)[:, h * 128:(h + 1) * 128], k_[:, h * 128:(h + 1) * 128], self.identb)
            qT_, kT_ = qT[b2], kT[b2]
            self.copy(T(qT_.ap.rearrange("p h t -> p (h t)"), qT_.s), psb(psq)[:, 0:512], e="dve")
            self.copy(T(kT_.ap.rearrange("p h t -> p (h t)"), kT_.s), psb(psk)[:, 0:512], e="act")
            psa = self.nextps()
            for h in range(4):
                self.mm(psa[:, h * 128:(h + 1) * 128], kT_[:, h, :], qT_[:, h, :])
            a_ = at[b2]
            self.ts(E1, psa, 1e30, ALU.min, -1e30, ALU.max)
            self.tt(a_, E1, m4, ALU.mult, e="pool")
            St_ = St[b2]
            for h in range(4):
                self.act(St_[:, h, :], S[:, h, :], AF.Copy, scale=e_[:, 3 * h:3 * h + 1])
            pso = self.nextps()
            for h in range(4):
                hs = slice(h * 128, (h + 1) * 128)
                self.mm(pso[:, hs], a_[:, hs], v_[:, hs], start=True, stop=False)
                self.mm(pso[:, hs], qT_[:, h, :], St_[:, h, :], start=False, stop=True)
            psd = self.nextps()
            for h in range(4):
                hs = slice(h * 128, (h + 1) * 128)
                self.mm(psd[:, hs], k_[:, hs], v_[:, hs])
            for h in range(4):
                hs = slice(h * 128, (h + 1) * 128)
                self.act(tmpS[:, h, :], psd[:, hs], AF.Copy, scale=e_[:, 3 * h + 2:3 * h + 3])
                self.stt(S[:, h, :], S[:, h, :], e_[:, 3 * h + 1:3 * h + 2], tmpS[:, h, :], ALU.mult, ALU.add)
            o_ = ob[b2]
            if d == 0:
                self.copy(o_, pso, e="dve")
                self.dma(self.OA.t(self.OA.ap[p0:p0 + 128, :], p0, 128), o_)
            else:
                oa_, ag_ = oa[b2], ag[b2]
                self.dma(oa_, self.OA.t(self.OA.ap[p0:p0 + 128, :], p0, 128))
                self.dma(ag_, self.Atm.t(self.Atm.ap[p0:p0 + 128, 2048:2560], p0, 128))
                self.tt(o_, pso, oa_, ALU.add)
                self.merge(o_, ag_, hng, junk, ss[b2], ya[b2], yaT[b2], 0, p0)
        self.barrier()


def merge(self, o_, gate_, gn, junk, ss_, ya_, yaT_, kt0, p0):
    for h in range(4):
        self.act(junk, o_[:, h * 128:(h + 1) * 128], AF.Square, accum=ss_[:, h:h + 1])
    self.act(ss_, ss_, AF.Sqrt, bias=self.epsc, scale=1.0 / 128)
    self.recip(ss_, ss_)
    self.tt(gate_, gate_, gn, ALU.mult, e="pool")
    for h in range(4):
        hs = slice(h * 128, (h + 1) * 128)
        self.stt(ya_[:, hs], o_[:, hs], ss_[:, h:h + 1], gate_[:, hs], ALU.mult, ALU.mult)
    pst = self.nextps()
    for h in range(4):
        self.tr(psb(pst)[:, h * 128:(h + 1) * 128], ya_[:, h * 128:(h + 1) * 128], self.identb)
    self.copy(T(yaT_.ap.rearrange("p h t -> p (h t)"), yaT_.s), psb(pst)[:, 0:512], e="act")
    self.dma(self.yT.t(self.yT.ap[:, kt0:kt0 + 4, p0:p0 + 128], p0, 128), yaT_)


def ab_setup(self, st):
    self.selc = self.sb(st, "selc", [128, 2, 3], F32)
    self.m4c = self.sb(st, "m4c", [128, 2, 512], F32)
    self.dma(self.selc, self.din["sel"])
    self.dma(self.m4c, self.din["m4"])
    hl = self.sb(st, "hlb", [128, 2, 3, 512], F32)
    self.dma(hl, self.din["hlb"])
    self.lbB = self.sb(st, "lbB", [128, 2, 512], F32)
    self.omlb = self.sb(st, "omlb", [128, 2, 512], F32)
    self.act(hl, hl, AF.Exp)
    self.tt(self.omlb, hl[:, :, 0, :], hl[:, :, 1, :], ALU.add)
    self.tt(self.omlb, self.omlb, hl[:, :, 2, :], ALU.add)
    self.recip(self.omlb, self.omlb)
    self.tt(self.lbB, hl[:, :, 0, :], self.omlb, ALU.mult)
    self.ts(self.omlb, self.lbB, -1.0, ALU.mult, 1.0, ALU.add)
    self.OA = DT(self.scr("OA", [P, 512], F32).ap)
    self.OB = DT(self.scr("OB", [P, 512], F32).ap)


Prog.ab_proj_phase = ab_proj_phase
Prog.hgrn_pass = hgrn_pass
Prog.merge = merge
Prog.ab_setup = ab_setup


def gdn_setup(self, st):
    self.gG = self.sb(st, "gG", [128, NCH, 8], F32)
    self.gB = self.sb(st, "gB", [128, NCH, 8], F32)
    self.gNB = self.sb(st, "gNB", [128, NCH, 8], F32)
    with ExitStack() as ph:
        gin = self.sb(ph, "gin", [128, NCH, 16], F32)
        al = self.sb(ph, "galog", [128, 8], F32)
        db = self.sb(ph, "gdtb", [128, 8], F32)
        self.dma(al, self.din["galog"])
        self.dma(db, self.din["gdtb"])
        for i in range(NCH):
            self.dma(gin[:, i, :], self.Atm.t(self.Atm.ap[i * 128:(i + 1) * 128, 3072:3088], i * 128, 128))
        for c in range(8):
            self.ts(self.gG[:, :, c], gin[:, :, c], db[:, c:c + 1], ALU.add)
        self.act(self.gG, self.gG, AF.Exp)
        self.act(self.gG, self.gG, AF.Ln, bias=self.ones[:, 0:1], scale=1.0)
        self.act(al, al, AF.Exp)
        for c in range(8):
            self.ts(self.gG[:, :, c], self.gG[:, :, c], al[:, c:c + 1], ALU.mult, -1.0, ALU.mult)
        self.act(self.gB, gin[:, :, 8:16], AF.Sigmoid)
        self.ts(self.gNB, self.gB, -1.0, ALU.mult)
        self.barrier()


def gdn_pass(self, d):
    nc = self.nc
    sc = 128.0 ** -0.5
    with ExitStack() as ph:
        S = self.sb(ph, "gS", [128, 4, 128], F32)
        Sb = self.sb(ph, "gSb", [128, 4, 128], BF16)
        self.memset(S, 0.0)
        self.memset(Sb, 0.0)
        TRI = self.cst[:, 2 + d, :]
        NEGT = self.cst[:, 6 + d, :]
        POSA = self.cst[:, 8 + d, :]
        gcw = self.sb(ph, "gcw", [128, 12, 4], F32)
        self.dma(gcw, self.din["gcw"])
        dwg = self.sb(ph, "gdw", [128, 12, 4, 128], BF16)
        for c in range(12):
            for k in range(4):
                self.ts(dwg[:, c, k, :], self.identb, gcw[:, c, k:k + 1], ALU.mult)
        I4 = self.sb(ph, "gI4", [128, 4, 128], F32)
        for h in range(4):
            self.copy(I4[:, h, :], self.ident, e="pool")
        pw = [self.sb(ph, "gpw%d" % k, [128, 12, 131], BF16) for k in range(2)]
        qs = self.sb(ph, "gqs", [128, 512], F32)
        ks = self.sb(ph, "gks", [128, 512], F32)
        vs = self.sb(ph, "gvs", [128, 512], F32)
        junk = self.sb(ph, "gjunk", [128, 128], F32)
        rn = self.sb(ph, "grn", [128, 8], F32)
        rnq = self.sb(ph, "grnq", [128, 4], F32)
        gcs = self.sb(ph, "ggcs", [128, 8], F32)
        ngc = self.sb(ph, "gngc", [128, 4], F32)
        egc = self.sb(ph, "gegc", [128, 4], F32)
        ekd = self.sb(ph, "gekd", [128, 4], F32)
        egl = self.sb(ph, "gegl", [128, 4], F32)
        bw = self.sb(ph, "gbw", [128, 4], F32)
        gbc = self.sb(ph, "ggbc", [128, 4, 128], F32)
        decT = self.sb(ph, "gdecT", [128, 512], F32)
        decA = self.sb(ph, "gdecA", [128, 4, 128], F32)
        khb = self.sb(ph, "gkhb", [128, 512], BF16)
        qhb = self.sb(ph, "gqhb", [128, 512], BF16)
        khT = self.sb(ph, "gkhT", [128, 4, 128], BF16)
        qhT = self.sb(ph, "gqhT", [128, 4, 128], BF16)
        X = [self.sb(ph, "gX%d" % k, [128, 4, 128], F32) for k in range(2)]
        XT = [self.sb(ph, "gXT%d" % k, [128, 4, 128], F32) for k in range(2)]
        PT = self.sb(ph, "gPT", [128, 4, 128], F32)
        RU = self.sb(ph, "gRU", [128, 4, 128], F32)
        RW = self.sb(ph, "gRW", [128, 4, 128], F32)
        u = self.sb(ph, "gu", [128, 512], F32)
        wT = self.sb(ph, "gwT", [128, 4, 128], BF16)
        atT = self.sb(ph, "gatT", [128, 512], BF16)
        qg = self.sb(ph, "gqg", [128, 512], BF16)
        qgT = self.sb(ph, "gqgT", [128, 4, 128], BF16)
        kd = self.sb(ph, "gkd", [128, 512], BF16)
        vn = self.sb(ph, "gvn", [128, 512], BF16)
        ob = [self.sb(ph, "gob%d" % k, [128, 512], F32) for k in range(2)]
        if d == 1:
            oa = [self.sb(ph, "goa%d" % k, [128, 512], F32) for k in range(2)]
            ag = [self.sb(ph, "gag%d" % k, [128, 512], F32) for k in range(2)]
            gng = self.sb(ph, "gng", [128, 512], F32)
            self.dma(gng, self.din["gng"])
            ss = [self.sb(ph, "gss%d" % k, [128, 4], F32) for k in range(2)]
            ya = [self.sb(ph, "gya%d" % k, [128, 512], BF16) for k in range(2)]
            yaT = [self.sb(ph, "gyaT%d" % k, [128, 4, 128], BF16) for k in range(2)]
        f2 = lambda t: T(t.ap.rearrange("p h t -> p (h t)"), t.s)
        H = [slice(h * 128, (h + 1) * 128) for h in range(4)]
        for it, ci in enumerate(chunk_order(d)):
            b2 = it % 2
            p0 = ci * 128
            off = CP0 + p0 if ci < 2 else CL0 + (p0 - TC)
            w_ = pw[b2]
            self.dma(w_, T(self.preT.ap[:, :, off - 2:off + 129].rearrange("c p t -> p c t"), self.preT.s))
            pss = [self.nextps() for _ in range(3)]
            for c in range(12):
                for k in range(4):
                    self.mm(pss[c // 4][:, H[c % 4]], w_[:, c, k:k + 128], dwg[:, c, k, :], start=(k == 0), stop=(k == 3))
            self.act(qs, pss[0], AF.Silu)
            self.act(ks, pss[1], AF.Silu)
            self.act(vs, pss[2], AF.Silu)
            for h in range(4):
                self.act(junk, qs[:, H[h]], AF.Square, accum=rn[:, h:h + 1])
                self.act(junk, ks[:, H[h]], AF.Square, accum=rn[:, 4 + h:5 + h])
            self.act(rn, rn, AF.Sqrt, bias=self.epsc, scale=1.0)
            self.recip(rn, rn)
            self.ts(rnq, rn[:, 0:4], sc, ALU.mult)
            g4 = self.gG[:, ci, 4 * d:4 * d + 4]
            b4 = self.gB[:, ci, 4 * d:4 * d + 4]
            nb4 = self.gNB[:, ci, 4 * d:4 * d + 4]
            psg = self.nextps()
            self.mm(psg[:, 0:4], TRI, g4)
            self.mm(psg[:, 4:8], self.ones, g4)
            self.copy(gcs, psg[:, 0:8], e="dve")
            self.ts(ngc, gcs[:, 0:4], -1.0, ALU.mult)
            self.act(egc, gcs[:, 0:4], AF.Exp)
            self.act(egl, gcs[:, 4:8], AF.Exp)
            self.tt(ekd, gcs[:, 4:8], gcs[:, 0:4], ALU.subtract)
            self.act(ekd, ekd, AF.Exp)
            self.tt(bw, b4, egc, ALU.mult)
            for h in range(4):
                self.ts(gbc[:, h, :], self.ones, g4[:, h:h + 1], ALU.mult, e="pool")
            psD = self.nextps()
            psA = self.nextps()
            for h in range(4):
                self.mm(psD[:, H[h]], gbc[:, h, :], TRI, start=True, stop=False)
                self.mm(psD[:, H[h]], self.ident, NEGT, start=False, stop=True)
            for h in range(4):
                self.mm(psA[:, H[h]], gbc[:, h, :], TRI, start=True, stop=False)
                self.mm(psA[:, H[h]], self.ident, POSA, start=False, stop=True)
            for h in range(4):
                self.act(decT[:, H[h]], psD[:, H[h]], AF.Exp, bias=ngc[:, h:h + 1], scale=1.0)
                self.act(decA[:, h, :], psA[:, H[h]], AF.Exp, bias=gcs[:, h:h + 1], scale=-1.0)
            for h in range(4):
                self.act(khb[:, H[h]], ks[:, H[h]], AF.Copy, scale=rn[:, 4 + h:5 + h])
                self.act(qhb[:, H[h]], qs[:, H[h]], AF.Copy, scale=rnq[:, h:h + 1])
            pst = self.nextps()
            pst2 = self.nextps()
            for h in range(4):
                self.tr(psb(pst)[:, H[h]], khb[:, H[h]], self.identb)
            for h in range(4):
                self.tr(psb(pst2)[:, H[h]], qhb[:, H[h]], self.identb)
            self.copy(f2(khT), psb(pst)[:, 0:512], e="dve")
            self.copy(f2(qhT), psb(pst2)[:, 0:512], e="act")
            psK = self.nextps()
            for h in range(4):
                self.mm(psK[:, H[h]], khT[:, h, :], khT[:, h, :])
            X0, XT0 = X[0], XT[0]
            for h in range(4):
                self.stt(X0[:, h, :], psK[:, H[h]], nb4[:, h:h + 1], decA[:, h, :], ALU.mult, ALU.mult)
            psn = self.nextps()
            for h in range(4):
                self.tr(psn[:, H[h]], X0[:, h, :], self.ident)
            self.copy(f2(XT0), psn, e="act")
            self.tt(f2(PT), f2(XT0), f2(I4), ALU.add, e="pool")
            cur = 0
            for m in range(6):
                Xc, XTc, Xn, XTn = X[cur], XT[cur], X[1 - cur], XT[1 - cur]
                ps1 = self.nextps()
                for h in range(4):
                    self.mm(ps1[:, H[h]], XTc[:, h, :], Xc[:, h, :])
                self.copy(f2(Xn), ps1, e="act")
                if m < 5:
                    ps2 = self.nextps()
                    for h in range(4):
                        self.mm(ps2[:, H[h]], Xc[:, h, :], XTc[:, h, :])
                    self.copy(f2(XTn), ps2, e="dve")
                ps3 = self.nextps()
                for h in range(4):
                    self.mm(ps3[:, H[h]], Xn[:, h, :], PT[:, h, :])
                self.tt(f2(PT), f2(PT), ps3, ALU.add)
                cur = 1 - cur
            for h in range(4):
                self.ts(RU[:, h, :], vs[:, H[h]], b4[:, h:h + 1], ALU.mult, e="pool")
                self.ts(RW[:, h, :], ks[:, H[h]], rn[:, 4 + h:5 + h], ALU.mult, bw[:, h:h + 1], ALU.mult, e="pool")
            psu = self.nextps()
            psw = self.nextps()
            for h in range(4):
                self.mm(psu[:, H[h]], PT[:, h, :], RU[:, h, :])
            for h in range(4):
                self.mm(psw[:, H[h]], RW[:, h, :], PT[:, h, :])
            self.copy(u, psu, e="act")
            self.copy(f2(wT), psw, e="dve")
            psq = self.nextps()
            for h in range(4):
                self.mm(psq[:, H[h]], khT[:, h, :], qhT[:, h, :])
            self.tt(atT, psq, decT, ALU.mult)
            for h in range(4):
                self.ts(qg[:, H[h]], qhb[:, H[h]], egc[:, h:h + 1], ALU.mult, e="pool")
                self.ts(kd[:, H[h]], khb[:, H[h]], ekd[:, h:h + 1], ALU.mult, e="pool")
            pst3 = self.nextps()
            for h in range(4):
                self.tr(psb(pst3)[:, H[h]], qg[:, H[h]], self.identb)
            self.copy(f2(qgT), psb(pst3)[:, 0:512], e="act")
            psws = self.nextps()
            for h in range(4):
                self.mm(psws[:, H[h]], wT[:, h, :], Sb[:, h, :])
            self.tt(vn, u, psws, ALU.subtract)
            pso = self.nextps()
            for h in range(4):
                self.mm(pso[:, H[h]], qgT[:, h, :], Sb[:, h, :], start=True, stop=False)
                self.mm(pso[:, H[h]], atT[:, H[h]], vn[:, H[h]], start=False, stop=True)
            psd = self.nextps()
            for h in range(4):
                self.mm(psd[:, H[h]], kd[:, H[h]], vn[:, H[h]])
            for h in range(4):
                self.stt(S[:, h, :], S[:, h, :], egl[:, h:h + 1], psd[:, H[h]], ALU.mult, ALU.add)
            self.copy(f2(Sb), f2(S), e="act")
            o_ = ob[b2]
            if d == 0:
                self.copy(o_, pso, e="dve")
                self.dma(self.OB.t(self.OB.ap[p0:p0 + 128, :], p0, 128), o_)
            else:
                oa_, ag_ = oa[b2], ag[b2]
                self.dma(oa_, self.OB.t(self.OB.ap[p0:p0 + 128, :], p0, 128))
                self.dma(ag_, self.Atm.t(self.Atm.ap[p0:p0 + 128, 2560:3072], p0, 128))
                self.tt(o_, pso, oa_, ALU.add)
                self.merge(o_, ag_, gng, junk, ss[b2], ya[b2], yaT[b2], 4, p0)
        self.barrier()


Prog.gdn_setup = gdn_setup
Prog.gdn_pass = gdn_pass


def build_full(debug=()):
    pg = Prog(debug=debug)
    pg.declare()
    pg.setup()
    pg.setup_streams()
    with ExitStack() as st:
        hT = pg.sb(st, "hT", [128, KT, P], BF16)
        pg.norm_phase(st, 0, 0, hT, TILES)
        pg.ab_proj_phase(hT)
        pg.barrier()
    with ExitStack() as st:
        pg.ab_setup2(st)
        pg.gdn_setup(st)
        pg.hgrn_pipe()
        pg.gdn_pipe()
        pg.merge_phase()
        pg.barrier()
    pg.outproj_phase(0, "ab_w_out", None, KT, pg.yT, 2, TILES)
    with ExitStack() as st:
        hT = pg.sb(st, "hT", [128, KT, P], BF16)
        pg.norm_phase(st, 0, 1, hT, TILES)
        pg.ffn_phase(0, hT, TILES)
        pg.barrier()
    pg.outproj_phase(0, "ffn_w_down", 0, FT, pg.uT, 5, TILES)
    with ExitStack() as st:
        hT = pg.sb(st, "hT", [128, KT, P], BF16)
        pg.norm_phase(st, 1, 0, hT, TILES)
        xbT, gelT = pg.mixer_c_phases(hT)
        pg.barrier()
    pg.mixer_c_scan(xbT, gelT)
    pg.outproj_phase(1, "c_w_out", None, KT, pg.yT, 2, LTILES)
    with ExitStack() as st:
        hT = pg.sb(st, "hT", [128, KT, P], BF16)
        pg.norm_phase(st, 1, 1, hT, LTILES)
        pg.ffn_phase(1, hT, LTILES)
        pg.barrier()
    pg.outproj_phase(1, "ffn_w_down", 1, FT, pg.uT, 5, LTILES)
    pg.final_phase()
    pg.em.finish(_sl(pg.outs) + _sl(getattr(pg, "out_tiles", [])))
    return pg


def kernel(**inputs):
    inp = {k: np.asarray(v) for k, v in inputs.items()}
    B = inp["x"].shape[0]
    pg = build_full()
    shared = host_prep(inp, 0)
    in_maps = []
    for b in range(B):
        m = dict(shared)
        m["x"] = np.ascontiguousarray(inp["x"][b], dtype=np.float32)
        m["ctx"] = np.ascontiguousarray(inp["ctx"][b], dtype=np.float32)
        cv = np.stack([inp["c"][b], inp["c_ctx"]], axis=-1)
        m["cvec"] = np.ascontiguousarray(cv.reshape(KT, 128, 2).transpose(1, 0, 2), dtype=np.float32)
        in_maps.append({k: v for k, v in m.items() if k in pg.din})
    res = run_bass_kernel_spmd(pg.nc, in_maps, core_ids=list(range(B)))
    out = np.stack([np.asarray(r["out"], dtype=np.float32) for r in res.results], axis=0)
    return out


def hgrn_stream(self, d, pool, ph):
    nc = self.nc
    sc = 128.0 ** -0.5
    NP = lambda: self.nextps(pool)
    if True:
        S = self.sb(ph, "hS", [128, 4, 128], F32)
        self.memset(S, 0.0)
        Mrel = self.cst[:, 4 + d, :]
        sel = self.selc[:, d, :]
        m4 = self.m4c[:, d, :]
        lbB = self.lbB[:, d, :]
        omlb = self.omlb[:, d, :]
        inb = [self.sb(ph, "hin%d" % k, [128, 2048], F32) for k in range(2)]
        sg = [self.sb(ph, "hsg%d" % k, [128, 512], F32) for k in range(2)]
        lf = [self.sb(ph, "hlf%d" % k, [128, 512], F32) for k in range(2)]
        kk = [self.sb(ph, "hkk%d" % k, [128, 512], F32) for k in range(2)]
        E1 = [self.sb(ph, "hE1%d" % k, [128, 512], F32) for k in range(2)]
        E2 = [self.sb(ph, "hE2%d" % k, [128, 512], F32) for k in range(2)]
        cl = self.sb(ph, "hcl", [128, 512], F32)
        Ee = [self.sb(ph, "hEe%d" % k, [128, 12], F32) for k in range(2)]
        qin = [self.sb(ph, "hqin%d" % k, [128, 512], BF16) for k in range(2)]
        kin = [self.sb(ph, "hkin%d" % k, [128, 512], BF16) for k in range(2)]
        vbf = [self.sb(ph, "hvbf%d" % k, [128, 512], BF16) for k in range(2)]
        qT = [self.sb(ph, "hqT%d" % k, [128, 4, 128], BF16) for k in range(2)]
        kT = [self.sb(ph, "hkT%d" % k, [128, 4, 128], BF16) for k in range(2)]
        at = [self.sb(ph, "hat%d" % k, [128, 512], BF16) for k in range(2)]
        St = [self.sb(ph, "hSt%d" % k, [128, 4, 128], BF16) for k in range(2)]
        tmpS = self.sb(ph, "htmpS", [128, 4, 128], F32)
        ob = [self.sb(ph, "hob%d" % k, [128, 512], F32) for k in range(2)]
        f2 = lambda t: T(t.ap.rearrange("p h t -> p (h t)"), t.s)
        H = [slice(h * 128, (h + 1) * 128) for h in range(4)]
        OAd = self.OAd[d]
        for it, ci in enumerate(chunk_order(d)):
            b2 = it % 2
            p0 = ci * 128
            x_ = inb[b2]
            self.dma(x_, self.Atm.t(self.Atm.ap[p0:p0 + 128, 0:2048], p0, 128))
            af = x_[:, 512 * (1 + d):512 * (2 + d)]
            s_ = sg[b2]
            self.act(s_, af, AF.Sigmoid)
            self.tt(s_, s_, omlb, ALU.mult)
            self.tt(s_, s_, lbB, ALU.add, e="pool")
            yield
            l_ = lf[b2]
            self.act(l_, s_, AF.Ln)
            self.ts(kk[b2], s_, -1.0, ALU.mult, 1.0, ALU.add, e="pool")
            psb_ = NP()
            self.mm(psb_, Mrel, l_)
            pse = NP()
            for h in range(4):
                self.mm(pse[:, 3 * h:3 * h + 3], l_[:, H[h]], sel)
            yield
            self.act(E1[b2], psb_, AF.Exp)
            self.act(E2[b2], psb_, AF.Exp, scale=-1.0)
            e_ = Ee[b2]
            self.act(e_, pse[:, 0:12], AF.Exp)
            yield
            q_, k_, v_ = qin[b2], kin[b2], vbf[b2]
            self.stt(q_, x_[:, 0:512], sc, E1[b2], ALU.mult, ALU.mult)
            self.tt(k_, kk[b2], E2[b2], ALU.mult, e="pool")
            self.copy(v_, x_[:, 1536:2048], e="pool")
            yield
            psq = NP()
            psk = NP()
            for h in range(4):
                self.tr(psb(psq)[:, H[h]], q_[:, H[h]], self.identb)
            for h in range(4):
                self.tr(psb(psk)[:, H[h]], k_[:, H[h]], self.identb)
            qT_, kT_ = qT[b2], kT[b2]
            self.copy(f2(qT_), psb(psq)[:, 0:512], e="dve")
            self.copy(f2(kT_), psb(psk)[:, 0:512], e="act")
            yield
            psa = NP()
            for h in range(4):
                self.mm(psa[:, H[h]], kT_[:, h, :], qT_[:, h, :])
            a_ = at[b2]
            self.ts(cl, psa, 1e30, ALU.min, -1e30, ALU.max)
            self.tt(a_, cl, m4, ALU.mult, e="pool")
            yield
            St_ = St[b2]
            for h in range(4):
                self.act(St_[:, h, :], S[:, h, :], AF.Copy, scale=e_[:, 3 * h:3 * h + 1])
            yield
            pso = NP()
            for h in range(4):
                self.mm(pso[:, H[h]], a_[:, H[h]], v_[:, H[h]], start=True, stop=False)
                self.mm(pso[:, H[h]], qT_[:, h, :], St_[:, h, :], start=False, stop=True)
            psd = NP()
            for h in range(4):
                self.mm(psd[:, H[h]], k_[:, H[h]], v_[:, H[h]])
            yield
            for h in range(4):
                self.act(tmpS[:, h, :], psd[:, H[h]], AF.Copy, scale=e_[:, 3 * h + 2:3 * h + 3])
                self.stt(S[:, h, :], S[:, h, :], e_[:, 3 * h + 1:3 * h + 2], tmpS[:, h, :], ALU.mult, ALU.add)
            o_ = ob[b2]
            self.copy(o_, pso, e="dve")
            self.dma(OAd.t(OAd.ap[p0:p0 + 128, :], p0, 128), o_)
            yield


def gdn_shared(self, st):
    gcw = self.sb(st, "gcw", [128, 12, 4], F32)
    self.dma(gcw, self.din["gcw"])
    self.dwg = self.sb(st, "gdw", [128, 12, 4, 128], BF16)
    for c in range(12):
        for k in range(4):
            self.ts(self.dwg[:, c, k, :], self.identb, gcw[:, c, k:k + 1], ALU.mult)
    self.I4 = self.sb(st, "gI4", [128, 4, 128], F32)
    for h in range(4):
        self.copy(self.I4[:, h, :], self.ident, e="pool")


def gdn_stream(self, d, pool, ph):
    nc = self.nc
    sc = 128.0 ** -0.5
    NP = lambda: self.nextps(pool)
    if True:
        S = self.sb(ph, "gS", [128, 4, 128], F32)
        Sb = self.sb(ph, "gSb", [128, 4, 128], BF16)
        self.memset(S, 0.0)
        self.memset(Sb, 0.0)
        TRI = self.cst[:, 2 + d, :]
        NEGT = self.cst[:, 6 + d, :]
        POSA = self.cst[:, 8 + d, :]
        dwg, I4 = self.dwg, self.I4
        pw = [self.sb(ph, "gpw%d" % k, [128, 12, 131], BF16) for k in range(2)]
        qs = self.sb(ph, "gqs", [128, 512], F32)
        ks = self.sb(ph, "gks", [128, 512], F32)
        vs = self.sb(ph, "gvs", [128, 512], F32)
        junk = self.sb(ph, "gjunk", [128, 128], F32)
        rn = self.sb(ph, "grn", [128, 8], F32)
        rnq = self.sb(ph, "grnq", [128, 4], F32)
        gcs = self.sb(ph, "ggcs", [128, 8], F32)
        ngc = self.sb(ph, "gngc", [128, 4], F32)
        egc = self.sb(ph, "gegc", [128, 4], F32)
        ekd = self.sb(ph, "gekd", [128, 4], F32)
        egl = self.sb(ph, "gegl", [128, 4], F32)
        bw = self.sb(ph, "gbw", [128, 4], F32)
        gbc = self.sb(ph, "ggbc", [128, 4, 128], F32)
        decT = self.sb(ph, "gdecT", [128, 512], F32)
        decA = self.sb(ph, "gdecA", [128, 4, 128], F32)
        khb = self.sb(ph, "gkhb", [128, 512], BF16)
        qhb = self.sb(ph, "gqhb", [128, 512], BF16)
        khT = self.sb(ph, "gkhT", [128, 4, 128], BF16)
        qhT = self.sb(ph, "gqhT", [128, 4, 128], BF16)
        X = [self.sb(ph, "gX%d" % k, [128, 4, 128], F32) for k in range(2)]
        XT = [self.sb(ph, "gXT%d" % k, [128, 4, 128], F32) for k in range(2)]
        PT = self.sb(ph, "gPT", [128, 4, 128], F32)
        RU = self.sb(ph, "gRU", [128, 4, 128], F32)
        RW = self.sb(ph, "gRW", [128, 4, 128], F32)
        u = self.sb(ph, "gu", [128, 512], F32)
        wT = self.sb(ph, "gwT", [128, 4, 128], BF16)
        atT = self.sb(ph, "gatT", [128, 512], BF16)
        qg = self.sb(ph, "gqg", [128, 512], BF16)
        qgT = self.sb(ph, "gqgT", [128, 4, 128], BF16)
        kd = self.sb(ph, "gkd", [128, 512], BF16)
        vn = self.sb(ph, "gvn", [128, 512], BF16)
        ob = [self.sb(ph, "gob%d" % k, [128, 512], F32) for k in range(2)]
        f2 = lambda t: T(t.ap.rearrange("p h t -> p (h t)"), t.s)
        R = (lambda t: T(t.ap.bitcast(F32R), t.s)) if USE_F32R else (lambda t: t)
        rb = self.sb(ph, "grb", [128, 4], F32)
        H = [slice(h * 128, (h + 1) * 128) for h in range(4)]
        OBd = self.OBd[d]
        for it, ci in enumerate(chunk_order(d)):
            b2 = it % 2
            p0 = ci * 128
            off = CP0 + p0 if ci < 2 else CL0 + (p0 - TC)
            w_ = pw[b2]
            self.dma(w_, T(self.preT.ap[:, :, off - 2:off + 129].rearrange("c p t -> p c t"), self.preT.s))
            pss = [NP() for _ in range(3)]
            for c in range(12):
                for k in range(4):
                    self.mm(pss[c // 4][:, H[c % 4]], w_[:, c, k:k + 128], dwg[:, c, k, :], start=(k == 0), stop=(k == 3))
                if c % 4 == 3:
                    yield
            self.act(qs, pss[0], AF.Silu)
            self.act(ks, pss[1], AF.Silu)
            self.act(vs, pss[2], AF.Silu)
            yield
            for h in range(4):
                self.act(junk, qs[:, H[h]], AF.Square, accum=rn[:, h:h + 1])
                self.act(junk, ks[:, H[h]], AF.Square, accum=rn[:, 4 + h:5 + h])
            self.act(rn, rn, AF.Sqrt, bias=self.epsc, scale=1.0)
            self.recip(rn, rn)
            self.ts(rnq, rn[:, 0:4], sc, ALU.mult)
            yield
            g4 = self.gG[:, ci, 4 * d:4 * d + 4]
            b4 = self.gB[:, ci, 4 * d:4 * d + 4]
            nb4 = self.gNB[:, ci, 4 * d:4 * d + 4]
            psg = NP()
            self.mm(psg[:, 0:4], TRI, g4)
            self.mm(psg[:, 4:8], self.ones, g4)
            self.copy(gcs, psg[:, 0:8], e="dve")
            self.ts(ngc, gcs[:, 0:4], -1.0, ALU.mult)
            self.act(egc, gcs[:, 0:4], AF.Exp)
            self.act(egl, gcs[:, 4:8], AF.Exp)
            self.tt(ekd, gcs[:, 4:8], gcs[:, 0:4], ALU.subtract)
            self.act(ekd, ekd, AF.Exp)
            self.tt(bw, b4, egc, ALU.mult)
            yield
            for h in range(4):
                self.ts(gbc[:, h, :], self.ones, g4[:, h:h + 1], ALU.mult, e=("pool" if h % 2 else "dve"))
            psD = NP()
            psA = NP()
            for h in range(4):
                self.mm(psD[:, H[h]], gbc[:, h, :], TRI, start=True, stop=False)
                self.mm(psD[:, H[h]], self.ident, NEGT, start=False, stop=True)
            yield
            for h in range(4):
                self.mm(psA[:, H[h]], gbc[:, h, :], TRI, start=True, stop=False)
                self.mm(psA[:, H[h]], self.ident, POSA, start=False, stop=True)
            yield
            for h in range(4):
                self.act(decT[:, H[h]], psD[:, H[h]], AF.Exp, bias=ngc[:, h:h + 1], scale=1.0)
                self.act(decA[:, h, :], psA[:, H[h]], AF.Exp, bias=gcs[:, h:h + 1], scale=-1.0)
            yield
            for h in range(4):
                self.act(khb[:, H[h]], ks[:, H[h]], AF.Copy, scale=rn[:, 4 + h:5 + h])
                self.ts(qhb[:, H[h]], qs[:, H[h]], rnq[:, h:h + 1], ALU.mult)
            yield
            pst = NP()
            pst2 = NP()
            for h in range(4):
                self.tr(psb(pst)[:, H[h]], khb[:, H[h]], self.identb)
            for h in range(4):
                self.tr(psb(pst2)[:, H[h]], qhb[:, H[h]], self.identb)
            self.copy(f2(khT), psb(pst)[:, 0:512], e="dve")
            self.copy(f2(qhT), psb(pst2)[:, 0:512], e="act")
            yield
            psK = NP()
            for h in range(4):
                self.mm(psK[:, H[h]], khT[:, h, :], khT[:, h, :])
            X0, XT0 = X[0], XT[0]
            for h in range(4):
                self.stt(R(X0[:, h, :]), psK[:, H[h]], nb4[:, h:h + 1], decA[:, h, :], ALU.mult, ALU.mult)
            yield
            psn = NP()
            for h in range(4):
                self.tr(psn[:, H[h]], X0[:, h, :], self.ident)
            self.copy(R(f2(XT0)), psn, e="act")
            self.tt(R(f2(PT)), f2(XT0), f2(I4), ALU.add)
            yield
            cur = 0
            for m in range(6):
                Xc, XTc, Xn, XTn = X[cur], XT[cur], X[1 - cur], XT[1 - cur]
                ps1 = NP()
                for h in range(4):
                    self.mm(ps1[:, H[h]], R(XTc[:, h, :]), R(Xc[:, h, :]))
                self.copy(R(f2(Xn)), ps1, e="act")
                yield
                if m < 5:
                    ps2 = NP()
                    for h in range(4):
                        self.mm(ps2[:, H[h]], R(Xc[:, h, :]), R(XTc[:, h, :]))
                    self.copy(R(f2(XTn)), ps2, e="dve")
                    yield
                ps3 = NP()
                for h in range(4):
                    self.mm(ps3[:, H[h]], R(Xn[:, h, :]), R(PT[:, h, :]))
                self.tt(R(f2(PT)), f2(PT), ps3, ALU.add)
                cur = 1 - cur
                yield
            for h in range(4):
                self.ts(R(RU[:, h, :]), vs[:, H[h]], b4[:, h:h + 1], ALU.mult)
                self.ts(R(RW[:, h, :]), ks[:, H[h]], rn[:, 4 + h:5 + h], ALU.mult, bw[:, h:h + 1], ALU.mult)
            yield
            psu = NP()
            psw = NP()
            for h in range(4):
                self.mm(psu[:, H[h]], R(PT[:, h, :]), R(RU[:, h, :]))
            for h in range(4):
                self.mm(psw[:, H[h]], R(RW[:, h, :]), R(PT[:, h, :]))
            self.copy(u, psu, e="act")
            self.copy(f2(wT), psw, e="dve")
            yield
            psq = NP()
            for h in range(4):
                self.mm(psq[:, H[h]], khT[:, h, :], qhT[:, h, :])
            self.tt(atT, psq, decT, ALU.mult)
            for h in range(4):
                self.ts(qg[:, H[h]], qhb[:, H[h]], egc[:, h:h + 1], ALU.mult, e="pool")
                self.act(kd[:, H[h]], khb[:, H[h]], AF.Copy, scale=ekd[:, h:h + 1])
            yield
            pst3 = NP()
            for h in range(4):
                self.tr(psb(pst3)[:, H[h]], qg[:, H[h]], self.identb)
            self.copy(f2(qgT), psb(pst3)[:, 0:512], e="act")
            yield
            psws = NP()
            for h in range(4):
                self.mm(psws[:, H[h]], wT[:, h, :], Sb[:, h, :])
            self.tt(vn, u, psws, ALU.subtract)
            yield
            pso = NP()
            for h in range(4):
                self.mm(pso[:, H[h]], qgT[:, h, :], Sb[:, h, :], start=True, stop=False)
                self.mm(pso[:, H[h]], atT[:, H[h]], vn[:, H[h]], start=False, stop=True)
            psd = NP()
            for h in range(4):
                self.mm(psd[:, H[h]], kd[:, H[h]], vn[:, H[h]])
            yield
            for h in range(4):
                self.stt(S[:, h, :], S[:, h, :], egl[:, h:h + 1], psd[:, H[h]], ALU.mult, ALU.add)
            self.copy(f2(Sb), f2(S), e="act")
            o_ = ob[b2]
            self.copy(o_, pso, e="dve")
            self.dma(OBd.t(OBd.ap[p0:p0 + 128, :], p0, 128), o_)
            yield


def merge_phase(self):
    with ExitStack() as ph:
        gn = [self.sb(ph, "mgn%d" % k, [128, 512], F32) for k in range(2)]
        self.dma(gn[0], self.din["hng"])
        self.dma(gn[1], self.din["gng"])
        o0 = [self.sb(ph, "mo0%d" % k, [128, 512], F32) for k in range(4)]
        o1 = [self.sb(ph, "mo1%d" % k, [128, 512], F32) for k in range(4)]
        ag = [self.sb(ph, "mag%d" % k, [128, 512], F32) for k in range(4)]
        junk = self.sb(ph, "mjunk", [128, 128], F32)
        ss = [self.sb(ph, "mss%d" % k, [128, 4], F32) for k in range(4)]
        ya = [self.sb(ph, "mya%d" % k, [128, 512], BF16) for k in range(4)]
        yaT = [self.sb(ph, "myaT%d" % k, [128, 4, 128], BF16) for k in range(4)]
        it = 0
        for ci in range(NCH):
            p0 = ci * 128
            for mx in range(2):
                b = it % 4
                it += 1
                src = self.OAd if mx == 0 else self.OBd
                gcol = 2048 if mx == 0 else 2560
                self.dma(o0[b], src[0].t(src[0].ap[p0:p0 + 128, :], p0, 128))
                self.dma(o1[b], src[1].t(src[1].ap[p0:p0 + 128, :], p0, 128))
                self.dma(ag[b], self.Atm.t(self.Atm.ap[p0:p0 + 128, gcol:gcol + 512], p0, 128))
                self.tt(o0[b], o0[b], o1[b], ALU.add, e="pool")
                self.merge(o0[b], ag[b], gn[mx], junk, ss[b], ya[b], yaT[b], 4 * mx, p0)
        self.barrier()


def ab_setup2(self, st):
    self.ab_setup(st)
    self.OAd = [DT(self.scr("OA%d" % k, [P, 512], F32).ap) for k in range(2)]
    self.OBd = [DT(self.scr("OB%d" % k, [P, 512], F32).ap) for k in range(2)]


Prog.hgrn_stream = hgrn_stream
Prog.gdn_shared = gdn_shared
Prog.gdn_stream = gdn_stream
Prog.merge_phase = merge_phase
Prog.ab_setup2 = ab_setup2


def ab_items_order():
    o0, o1 = chunk_order(0), chunk_order(1)
    out = []
    pa = pb = None
    for a, b in zip(o0, o1):
        out.append((0, a, pa))
        out.append((1, b, pb))
        pa, pb = (0, a), (1, b)
    return out


def hgrn_pipe(self):
    sc = 128.0 ** -0.5
    K = 4
    H = [slice(h * 128, (h + 1) * 128) for h in range(4)]
    f2 = lambda t: T(t.ap.rearrange("p h t -> p (h t)"), t.s)
    with ExitStack() as ph:
        S = [self.sb(ph, "hS%d" % d, [128, 4, 128], F32) for d in range(2)]
        for d in range(2):
            self.memset(S[d], 0.0)
        B = []
        for k in range(K):
            b = {}
            for nm, shp, dt in (("x", [128, 2048], F32), ("sg", [128, 512], F32), ("lf", [128, 512], F32),
                                ("kk", [128, 512], F32), ("E1", [128, 512], F32), ("E2", [128, 512], F32),
                                ("cl", [128, 512], F32), ("Ee", [128, 12], F32), ("q", [128, 512], BF16),
                                ("k", [128, 512], BF16), ("v", [128, 512], BF16), ("qT", [128, 4, 128], BF16),
                                ("kT", [128, 4, 128], BF16), ("at", [128, 512], BF16), ("St", [128, 4, 128], BF16),
                                ("tS", [128, 4, 128], F32), ("o", [128, 512], F32)):
                b[nm] = self.sb(ph, "h%s%d" % (nm, k), shp, dt)
            B.append(b)

        done = set()

        def item(d, ci, slot, prev):
            b = B[slot]
            cnt = [0]

            def NP():
                cnt[0] += 1
                return self.PS[2 * slot + (cnt[0] % 2)]
            p0 = ci * 128
            x_ = b["x"]
            self.dma(x_, self.Atm.t(self.Atm.ap[p0:p0 + 128, 0:2048], p0, 128))
            af = x_[:, 512 * (1 + d):512 * (2 + d)]
            s_ = b["sg"]
            self.act(s_, af, AF.Exp, scale=-1.0)
            yield
            self.ts(s_, s_, 1.0, ALU.add, e="pool")
            self.recip(s_, s_)
            yield
            self.tt(s_, s_, self.omlb[:, d, :], ALU.mult)
            self.tt(s_, s_, self.lbB[:, d, :], ALU.add, e="pool")
            yield
            l_ = b["lf"]
            self.act(l_, s_, AF.Ln)
            self.ts(b["kk"], s_, -1.0, ALU.mult, 1.0, ALU.add, e="pool")
            yield
            psb_ = NP()
            self.mm(psb_, self.cst[:, 4 + d, :], l_)
            pse = NP()
            for h in range(4):
                self.mm(pse[:, 3 * h:3 * h + 3], l_[:, H[h]], self.selc[:, d, :])
            yield
            self.act(b["E1"], psb_, AF.Exp)
            self.act(b["E2"], psb_, AF.Exp, scale=-1.0)
            e_ = b["Ee"]
            self.act(e_, pse[:, 0:12], AF.Exp)
            yield
            q_, k_, v_ = b["q"], b["k"], b["v"]
            self.stt(q_, x_[:, 0:512], sc, b["E1"], ALU.mult, ALU.mult)
            self.tt(k_, b["kk"], b["E2"], ALU.mult, e="pool")
            self.copy(v_, x_[:, 1536:2048], e="pool")
            yield
            psq = NP()
            psk = NP()
            for h in range(4):
                self.tr(psb(psq)[:, H[h]], q_[:, H[h]], self.identb)
            for h in range(4):
                self.tr(psb(psk)[:, H[h]], k_[:, H[h]], self.identb)
            qT_, kT_ = b["qT"], b["kT"]
            self.copy(f2(qT_), psb(psq)[:, 0:512], e="dve")
            self.copy(f2(kT_), psb(psk)[:, 0:512], e="act")
            yield
            psa = NP()
            for h in range(4):
                self.mm(psa[:, H[h]], kT_[:, h, :], qT_[:, h, :])
            a_ = b["at"]
            self.ts(b["cl"], psa, 1e30, ALU.min, -1e30, ALU.max)
            yield
            self.tt(a_, b["cl"], self.m4c[:, d, :], ALU.mult, e="pool")
            yield
            St_ = b["St"]
            Sd = S[d]
            while prev is not None and prev not in done:
                yield
            for h in range(4):
                self.ts(St_[:, h, :], Sd[:, h, :], e_[:, 3 * h:3 * h + 1], ALU.mult, e="pool")
            for h in range(4):
                self.ts(Sd[:, h, :], Sd[:, h, :], e_[:, 3 * h + 1:3 * h + 2], ALU.mult, e="pool")
            yield
            pso = NP()
            for h in range(4):
                self.mm(pso[:, H[h]], a_[:, H[h]], v_[:, H[h]], start=True, stop=False)
                self.mm(pso[:, H[h]], qT_[:, h, :], St_[:, h, :], start=False, stop=True)
            psd = NP()
            for h in range(4):
                self.mm(psd[:, H[h]], k_[:, H[h]], v_[:, H[h]])
            yield
            for h in range(4):
                self.stt(Sd[:, h, :], psd[:, H[h]], e_[:, 3 * h + 2:3 * h + 3], Sd[:, h, :], ALU.mult, ALU.add)
            done.add((d, ci))
            yield
            o_ = b["o"]
            self.copy(o_, pso, e="dve")
            OAd = self.OAd[d]
            self.dma(OAd.t(OAd.ap[p0:p0 + 128, :], p0, 128), o_)
            yield

        self.run_pipe((item(d, ci, i % K, pv) for i, (d, ci, pv) in enumerate(ab_items_order())), K, gap=4)
        self.barrier()


def gdn_pipe(self):
    sc = 128.0 ** -0.5
    K = 3
    H = [slice(h * 128, (h + 1) * 128) for h in range(4)]
    f2 = lambda t: T(t.ap.rearrange("p h t -> p (h t)"), t.s)
    with ExitStack() as ph:
        self.gdn_shared(ph)
        dwg, I4 = self.dwg, self.I4
        S = [self.sb(ph, "gS%d" % d, [128, 4, 128], F32) for d in range(2)]
        Sb = [self.sb(ph, "gSb%d" % d, [128, 4, 128], BF16) for d in range(2)]
        junk = self.sb(ph, "gjunk", [128, 128], F32)
        for d in range(2):
            self.memset(S[d], 0.0)
            self.memset(Sb[d], 0.0)
        B = []
        for k in range(K):
            b = {}
            for nm, shp, dt in (("pw", [128, 12, 131], BF16), ("qs", [128, 512], F32), ("ks", [128, 512], F32),
                                ("vs", [128, 512], F32), ("rn", [128, 8], F32), ("rnq", [128, 4], F32),
                                ("gcs", [128, 8], F32), ("ngc", [128, 4], F32), ("egc", [128, 4], F32),
                                ("ekd", [128, 4], F32), ("egl", [128, 4], F32), ("bw", [128, 4], F32),
                                ("gbc", [128, 4, 128], F32), ("decT", [128, 512], F32), ("decA", [128, 4, 128], F32),
                                ("khb", [128, 512], BF16), ("qhb", [128, 512], BF16), ("khT", [128, 4, 128], BF16),
                                ("qhT", [128, 4, 128], BF16), ("X0", [128, 4, 128], F32), ("X1", [128, 4, 128], F32),
                                ("XT0", [128, 4, 128], F32), ("XT1", [128, 4, 128], F32), ("PT", [128, 4, 128], F32),
                                ("RU", [128, 4, 128], F32), ("RW", [128, 4, 128], F32), ("u", [128, 512], F32),
                                ("wT", [128, 4, 128], BF16), ("atT", [128, 512], BF16), ("qg", [128, 512], BF16),
                                ("qgT", [128, 4, 128], BF16), ("kd", [128, 512], BF16), ("vn", [128, 512], BF16),
                                ("o", [128, 512], F32)):
                b[nm] = self.sb(ph, "g%s%d" % (nm, k), shp, dt)
            B.append(b)
        banks = [(0, 1, 2), (3, 4, 5), (6, 7)]

        done = set()

        def item(d, ci, slot, prev):
            b = B[slot]
            cnt = [0]
            bk = banks[slot]

            def NP():
                cnt[0] += 1
                return self.PS[bk[cnt[0] % len(bk)]]
            TRI = self.cst[:, 2 + d, :]
            NEGT = self.cst[:, 6 + d, :]
            POSA = self.cst[:, 8 + d, :]
            p0 = ci * 128
            off = CP0 + p0 if ci < 2 else CL0 + (p0 - TC)
            w_ = b["pw"]
            qs, ks, vs, rn, rnq, gcs, ngc = b["qs"], b["ks"], b["vs"], b["rn"], b["rnq"], b["gcs"], b["ngc"]
            egc, ekd, egl, bw, gbc, decT, decA = b["egc"], b["ekd"], b["egl"], b["bw"], b["gbc"], b["decT"], b["decA"]
            khb, qhb, khT, qhT, PT, RU, RW = b["khb"], b["qhb"], b["khT"], b["qhT"], b["PT"], b["RU"], b["RW"]
            u, wT, atT, qg, qgT, kd, vn = b["u"], b["wT"], b["atT"], b["qg"], b["qgT"], b["kd"], b["vn"]
            X = [b["X0"], b["X1"]]
            XT = [b["XT0"], b["XT1"]]
            self.dma(w_, T(self.preT.ap[:, :, off - 2:off + 129].rearrange("c p t -> p c t"), self.preT.s))
            for grp, dst in enumerate((qs, ks, vs)):
                ps = NP()
                for c4 in range(4):
                    c = grp * 4 + c4
                    for k in range(4):
                        self.mm(ps[:, H[c4]], w_[:, c, k:k + 128], dwg[:, c, k, :], start=(k == 0), stop=(k == 3))
                self.act(dst, ps, AF.Silu)
                yield
            for h in range(4):
                self.act(junk, qs[:, H[h]], AF.Square, accum=rn[:, h:h + 1])
                self.act(junk, ks[:, H[h]], AF.Square, accum=rn[:, 4 + h:5 + h])
            yield
            self.act(rn, rn, AF.Sqrt, bias=self.epsc, scale=1.0)
            self.recip(rn, rn)
            self.ts(rnq, rn[:, 0:4], sc, ALU.mult)
            yield
            g4 = self.gG[:, ci, 4 * d:4 * d + 4]
            b4 = self.gB[:, ci, 4 * d:4 * d + 4]
            nb4 = self.gNB[:, ci, 4 * d:4 * d + 4]
            psg = NP()
            self.mm(psg[:, 0:4], TRI, g4)
            self.mm(psg[:, 4:8], self.ones, g4)
            self.copy(gcs, psg[:, 0:8], e="dve")
            yield
            self.ts(ngc, gcs[:, 0:4], -1.0, ALU.mult)
            self.act(egc, gcs[:, 0:4], AF.Exp)
            self.act(egl, gcs[:, 4:8], AF.Exp)
            self.tt(ekd, gcs[:, 4:8], gcs[:, 0:4], ALU.subtract)
            yield
            self.act(ekd, ekd, AF.Exp)
            self.tt(bw, b4, egc, ALU.mult)
            for h in range(4):
                self.ts(gbc[:, h, :], self.ones, g4[:, h:h + 1], ALU.mult, e=("pool" if h % 2 else "dve"))
            yield
            psD = NP()
            for h in range(4):
                self.mm(psD[:, H[h]], gbc[:, h, :], TRI, start=True, stop=False)
                self.mm(psD[:, H[h]], self.ident, NEGT, start=False, stop=True)
            for h in range(4):
                self.act(decT[:, H[h]], psD[:, H[h]], AF.Exp, bias=ngc[:, h:h + 1], scale=1.0)
            yield
            psA = NP()
            for h in range(4):
                self.mm(psA[:, H[h]], gbc[:, h, :], TRI, start=True, stop=False)
                self.mm(psA[:, H[h]], self.ident, POSA, start=False, stop=True)
            for h in range(4):
                self.act(decA[:, h, :], psA[:, H[h]], AF.Exp, bias=gcs[:, h:h + 1], scale=-1.0)
            yield
            for h in range(4):
                self.act(khb[:, H[h]], ks[:, H[h]], AF.Copy, scale=rn[:, 4 + h:5 + h])
                self.ts(qhb[:, H[h]], qs[:, H[h]], rnq[:, h:h + 1], ALU.mult)
            yield
            pst = NP()
            for h in range(4):
                self.tr(psb(pst)[:, H[h]], khb[:, H[h]], self.identb)
            self.copy(f2(khT), psb(pst)[:, 0:512], e="dve")
            pst2 = NP()
            for h in range(4):
                self.tr(psb(pst2)[:, H[h]], qhb[:, H[h]], self.identb)
            self.copy(f2(qhT), psb(pst2)[:, 0:512], e="act")
            yield
            psK = NP()
            for h in range(4):
                self.mm(psK[:, H[h]], khT[:, h, :], khT[:, h, :])
            for h in range(4):
                self.stt(X[0][:, h, :], psK[:, H[h]], nb4[:, h:h + 1], decA[:, h, :], ALU.mult, ALU.mult)
            yield
            psn = NP()
            for h in range(4):
                self.tr(psn[:, H[h]], X[0][:, h, :], self.ident)
            self.copy(f2(XT[0]), psn, e="act")
            yield
            self.tt(f2(PT), f2(XT[0]), f2(I4), ALU.add, e="pool")
            yield
            cur = 0
            for m in range(6):
                Xc, XTc, Xn, XTn = X[cur], XT[cur], X[1 - cur], XT[1 - cur]
                ps1 = NP()
                for h in range(4):
                    self.mm(ps1[:, H[h]], XTc[:, h, :], Xc[:, h, :])
                self.copy(f2(Xn), ps1, e="act")
                yield
                if m < 5:
                    ps2 = NP()
                    for h in range(4):
                        self.mm(ps2[:, H[h]], Xc[:, h, :], XTc[:, h, :])
                    self.copy(f2(XTn), ps2, e="dve")
                    yield
                ps3 = NP()
                for h in range(4):
                    self.mm(ps3[:, H[h]], Xn[:, h, :], PT[:, h, :])
                self.tt(f2(PT), f2(PT), ps3, ALU.add)
                cur = 1 - cur
                yield
            for h in range(4):
                self.ts(RU[:, h, :], vs[:, H[h]], b4[:, h:h + 1], ALU.mult, e=("pool" if h % 2 else "dve"))
                self.ts(RW[:, h, :], ks[:, H[h]], rn[:, 4 + h:5 + h], ALU.mult, bw[:, h:h + 1], ALU.mult,
                        e=("dve" if h % 2 else "pool"))
            yield
            psu = NP()
            for h in range(4):
                self.mm(psu[:, H[h]], PT[:, h, :], RU[:, h, :])
            self.copy(u, psu, e="act")
            psw = NP()
            for h in range(4):
                self.mm(psw[:, H[h]], RW[:, h, :], PT[:, h, :])
            self.copy(f2(wT), psw, e="dve")
            yield
            psq = NP()
            for h in range(4):
                self.mm(psq[:, H[h]], khT[:, h, :], qhT[:, h, :])
            self.tt(atT, psq, decT, ALU.mult)
            yield
            for h in range(4):
                self.ts(qg[:, H[h]], qhb[:, H[h]], egc[:, h:h + 1], ALU.mult, e="pool")
                self.act(kd[:, H[h]], khb[:, H[h]], AF.Copy, scale=ekd[:, h:h + 1])
            yield
            pst3 = NP()
            for h in range(4):
                self.tr(psb(pst3)[:, H[h]], qg[:, H[h]], self.identb)
            self.copy(f2(qgT), psb(pst3)[:, 0:512], e="act")
            yield
            Sd, Sbd = S[d], Sb[d]
            while prev is not None and prev not in done:
                yield
            psws = NP()
            for h in range(4):
                self.mm(psws[:, H[h]], wT[:, h, :], Sbd[:, h, :])
            self.tt(vn, u, psws, ALU.subtract)
            yield
            pso = NP()
            for h in range(4):
                self.mm(pso[:, H[h]], qgT[:, h, :], Sbd[:, h, :], start=True, stop=False)
                self.mm(pso[:, H[h]], atT[:, H[h]], vn[:, H[h]], start=False, stop=True)
            o_ = b["o"]
            self.copy(o_, pso, e="act")
            psd = NP()
            for h in range(4):
                self.mm(psd[:, H[h]], kd[:, H[h]], vn[:, H[h]])
            for h in range(4):
                self.stt(Sd[:, h, :], Sd[:, h, :], egl[:, h:h + 1], psd[:, H[h]], ALU.mult, ALU.add)
            yield
            self.copy(f2(Sbd), f2(Sd), e="act")
            done.add((d, ci))
            OBd = self.OBd[d]
            self.dma(OBd.t(OBd.ap[p0:p0 + 128, :], p0, 128), o_)
            yield

        self.run_pipe((item(d, ci, i % K, pv) for i, (d, ci, pv) in enumerate(ab_items_order())), K, gap=14)
        self.barrier()


Prog.hgrn_pipe = hgrn_pipe
Prog.gdn_pipe = gdn_pipe


def xinit_phase2(self):
    K = 4
    with ExitStack() as st:
        xin = [self.sb(st, "xin%d" % k, [128, D], F32) for k in range(K)]
        xo = [self.sb(st, "xo%d" % k, [128, KT, 128], F32) for k in range(K)]

        def item(i, slot):
            src = self.din["ctx"][i * 128:(i + 1) * 128, :] if i < 2 else self.din["x"][(i - 2) * 128:(i - 1) * 128, :]
            xi = xin[slot]
            self.dma(xi, src)
            yield
            for half in range(2):
                ps = self.PS[2 * slot + half]
                for k4 in range(4):
                    kt = half * 4 + k4
                    self.tr(ps[:, k4 * 128:(k4 + 1) * 128], xi[:, kt * 128:(kt + 1) * 128], self.ident)
                dst = xo[slot][:, half * 4:(half + 1) * 4, :]
                self.copy(dst, T(ps.ap.rearrange("p (k t) -> p k t", t=128), ps.s), e=("act" if half else "dve"))
                yield
            self.dma(self.xT.t(self.xT.ap[:, :, i * 128:(i + 1) * 128], i * 128, 128), xo[slot])
            yield
        self.run_pipe((item(i, i % K) for i in range(NCH)), K, gap=1)
        self.barrier()


def norm_phase2(self, st, l, w_, hT, tiles):
    sec_sh = 0 if w_ == 0 else 3
    K = 3
    with ExitStack() as ph:
        xs = [self.sb(ph, "nxs%d" % k, [128, KT, 512], F32) for k in range(K)]
        sq = [self.sb(ph, "nsq%d" % k, [128, 2, 512], BF16) for k in range(K)]
        rs = [self.sb(ph, "nrs%d" % k, [128, 512], F32) for k in range(K)]
        tm = [self.sb(ph, "ntm%d" % k, [128, 2, 512], F32) for k in range(K)]

        def item(p0, n, j, slot):
            x_ = xs[slot]
            self.dma(x_[:, :, :n], self.xT.t(self.xT.ap[:, :, p0:p0 + n], p0, n))
            yield
            ps = self.PS[2 * slot]
            for kt in range(KT):
                s_ = sq[slot][:, kt % 2, :n]
                if kt % 2:
                    self.tt(s_, x_[:, kt, :n], x_[:, kt, :n], ALU.mult, e="pool")
                else:
                    self.act(s_, x_[:, kt, :n], AF.Square)
                self.mm(ps[:, :n], self.onesb, s_, start=(kt == 0), stop=(kt == KT - 1))
                yield
            r_ = rs[slot]
            self.act(r_[:, :n], ps[:, :n], AF.Sqrt, bias=self.epsc, scale=1.0 / D)
            self.recip(r_[:, :n], r_[:, :n])
            yield
            for kt in range(KT):
                t_ = tm[slot][:, kt % 2, :n]
                self.stt(t_, x_[:, kt, :n], self.gs[l][w_][:, kt, j:j + 1], r_[:, :n], ALU.mult, ALU.mult)
                self.act(hT[:, kt, p0:p0 + n], t_, AF.Identity,
                         bias=self.modv[l][:, sec_sh * 8 + kt, j:j + 1], scale=1.0)
                yield
        self.run_pipe((item(p0, n, j, i % K) for i, (p0, n, j) in enumerate(tiles)), K, gap=6)
        self.barrier()


def outproj_phase2(self, l, wname, widx, nkt, src, gsec, tiles):
    K = 2
    with ExitStack() as ph:
        wsb = self.sb(ph, "opw", [128, nkt, D], BF16)
        wd = self.din[wname] if widx is None else self.din[wname][widx]
        for kt in range(nkt):
            self.dma(wsb[:, kt, :], wd[:, kt, :], q="pool")
        ub = [self.sb(ph, "opu%d" % k, [128, nkt, 512], BF16) for k in range(K)]
        xs = [self.sb(ph, "opx%d" % k, [128, KT, 512], F32) for k in range(K)]

        def item(p0, n, j, slot):
            u = ub[slot]
            x_ = xs[slot]
            self.dma(u[:, :, :n], src.t(src.ap[:, :, p0:p0 + n], p0, n))
            self.dma(x_[:, :, :n], self.xT.t(self.xT.ap[:, :, p0:p0 + n], p0, n))
            yield
            for nt in range(KT):
                ps = self.PS[4 * slot + nt % 4]
                for kt in range(nkt):
                    self.mm(ps[:, :n], wsb[:, kt, nt * 128:(nt + 1) * 128], u[:, kt, :n],
                            start=(kt == 0), stop=(kt == nkt - 1))
                self.stt(x_[:, nt, :n], ps[:, :n], self.modv[l][:, gsec * 8 + nt, j:j + 1], x_[:, nt, :n],
                         ALU.mult, ALU.add)
                yield
            self.dma(self.xT.t(self.xT.ap[:, :, p0:p0 + n], p0, n), x_[:, :, :n])
            yield
        self.run_pipe((item(p0, n, j, i % K) for i, (p0, n, j) in enumerate(tiles)), K, gap=5)
        self.barrier()


def final_phase2(self):
    K = 4
    with ExitStack() as ph:
        fg = self.sb(ph, "fng", [128, D], F32)
        self.dma(fg, self.din["fng"])
        xs = [self.sb(ph, "fx%d" % k, [128, KT, 128], F32) for k in range(K)]
        xt = [self.sb(ph, "fxt%d" % k, [128, D], F32) for k in range(K)]
        junk = self.sb(ph, "fjunk", [128, D], F32)
        ss = [self.sb(ph, "fss%d" % k, [128, 1], F32) for k in range(K)]
        outs = [T(self.out.ap[i * 128:(i + 1) * 128, :]) for i in range(TX // 128)]
        self.out_tiles = outs

        def item(i, slot):
            p0 = TC + i * 128
            x_ = xs[slot]
            self.dma(x_, self.xT.t(self.xT.ap[:, :, p0:p0 + 128], p0, 128))
            yield
            o_ = xt[slot]
            for half in range(2):
                ps = self.PS[2 * slot + half]
                for k4 in range(4):
                    self.tr(ps[:, k4 * 128:(k4 + 1) * 128], x_[:, half * 4 + k4, :], self.ident)
                self.copy(o_[:, half * 512:(half + 1) * 512], ps, e=("act" if half else "dve"))
                yield
            s_ = ss[slot]
            self.act(junk, o_, AF.Square, accum=s_)
            yield
            self.act(s_, s_, AF.Sqrt, bias=self.epsc, scale=1.0 / D)
            self.recip(s_, s_)
            yield
            self.stt(o_, o_, s_, fg, ALU.mult, ALU.mult)
            yield
            self.dma(outs[i], o_)
            yield
        self.run_pipe((item(i, i % K) for i in range(TX // 128)), K, gap=2)
        self.barrier()


def merge_phase2(self):
    K = 4
    H = [slice(h * 128, (h + 1) * 128) for h in range(4)]
    with ExitStack() as ph:
        gn = [self.sb(ph, "mgn%d" % k, [128, 512], F32) for k in range(2)]
        self.dma(gn[0], self.din["hng"])
        self.dma(gn[1], self.din["gng"])
        o0 = [self.sb(ph, "mo0%d" % k, [128, 512], F32) for k in range(K)]
        o1 = [self.sb(ph, "mo1%d" % k, [128, 512], F32) for k in range(K)]
        ag = [self.sb(ph, "mag%d" % k, [128, 512], F32) for k in range(K)]
        junk = self.sb(ph, "mjunk", [128, 128], F32)
        ss = [self.sb(ph, "mss%d" % k, [128, 4], F32) for k in range(K)]
        ya = [self.sb(ph, "mya%d" % k, [128, 512], BF16) for k in range(K)]
        yaT = [self.sb(ph, "myaT%d" % k, [128, 4, 128], BF16) for k in range(K)]

        def item(ci, mx, slot):
            p0 = ci * 128
            src = self.OAd if mx == 0 else self.OBd
            gcol = 2048 if mx == 0 else 2560
            o_, g_, s_ = o0[slot], ag[slot], ss[slot]
            self.dma(o_, src[0].t(src[0].ap[p0:p0 + 128, :], p0, 128))
            self.dma(o1[slot], src[1].t(src[1].ap[p0:p0 + 128, :], p0, 128))
            self.dma(g_, self.Atm.t(self.Atm.ap[p0:p0 + 128, gcol:gcol + 512], p0, 128))
            yield
            self.tt(o_, o_, o1[slot], ALU.add, e="pool")
            self.tt(g_, g_, gn[mx], ALU.mult, e="pool")
            yield
            for h in range(4):
                self.act(junk, o_[:, H[h]], AF.Square, accum=s_[:, h:h + 1])
            yield
            self.act(s_, s_, AF.Sqrt, bias=self.epsc, scale=1.0 / 128)
            self.recip(s_, s_)
            yield
            for h in range(4):
                self.stt(ya[slot][:, H[h]], o_[:, H[h]], s_[:, h:h + 1], g_[:, H[h]], ALU.mult, ALU.mult)
            yield
            pst = self.PS[2 * slot]
            for h in range(4):
                self.tr(psb(pst)[:, H[h]], ya[slot][:, H[h]], self.identb)
            self.copy(T(yaT[slot].ap.rearrange("p h t -> p (h t)"), yaT[slot].s), psb(pst)[:, 0:512], e="act")
            yield
            self.dma(self.yT.t(self.yT.ap[:, 4 * mx:4 * mx + 4, p0:p0 + 128], p0, 128), yaT[slot])
            yield
        items = [(ci, mx) for ci in range(NCH) for mx in range(2)]
        self.run_pipe((item(ci, mx, i % K) for i, (ci, mx) in enumerate(items)), K, gap=2)
        self.barrier()


Prog.xinit_phase = xinit_phase2
Prog.norm_phase = norm_phase2
Prog.outproj_phase = outproj_phase2
Prog.final_phase = final_phase2
Prog.merge_phase = merge_phase2


def mixer_c_phases2(self, hT):
    K = 3
    xbT = DT(self.scr("xbT", [KT, 128, P], F32).ap.rearrange("k p t -> p k t"))
    gelT = DT(self.scr("gelT", [KT, 128, P], F32).ap.rearrange("k p t -> p k t"))
    win = self.din["c_w_in"]
    with ExitStack() as ph:
        pre = [self.sb(ph, "cpre%d" % k, [128, CPAD], BF16) for k in range(2)]
        for k in range(2):
            self.memset(pre[k], 0.0)
        ccw = self.sb(ph, "ccw", [128, KT, 4], F32)
        ccb = self.sb(ph, "ccb", [128, KT], F32)
        self.dma(ccw, self.din["ccw"])
        self.dma(ccb, self.din["ccb"])
        wg = [self.sb(ph, "cwg%d" % k, [128, KT, 128], BF16) for k in range(2)]
        wx = [self.sb(ph, "cwx%d" % k, [128, KT, 128], BF16) for k in range(2)]
        dw = [self.sb(ph, "cdw%d" % k, [128, 4, 128], BF16) for k in range(2)]
        gx = [self.sb(ph, "cgx%d" % k, [128, 512], F32) for k in range(K)]
        gt = [self.sb(ph, "cgt%d" % k, [128, 512], F32) for k in range(K)]
        xo = [self.sb(ph, "cxo%d" % k, [128, 512], F32) for k in range(K)]
        a_done = [0] * KT

        def W(c):
            self.dma(wg[c % 2], win[:, :, c * 128:(c + 1) * 128], q="pool")
            self.dma(wx[c % 2], win[:, :, D + c * 128:D + (c + 1) * 128], q="pool")
            for k in range(4):
                self.ts(dw[c % 2][:, k, :], self.identb, ccw[:, c, k:k + 1], ALU.mult)
            yield

        def A(c, p0, n, j, slot):
            ps = self.PS[2 * slot]
            for kt in range(KT):
                self.mm(ps[:, :n], wx[c % 2][:, kt, :], hT[:, kt, p0:p0 + n], start=(kt == 0), stop=(kt == KT - 1))
            yield
            off = CP0 + p0 if j == 1 else CL0 + (p0 - TC)
            self.copy(pre[c % 2][:, off:off + n], ps[:, :n], e="act")
            a_done[c] += 1
            yield

        def Bi(c, p0, n, j, slot):
            while a_done[c] < len(TILES):
                yield
            off = CP0 + p0 if j == 1 else CL0 + (p0 - TC)
            ps = self.PS[2 * slot]
            pr = pre[c % 2]
            for k in range(4):
                self.mm(ps[:, :n], dw[c % 2][:, k, :], pr[:, off + k - 2:off + k - 2 + n], start=(k == 0), stop=(k == 3))
            x_o = xo[slot]
            self.act(x_o[:, :n], ps[:, :n], AF.Identity, bias=ccb[:, c:c + 1], scale=1.0)
            yield
            self.dma(xbT.t(xbT.ap[:, c, p0:p0 + n], p0, n), x_o[:, :n])
            if j == 0:
                ps2 = self.PS[2 * slot + 1]
                for kt in range(KT):
                    self.mm(ps2[:, :n], wg[c % 2][:, kt, :], hT[:, kt, p0:p0 + n], start=(kt == 0), stop=(kt == KT - 1))
                g_x, g_t = gx[slot], gt[slot]
                self.copy(g_x[:, :n], ps2[:, :n], e="act")
                yield
                self.tt(g_t[:, :n], g_x[:, :n], g_x[:, :n], ALU.mult, e="pool")
                yield
                self.ts(g_t[:, :n], g_t[:, :n], 0.044715, ALU.mult, 1.0, ALU.add)
                yield
                self.tt(g_t[:, :n], g_t[:, :n], g_x[:, :n], ALU.mult, e="pool")
                yield
                self.act(g_t[:, :n], g_t[:, :n], AF.Exp, scale=-GELU_C)
                yield
                self.ts(g_t[:, :n], g_t[:, :n], 1.0, ALU.add, e="pool")
                yield
                self.recip(g_t[:, :n], g_t[:, :n])
                yield
                self.tt(g_x[:, :n], g_x[:, :n], g_t[:, :n], ALU.mult)
                yield
                self.dma(gelT.t(gelT.ap[:, c, p0:p0 + n], p0, n), g_x[:, :n])
            yield

        def items():
            i = 0
            for c in range(KT):
                yield W(c)
                for (p0, n, j) in TILES:
                    yield A(c, p0, n, j, i % K)
                    i += 1
                for (p0, n, j) in TILES:
                    yield Bi(c, p0, n, j, i % K)
                    i += 1
        self.run_pipe(items(), K, gap=1)
        self.barrier()
    return xbT, gelT


def mixer_c_scan2(self, xbT, gelT):
    nc = self.nc
    K = 3
    with ExitStack() as ph:
        cgw = self.sb(ph, "cgw", [128, 16, 512], BF16)
        for k in range(16):
            self.dma(cgw[:, k, :], self.din["cgw"][:, k, :], q="pool")
        cgb = self.sb(ph, "cgb", [128, 2, 4, 4], F32)
        lam = self.sb(ph, "clam", [128, 2, KT], F32)
        scl = self.sb(ph, "cscl", [128, 2, KT], F32)
        scl2 = self.sb(ph, "cscl2", [128, 2, KT], F32)
        self.dma(cgb, self.din["cgb"])
        self.dma(lam, self.din["clam"])
        ncgb = self.sb(ph, "ncgb", [128, 2, 4, 4], F32)
        self.ts(ncgb, cgb, -1.0, ALU.mult)
        self.act(scl, lam, AF.Exp, scale=-1.0)
        self.act(scl, scl, AF.Ln, bias=self.ones[:, 0:1], scale=1.0)
        self.ts(scl2, scl, -16.0, ALU.mult)
        self.ts(scl, scl, -8.0, ALU.mult)
        xbb = [self.sb(ph, "cxbb%d" % k, [128, 2, P], BF16) for k in range(2)]
        xbf = [self.sb(ph, "cxbf%d" % k, [128, P], F32) for k in range(2)]
        hs = [[self.sb(ph, "chs%d_%d" % (k, d), [128, P], F32) for d in range(2)] for k in range(2)]
        rb = [self.sb(ph, "crb%d" % k, [128, 512], F32) for k in range(K)]
        ib = [self.sb(ph, "cib%d" % k, [128, 512], F32) for k in range(K)]
        ab = [self.sb(ph, "cab%d" % k, [128, 512], F32) for k in range(K)]
        yo = [self.sb(ph, "cyo%d" % k, [128, 512], BF16) for k in range(K)]
        scans = [0] * KT
        allslots = tuple(xbT.slots)

        def L(c):
            h, half = c // 2, c % 2
            if half == 0:
                for k2 in range(2):
                    self.dma(xbb[h % 2][:, k2, :], T(xbT.ap[:, 2 * h + k2, :], allslots), q="pool")
            self.dma(xbf[c % 2], T(xbT.ap[:, c, :], allslots))
            yield

        def S(c, d, p0, n, init, slot):
            h, half = c // 2, c % 2
            psr, psi = self.PS[2 * slot], self.PS[2 * slot + 1]
            xb_ = xbb[h % 2]
            for k2 in range(2):
                wbase = cgw[:, (d * 4 + h) * 2 + k2, :]
                self.mm(psr[:, :n], wbase[:, half * 128:(half + 1) * 128], xb_[:, k2, p0:p0 + n],
                        start=(k2 == 0), stop=(k2 == 1))
            for k2 in range(2):
                wbase = cgw[:, (d * 4 + h) * 2 + k2, :]
                self.mm(psi[:, :n], wbase[:, 256 + half * 128:256 + (half + 1) * 128], xb_[:, k2, p0:p0 + n],
                        start=(k2 == 0), stop=(k2 == 1))
            yield
            r_, i_, a_ = rb[slot][:, :n], ib[slot][:, :n], ab[slot][:, :n]
            self.act(r_, psr[:, :n], AF.Exp, bias=ncgb[:, d, h, half:half + 1], scale=-1.0)
            self.act(i_, psi[:, :n], AF.Exp, bias=ncgb[:, d, h, 2 + half:3 + half], scale=-1.0)
            yield
            self.ts(r_, r_, 1.0, ALU.add, e="pool")
            self.ts(i_, i_, 1.0, ALU.add, e="pool")
            yield
            self.recip(r_, r_)
            self.recip(i_, i_)
            yield
            self.act(a_, r_, AF.Exp, scale=scl[:, d, c:c + 1])
            self.act(r_, r_, AF.Exp, scale=scl2[:, d, c:c + 1])
            yield
            self.ts(r_, r_, -1.0, ALU.mult, 1.0, ALU.add)
            self.tt(i_, i_, xbf[c % 2][:, p0:p0 + n], ALU.mult, e="pool")
            yield
            self.ts(r_, r_, 1e30, ALU.min, 1e-30, ALU.max, e="pool")
            yield
            self.act(r_, r_, AF.Ln)
            self.act(r_, r_, AF.Exp, scale=0.5)
            yield
            self.tt(r_, r_, i_, ALU.mult)
            yield
            h_ = hs[c % 2][d]
            if d == 0:
                o_ap, a_ap, u_ap = h_.ap[:, p0:p0 + n], a_.ap, r_.ap
            else:
                lo = p0 - 1 if p0 > 0 else None
                o_ap = h_.ap[:, p0 + n - 1:lo:-1]
                a_ap = a_.ap[:, ::-1]
                u_ap = r_.ap[:, ::-1]
            iv = 0.0 if init is None else h_.ap[:, init:init + 1]
            self.op("dve", lambda: nc.vector.tensor_tensor_scan(o_ap, a_ap, u_ap, iv, ALU.mult, ALU.add),
                    r=[a_, r_, h_], w=[h_])
            scans[c] += 1
            yield

        def O(c, p0, n, slot):
            while scans[c] < 2 * len(TILES):
                yield
            g_ = ab[slot][:, :n]
            self.dma(g_, gelT.t(gelT.ap[:, c, p0:p0 + n], p0, n))
            t_ = rb[slot][:, :n]
            self.tt(t_, hs[c % 2][0][:, p0:p0 + n], hs[c % 2][1][:, p0:p0 + n], ALU.add, e="pool")
            yield
            y_ = yo[slot][:, :n]
            self.tt(y_, g_, t_, ALU.mult)
            yield
            self.dma(self.yT.t(self.yT.ap[:, c, p0:p0 + n], p0, n), y_)
            yield

        def items():
            i = 0
            for c in range(KT):
                yield L(c)
                fw = []
                prev = None
                for (p0, n, j) in TILES:
                    fw.append((0, p0, n, prev))
                    prev = p0 + n - 1
                bw = [(1, 0, TC, None)]
                prev = 0
                for (p0, n, j) in reversed(LTILES):
                    bw.append((1, p0, n, prev))
                    prev = p0
                for f_, b_ in zip(fw, bw):
                    for (d, p0, n, init) in (f_, b_):
                        yield S(c, d, p0, n, init, i % K)
                        i += 1
                for (p0, n, j) in LTILES:
                    yield O(c, p0, n, i % K)
                    i += 1
        self.run_pipe(items(), K, gap=2)
        self.barrier()


Prog.mixer_c_phases = mixer_c_phases2
Prog.mixer_c_scan = mixer_c_scan2


def setup_streams(self):
    K = 3
    with ExitStack() as st:
        cv = self.sb(st, "cv", [128, KT, 2], F32)
        sv = self.sb(st, "sv", [128, KT, 2], F32)
        mb = [self.sb(st, "mb%d" % l, [128, 48], F32) for l in range(2)]
        tmp = self.sb(st, "mtmp", [128, KT, 2], F32)
        wb = [self.sb(st, "mw%d" % k, [128, KT, 128], F32) for k in range(3)]
        xin = [self.sb(st, "xin%d" % k, [128, D], F32) for k in range(K)]
        xo = [self.sb(st, "xo%d" % k, [128, KT, 128], F32) for k in range(K)]
        self.dma(cv, self.din["cvec"])
        for l in range(2):
            self.dma(mb[l], self.din["mod_b"][l])
        self.act(sv, cv, AF.Silu)
        psm = [self.PS[6], self.PS[7]]

        def mod_stream():
            q = 0
            for l in range(2):
                ps = psm[l]
                for n in range(48):
                    w = wb[q % 3]
                    q += 1
                    self.dma(w, self.din["mod_w"][l, :, :, n * 128:(n + 1) * 128])
                    for kt in range(KT):
                        self.mm(ps[:, 2 * n:2 * n + 2], w[:, kt, :], sv[:, kt, :], start=(kt == 0), stop=(kt == KT - 1))
                    yield
                pv = ps[:, 0:96]
                for j in range(2):
                    self.tt(self.modv[l][:, :, j], T(pv.ap.rearrange("p (n j) -> p n j", j=2)[:, :, j], pv.s), mb[l], ALU.add)
                for w_, sec in ((0, 1), (1, 4)):
                    self.ts(tmp, self.modv[l][:, sec * 8:(sec + 1) * 8, :], 1.0, ALU.add)
                    for j in range(2):
                        self.tt(self.gs[l][w_][:, :, j], tmp[:, :, j], self.ng[:, 2 * l + w_, :], ALU.mult)
                yield

        def xitem(i, slot):
            src = self.din["ctx"][i * 128:(i + 1) * 128, :] if i < 2 else self.din["x"][(i - 2) * 128:(i - 1) * 128, :]
            xi = xin[slot]
            self.dma(xi, src)
            yield
            for half in range(2):
                ps = self.PS[2 * slot + half]
                for k4 in range(4):
                    kt = half * 4 + k4
                    self.tr(ps[:, k4 * 128:(k4 + 1) * 128], xi[:, kt * 128:(kt + 1) * 128], self.ident)
                dst = xo[slot][:, half * 4:(half + 1) * 4, :]
                self.copy(dst, T(ps.ap.rearrange("p (k t) -> p k t", t=128), ps.s), e=("act" if half else "dve"))
                yield
            self.dma(self.xT.t(self.xT.ap[:, :, i * 128:(i + 1) * 128], i * 128, 128), xo[slot])
            yield

        def xstream():
            it = iter(xitem(i, i % K) for i in range(NCH))
            active = []
            more = True
            while more or active:
                if more and len(active) < K:
                    g = next(it, None)
                    if g is None:
                        more = False
                    else:
                        active.append(g)
                for g in list(active):
                    try:
                        next(g)
                    except StopIteration:
                        active.remove(g)
                yield
        self.run_streams([mod_stream(), xstream()])
        self.barrier()


Prog.setup_streams = setup_streams
```

```python
import numpy as np
from contextlib import ExitStack
import concourse.bass as bass
import concourse.mybir as mybir
from concourse.bass_utils import run_bass_kernel_spmd

F32 = mybir.dt.float32
BF16 = mybir.dt.bfloat16
F32R = mybir.dt.float32r
AF = mybir.ActivationFunctionType
ALU = mybir.AluOpType
AX = mybir.AxisListType

SAME_ENGINE_SYNC = True
USE_F32R = False
N_DMA_SEMS = 8


class Slot:
    __slots__ = ("w", "r")

    def __init__(self):
        self.w = None
        self.r = {}


class Em:
    ENG = ("pe", "dve", "act", "pool", "sp")

    def __init__(self, nc, stack):
        self.nc = nc
        self.eng = {"pe": nc.tensor, "dve": nc.vector, "act": nc.scalar, "pool": nc.gpsimd, "sp": nc.sync}
        self.prog = {e: [] for e in self.ENG}
        self.sem = {}
        self.cnt = {}
        for e in ("pe", "dve", "act", "pool"):
            self.sem[e] = stack.enter_context(nc.semaphore("s_" + e))
            self.cnt[e] = 0
        self.dq = {}
        for q in ("sp", "pool", "act"):
            lst = []
            for i in range(N_DMA_SEMS):
                k = "d_%s_%d" % (q, i)
                self.sem[k] = stack.enter_context(nc.semaphore(k))
                self.cnt[k] = 0
                lst.append(k)
            self.dq[q] = [lst, 0]
        self.seen = {e: {} for e in self.ENG}
        self.ninst = 0

    def slots(self, n):
        return [Slot() for _ in range(n)]

    def _need(self, e, reads, writes, is_pe_mm=False):
        need = {}

        def add(tok):
            if tok is None:
                return
            k, v = tok
            if k == e and not SAME_ENGINE_SYNC:
                return
            if need.get(k, 0) < v:
                need[k] = v
        for s in reads:
            add(s.w)
        for s in writes:
            if s.w is not None and s.w[0] != e:
                add(s.w)
            for k, v in s.r.items():
                if k == e:
                    continue
                add((k, v))
        return need

    def _emit_waits(self, e, need):
        seen = self.seen[e]
        for k, v in need.items():
            if seen.get(k, 0) >= v:
                continue
            seen[k] = v
            sem = self.sem[k]
            self.prog[e].append(lambda eng, sem=sem, v=v: eng.wait_ge(sem, v))

    def _mark(self, tok, reads, writes):
        for s in reads:
            if s.r.get(tok[0], 0) < tok[1]:
                s.r[tok[0]] = tok[1]
        for s in writes:
            s.w = tok
            s.r = {}

    def op(self, e, fn, reads=(), writes=(), mm=False):
        need = self._need(e, reads, writes, is_pe_mm=mm)
        self._emit_waits(e, need)
        self.cnt[e] += 1
        n = self.cnt[e]
        sem = self.sem[e]
        self.prog[e].append(lambda eng, fn=fn, sem=sem: fn().then_inc(sem, 1))
        self._mark((e, n), reads, writes)
        self.ninst += 1

    def dma(self, out, in_, reads=(), writes=(), q="sp", **kw):
        lst, idx = self.dq[q]
        k = lst[idx % len(lst)]
        self.dq[q][1] = idx + 1
        need = self._need(q, reads, writes)
        if self.cnt[k] > 0:
            need[k] = max(need.get(k, 0), self.cnt[k])
        self._emit_waits(q, need)
        self.cnt[k] += 16
        v = self.cnt[k]
        sem = self.sem[k]
        eng = self.eng[q]
        self.prog[q].append(lambda _e, eng=eng, out=out, in_=in_, sem=sem, kw=kw:
                            eng.dma_start(out=out, in_=in_, **kw).then_inc(sem, 16))
        self._mark((k, v), reads, writes)
        self.ninst += 1

    def wait_all(self, e, slots):
        need = {}
        for s in slots:
            if s.w is not None:
                need[s.w[0]] = max(need.get(s.w[0], 0), s.w[1])
        self._emit_waits(e, need)

    def finish(self, out_slots):
        self.wait_all("sp", out_slots)
        nc = self.nc
        with nc.Block() as block:
            @block.sync
            def _(eng):
                for f in self.prog["sp"]:
                    f(eng)

            @block.tensor
            def _(eng):
                for f in self.prog["pe"]:
                    f(eng)

            @block.vector
            def _(eng):
                for f in self.prog["dve"]:
                    f(eng)

            @block.scalar
            def _(eng):
                for f in self.prog["act"]:
                    f(eng)

            @block.gpsimd
            def _(eng):
                for f in self.prog["pool"]:
                    f(eng)


class T:
    __slots__ = ("ap", "s")

    def __init__(self, ap, slot=None):
        self.ap = ap
        self.s = slot if slot is not None else Slot()

    def __getitem__(self, key):
        return T(self.ap[key], self.s)


def _sl(lst):
    out = []
    for x in lst:
        s = getattr(x, "s", x)
        if isinstance(s, (list, tuple)):
            out.extend(s)
        else:
            out.append(s)
    return out


class DT:
    def __init__(self, ap, n=34):
        self.ap = ap
        self.slots = [Slot() for _ in range(n)]

    def t(self, ap, p0, n):
        a, b = p0 // 128, (p0 + n + 127) // 128
        return T(ap, tuple(self.slots[a:b]))


D = 1024
KT = 8
TC = 256
TX = 4096
P = TC + TX
NCH = P // 128
FF = 2816
FT = 22
ABN = 4624
EPS = 1e-6
TILES = [(0, 256, 1)] + [(TC + 512 * i, 512, 0) for i in range(8)]
LTILES = TILES[1:]


class Prog:
    def __init__(self, debug=()):
        self.debug = set(debug)
        nc = bass.Bass("TRN2", target_bir_lowering=False)
        self.nc = nc
        self.stack = ExitStack()
        self.em = Em(nc, self.stack)
        self.din = {}
        self.dscr = {}
        self.outs = []
        self.psi = 0

    def inp(self, name, shape, dt=F32):
        t = self.nc.dram_tensor(name, list(shape), dt, kind="ExternalInput").ap()
        self.din[name] = T(t)
        return self.din[name]

    def scr(self, name, shape, dt):
        kind = "ExternalOutput" if name in self.debug else "Internal"
        t = T(self.nc.dram_tensor(name, list(shape), dt, kind=kind).ap())
        self.dscr[name] = t
        if name in self.debug:
            self.outs.append(t)
        return t

    def sb(self, st, name, shape, dt):
        self.uid = getattr(self, "uid", 0) + 1
        return T(st.enter_context(self.nc.sbuf_tensor("s%d_%s" % (self.uid, name), list(shape), dt)).ap())

    def op(self, e, fn, r=(), w=(), mm=False):
        self.em.op(e, fn, _sl(r), _sl(w), mm=mm)

    def dma(self, out, in_, q="sp", **kw):
        self.em.dma(out.ap, in_.ap, reads=_sl([in_]), writes=_sl([out]), q=q, **kw)

    def mm(self, out, lhsT, rhs, start=True, stop=True):
        nc = self.nc
        self.op("pe", lambda: nc.tensor.matmul(out.ap, lhsT.ap, rhs.ap, start=start, stop=stop),
                r=[lhsT, rhs], w=[out], mm=True)

    def tr(self, out, in_, ident):
        nc = self.nc
        self.op("pe", lambda: nc.tensor.transpose(out.ap, in_.ap, ident.ap), r=[in_, ident], w=[out], mm=True)

    def act(self, out, in_, func, bias=None, scale=None, accum=None, extra_r=()):
        nc = self.nc
        kw = {}
        r = [in_] + list(extra_r)
        if bias is not None:
            kw["bias"] = bias.ap if isinstance(bias, T) else bias
            if isinstance(bias, T):
                r.append(bias)
        if scale is not None:
            kw["scale"] = scale.ap if isinstance(scale, T) else scale
            if isinstance(scale, T):
                r.append(scale)
        w = [out]
        if accum is not None:
            kw["accum_out"] = accum.ap
            w.append(accum)
        self.op("act", lambda: nc.scalar.activation(out.ap, in_.ap, func, **kw), r=r, w=w)

    def tt(self, out, a, b, op, e="dve"):
        eng = self.nc.vector if e == "dve" else self.nc.gpsimd
        self.op(e, lambda: eng.tensor_tensor(out.ap, a.ap, b.ap, op), r=[a, b], w=[out])

    def ts(self, out, a, s1, op0, s2=None, op1=None, e="dve"):
        eng = self.nc.vector if e == "dve" else self.nc.gpsimd
        r = [a]
        v1 = s1.ap if isinstance(s1, T) else s1
        v2 = s2.ap if isinstance(s2, T) else s2
        if isinstance(s1, T):
            r.append(s1)
        if isinstance(s2, T):
            r.append(s2)
        if op1 is None and e == "pool" and op0 in (ALU.mult, ALU.add):
            o1, c1 = (ALU.add, 0.0) if op0 == ALU.mult else (ALU.mult, 1.0)
            self.op(e, lambda: eng.tensor_scalar(out.ap, a.ap, v1, c1, op0, o1), r=r, w=[out])
        elif op1 is None:
            self.op(e, lambda: eng.tensor_scalar(out.ap, a.ap, v1, None, op0), r=r, w=[out])
        else:
            self.op(e, lambda: eng.tensor_scalar(out.ap, a.ap, v1, v2, op0, op1), r=r, w=[out])

    def stt(self, out, a, s, b, op0, op1):
        nc = self.nc
        r = [a, b]
        v = s.ap if isinstance(s, T) else s
        if isinstance(s, T):
            r.append(s)
        self.op("dve", lambda: nc.vector.scalar_tensor_tensor(out.ap, a.ap, v, b.ap, op0, op1), r=r, w=[out])

    def copy(self, out, in_, e="dve"):
        nc = self.nc
        if e == "act":
            self.op("act", lambda: nc.scalar.copy(out.ap, in_.ap), r=[in_], w=[out])
        elif e == "pool":
            self.op("pool", lambda: nc.gpsimd.tensor_copy(out.ap, in_.ap), r=[in_], w=[out])
        else:
            self.op("dve", lambda: nc.vector.tensor_copy(out.ap, in_.ap), r=[in_], w=[out])

    def memset(self, out, val, e="pool"):
        eng = self.nc.gpsimd if e == "pool" else self.nc.vector
        self.op(e, lambda: eng.memset(out.ap, val), w=[out])

    def recip(self, out, in_):
        nc = self.nc
        self.op("dve", lambda: nc.vector.reciprocal(out.ap, in_.ap), r=[in_], w=[out])

    def nextps(self, pool=None):
        if pool is None:
            t = self.PS[self.psi % len(self.PS)]
            self.psi += 1
            return t
        self.psp = getattr(self, "psp", {})
        k = self.psp.get(pool, 0)
        self.psp[pool] = k + 1
        return self.PS[4 * pool + (k % 4)]

    def run_pipe(self, gens, K, gap=1):
        it = iter(gens)
        active = []
        more = True
        rnd = 0
        while True:
            if gap == 0:
                if more and not active:
                    while len(active) < K:
                        g = next(it, None)
                        if g is None:
                            more = False
                            break
                        active.append(g)
            elif more and len(active) < K and rnd % gap == 0:
                g = next(it, None)
                if g is None:
                    more = False
                else:
                    active.append(g)
            rnd += 1
            if not active:
                if not more:
                    break
                continue
            for g in list(active):
                try:
                    next(g)
                except StopIteration:
                    active.remove(g)

    def run_streams(self, gens):
        gens = list(gens)
        while gens:
            for g in list(gens):
                try:
                    next(g)
                except StopIteration:
                    gens.remove(g)

    def barrier(self):
        em = self.em
        need = {k: v for k, v in em.cnt.items() if v > 0}
        for e in Em.ENG:
            em._emit_waits(e, dict(need))

    def declare(self):
        i = self.inp
        i("x", [TX, D]); i("ctx", [TC, D]); i("cvec", [128, KT, 2])
        i("mod_w", [2, 128, KT, 6 * D]); i("mod_b", [2, 128, 48]); i("ng", [128, 4, KT]); i("fng", [128, D])
        i("ab_w_in", [128, KT, ABN]); i("hlb", [128, 2, 3, 512]); i("hng", [128, 512]); i("gng", [128, 512])
        i("gcw", [128, 12, 4]); i("galog", [128, 8]); i("gdtb", [128, 8]); i("ab_w_out", [128, KT, D])
        i("c_w_in", [128, KT, 2 * D]); i("ccw", [128, KT, 4]); i("ccb", [128, KT]); i("cgw", [128, 16, 512])
        i("cgb", [128, 2, 4, 4]); i("clam", [128, 2, KT]); i("c_w_out", [128, KT, D])
        i("ffn_w_up", [2, 128, KT, 2 * FF]); i("fcw", [2, 128, FT, 9]); i("fcb", [2, 128, FT])
        i("ffn_w_down", [2, 128, FT, D])
        i("cst", [128, 10, 128]); i("sel", [128, 2, 3]); i("m4", [128, 2, 512])
        nc = self.nc
        self.out = T(nc.dram_tensor("out", [TX, D], F32, kind="ExternalOutput").ap())
        self.outs.append(self.out)
        self.xT = DT(self.scr("xT", [KT, 128, P], F32).ap.rearrange("k p t -> p k t"))
        self.yT = DT(self.scr("yT", [KT, 128, P], BF16).ap.rearrange("k p t -> p k t"))
        self.uT = DT(self.scr("uT", [FT, 128, P], BF16).ap.rearrange("k p t -> p k t"))

    def setup(self):
        st = self.stack
        nc = self.nc
        self.PS = [T(st.enter_context(nc.psum_tensor("ps%d" % k, [128, 512], F32)).ap()) for k in range(8)]
        self.cst = self.sb(st, "cst", [128, 10, 128], F32)
        self.dma(self.cst, self.din["cst"])
        self.ident = self.cst[:, 0, :]
        self.ones = self.cst[:, 1, :]
        self.identb = self.sb(st, "identb", [128, 128], BF16)
        self.copy(self.identb, self.ident)
        self.onesb = self.sb(st, "onesb", [128, 128], BF16)
        self.copy(self.onesb, self.ones)
        self.ng = self.sb(st, "ng", [128, 4, KT], F32)
        self.dma(self.ng, self.din["ng"])
        self.modv = [self.sb(st, "modv%d" % l, [128, 48, 2], F32) for l in range(2)]
        self.gs = [[self.sb(st, "gs%d_%d" % (l, w), [128, KT, 2], F32) for w in range(2)] for l in range(2)]
        self.epsc = self.sb(st, "epsc", [128, 1], F32)
        self.memset(self.epsc, EPS)

    def mod_phase(self, l):
        nc = self.nc
        with ExitStack() as st:
            cv = self.sb(st, "cv", [128, KT, 2], F32)
            sv = self.sb(st, "sv", [128, KT, 2], F32)
            mb = self.sb(st, "mb", [128, 48], F32)
            tmp = self.sb(st, "mtmp", [128, KT, 2], F32)
            wb = [self.sb(st, "mw%d" % k, [128, KT, 128], F32) for k in range(2)]
            self.dma(cv, self.din["cvec"])
            self.dma(mb, self.din["mod_b"][l])
            self.act(sv, cv, AF.Silu)
            ps = self.nextps()
            for n in range(48):
                w = wb[n % 2]
                self.dma(w, self.din["mod_w"][l, :, :, n * 128:(n + 1) * 128])
                for kt in range(KT):
                    self.mm(ps[:, 2 * n:2 * n + 2], w[:, kt, :], sv[:, kt, :], start=(kt == 0), stop=(kt == KT - 1))
            pv = ps[:, 0:96]
            for j in range(2):
                self.tt(self.modv[l][:, :, j], T(pv.ap.rearrange("p (n j) -> p n j", j=2)[:, :, j], pv.s), mb, ALU.add)
            for w_, sec in ((0, 1), (1, 4)):
                self.ts(tmp, self.modv[l][:, sec * 8:(sec + 1) * 8, :], 1.0, ALU.add)
                for j in range(2):
                    self.tt(self.gs[l][w_][:, :, j], tmp[:, :, j], self.ng[:, 2 * l + w_, :], ALU.mult)
            self.barrier()

    def xinit_phase(self):
        with ExitStack() as st:
            xin = [self.sb(st, "xin%d" % k, [128, D], F32) for k in range(2)]
            xo = [self.sb(st, "xo%d" % k, [128, KT, 128], F32) for k in range(2)]
            for i in range(NCH):
                src = self.din["ctx"][i * 128:(i + 1) * 128, :] if i < 2 else self.din["x"][(i - 2) * 128:(i - 1) * 128, :]
                xi = xin[i % 2]
                self.dma(xi, src)
                for half in range(2):
                    ps = self.nextps()
                    for k4 in range(4):
                        kt = half * 4 + k4
                        self.tr(ps[:, k4 * 128:(k4 + 1) * 128], xi[:, kt * 128:(kt + 1) * 128], self.ident)
                    dst = xo[i % 2][:, half * 4:(half + 1) * 4, :]
                    src_ps = T(ps.ap.rearrange("p (k t) -> p k t", t=128), ps.s)
                    self.copy(dst, src_ps, e=("act" if half else "dve"))
                self.dma(self.xT.t(self.xT.ap[:, :, i * 128:(i + 1) * 128], i * 128, 128), xo[i % 2])
            self.barrier()

    def norm_phase(self, st, l, w_, hT, tiles):
        sec_sh = 0 if w_ == 0 else 3
        with ExitStack() as ph:
            xs = [self.sb(ph, "nxs%d" % k, [128, KT, 512], F32) for k in range(2)]
            sq = [self.sb(ph, "nsq%d" % k, [128, KT, 512], F32) for k in range(2)]
            rs = [self.sb(ph, "nrs%d" % k, [128, 512], F32) for k in range(2)]
            tm = [self.sb(ph, "ntm%d" % k, [128, 512], F32) for k in range(2)]
            for i, (p0, n, j) in enumerate(tiles):
                x_ = xs[i % 2]
                self.dma(x_[:, :, :n], self.xT.t(self.xT.ap[:, :, p0:p0 + n], p0, n))
                self.act(sq[i % 2][:, :, :n], x_[:, :, :n], AF.Square)
                ps = self.nextps()
                for kt in range(KT):
                    self.mm(ps[:, :n], self.ones, sq[i % 2][:, kt, :n], start=(kt == 0), stop=(kt == KT - 1))
                r_ = rs[i % 2]
                self.act(r_[:, :n], ps[:, :n], AF.Sqrt, bias=self.epsc, scale=1.0 / D)
                self.recip(r_[:, :n], r_[:, :n])
                for kt in range(KT):
                    t_ = tm[kt % 2]
                    self.stt(t_[:, :n], x_[:, kt, :n], self.gs[l][w_][:, kt, j:j + 1], r_[:, :n], ALU.mult, ALU.mult)
                    self.act(hT[:, kt, p0:p0 + n], t_[:, :n], AF.Identity,
                             bias=self.modv[l][:, sec_sh * 8 + kt, j:j + 1], scale=1.0)
            self.barrier()

    def outproj_phase(self, l, wname, widx, nkt, src, gsec, tiles):
        with ExitStack() as ph:
            wsb = self.sb(ph, "opw", [128, nkt, D], BF16)
            wd = self.din[wname] if widx is None else self.din[wname][widx]
            for kt in range(nkt):
                self.dma(wsb[:, kt, :], wd[:, kt, :], q="pool")
            ub = [self.sb(ph, "opu%d" % k, [128, nkt, 512], BF16) for k in range(2)]
            xs = [self.sb(ph, "opx%d" % k, [128, KT, 512], F32) for k in range(2)]
            for i, (p0, n, j) in enumerate(tiles):
                u = ub[i % 2]
                x_ = xs[i % 2]
                self.dma(u[:, :, :n], src.t(src.ap[:, :, p0:p0 + n], p0, n))
                self.dma(x_[:, :, :n], self.xT.t(self.xT.ap[:, :, p0:p0 + n], p0, n))
                for nt in range(KT):
                    ps = self.nextps()
                    for kt in range(nkt):
                        self.mm(ps[:, :n], wsb[:, kt, nt * 128:(nt + 1) * 128], u[:, kt, :n],
                                start=(kt == 0), stop=(kt == nkt - 1))
                    self.stt(x_[:, nt, :n], ps[:, :n], self.modv[l][:, gsec * 8 + nt, j:j + 1], x_[:, nt, :n],
                             ALU.mult, ALU.add)
                self.dma(self.xT.t(self.xT.ap[:, :, p0:p0 + n], p0, n), x_[:, :, :n])
            self.barrier()

    def ffn_phase(self, l, hT, tiles):
        nc = self.nc
        L0 = 323
        PADLEN = L0 + TX + 65
        with ExitStack() as ph:
            aT = self.sb(ph, "faT", [128, PADLEN], BF16)
            self.memset(aT, 0.0)
            cw = self.sb(ph, "fcw", [128, FT, 9], F32)
            cb = self.sb(ph, "fcb", [128, FT], F32)
            self.dma(cw, self.din["fcw"][l])
            self.dma(cb, self.din["fcb"][l])
            wa = [self.sb(ph, "fwa%d" % k, [128, KT, 128], BF16) for k in range(2)]
            wv = [self.sb(ph, "fwv%d" % k, [128, KT, 128], BF16) for k in range(2)]
            dw = [self.sb(ph, "fdw%d" % k, [128, 9, 128], BF16) for k in range(2)]
            sa = [self.sb(ph, "fsa%d" % k, [128, 512], F32) for k in range(2)]
            uo = [self.sb(ph, "fuo%d" % k, [128, 512], BF16) for k in range(2)]
            wup = self.din["ffn_w_up"][l]
            it = 0
            for c in range(FT):
                a_w, v_w, d_w = wa[c % 2], wv[c % 2], dw[c % 2]
                self.dma(a_w, wup[:, :, c * 128:(c + 1) * 128], q="pool")
                self.dma(v_w, wup[:, :, FF + c * 128:FF + (c + 1) * 128], q="pool")
                for tap in range(9):
                    self.ts(d_w[:, tap, :], self.identb, cw[:, c, tap:tap + 1], ALU.mult)
                for (p0, n, j) in tiles:
                    ps = self.nextps()
                    for kt in range(KT):
                        self.mm(ps[:, :n], a_w[:, kt, :], hT[:, kt, p0:p0 + n], start=(kt == 0), stop=(kt == KT - 1))
                    off = 1 + p0 if j == 1 else L0 + (p0 - TC)
                    self.copy(aT[:, off:off + n], ps[:, :n], e="act")
                for (p0, n, j) in tiles:
                    psv = self.nextps()
                    for kt in range(KT):
                        self.mm(psv[:, :n], v_w[:, kt, :], hT[:, kt, p0:p0 + n], start=(kt == 0), stop=(kt == KT - 1))
                    psc = self.nextps()
                    if j == 1:
                        for q_, kw in enumerate((1, 0, 2)):
                            o = 1 + p0 + (kw - 1)
                            self.mm(psc[:, :n], d_w[:, 3 + kw, :], aT[:, o:o + n], start=(q_ == 0), stop=(q_ == 2))
                    else:
                        t0 = L0 + (p0 - TC)
                        order = [(1, 1)] + [(kh, kw) for kh in range(3) for kw in range(3) if (kh, kw) != (1, 1)]
                        pv3 = T(psc.ap.rearrange("p (r c) -> p r c", c=64), psc.s)
                        for q_, (kh, kw) in enumerate(order):
                            o = t0 + 64 * (kh - 1)
                            src3 = T(aT.ap[:, o:o + 512].rearrange("p (r c) -> p r c", c=64), aT.s)
                            if kw == 1:
                                dst_, s_ = pv3, src3
                            elif kw == 2:
                                dst_, s_ = pv3[:, :, 0:63], src3[:, :, 1:64]
                            else:
                                dst_, s_ = pv3[:, :, 1:64], src3[:, :, 0:63]
                            self.mm(dst_, d_w[:, kh * 3 + kw, :], s_, start=(q_ == 0), stop=(q_ == 8))
                    s_a = sa[it % 2]
                    u_o = uo[it % 2]
                    it += 1
                    self.act(s_a[:, :n], psc[:, :n], AF.Silu, bias=cb[:, c:c + 1], scale=1.0)
                    self.tt(u_o[:, :n], s_a[:, :n], psv[:, :n], ALU.mult)
                    self.dma(self.uT.t(self.uT.ap[:, c, p0:p0 + n], p0, n), u_o[:, :n])
            self.barrier()

    def final_phase(self):
        with ExitStack() as ph:
            fg = self.sb(ph, "fng", [128, D], F32)
            self.dma(fg, self.din["fng"])
            xs = [self.sb(ph, "fx%d" % k, [128, KT, 128], F32) for k in range(2)]
            xt = [self.sb(ph, "fxt%d" % k, [128, D], F32) for k in range(2)]
            junk = self.sb(ph, "fjunk", [128, D], F32)
            ss = [self.sb(ph, "fss%d" % k, [128, 1], F32) for k in range(2)]
            for i in range(TX // 128):
                p0 = TC + i * 128
                x_ = xs[i % 2]
                self.dma(x_, self.xT.t(self.xT.ap[:, :, p0:p0 + 128], p0, 128))
                o_ = xt[i % 2]
                for half in range(2):
                    ps = self.nextps()
                    for k4 in range(4):
                        self.tr(ps[:, k4 * 128:(k4 + 1) * 128], x_[:, half * 4 + k4, :], self.ident)
                    self.copy(o_[:, half * 512:(half + 1) * 512], ps, e=("act" if half else "dve"))
                s_ = ss[i % 2]
                self.act(junk, o_, AF.Square, accum=s_)
                self.act(s_, s_, AF.Sqrt, bias=self.epsc, scale=1.0 / D)
                self.recip(s_, s_)
                self.stt(o_, o_, s_, fg, ALU.mult, ALU.mult)
                self.dma(self.out[i * 128:(i + 1) * 128, :], o_)
            self.barrier()


def _ktm(w, kt):
    return np.ascontiguousarray(w.reshape(kt, 128, -1).transpose(1, 0, 2))


def _col(v, kt):
    return np.ascontiguousarray(v.reshape(kt, 128).T)


def _consts():
    j = np.arange(128)[:, None]
    i = np.arange(128)[None, :]
    c = np.zeros((128, 10, 128), np.float32)
    c[:, 0] = (j == i)
    c[:, 1] = 1.0
    c[:, 2] = (j <= i)
    c[:, 3] = (j >= i)
    c[:, 4] = 1.0 * ((j >= 64) & (j <= i)) - 1.0 * ((j > i) & (j <= 63))
    c[:, 5] = 1.0 * ((j >= i) & (j <= 63)) - 1.0 * ((j >= 64) & (j < i))
    c[:, 6] = np.where(j <= i, 0.0, -30000.0)
    c[:, 7] = np.where(j >= i, 0.0, -30000.0)
    c[:, 8] = np.where(i < j, 0.0, 30000.0)
    c[:, 9] = np.where(i > j, 0.0, 30000.0)
    t = np.arange(128)
    sel = np.zeros((128, 2, 3), np.float32)
    sel[:, 0, 0] = (t <= 63); sel[:, 0, 1] = 1.0; sel[:, 0, 2] = (t > 63)
    sel[:, 1, 0] = (t >= 64); sel[:, 1, 1] = 1.0; sel[:, 1, 2] = (t < 64)
    m4 = np.zeros((128, 2, 512), np.float32)
    m4[:, 0] = np.tile(c[:, 2], (1, 4))
    m4[:, 1] = np.tile(c[:, 3], (1, 4))
    return c, sel, m4


def host_prep(inp, b):
    f = lambda a: np.ascontiguousarray(np.asarray(a, dtype=np.float32))
    m = {}
    m["x"] = f(inp["x"][b]); m["ctx"] = f(inp["ctx"][b])
    cv = np.stack([inp["c"][b], inp["c_ctx"]], axis=-1)
    m["cvec"] = f(cv.reshape(KT, 128, 2).transpose(1, 0, 2))
    m["mod_w"] = f(np.stack([_ktm(inp["mod_w"][l], KT) for l in range(2)]))
    m["mod_b"] = f(np.stack([_col(inp["mod_b"][l], 48) for l in range(2)]))
    m["ng"] = f(np.stack([_col(inp["norm_mix_g"][0], KT), _col(inp["norm_ffn_g"][0], KT),
                          _col(inp["norm_mix_g"][1], KT), _col(inp["norm_ffn_g"][1], KT)], axis=1))
    m["fng"] = f(np.tile(inp["final_norm_g"][None, :], (128, 1)))
    m["ab_w_in"] = f(_ktm(inp["ab_w_in"][0], KT))
    m["hlb"] = f(np.tile(inp["hgrn_lb"][None], (128, 1, 1, 1)))
    m["hng"] = f(np.tile(np.tile(inp["hgrn_norm_g"][0], 4)[None, :], (128, 1)))
    m["gng"] = f(np.tile(np.tile(inp["gdn_norm_g"][0], 4)[None, :], (128, 1)))
    m["gcw"] = f(inp["gdn_conv_w"][0].reshape(4, 12, 128).transpose(2, 1, 0))
    m["galog"] = f(np.tile(inp["gdn_a_log"][0].reshape(1, 8), (128, 1)))
    m["gdtb"] = f(np.tile(inp["gdn_dt_bias"][0].reshape(1, 8), (128, 1)))
    m["ab_w_out"] = f(_ktm(inp["ab_w_out"][0], KT))
    m["c_w_in"] = f(_ktm(inp["c_w_in"][0], KT))
    m["ccw"] = f(inp["c_conv_w"][0].reshape(4, KT, 128).transpose(2, 1, 0))
    m["ccb"] = f(_col(inp["c_conv_b"][0], KT))
    gw = inp["c_gate_w"][0]
    m["cgw"] = f(gw.reshape(2, 4, 2, 128, 512).transpose(3, 0, 1, 2, 4).reshape(128, 16, 512))
    m["cgb"] = f(inp["c_gate_b"][0].reshape(2, 4, 4, 128).transpose(3, 0, 1, 2))
    m["clam"] = f(inp["c_lambda"][0].reshape(2, KT, 128).transpose(2, 0, 1))
    m["c_w_out"] = f(_ktm(inp["c_w_out"][0], KT))
    m["ffn_w_up"] = f(np.stack([_ktm(inp["ffn_w_up"][l], KT) for l in range(2)]))
    m["fcw"] = f(np.stack([inp["ffn_conv_w"][l].reshape(9, FT, 128).transpose(2, 1, 0) for l in range(2)]))
    m["fcb"] = f(np.stack([_col(inp["ffn_conv_b"][l], FT) for l in range(2)]))
    m["ffn_w_down"] = f(np.stack([_ktm(inp["ffn_w_down"][l], FT) for l in range(2)]))
    c, sel, m4 = _consts()
    m["cst"] = c; m["sel"] = sel; m["m4"] = m4
    return m


GELU_C = 1.5957691216057308


def mixer_c_phases(self, hT):
    nc = self.nc
    CP0, CL0 = 2, 261
    CPAD = 4358
    xbT = DT(self.scr("xbT", [KT, 128, P], F32).ap.rearrange("k p t -> p k t"))
    gelT = DT(self.scr("gelT", [KT, 128, P], F32).ap.rearrange("k p t -> p k t"))
    win = self.din["c_w_in"]
    with ExitStack() as ph:
        pre = self.sb(ph, "cpre", [128, CPAD], BF16)
        self.memset(pre, 0.0)
        ccw = self.sb(ph, "ccw", [128, KT, 4], F32)
        ccb = self.sb(ph, "ccb", [128, KT], F32)
        self.dma(ccw, self.din["ccw"])
        self.dma(ccb, self.din["ccb"])
        wg = [self.sb(ph, "cwg%d" % k, [128, KT, 128], BF16) for k in range(2)]
        wx = [self.sb(ph, "cwx%d" % k, [128, KT, 128], BF16) for k in range(2)]
        dw = [self.sb(ph, "cdw%d" % k, [128, 4, 128], BF16) for k in range(2)]
        gx = [self.sb(ph, "cgx%d" % k, [128, 512], F32) for k in range(2)]
        gt = [self.sb(ph, "cgt%d" % k, [128, 512], F32) for k in range(2)]
        xo = [self.sb(ph, "cxo%d" % k, [128, 512], F32) for k in range(2)]
        it = 0
        for c in range(KT):
            g_w, x_w, d_w = wg[c % 2], wx[c % 2], dw[c % 2]
            self.dma(g_w, win[:, :, c * 128:(c + 1) * 128], q="pool")
            self.dma(x_w, win[:, :, D + c * 128:D + (c + 1) * 128], q="pool")
            for k in range(4):
                self.ts(d_w[:, k, :], self.identb, ccw[:, c, k:k + 1], ALU.mult)
            for (p0, n, j) in TILES:
                ps = self.nextps()
                for kt in range(KT):
                    self.mm(ps[:, :n], x_w[:, kt, :], hT[:, kt, p0:p0 + n], start=(kt == 0), stop=(kt == KT - 1))
                off = CP0 + p0 if j == 1 else CL0 + (p0 - TC)
                self.copy(pre[:, off:off + n], ps[:, :n], e="act")
            for (p0, n, j) in TILES:
                off = CP0 + p0 if j == 1 else CL0 + (p0 - TC)
                ps = self.nextps()
                for k in range(4):
                    self.mm(ps[:, :n], d_w[:, k, :], pre[:, off + k - 2:off + k - 2 + n], start=(k == 0), stop=(k == 3))
                x_o = xo[it % 2]
                self.act(x_o[:, :n], ps[:, :n], AF.Identity, bias=ccb[:, c:c + 1], scale=1.0)
                self.dma(xbT.t(xbT.ap[:, c, p0:p0 + n], p0, n), x_o[:, :n])
                if j == 0:
                    ps2 = self.nextps()
                    for kt in range(KT):
                        self.mm(ps2[:, :n], g_w[:, kt, :], hT[:, kt, p0:p0 + n], start=(kt == 0), stop=(kt == KT - 1))
                    g_x, g_t = gx[it % 2], gt[it % 2]
                    self.copy(g_x[:, :n], ps2[:, :n], e="act")
                    self.tt(g_t[:, :n], g_x[:, :n], g_x[:, :n], ALU.mult, e="pool")
                    self.ts(g_t[:, :n], g_t[:, :n], 0.044715, ALU.mult, 1.0, ALU.add)
                    self.tt(g_t[:, :n], g_t[:, :n], g_x[:, :n], ALU.mult, e="pool")
                    self.act(g_t[:, :n], g_t[:, :n], AF.Sigmoid, scale=GELU_C)
                    self.tt(g_x[:, :n], g_x[:, :n], g_t[:, :n], ALU.mult)
                    self.dma(gelT.t(gelT.ap[:, c, p0:p0 + n], p0, n), g_x[:, :n])
                it += 1
        self.barrier()
    return xbT, gelT


def mixer_c_scan(self, xbT, gelT):
    nc = self.nc
    with ExitStack() as ph:
        cgw = self.sb(ph, "cgw", [128, 16, 512], BF16)
        for k in range(16):
            self.dma(cgw[:, k, :], self.din["cgw"][:, k, :], q="pool")
        cgb = self.sb(ph, "cgb", [128, 2, 4, 4], F32)
        lam = self.sb(ph, "clam", [128, 2, KT], F32)
        scl = self.sb(ph, "cscl", [128, 2, KT], F32)
        scl2 = self.sb(ph, "cscl2", [128, 2, KT], F32)
        self.dma(cgb, self.din["cgb"])
        self.dma(lam, self.din["clam"])
        one = self.ones[:, 0:1]
        self.act(scl, lam, AF.Exp, scale=-1.0)
        self.act(scl, scl, AF.Ln, bias=one, scale=1.0)
        self.ts(scl2, scl, -16.0, ALU.mult)
        self.ts(scl, scl, -8.0, ALU.mult)
        xbf = self.sb(ph, "cxbf", [128, 2, P], F32)
        xbb = self.sb(ph, "cxbb", [128, 2, P], BF16)
        rb = self.sb(ph, "crb", [128, P], F32)
        ib = self.sb(ph, "cib", [128, P], F32)
        ab = self.sb(ph, "cab", [128, P], F32)
        hs = [self.sb(ph, "chs%d" % k, [128, P], F32) for k in range(2)]
        gl = [self.sb(ph, "cgl%d" % k, [128, 512], F32) for k in range(2)]
        yo = [self.sb(ph, "cyo%d" % k, [128, 512], BF16) for k in range(2)]
        it = 0
        for h in range(4):
            for half in range(2):
                self.dma(xbf[:, half, :], T(xbT.ap[:, 2 * h + half, :], tuple(xbT.slots)))
                self.copy(xbb[:, half, :], xbf[:, half, :], e="pool")
            for half in range(2):
                c = 2 * h + half
                for d in range(2):
                    for (p0, n, j) in TILES:
                        psr = self.nextps()
                        psi = self.nextps()
                        for k2 in range(2):
                            wbase = cgw[:, (d * 4 + h) * 2 + k2, :]
                            self.mm(psr[:, :n], wbase[:, half * 128:(half + 1) * 128], xbb[:, k2, p0:p0 + n],
                                    start=(k2 == 0), stop=(k2 == 1))
                        for k2 in range(2):
                            wbase = cgw[:, (d * 4 + h) * 2 + k2, :]
                            self.mm(psi[:, :n], wbase[:, 256 + half * 128:256 + (half + 1) * 128], xbb[:, k2, p0:p0 + n],
                                    start=(k2 == 0), stop=(k2 == 1))
                        self.act(rb[:, p0:p0 + n], psr[:, :n], AF.Sigmoid, bias=cgb[:, d, h, half:half + 1], scale=1.0)
                        self.act(ib[:, p0:p0 + n], psi[:, :n], AF.Sigmoid, bias=cgb[:, d, h, 2 + half:3 + half], scale=1.0)
                    self.act(ab, rb, AF.Exp, scale=scl[:, d, c:c + 1])
                    self.act(rb, rb, AF.Exp, scale=scl2[:, d, c:c + 1])
                    self.ts(rb, rb, -1.0, ALU.mult, 1.0, ALU.add)
                    self.act(rb, rb, AF.Sqrt)
                    self.tt(ib, ib, xbf[:, half, :], ALU.mult, e="pool")
                    self.tt(rb, rb, ib, ALU.mult)
                    h_ = hs[d]
                    if d == 0:
                        self.op("dve", lambda h_=h_: nc.vector.tensor_tensor_scan(h_.ap, ab.ap, rb.ap, 0.0, ALU.mult, ALU.add),
                                r=[ab, rb], w=[h_])
                    else:
                        self.op("dve", lambda h_=h_: nc.vector.tensor_tensor_scan(
                            h_.ap[:, TC - 1::-1], ab.ap[:, TC - 1::-1], rb.ap[:, TC - 1::-1], 0.0, ALU.mult, ALU.add),
                            r=[ab, rb], w=[h_])
                        self.op("dve", lambda h_=h_: nc.vector.tensor_tensor_scan(
                            h_.ap[:, P - 1:TC - 1:-1], ab.ap[:, P - 1:TC - 1:-1], rb.ap[:, P - 1:TC - 1:-1],
                            h_.ap[:, 0:1], ALU.mult, ALU.add), r=[ab, rb, h_], w=[h_])
                self.tt(hs[0], hs[0], hs[1], ALU.add, e="pool")
                for (p0, n, j) in LTILES:
                    g_ = gl[it % 2]
                    y_ = yo[it % 2]
                    it += 1
                    self.dma(g_[:, :n], gelT.t(gelT.ap[:, c, p0:p0 + n], p0, n))
                    self.tt(y_[:, :n], g_[:, :n], hs[0][:, p0:p0 + n], ALU.mult)
                    self.dma(self.yT.t(self.yT.ap[:, c, p0:p0 + n], p0, n), y_[:, :n])
        self.barrier()


Prog.mixer_c_phases = mixer_c_phases
Prog.mixer_c_scan = mixer_c_scan


NTM = 3088
CP0, CL0, CPAD = 2, 261, 4358


def psb(ps):
    return T(ps.ap.bitcast(BF16), ps.s)


def ab_proj_phase(self, hT):
    self.Atm = DT(self.scr("Atm", [P, NTM], F32).ap)
    self.preT = self.scr("preT", [12, 128, CPAD], BF16)
    win = self.din["ab_w_in"]
    with ExitStack() as ph:
        wb = [self.sb(ph, "abw%d" % k, [128, KT, 512], BF16) for k in range(2)]
        ob = [self.sb(ph, "abo%d" % k, [128, 512], F32) for k in range(3)]
        blocks = [(0, 0, 512, "silu"), (512, 512, 512, "c"), (1024, 1024, 512, "c"), (1536, 1536, 512, "c"),
                  (2048, 2048, 512, "silu"), (4096, 2560, 512, "silu"), (4608, 3072, 16, "c")]
        it = 0
        for bi, (wc, oc, ncol, kind) in enumerate(blocks):
            w = wb[bi % 2]
            self.dma(w[:, :, :ncol], win[:, :, wc:wc + ncol], q="pool")
            for i in range(NCH):
                ps = self.nextps()
                for kt in range(KT):
                    self.mm(ps[:, :ncol], hT[:, kt, i * 128:(i + 1) * 128], w[:, kt, :ncol],
                            start=(kt == 0), stop=(kt == KT - 1))
                o = ob[it % 3]
                it += 1
                if kind == "silu":
                    self.act(o[:, :ncol], ps[:, :ncol], AF.Silu)
                else:
                    self.copy(o[:, :ncol], ps[:, :ncol], e="dve")
                self.dma(self.Atm.t(self.Atm.ap[i * 128:(i + 1) * 128, oc:oc + ncol], i * 128, 128), o[:, :ncol])
        pre = [self.sb(ph, "abpre%d" % k, [128, CPAD], BF16) for k in range(2)]
        wq = [self.sb(ph, "abwq%d" % k, [128, KT, 128], BF16) for k in range(2)]
        gcw = self.sb(ph, "abgcw", [128, 12, 4], F32)
        self.dma(gcw, self.din["gcw"])
        dwq = [self.sb(ph, "abdw%d" % k, [128, 4, 128], BF16) for k in range(2)]
        so = [self.sb(ph, "abso%d" % k, [128, 512], BF16) for k in range(3)]
        self.qkvT = DT(self.scr("qkvT", [12, 128, P], BF16).ap.rearrange("c p t -> p c t"))
        for k in range(2):
            self.memset(pre[k], 0.0)
        it = 0
        for c in range(12):
            w = wq[c % 2]
            pr = pre[c % 2]
            self.dma(w, win[:, :, 2560 + c * 128:2560 + (c + 1) * 128], q="pool")
            for k in range(4):
                self.ts(dwq[c % 2][:, k, :], self.identb, gcw[:, c, k:k + 1], ALU.mult)
            for (p0, n, j) in TILES:
                ps = self.nextps()
                for kt in range(KT):
                    self.mm(ps[:, :n], w[:, kt, :], hT[:, kt, p0:p0 + n], start=(kt == 0), stop=(kt == KT - 1))
                off = CP0 + p0 if j == 1 else CL0 + (p0 - TC)
                self.copy(pr[:, off:off + n], ps[:, :n], e=("act" if (p0 // 512) % 2 else "dve"))
            for (p0, n, j) in TILES:
                off = CP0 + p0 if j == 1 else CL0 + (p0 - TC)
                ps = self.nextps()
                for k in range(4):
                    self.mm(ps[:, :n], dwq[c % 2][:, k, :], pr[:, off + k - 2:off + k - 2 + n], start=(k == 0), stop=(k == 3))
                o = so[it % 3]
                it += 1
                self.act(o[:, :n], ps[:, :n], AF.Silu)
                self.dma(self.qkvT.t(self.qkvT.ap[:, c, p0:p0 + n], p0, n), o[:, :n])
        self.barrier()


def chunk_order(d):
    return list(range(NCH)) if d == 0 else [1, 0] + list(range(NCH - 1, 1, -1))


def hgrn_pass(self, d, st_p):
    nc = self.nc
    sc = 128.0 ** -0.5
    with ExitStack() as ph:
        S = self.sb(ph, "hS", [128, 4, 128], F32)
        self.memset(S, 0.0)
        Mrel = self.cst[:, 4 + d, :]
        sel = self.selc[:, d, :]
        m4 = self.m4c[:, d, :]
        lbB = self.lbB[:, d, :]
        omlb = self.omlb[:, d, :]
        inb = [self.sb(ph, "hin%d" % k, [128, 2048], F32) for k in range(2)]
        sg = self.sb(ph, "hsg", [128, 512], F32)
        lf = [self.sb(ph, "hlf%d" % k, [128, 512], F32) for k in range(2)]
        kk = self.sb(ph, "hkk", [128, 512], F32)
        E1 = self.sb(ph, "hE1", [128, 512], F32)
        E2 = self.sb(ph, "hE2", [128, 512], F32)
        Ee = [self.sb(ph, "hEe%d" % k, [128, 12], F32) for k in range(2)]
        qin = [self.sb(ph, "hqin%d" % k, [128, 512], BF16) for k in range(2)]
        kin = [self.sb(ph, "hkin%d" % k, [128, 512], BF16) for k in range(2)]
        vbf = [self.sb(ph, "hvbf%d" % k, [128, 512], BF16) for k in range(2)]
        qT = [self.sb(ph, "hqT%d" % k, [128, 4, 128], BF16) for k in range(2)]
        kT = [self.sb(ph, "hkT%d" % k, [128, 4, 128], BF16) for k in range(2)]
        at = [self.sb(ph, "hat%d" % k, [128, 512], BF16) for k in range(2)]
        St = [self.sb(ph, "hSt%d" % k, [128, 4, 128], BF16) for k in range(2)]
        tmpS = self.sb(ph, "htmpS", [128, 4, 128], F32)
        ob = [self.sb(ph, "hob%d" % k, [128, 512], F32) for k in range(2)]
        if d == 1:
            oa = [self.sb(ph, "hoa%d" % k, [128, 512], F32) for k in range(2)]
            ag = [self.sb(ph, "hag%d" % k, [128, 512], F32) for k in range(2)]
            hng = self.sb(ph, "hng", [128, 512], F32)
            self.dma(hng, self.din["hng"])
            junk = self.sb(ph, "hjunk", [128, 128], F32)
            ss = [self.sb(ph, "hss%d" % k, [128, 4], F32) for k in range(2)]
            ya = [self.sb(ph, "hya%d" % k, [128, 512], BF16) for k in range(2)]
            yaT = [self.sb(ph, "hyaT%d" % k, [128, 4, 128], BF16) for k in range(2)]
        for it, ci in enumerate(chunk_order(d)):
            b2 = it % 2
            p0 = ci * 128
            x_ = inb[b2]
            self.dma(x_, self.Atm.t(self.Atm.ap[p0:p0 + 128, 0:2048], p0, 128))
            af = x_[:, 512 * (1 + d):512 * (2 + d)]
            self.act(sg, af, AF.Sigmoid)
            self.tt(sg, sg, omlb, ALU.mult)
            self.tt(sg, sg, lbB, ALU.add, e="pool")
            l_ = lf[b2]
            self.act(l_, sg, AF.Ln)
            self.ts(kk, sg, -1.0, ALU.mult, 1.0, ALU.add, e="pool")
            psb_ = self.nextps()
            self.mm(psb_, Mrel, l_)
            pse = self.nextps()
            for h in range(4):
                self.mm(pse[:, 3 * h:3 * h + 3], l_[:, h * 128:(h + 1) * 128], sel)
            self.act(E1, psb_, AF.Exp)
            self.act(E2, psb_, AF.Exp, scale=-1.0)
            e_ = Ee[b2]
            self.act(e_, pse[:, 0:12], AF.Exp)
            q_, k_, v_ = qin[b2], kin[b2], vbf[b2]
            self.stt(q_, x_[:, 0:512], sc, E1, ALU.mult, ALU.mult)
            self.tt(k_, kk, E2, ALU.mult, e="pool")
            self.copy(v_, x_[:, 1536:2048], e="pool")
            psq = self.nextps()
            psk = self.nextps()
            for h in range(4):
                self.tr(psb(psq)[:, h * 128:(h + 1) * 128], q_[:, h * 128:(h + 1) * 128], self.identb)
            for h in range(4):
                self.tr(psb(psk)[:, h * 128:(h + 1) * 128], k_[:, h * 128:(h + 1) * 128], self.identb)
            qT_, kT_ = qT[b2], kT[b2]
            self.copy(T(qT_.ap.rearrange("p h t -> p (h t)"), qT_.s), psb(psq)[:, 0:512], e="dve")
            self.copy(T(kT_.ap.rearrange("p h t -> p (h t)"), kT_.s), psb(psk)[:, 0:512], e="act")
            psa = self.nextps()
            for h in range(4):
                self.mm(psa[:, h * 128:(h + 1) * 128], kT_[:, h, :], qT_[:, h, :])
            a_ = at[b2]
            self.ts(E1, psa, 1e30, ALU.min, -1e30, ALU.max)
            self.tt(a_, E1, m4, ALU.mult, e="pool")
            St_ = St[b2]
            for h in range(4):
                self.act(St_[:, h, :], S[:, h, :], AF.Copy, scale=e_[:, 3 * h:3 * h + 1])
            pso = self.nextps()
            for h in range(4):
                hs = slice(h * 128, (h + 1) * 128)
                self.mm(pso[:, hs], a_[:, hs], v_[:, hs], start=True, stop=False)
                self.mm(pso[:, hs], qT_[:, h, :], St_[:, h, :], start=False, stop=True)
            psd = self.nextps()
            for h in range(4):
                hs = slice(h * 128, (h + 1) * 128)
                self.mm(psd[:, hs], k_[:, hs], v_[:, hs])
            for h in range(4):
                hs = slice(h * 128, (h + 1) * 128)
                self.act(tmpS[:, h, :], psd[:, hs], AF.Copy, scale=e_[:, 3 * h + 2:3 * h + 3])
                self.stt(S[:, h, :], S[:, h, :], e_[:, 3 * h + 1:3 * h + 2], tmpS[:, h, :], ALU.mult, ALU.add)
            o_ = ob[b2]
            if d == 0:
                self.copy(o_, pso, e="dve")
                self.dma(self.OA.t(self.OA.ap[p0:p0 + 128, :], p0, 128), o_)
            else:
                oa_, ag_ = oa[b2], ag[b2]
                self.dma(oa_, self.OA.t(self.OA.ap[p0:p0 + 128, :], p0, 128))
                self.dma(ag_, self.Atm.t(self.Atm.ap[p0:p0 + 128, 2048:2560], p0, 128))
                self.tt(o_, pso, oa_, ALU.add)
                self.merge(o_, ag_, hng, junk, ss[b2], ya[b2], yaT[b2], 0, p0)
        self.barrier()


def merge(self, o_, gate_, gn, junk, ss_, ya_, yaT_, kt0, p0):
    for h in range(4):
        self.act(junk, o_[:, h * 128:(h + 1) * 128], AF.Square, accum=ss_[:, h:h + 1])
    self.act(ss_, ss_, AF.Sqrt, bias=self.epsc, scale=1.0 / 128)
    self.recip(ss_, ss_)
    self.tt(gate_, gate_, gn, ALU.mult, e="pool")
    for h in range(4):
        hs = slice(h * 128, (h + 1) * 128)
        self.stt(ya_[:, hs], o_[:, hs], ss_[:, h:h + 1], gate_[:, hs], ALU.mult, ALU.mult)
    pst = self.nextps()
    for h in range(4):
        self.tr(psb(pst)[:, h * 128:(h + 1) * 128], ya_[:, h * 128:(h + 1) * 128], self.identb)
    self.copy(T(yaT_.ap.rearrange("p h t -> p (h t)"), yaT_.s), psb(pst)[:, 0:512], e="act")
    self.dma(self.yT.t(self.yT.ap[:, kt0:kt0 + 4, p0:p0 + 128], p0, 128), yaT_)


def ab_setup(self, st):
    self.selc = self.sb(st, "selc", [128, 2, 3], F32)
    self.m4c = self.sb(st, "m4c", [128, 2, 512], F32)
    self.dma(self.selc, self.din["sel"])
    self.dma(self.m4c, self.din["m4"])
    hl = self.sb(st, "hlb", [128, 2, 3, 512], F32)
    self.dma(hl, self.din["hlb"])
    self.lbB = self.sb(st, "lbB", [128, 2, 512], F32)
    self.omlb = self.sb(st, "omlb", [128, 2, 512], F32)
    self.act(hl, hl, AF.Exp)
    self.tt(self.omlb, hl[:, :, 0, :], hl[:, :, 1, :], ALU.add)
    self.tt(self.omlb, self.omlb, hl[:, :, 2, :], ALU.add)
    self.recip(self.omlb, self.omlb)
    self.tt(self.lbB, hl[:, :, 0, :], self.omlb, ALU.mult)
    self.ts(self.omlb, self.lbB, -1.0, ALU.mult, 1.0, ALU.add)
    self.OA = DT(self.scr("OA", [P, 512], F32).ap)
    self.OB = DT(self.scr("OB", [P, 512], F32).ap)


Prog.ab_proj_phase = ab_proj_phase
Prog.hgrn_pass = hgrn_pass
Prog.merge = merge
Prog.ab_setup = ab_setup


def gdn_setup(self, st):
    self.gG = self.sb(st, "gG", [128, NCH, 8], F32)
    self.gB = self.sb(st, "gB", [128, NCH, 8], F32)
    self.gNB = self.sb(st, "gNB", [128, NCH, 8], F32)
    with ExitStack() as ph:
        gin = self.sb(ph, "gin", [128, NCH, 16], F32)
        al = self.sb(ph, "galog", [128, 8], F32)
        db = self.sb(ph, "gdtb", [128, 8], F32)
        self.dma(al, self.din["galog"])
        self.dma(db, self.din["gdtb"])
        for i in range(NCH):
            self.dma(gin[:, i, :], self.Atm.t(self.Atm.ap[i * 128:(i + 1) * 128, 3072:3088], i * 128, 128))
        for c in range(8):
            self.ts(self.gG[:, :, c], gin[:, :, c], db[:, c:c + 1], ALU.add)
        self.act(self.gG, self.gG, AF.Exp)
        self.act(self.gG, self.gG, AF.Ln, bias=self.ones[:, 0:1], scale=1.0)
        self.act(al, al, AF.Exp)
        for c in range(8):
            self.ts(self.gG[:, :, c], self.gG[:, :, c], al[:, c:c + 1], ALU.mult, -1.0, ALU.mult)
        self.act(self.gB, gin[:, :, 8:16], AF.Sigmoid)
        self.ts(self.gNB, self.gB, -1.0, ALU.mult)
        self.barrier()


def gdn_pass(self, d):
    nc = self.nc
    sc = 128.0 ** -0.5
    with ExitStack() as ph:
        S = self.sb(ph, "gS", [128, 4, 128], F32)
        Sb = self.sb(ph, "gSb", [128, 4, 128], BF16)
        self.memset(S, 0.0)
        self.memset(Sb, 0.0)
        TRI = self.cst[:, 2 + d, :]
        NEGT = self.cst[:, 6 + d, :]
        POSA = self.cst[:, 8 + d, :]
        gcw = self.sb(ph, "gcw", [128, 12, 4], F32)
        self.dma(gcw, self.din["gcw"])
        dwg = self.sb(ph, "gdw", [128, 12, 4, 128], BF16)
        for c in range(12):
            for k in range(4):
                self.ts(dwg[:, c, k, :], self.identb, gcw[:, c, k:k + 1], ALU.mult)
        I4 = self.sb(ph, "gI4", [128, 4, 128], F32)
        for h in range(4):
            self.copy(I4[:, h, :], self.ident, e="pool")
        pw = [self.sb(ph, "gpw%d" % k, [128, 12, 131], BF16) for k in range(2)]
        qs = self.sb(ph, "gqs", [128, 512], F32)
        ks = self.sb(ph, "gks", [128, 512], F32)
        vs = self.sb(ph, "gvs", [128, 512], F32)
        junk = self.sb(ph, "gjunk", [128, 128], F32)
        rn = self.sb(ph, "grn", [128, 8], F32)
        rnq = self.sb(ph, "grnq", [128, 4], F32)
        gcs = self.sb(ph, "ggcs", [128, 8], F32)
        ngc = self.sb(ph, "gngc", [128, 4], F32)
        egc = self.sb(ph, "gegc", [128, 4], F32)
        ekd = self.sb(ph, "gekd", [128, 4], F32)
        egl = self.sb(ph, "gegl", [128, 4], F32)
        bw = self.sb(ph, "gbw", [128, 4], F32)
        gbc = self.sb(ph, "ggbc", [128, 4, 128], F32)
        decT = self.sb(ph, "gdecT", [128, 512], F32)
        decA = self.sb(ph, "gdecA", [128, 4, 128], F32)
        khb = self.sb(ph, "gkhb", [128, 512], BF16)
        qhb = self.sb(ph, "gqhb", [128, 512], BF16)
        khT = self.sb(ph, "gkhT", [128, 4, 128], BF16)
        qhT = self.sb(ph, "gqhT", [128, 4, 128], BF16)
        X = [self.sb(ph, "gX%d" % k, [128, 4, 128], F32) for k in range(2)]
        XT = [self.sb(ph, "gXT%d" % k, [128, 4, 128], F32) for k in range(2)]
        PT = self.sb(ph, "gPT", [128, 4, 128], F32)
        RU = self.sb(ph, "gRU", [128, 4, 128], F32)
        RW = self.sb(ph, "gRW", [128, 4, 128], F32)
        u = self.sb(ph, "gu", [128, 512], F32)
        wT = self.sb(ph, "gwT", [128, 4, 128], BF16)
        atT = self.sb(ph, "gatT", [128, 512], BF16)
        qg = self.sb(ph, "gqg", [128, 512], BF16)
        qgT = self.sb(ph, "gqgT", [128, 4, 128], BF16)
        kd = self.sb(ph, "gkd", [128, 512], BF16)
        vn = self.sb(ph, "gvn", [128, 512], BF16)
        ob = [self.sb(ph, "gob%d" % k, [128, 512], F32) for k in range(2)]
        if d == 1:
            oa = [self.sb(ph, "goa%d" % k, [128, 512], F32) for k in range(2)]
            ag = [self.sb(ph, "gag%d" % k, [128, 512], F32) for k in range(2)]
            gng = self.sb(ph, "gng", [128, 512], F32)
            self.dma(gng, self.din["gng"])
            ss = [self.sb(ph, "gss%d" % k, [128, 4], F32) for k in range(2)]
            ya = [self.sb(ph, "gya%d" % k, [128, 512], BF16) for k in range(2)]
            yaT = [self.sb(ph, "gyaT%d" % k, [128, 4, 128], BF16) for k in range(2)]
        f2 = lambda t: T(t.ap.rearrange("p h t -> p (h t)"), t.s)
        H = [slice(h * 128, (h + 1) * 128) for h in range(4)]
        for it, ci in enumerate(chunk_order(d)):
            b2 = it % 2
            p0 = ci * 128
            off = CP0 + p0 if ci < 2 else CL0 + (p0 - TC)
            w_ = pw[b2]
            self.dma(w_, T(self.preT.ap[:, :, off - 2:off + 129].rearrange("c p t -> p c t"), self.preT.s))
            pss = [self.nextps() for _ in range(3)]
            for c in range(12):
                for k in range(4):
                    self.mm(pss[c // 4][:, H[c % 4]], w_[:, c, k:k + 128], dwg[:, c, k, :], start=(k == 0), stop=(k == 3))
            self.act(qs, pss[0], AF.Silu)
            self.act(ks, pss[1], AF.Silu)
            self.act(vs, pss[2], AF.Silu)
            for h in range(4):
                self.act(junk, qs[:, H[h]], AF.Square, accum=rn[:, h:h + 1])
                self.act(junk, ks[:, H[h]], AF.Square, accum=rn[:, 4 + h:5 + h])
            self.act(rn, rn, AF.Sqrt, bias=self.epsc, scale=1.0)
            self.recip(rn, rn)
            self.ts(rnq, rn[:, 0:4], sc, ALU.mult)
            g4 = self.gG[:, ci, 4 * d:4 * d + 4]
            b4 = self.gB[:, ci, 4 * d:4 * d + 4]
            nb4 = self.gNB[:, ci, 4 * d:4 * d + 4]
            psg = self.nextps()
            self.mm(psg[:, 0:4], TRI, g4)
            self.mm(psg[:, 4:8], self.ones, g4)
            self.copy(gcs, psg[:, 0:8], e="dve")
            self.ts(ngc, gcs[:, 0:4], -1.0, ALU.mult)
            self.act(egc, gcs[:, 0:4], AF.Exp)
            self.act(egl, gcs[:, 4:8], AF.Exp)
            self.tt(ekd, gcs[:, 4:8], gcs[:, 0:4], ALU.subtract)
            self.act(ekd, ekd, AF.Exp)
            self.tt(bw, b4, egc, ALU.mult)
            for h in range(4):
                self.ts(gbc[:, h, :], self.ones, g4[:, h:h + 1], ALU.mult, e="pool")
            psD = self.nextps()
            psA = self.nextps()
            for h in range(4):
                self.mm(psD[:, H[h]], gbc[:, h, :], TRI, start=True, stop=False)
                self.mm(psD[:, H[h]], self.ident, NEGT, start=False, stop=True)
            for h in range(4):
                self.mm(psA[:, H[h]], gbc[:, h, :], TRI, start=True, stop=False)
                self.mm(psA[:, H[h]], self.ident, POSA, start=False, stop=True)
            for h in range(4):
                self.act(decT[:, H[h]], psD[:, H[h]], AF.Exp, bias=ngc[:, h:h + 1], scale=1.0)
                self.act(decA[:, h, :], psA[:, H[h]], AF.Exp, bias=gcs[:, h:h + 1], scale=-1.0)
            for h in range(4):
                self.act(khb[:, H[h]], ks[:, H[h]], AF.Copy, scale=rn[:, 4 + h:5 + h])
                self.act(qhb[:, H[h]], qs[:, H[h]], AF.Copy, scale=rnq[:, h:h + 1])
            pst = self.nextps()
            pst2 = self.nextps()
            for h in range(4):
                self.tr(psb(pst)[:, H[h]], khb[:, H[h]], self.identb)
            for h in range(4):
                self.tr(psb(pst2)[:, H[h]], qhb[:, H[h]], self.identb)
            self.copy(f2(khT), psb(pst)[:, 0:512], e="dve")
            self.copy(f2(qhT), psb(pst2)[:, 0:512], e="act")
            psK = self.nextps()
            for h in range(4):
                self.mm(psK[:, H[h]], khT[:, h, :], khT[:, h, :])
            X0, XT0 = X[0], XT[0]
            for h in range(4):
                self.stt(X0[:, h, :], psK[:, H[h]], nb4[:, h:h + 1], decA[:, h, :], ALU.mult, ALU.mult)
            psn = self.nextps()
            for h in range(4):
                self.tr(psn[:, H[h]], X0[:, h, :], self.ident)
            self.copy(f2(XT0), psn, e="act")
            self.tt(f2(PT), f2(XT0), f2(I4), ALU.add, e="pool")
            cur = 0
            for m in range(6):
                Xc, XTc, Xn, XTn = X[cur], XT[cur], X[1 - cur], XT[1 - cur]
                ps1 = self.nextps()
                for h in range(4):
                    self.mm(ps1[:, H[h]], XTc[:, h, :], Xc[:, h, :])
                self.copy(f2(Xn), ps1, e="act")
                if m < 5:
                    ps2 = self.nextps()
                    for h in range(4):
                        self.mm(ps2[:, H[h]], Xc[:, h, :], XTc[:, h, :])
                    self.copy(f2(XTn), ps2, e="dve")
                ps3 = self.nextps()
                for h in range(4):
                    self.mm(ps3[:, H[h]], Xn[:, h, :], PT[:, h, :])
                self.tt(f2(PT), f2(PT), ps3, ALU.add)
                cur = 1 - cur
            for h in range(4):
                self.ts(RU[:, h, :], vs[:, H[h]], b4[:, h:h + 1], ALU.mult, e="pool")
                self.ts(RW[:, h, :], ks[:, H[h]], rn[:, 4 + h:5 + h], ALU.mult, bw[:, h:h + 1], ALU.mult, e="pool")
            psu = self.nextps()
            psw = self.nextps()
            for h in range(4):
                self.mm(psu[:, H[h]], PT[:, h, :], RU[:, h, :])
            for h in range(4):
                self.mm(psw[:, H[h]], RW[:, h, :], PT[:, h, :])
            self.copy(u, psu, e="act")
            self.copy(f2(wT), psw, e="dve")
            psq = self.nextps()
            for h in range(4):
                self.mm(psq[:, H[h]], khT[:, h, :], qhT[:, h, :])
            self.tt(atT, psq, decT, ALU.mult)
            for h in range(4):
                self.ts(qg[:, H[h]], qhb[:, H[h]], egc[:, h:h + 1], ALU.mult, e="pool")
                self.ts(kd[:, H[h]], khb[:, H[h]], ekd[:, h:h + 1], ALU.mult, e="pool")
            pst3 = self.nextps()
            for h in range(4):
                self.tr(psb(pst3)[:, H[h]], qg[:, H[h]], self.identb)
            self.copy(f2(qgT), psb(pst3)[:, 0:512], e="act")
            psws = self.nextps()
            for h in range(4):
                self.mm(psws[:, H[h]], wT[:, h, :], Sb[:, h, :])
            self.tt(vn, u, psws, ALU.subtract)
            pso = self.nextps()
            for h in range(4):
                self.mm(pso[:, H[h]], qgT[:, h, :], Sb[:, h, :], start=True, stop=False)
                self.mm(pso[:, H[h]], atT[:, H[h]], vn[:, H[h]], start=False, stop=True)
            psd = self.nextps()
            for h in range(4):
                self.mm(psd[:, H[h]], kd[:, H[h]], vn[:, H[h]])
            for h in range(4):
                self.stt(S[:, h, :], S[:, h, :], egl[:, h:h + 1], psd[:, H[h]], ALU.mult, ALU.add)
            self.copy(f2(Sb), f2(S), e="act")
            o_ = ob[b2]
            if d == 0:
                self.copy(o_, pso, e="dve")
                self.dma(self.OB.t(self.OB.ap[p0:p0 + 128, :], p0, 128), o_)
            else:
                oa_, ag_ = oa[b2], ag[b2]
                self.dma(oa_, self.OB.t(self.OB.ap[p0:p0 + 128, :], p0, 128))
                self.dma(ag_, self.Atm.t(self.Atm.ap[p0:p0 + 128, 2560:3072], p0, 128))
                self.tt(o_, pso, oa_, ALU.add)
                self.merge(o_, ag_, gng, junk, ss[b2], ya[b2], yaT[b2], 4, p0)
        self.barrier()


Prog.gdn_setup = gdn_setup
Prog.gdn_pass = gdn_pass


def build_full(debug=()):
    pg = Prog(debug=debug)
    pg.declare()
    pg.setup()
    pg.setup_streams()
    with ExitStack() as st:
        hT = pg.sb(st, "hT", [128, KT, P], BF16)
        pg.norm_phase(st, 0, 0, hT, TILES)
        pg.ab_proj_phase(hT)
        pg.barrier()
    with ExitStack() as st:
        pg.ab_setup2(st)
        pg.gdn_setup(st)
        pg.hgrn_pipe()
        pg.gdn_pipe()
        pg.merge_phase()
        pg.barrier()
    pg.outproj_phase(0, "ab_w_out", None, KT, pg.yT, 2, TILES)
    with ExitStack() as st:
        hT = pg.sb(st, "hT", [128, KT, P], BF16)
        pg.norm_phase(st, 0, 1, hT, TILES)
        pg.ffn_phase(0, hT, TILES)
        pg.barrier()
    pg.outproj_phase(0, "ffn_w_down", 0, FT, pg.uT, 5, TILES)
    with ExitStack() as st:
        hT = pg.sb(st, "hT", [128, KT, P], BF16)
        pg.norm_phase(st, 1, 0, hT, TILES)
        xbT, gelT = pg.mixer_c_phases(hT)
        pg.barrier()
    pg.mixer_c_scan(xbT, gelT)
    pg.outproj_phase(1, "c_w_out", None, KT, pg.yT, 2, LTILES)
    with ExitStack() as st:
        hT = pg.sb(st, "hT", [128, KT, P], BF16)
        pg.norm_phase(st, 1, 1, hT, LTILES)
        pg.ffn_phase(1, hT, LTILES)
        pg.barrier()
    pg.outproj_phase(1, "ffn_w_down", 1, FT, pg.uT, 5, LTILES)
    pg.final_phase()
    pg.em.finish(_sl(pg.outs) + _sl(getattr(pg, "out_tiles", [])))
    return pg


def kernel(**inputs):
    inp = {k: np.asarray(v) for k, v in inputs.items()}
    B = inp["x"].shape[0]
    pg = build_full()
    shared = host_prep(inp, 0)
    in_maps = []
    for b in range(B):
        m = dict(shared)
        m["x"] = np.ascontiguousarray(inp["x"][b], dtype=np.float32)
        m["ctx"] = np.ascontiguousarray(inp["ctx"][b], dtype=np.float32)
        cv = np.stack([inp["c"][b], inp["c_ctx"]], axis=-1)
        m["cvec"] = np.ascontiguousarray(cv.reshape(KT, 128, 2).transpose(1, 0, 2), dtype=np.float32)
        in_maps.append({k: v for k, v in m.items() if k in pg.din})
    res = run_bass_kernel_spmd(pg.nc, in_maps, core_ids=list(range(B)))
    out = np.stack([np.asarray(r["out"], dtype=np.float32) for r in res.results], axis=0)
    return out


def hgrn_stream(self, d, pool, ph):
    nc = self.nc
    sc = 128.0 ** -0.5
    NP = lambda: self.nextps(pool)
    if True:
        S = self.sb(ph, "hS", [128, 4, 128], F32)
        self.memset(S, 0.0)
        Mrel = self.cst[:, 4 + d, :]
        sel = self.selc[:, d, :]
        m4 = self.m4c[:, d, :]
        lbB = self.lbB[:, d, :]
        omlb = self.omlb[:, d, :]
        inb = [self.sb(ph, "hin%d" % k, [128, 2048], F32) for k in range(2)]
        sg = [self.sb(ph, "hsg%d" % k, [128, 512], F32) for k in range(2)]
        lf = [self.sb(ph, "hlf%d" % k, [128, 512], F32) for k in range(2)]
        kk = [self.sb(ph, "hkk%d" % k, [128, 512], F32) for k in range(2)]
        E1 = [self.sb(ph, "hE1%d" % k, [128, 512], F32) for k in range(2)]
        E2 = [self.sb(ph, "hE2%d" % k, [128, 512], F32) for k in range(2)]
        cl = self.sb(ph, "hcl", [128, 512], F32)
        Ee = [self.sb(ph, "hEe%d" % k, [128, 12], F32) for k in range(2)]
        qin = [self.sb(ph, "hqin%d" % k, [128, 512], BF16) for k in range(2)]
        kin = [self.sb(ph, "hkin%d" % k, [128, 512], BF16) for k in range(2)]
        vbf = [self.sb(ph, "hvbf%d" % k, [128, 512], BF16) for k in range(2)]
        qT = [self.sb(ph, "hqT%d" % k, [128, 4, 128], BF16) for k in range(2)]
        kT = [self.sb(ph, "hkT%d" % k, [128, 4, 128], BF16) for k in range(2)]
        at = [self.sb(ph, "hat%d" % k, [128, 512], BF16) for k in range(2)]
        St = [self.sb(ph, "hSt%d" % k, [128, 4, 128], BF16) for k in range(2)]
        tmpS = self.sb(ph, "htmpS", [128, 4, 128], F32)
        ob = [self.sb(ph, "hob%d" % k, [128, 512], F32) for k in range(2)]
        f2 = lambda t: T(t.ap.rearrange("p h t -> p (h t)"), t.s)
        H = [slice(h * 128, (h + 1) * 128) for h in range(4)]
        OAd = self.OAd[d]
        for it, ci in enumerate(chunk_order(d)):
            b2 = it % 2
            p0 = ci * 128
            x_ = inb[b2]
            self.dma(x_, self.Atm.t(self.Atm.ap[p0:p0 + 128, 0:2048], p0, 128))
            af = x_[:, 512 * (1 + d):512 * (2 + d)]
            s_ = sg[b2]
            self.act(s_, af, AF.Sigmoid)
            self.tt(s_, s_, omlb, ALU.mult)
            self.tt(s_, s_, lbB, ALU.add, e="pool")
            yield
            l_ = lf[b2]
            self.act(l_, s_, AF.Ln)
            self.ts(kk[b2], s_, -1.0, ALU.mult, 1.0, ALU.add, e="pool")
            psb_ = NP()
            self.mm(psb_, Mrel, l_)
            pse = NP()
            for h in range(4):
                self.mm(pse[:, 3 * h:3 * h + 3], l_[:, H[h]], sel)
            yield
            self.act(E1[b2], psb_, AF.Exp)
            self.act(E2[b2], psb_, AF.Exp, scale=-1.0)
            e_ = Ee[b2]
            self.act(e_, pse[:, 0:12], AF.Exp)
            yield
            q_, k_, v_ = qin[b2], kin[b2], vbf[b2]
            self.stt(q_, x_[:, 0:512], sc, E1[b2], ALU.mult, ALU.mult)
            self.tt(k_, kk[b2], E2[b2], ALU.mult, e="pool")
            self.copy(v_, x_[:, 1536:2048], e="pool")
            yield
            psq = NP()
            psk = NP()
            for h in range(4):
                self.tr(psb(psq)[:, H[h]], q_[:, H[h]], self.identb)
            for h in range(4):
                self.tr(psb(psk)[:, H[h]], k_[:, H[h]], self.identb)
            qT_, kT_ = qT[b2], kT[b2]
            self.copy(f2(qT_), psb(psq)[:, 0:512], e="dve")
            self.copy(f2(kT_), psb(psk)[:, 0:512], e="act")
            yield
            psa = NP()
            for h in range(4):
                self.mm(psa[:, H[h]], kT_[:, h, :], qT_[:, h, :])
            a_ = at[b2]
            self.ts(cl, psa, 1e30, ALU.min, -1e30, ALU.max)
            self.tt(a_, cl, m4, ALU.mult, e="pool")
            yield
            St_ = St[b2]
            for h in range(4):
                self.act(St_[:, h, :], S[:, h, :], AF.Copy, scale=e_[:, 3 * h:3 * h + 1])
            yield
            pso = NP()
            for h in range(4):
                self.mm(pso[:, H[h]], a_[:, H[h]], v_[:, H[h]], start=True, stop=False)
                self.mm(pso[:, H[h]], qT_[:, h, :], St_[:, h, :], start=False, stop=True)
            psd = NP()
            for h in range(4):
                self.mm(psd[:, H[h]], k_[:, H[h]], v_[:, H[h]])
            yield
            for h in range(4):
                self.act(tmpS[:, h, :], psd[:, H[h]], AF.Copy, scale=e_[:, 3 * h + 2:3 * h + 3])
                self.stt(S[:, h, :], S[:, h, :], e_[:, 3 * h + 1:3 * h + 2], tmpS[:, h, :], ALU.mult, ALU.add)
            o_ = ob[b2]
            self.copy(o_, pso, e="dve")
            self.dma(OAd.t(OAd.ap[p0:p0 + 128, :], p0, 128), o_)
            yield


def gdn_shared(self, st):
    gcw = self.sb(st, "gcw", [128, 12, 4], F32)
    self.dma(gcw, self.din["gcw"])
    self.dwg = self.sb(st, "gdw", [128, 12, 4, 128], BF16)
    for c in range(12):
        for k in range(4):
            self.ts(self.dwg[:, c, k, :], self.identb, gcw[:, c, k:k + 1], ALU.mult)
    self.I4 = self.sb(st, "gI4", [128, 4, 128], F32)
    for h in range(4):
        self.copy(self.I4[:, h, :], self.ident, e="pool")


def gdn_stream(self, d, pool, ph):
    nc = self.nc
    sc = 128.0 ** -0.5
    NP = lambda: self.nextps(pool)
    if True:
        S = self.sb(ph, "gS", [128, 4, 128], F32)
        Sb = self.sb(ph, "gSb", [128, 4, 128], BF16)
        self.memset(S, 0.0)
        self.memset(Sb, 0.0)
        TRI = self.cst[:, 2 + d, :]
        NEGT = self.cst[:, 6 + d, :]
        POSA = self.cst[:, 8 + d, :]
        dwg, I4 = self.dwg, self.I4
        pw = [self.sb(ph, "gpw%d" % k, [128, 12, 131], BF16) for k in range(2)]
        qs = self.sb(ph, "gqs", [128, 512], F32)
        ks = self.sb(ph, "gks", [128, 512], F32)
        vs = self.sb(ph, "gvs", [128, 512], F32)
        junk = self.sb(ph, "gjunk", [128, 128], F32)
        rn = self.sb(ph, "grn", [128, 8], F32)
        rnq = self.sb(ph, "grnq", [128, 4], F32)
        gcs = self.sb(ph, "ggcs", [128, 8], F32)
        ngc = self.sb(ph, "gngc", [128, 4], F32)
        egc = self.sb(ph, "gegc", [128, 4], F32)
        ekd = self.sb(ph, "gekd", [128, 4], F32)
        egl = self.sb(ph, "gegl", [128, 4], F32)
        bw = self.sb(ph, "gbw", [128, 4], F32)
        gbc = self.sb(ph, "ggbc", [128, 4, 128], F32)
        decT = self.sb(ph, "gdecT", [128, 512], F32)
        decA = self.sb(ph, "gdecA", [128, 4, 128], F32)
        khb = self.sb(ph, "gkhb", [128, 512], BF16)
        qhb = self.sb(ph, "gqhb", [128, 512], BF16)
        khT = self.sb(ph, "gkhT", [128, 4, 128], BF16)
        qhT = self.sb(ph, "gqhT", [128, 4, 128], BF16)
        X = [self.sb(ph, "gX%d" % k, [128, 4, 128], F32) for k in range(2)]
        XT = [self.sb(ph, "gXT%d" % k, [128, 4, 128], F32) for k in range(2)]
        PT = self.sb(ph, "gPT", [128, 4, 128], F32)
        RU = self.sb(ph, "gRU", [128, 4, 128], F32)
        RW = self.sb(ph, "gRW", [128, 4, 128], F32)
        u = self.sb(ph, "gu", [128, 512], F32)
        wT = self.sb(ph, "gwT", [128, 4, 128], BF16)
        atT = self.sb(ph, "gatT", [128, 512], BF16)
        qg = self.sb(ph, "gqg", [128, 512], BF16)
        qgT = self.sb(ph, "gqgT", [128, 4, 128], BF16)
        kd = self.sb(ph, "gkd", [128, 512], BF16)
        vn = self.sb(ph, "gvn", [128, 512], BF16)
        ob = [self.sb(ph, "gob%d" % k, [128, 512], F32) for k in range(2)]
        f2 = lambda t: T(t.ap.rearrange("p h t -> p (h t)"), t.s)
        R = (lambda t: T(t.ap.bitcast(F32R), t.s)) if USE_F32R else (lambda t: t)
        rb = self.sb(ph, "grb", [128, 4], F32)
        H = [slice(h * 128, (h + 1) * 128) for h in range(4)]
        OBd = self.OBd[d]
        for it, ci in enumerate(chunk_order(d)):
            b2 = it % 2
            p0 = ci * 128
            off = CP0 + p0 if ci < 2 else CL0 + (p0 - TC)
            w_ = pw[b2]
            self.dma(w_, T(self.preT.ap[:, :, off - 2:off + 129].rearrange("c p t -> p c t"), self.preT.s))
            pss = [NP() for _ in range(3)]
            for c in range(12):
                for k in range(4):
                    self.mm(pss[c // 4][:, H[c % 4]], w_[:, c, k:k + 128], dwg[:, c, k, :], start=(k == 0), stop=(k == 3))
                if c % 4 == 3:
                    yield
            self.act(qs, pss[0], AF.Silu)
            self.act(ks, pss[1], AF.Silu)
            self.act(vs, pss[2], AF.Silu)
            yield
            for h in range(4):
                self.act(junk, qs[:, H[h]], AF.Square, accum=rn[:, h:h + 1])
                self.act(junk, ks[:, H[h]], AF.Square, accum=rn[:, 4 + h:5 + h])
            self.act(rn, rn, AF.Sqrt, bias=self.epsc, scale=1.0)
            self.recip(rn, rn)
            self.ts(rnq, rn[:, 0:4], sc, ALU.mult)
            yield
            g4 = self.gG[:, ci, 4 * d:4 * d + 4]
            b4 = self.gB[:, ci, 4 * d:4 * d + 4]
            nb4 = self.gNB[:, ci, 4 * d:4 * d + 4]
            psg = NP()
            self.mm(psg[:, 0:4], TRI, g4)
            self.mm(psg[:, 4:8], self.ones, g4)
            self.copy(gcs, psg[:, 0:8], e="dve")
            self.ts(ngc, gcs[:, 0:4], -1.0, ALU.mult)
            self.act(egc, gcs[:, 0:4], AF.Exp)
            self.act(egl, gcs[:, 4:8], AF.Exp)
            self.tt(ekd, gcs[:, 4:8], gcs[:, 0:4], ALU.subtract)
            self.act(ekd, ekd, AF.Exp)
            self.tt(bw, b4, egc, ALU.mult)
            yield
            for h in range(4):
                self.ts(gbc[:, h, :], self.ones, g4[:, h:h + 1], ALU.mult, e=("pool" if h % 2 else "dve"))
            psD = NP()
            psA = NP()
            for h in range(4):
                self.mm(psD[:, H[h]], gbc[:, h, :], TRI, start=True, stop=False)
                self.mm(psD[:, H[h]], self.ident, NEGT, start=False, stop=True)
            yield
            for h in range(4):
                self.mm(psA[:, H[h]], gbc[:, h, :], TRI, start=True, stop=False)
                self.mm(psA[:, H[h]], self.ident, POSA, start=False, stop=True)
            yield
            for h in range(4):
                self.act(decT[:, H[h]], psD[:, H[h]], AF.Exp, bias=ngc[:, h:h + 1], scale=1.0)
                self.act(decA[:, h, :], psA[:, H[h]], AF.Exp, bias=gcs[:, h:h + 1], scale=-1.0)
            yield
            for h in range(4):
                self.act(khb[:, H[h]], ks[:, H[h]], AF.Copy, scale=rn[:, 4 + h:5 + h])
                self.ts(qhb[:, H[h]], qs[:, H[h]], rnq[:, h:h + 1], ALU.mult)
            yield
            pst = NP()
            pst2 = NP()
            for h in range(4):
                self.tr(psb(pst)[:, H[h]], khb[:, H[h]], self.identb)
            for h in range(4):
                self.tr(psb(pst2)[:, H[h]], qhb[:, H[h]], self.identb)
            self.copy(f2(khT), psb(pst)[:, 0:512], e="dve")
            self.copy(f2(qhT), psb(pst2)[:, 0:512], e="act")
            yield
            psK = NP()
            for h in range(4):
                self.mm(psK[:, H[h]], khT[:, h, :], khT[:, h, :])
            X0, XT0 = X[0], XT[0]
            for h in range(4):
                self.stt(R(X0[:, h, :]), psK[:, H[h]], nb4[:, h:h + 1], decA[:, h, :], ALU.mult, ALU.mult)
            yield
            psn = NP()
            for h in range(4):
                self.tr(psn[:, H[h]], X0[:, h, :], self.ident)
            self.copy(R(f2(XT0)), psn, e="act")
            self.tt(R(f2(PT)), f2(XT0), f2(I4), ALU.add)
            yield
            cur = 0
            for m in range(6):
                Xc, XTc, Xn, XTn = X[cur], XT[cur], X[1 - cur], XT[1 - cur]
                ps1 = NP()
                for h in range(4):
                    self.mm(ps1[:, H[h]], R(XTc[:, h, :]), R(Xc[:, h, :]))
                self.copy(R(f2(Xn)), ps1, e="act")
                yield
                if m < 5:
                    ps2 = NP()
                    for h in range(4):
                        self.mm(ps2[:, H[h]], R(Xc[:, h, :]), R(XTc[:, h, :]))
                    self.copy(R(f2(XTn)), ps2, e="dve")
                    yield
                ps3 = NP()
                for h in range(4):
                    self.mm(ps3[:, H[h]], R(Xn[:, h, :]), R(PT[:, h, :]))
                self.tt(R(f2(PT)), f2(PT), ps3, ALU.add)
                cur = 1 - cur
                yield
            for h in range(4):
                self.ts(R(RU[:, h, :]), vs[:, H[h]], b4[:, h:h + 1], ALU.mult)
                self.ts(R(RW[:, h, :]), ks[:, H[h]], rn[:, 4 + h:5 + h], ALU.mult, bw[:, h:h + 1], ALU.mult)
            yield
            psu = NP()
            psw = NP()
            for h in range(4):
                self.mm(psu[:, H[h]], R(PT[:, h, :]), R(RU[:, h, :]))
            for h in range(4):
                self.mm(psw[:, H[h]], R(RW[:, h, :]), R(PT[:, h, :]))
            self.copy(u, psu, e="act")
            self.copy(f2(wT), psw, e="dve")
            yield
            psq = NP()
            for h in range(4):
                self.mm(psq[:, H[h]], khT[:, h, :], qhT[:, h, :])
            self.tt(atT, psq, decT, ALU.mult)
            for h in range(4):
                self.ts(qg[:, H[h]], qhb[:, H[h]], egc[:, h:h + 1], ALU.mult, e="pool")
                self.act(kd[:, H[h]], khb[:, H[h]], AF.Copy, scale=ekd[:, h:h + 1])
            yield
            pst3 = NP()
            for h in range(4):
                self.tr(psb(pst3)[:, H[h]], qg[:, H[h]], self.identb)
            self.copy(f2(qgT), psb(pst3)[:, 0:512], e="act")
            yield
            psws = NP()
            for h in range(4):
                self.mm(psws[:, H[h]], wT[:, h, :], Sb[:, h, :])
            self.tt(vn, u, psws, ALU.subtract)
            yield
            pso = NP()
            for h in range(4):
                self.mm(pso[:, H[h]], qgT[:, h, :], Sb[:, h, :], start=True, stop=False)
                self.mm(pso[:, H[h]], atT[:, H[h]], vn[:, H[h]], start=False, stop=True)
            psd = NP()
            for h in range(4):
                self.mm(psd[:, H[h]], kd[:, H[h]], vn[:, H[h]])
            yield
            for h in range(4):
                self.stt(S[:, h, :], S[:, h, :], egl[:, h:h + 1], psd[:, H[h]], ALU.mult, ALU.add)
            self.copy(f2(Sb), f2(S), e="act")
            o_ = ob[b2]
            self.copy(o_, pso, e="dve")
            self.dma(OBd.t(OBd.ap[p0:p0 + 128, :], p0, 128), o_)
            yield


def merge_phase(self):
    with ExitStack() as ph:
        gn = [self.sb(ph, "mgn%d" % k, [128, 512], F32) for k in range(2)]
        self.dma(gn[0], self.din["hng"])
        self.dma(gn[1], self.din["gng"])
        o0 = [self.sb(ph, "mo0%d" % k, [128, 512], F32) for k in range(4)]
        o1 = [self.sb(ph, "mo1%d" % k, [128, 512], F32) for k in range(4)]
        ag = [self.sb(ph, "mag%d" % k, [128, 512], F32) for k in range(4)]
        junk = self.sb(ph, "mjunk", [128, 128], F32)
        ss = [self.sb(ph, "mss%d" % k, [128, 4], F32) for k in range(4)]
        ya = [self.sb(ph, "mya%d" % k, [128, 512], BF16) for k in range(4)]
        yaT = [self.sb(ph, "myaT%d" % k, [128, 4, 128], BF16) for k in range(4)]
        it = 0
        for ci in range(NCH):
            p0 = ci * 128
            for mx in range(2):
                b = it % 4
                it += 1
                src = self.OAd if mx == 0 else self.OBd
                gcol = 2048 if mx == 0 else 2560
                self.dma(o0[b], src[0].t(src[0].ap[p0:p0 + 128, :], p0, 128))
                self.dma(o1[b], src[1].t(src[1].ap[p0:p0 + 128, :], p0, 128))
                self.dma(ag[b], self.Atm.t(self.Atm.ap[p0:p0 + 128, gcol:gcol + 512], p0, 128))
                self.tt(o0[b], o0[b], o1[b], ALU.add, e="pool")
                self.merge(o0[b], ag[b], gn[mx], junk, ss[b], ya[b], yaT[b], 4 * mx, p0)
        self.barrier()


def ab_setup2(self, st):
    self.ab_setup(st)
    self.OAd = [DT(self.scr("OA%d" % k, [P, 512], F32).ap) for k in range(2)]
    self.OBd = [DT(self.scr("OB%d" % k, [P, 512], F32).ap) for k in range(2)]


Prog.hgrn_stream = hgrn_stream
Prog.gdn_shared = gdn_shared
Prog.gdn_stream = gdn_stream
Prog.merge_phase = merge_phase
Prog.ab_setup2 = ab_setup2


def ab_items_order():
    o0, o1 = chunk_order(0), chunk_order(1)
    out = []
    pa = pb = None
    for a, b in zip(o0, o1):
        out.append((0, a, pa))
        out.append((1, b, pb))
        pa, pb = (0, a), (1, b)
    return out


def hgrn_pipe(self):
    sc = 128.0 ** -0.5
    K = 4
    H = [slice(h * 128, (h + 1) * 128) for h in range(4)]
    f2 = lambda t: T(t.ap.rearrange("p h t -> p (h t)"), t.s)
    with ExitStack() as ph:
        S = [self.sb(ph, "hS%d" % d, [128, 4, 128], F32) for d in range(2)]
        for d in range(2):
            self.memset(S[d], 0.0)
        B = []
        for k in range(K):
            b = {}
            for nm, shp, dt in (("x", [128, 2048], F32), ("sg", [128, 512], F32), ("lf", [128, 512], F32),
                                ("kk", [128, 512], F32), ("E1", [128, 512], F32), ("E2", [128, 512], F32),
                                ("cl", [128, 512], F32), ("Ee", [128, 12], F32), ("q", [128, 512], BF16),
                                ("k", [128, 512], BF16), ("v", [128, 512], BF16), ("qT", [128, 4, 128], BF16),
                                ("kT", [128, 4, 128], BF16), ("at", [128, 512], BF16), ("St", [128, 4, 128], BF16),
                                ("tS", [128, 4, 128], F32), ("o", [128, 512], F32)):
                b[nm] = self.sb(ph, "h%s%d" % (nm, k), shp, dt)
            B.append(b)

        done = set()

        def item(d, ci, slot, prev):
            b = B[slot]
            cnt = [0]

            def NP():
                cnt[0] += 1
                return self.PS[2 * slot + (cnt[0] % 2)]
            p0 = ci * 128
            x_ = b["x"]
            self.dma(x_, self.Atm.t(self.Atm.ap[p0:p0 + 128, 0:2048], p0, 128))
            af = x_[:, 512 * (1 + d):512 * (2 + d)]
            s_ = b["sg"]
            self.act(s_, af, AF.Exp, scale=-1.0)
            yield
            self.ts(s_, s_, 1.0, ALU.add, e="pool")
            self.recip(s_, s_)
            yield
            self.tt(s_, s_, self.omlb[:, d, :], ALU.mult)
            self.tt(s_, s_, self.lbB[:, d, :], ALU.add, e="pool")
            yield
            l_ = b["lf"]
            self.act(l_, s_, AF.Ln)
            self.ts(b["kk"], s_, -1.0, ALU.mult, 1.0, ALU.add, e="pool")
            yield
            psb_ = NP()
            self.mm(psb_, self.cst[:, 4 + d, :], l_)
            pse = NP()
            for h in range(4):
                self.mm(pse[:, 3 * h:3 * h + 3], l_[:, H[h]], self.selc[:, d, :])
            yield
            self.act(b["E1"], psb_, AF.Exp)
            self.act(b["E2"], psb_, AF.Exp, scale=-1.0)
            e_ = b["Ee"]
            self.act(e_, pse[:, 0:12], AF.Exp)
            yield
            q_, k_, v_ = b["q"], b["k"], b["v"]
            self.stt(q_, x_[:, 0:512], sc, b["E1"], ALU.mult, ALU.mult)
            self.tt(k_, b["kk"], b["E2"], ALU.mult, e="pool")
            self.copy(v_, x_[:, 1536:2048], e="pool")
            yield
            psq = NP()
            psk = NP()
            for h in range(4):
                self.tr(psb(psq)[:, H[h]], q_[:, H[h]], self.identb)
            for h in range(4):
                self.tr(psb(psk)[:, H[h]], k_[:, H[h]], self.identb)
            qT_, kT_ = b["qT"], b["kT"]
            self.copy(f2(qT_), psb(psq)[:, 0:512], e="dve")
            self.copy(f2(kT_), psb(psk)[:, 0:512], e="act")
            yield
            psa = NP()
            for h in range(4):
                self.mm(psa[:, H[h]], kT_[:, h, :], qT_[:, h, :])
            a_ = b["at"]
            self.ts(b["cl"], psa, 1e30, ALU.min, -1e30, ALU.max)
            yield
            self.tt(a_, b["cl"], self.m4c[:, d, :], ALU.mult, e="pool")
            yield
            St_ = b["St"]
            Sd = S[d]
            while prev is not None and prev not in done:
                yield
            for h in range(4):
                self.ts(St_[:, h, :], Sd[:, h, :], e_[:, 3 * h:3 * h + 1], ALU.mult, e="pool")
            for h in range(4):
                self.ts(Sd[:, h, :], Sd[:, h, :], e_[:, 3 * h + 1:3 * h + 2], ALU.mult, e="pool")
            yield
            pso = NP()
            for h in range(4):
                self.mm(pso[:, H[h]], a_[:, H[h]], v_[:, H[h]], start=True, stop=False)
                self.mm(pso[:, H[h]], qT_[:, h, :], St_[:, h, :], start=False, stop=True)
            psd = NP()
            for h in range(4):
                self.mm(psd[:, H[h]], k_[:, H[h]], v_[:, H[h]])
            yield
            for h in range(4):
                self.stt(Sd[:, h, :], psd[:, H[h]], e_[:, 3 * h + 2:3 * h + 3], Sd[:, h, :], ALU.mult, ALU.add)
            done.add((d, ci))
            yield
            o_ = b["o"]
            self.copy(o_, pso, e="dve")
            OAd = self.OAd[d]
            self.dma(OAd.t(OAd.ap[p0:p0 + 128, :], p0, 128), o_)
            yield

        self.run_pipe((item(d, ci, i % K, pv) for i, (d, ci, pv) in enumerate(ab_items_order())), K, gap=4)
        self.barrier()


def gdn_pipe(self):
    sc = 128.0 ** -0.5
    K = 3
    H = [slice(h * 128, (h + 1) * 128) for h in range(4)]
    f2 = lambda t: T(t.ap.rearrange("p h t -> p (h t)"), t.s)
    with ExitStack() as ph:
        self.I4 = self.sb(ph, "gI4", [128, 4, 128], F32)
        for h in range(4):
            self.copy(self.I4[:, h, :], self.ident, e="pool")
        I4 = self.I4
        S = [self.sb(ph, "gS%d" % d, [128, 4, 128], F32) for d in range(2)]
        Sb = [self.sb(ph, "gSb%d" % d, [128, 4, 128], BF16) for d in range(2)]
        junk = self.sb(ph, "gjunk", [128, 128], F32)
        for d in range(2):
            self.memset(S[d], 0.0)
            self.memset(Sb[d], 0.0)
        B = []
        for k in range(K):
            b = {}
            for nm, shp, dt in (("pw", [128, 12, 131], BF16), ("qs", [128, 512], F32), ("ks", [128, 512], F32),
                                ("vs", [128, 512], F32), ("rn", [128, 8], F32), ("rnq", [128, 4], F32),
                                ("gcs", [128, 8], F32), ("ngc", [128, 4], F32), ("egc", [128, 4], F32),
                                ("ekd", [128, 4], F32), ("egl", [128, 4], F32), ("bw", [128, 4], F32),
                                ("gbc", [128, 4, 128], F32), ("decT", [128, 512], F32), ("decA", [128, 4, 128], F32),
                                ("khb", [128, 512], BF16), ("qhb", [128, 512], BF16), ("khT", [128, 4, 128], BF16),
                                ("qhT", [128, 4, 128], BF16), ("X0", [128, 4, 128], F32), ("X1", [128, 4, 128], F32),
                                ("XT0", [128, 4, 128], F32), ("XT1", [128, 4, 128], F32), ("PT", [128, 4, 128], F32),
                                ("RU", [128, 4, 128], F32), ("RW", [128, 4, 128], F32), ("u", [128, 512], F32),
                                ("wT", [128, 4, 128], BF16), ("atT", [128, 512], BF16), ("qg", [128, 512], BF16),
                                ("qgT", [128, 4, 128], BF16), ("kd", [128, 512], BF16), ("vn", [128, 512], BF16),
                                ("o", [128, 512], F32)):
                b[nm] = self.sb(ph, "g%s%d" % (nm, k), shp, dt)
            B.append(b)
        banks = [(0, 1, 2), (3, 4, 5), (6, 7)]

        done = set()

        def item(d, ci, slot, prev):
            b = B[slot]
            cnt = [0]
            bk = banks[slot]

            def NP():
                cnt[0] += 1
                return self.PS[bk[cnt[0] % len(bk)]]
            TRI = self.cst[:, 2 + d, :]
            NEGT = self.cst[:, 6 + d, :]
            POSA = self.cst[:, 8 + d, :]
            p0 = ci * 128
            off = CP0 + p0 if ci < 2 else CL0 + (p0 - TC)
            w_ = b["pw"]
            qs, ks, vs, rn, rnq, gcs, ngc = b["qs"], b["ks"], b["vs"], b["rn"], b["rnq"], b["gcs"], b["ngc"]
            egc, ekd, egl, bw, gbc, decT, decA = b["egc"], b["ekd"], b["egl"], b["bw"], b["gbc"], b["decT"], b["decA"]
            khb, qhb, khT, qhT, PT, RU, RW = b["khb"], b["qhb"], b["khT"], b["qhT"], b["PT"], b["RU"], b["RW"]
            u, wT, atT, qg, qgT, kd, vn = b["u"], b["wT"], b["atT"], b["qg"], b["qgT"], b["kd"], b["vn"]
            X = [b["X0"], b["X1"]]
            XT = [b["XT0"], b["XT1"]]
            self.dma(w_[:, :, 0:128], self.qkvT.t(self.qkvT.ap[:, :, p0:p0 + 128], p0, 128))
            for grp, dst in enumerate((qs, ks, vs)):
                ps = NP()
                for c4 in range(4):
                    self.tr(psb(ps)[:, H[c4]], w_[:, grp * 4 + c4, 0:128], self.identb)
                self.copy(dst, psb(ps)[:, 0:512], e=("dve" if grp == 1 else "act"))
                yield
            for h in range(4):
                self.act(junk, qs[:, H[h]], AF.Square, accum=rn[:, h:h + 1])
                self.act(junk, ks[:, H[h]], AF.Square, accum=rn[:, 4 + h:5 + h])
            yield
            self.act(rn, rn, AF.Sqrt, bias=self.epsc, scale=1.0)
            self.recip(rn, rn)
            self.ts(rnq, rn[:, 0:4], sc, ALU.mult)
            yield
            g4 = self.gG[:, ci, 4 * d:4 * d + 4]
            b4 = self.gB[:, ci, 4 * d:4 * d + 4]
            nb4 = self.gNB[:, ci, 4 * d:4 * d + 4]
            psg = NP()
            self.mm(psg[:, 0:4], TRI, g4)
            self.mm(psg[:, 4:8], self.ones, g4)
            self.copy(gcs, psg[:, 0:8], e="dve")
            yield
            self.ts(ngc, gcs[:, 0:4], -1.0, ALU.mult)
            self.act(egc, gcs[:, 0:4], AF.Exp)
            self.act(egl, gcs[:, 4:8], AF.Exp)
            self.tt(ekd, gcs[:, 4:8], gcs[:, 0:4], ALU.subtract)
            yield
            self.act(ekd, ekd, AF.Exp)
            self.tt(bw, b4, egc, ALU.mult)
            for h in range(4):
                self.ts(gbc[:, h, :], self.ones, g4[:, h:h + 1], ALU.mult, e=("pool" if h % 2 else "dve"))
            yield
            psD = NP()
            for h in range(4):
                self.mm(psD[:, H[h]], gbc[:, h, :], TRI, start=True, stop=False)
                self.mm(psD[:, H[h]], self.ident, NEGT, start=False, stop=True)
            for h in range(4):
                self.act(decT[:, H[h]], psD[:, H[h]], AF.Exp, bias=ngc[:, h:h + 1], scale=1.0)
            yield
            psA = NP()
            for h in range(4):
                self.mm(psA[:, H[h]], gbc[:, h, :], TRI, start=True, stop=False)
                self.mm(psA[:, H[h]], self.ident, POSA, start=False, stop=True)
            for h in range(4):
                self.act(decA[:, h, :], psA[:, H[h]], AF.Exp, bias=gcs[:, h:h + 1], scale=-1.0)
            yield
            for h in range(4):
                self.act(khb[:, H[h]], ks[:, H[h]], AF.Copy, scale=rn[:, 4 + h:5 + h])
                self.ts(qhb[:, H[h]], qs[:, H[h]], rnq[:, h:h + 1], ALU.mult)
            yield
            pst = NP()
            for h in range(4):
                self.tr(psb(pst)[:, H[h]], khb[:, H[h]], self.identb)
            self.copy(f2(khT), psb(pst)[:, 0:512], e="dve")
            pst2 = NP()
            for h in range(4):
                self.tr(psb(pst2)[:, H[h]], qhb[:, H[h]], self.identb)
            self.copy(f2(qhT), psb(pst2)[:, 0:512], e="act")
            yield
            psK = NP()
            for h in range(4):
                self.mm(psK[:, H[h]], khT[:, h, :], khT[:, h, :])
            for h in range(4):
                self.stt(X[0][:, h, :], psK[:, H[h]], nb4[:, h:h + 1], decA[:, h, :], ALU.mult, ALU.mult)
            yield
            psn = NP()
            for h in range(4):
                self.tr(psn[:, H[h]], X[0][:, h, :], self.ident)
            self.copy(f2(XT[0]), psn, e="act")
            yield
            self.tt(f2(PT), f2(XT[0]), f2(I4), ALU.add, e="pool")
            yield
            cur = 0
            for m in range(6):
                Xc, XTc, Xn, XTn = X[cur], XT[cur], X[1 - cur], XT[1 - cur]
                ps1 = NP()
                for h in range(4):
                    self.mm(ps1[:, H[h]], XTc[:, h, :], Xc[:, h, :])
                self.copy(f2(Xn), ps1, e="act")
                yield
                if m < 5:
                    ps2 = NP()
                    for h in range(4):
                        self.mm(ps2[:, H[h]], Xc[:, h, :], XTc[:, h, :])
                    self.copy(f2(XTn), ps2, e="dve")
                    yield
                ps3 = NP()
                for h in range(4):
                    self.mm(ps3[:, H[h]], Xn[:, h, :], PT[:, h, :])
                self.tt(f2(PT), f2(PT), ps3, ALU.add)
                cur = 1 - cur
                yield
            for h in range(4):
                self.ts(RU[:, h, :], vs[:, H[h]], b4[:, h:h + 1], ALU.mult, e=("pool" if h % 2 else "dve"))
                self.ts(RW[:, h, :], ks[:, H[h]], rn[:, 4 + h:5 + h], ALU.mult, bw[:, h:h + 1], ALU.mult,
                        e=("dve" if h % 2 else "pool"))
            yield
            psu = NP()
            for h in range(4):
                self.mm(psu[:, H[h]], PT[:, h, :], RU[:, h, :])
            self.copy(u, psu, e="act")
            psw = NP()
            for h in range(4):
                self.mm(psw[:, H[h]], RW[:, h, :], PT[:, h, :])
            self.copy(f2(wT), psw, e="dve")
            yield
            psq = NP()
            for h in range(4):
                self.mm(psq[:, H[h]], khT[:, h, :], qhT[:, h, :])
            self.tt(atT, psq, decT, ALU.mult)
            yield
            for h in range(4):
                self.ts(qg[:, H[h]], qhb[:, H[h]], egc[:, h:h + 1], ALU.mult, e="pool")
                self.act(kd[:, H[h]], khb[:, H[h]], AF.Copy, scale=ekd[:, h:h + 1])
            yield
            pst3 = NP()
            for h in range(4):
                self.tr(psb(pst3)[:, H[h]], qg[:, H[h]], self.identb)
            self.copy(f2(qgT), psb(pst3)[:, 0:512], e="act")
            yield
            Sd, Sbd = S[d], Sb[d]
            while prev is not None and prev not in done:
                yield
            psws = NP()
            for h in range(4):
                self.mm(psws[:, H[h]], wT[:, h, :], Sbd[:, h, :])
            self.tt(vn, u, psws, ALU.subtract)
            yield
            pso = NP()
            for h in range(4):
                self.mm(pso[:, H[h]], qgT[:, h, :], Sbd[:, h, :], start=True, stop=False)
                self.mm(pso[:, H[h]], atT[:, H[h]], vn[:, H[h]], start=False, stop=True)
            o_ = b["o"]
            self.copy(o_, pso, e="act")
            psd = NP()
            for h in range(4):
                self.mm(psd[:, H[h]], kd[:, H[h]], vn[:, H[h]])
            for h in range(4):
                self.stt(Sd[:, h, :], Sd[:, h, :], egl[:, h:h + 1], psd[:, H[h]], ALU.mult, ALU.add)
            yield
            self.copy(f2(Sbd), f2(Sd), e="act")
            done.add((d, ci))
            OBd = self.OBd[d]
            self.dma(OBd.t(OBd.ap[p0:p0 + 128, :], p0, 128), o_)
            yield

        self.run_pipe((item(d, ci, i % K, pv) for i, (d, ci, pv) in enumerate(ab_items_order())), K, gap=14)
        self.barrier()


Prog.hgrn_pipe = hgrn_pipe
Prog.gdn_pipe = gdn_pipe


def xinit_phase2(self):
    K = 4
    with ExitStack() as st:
        xin = [self.sb(st, "xin%d" % k, [128, D], F32) for k in range(K)]
        xo = [self.sb(st, "xo%d" % k, [128, KT, 128], F32) for k in range(K)]

        def item(i, slot):
            src = self.din["ctx"][i * 128:(i + 1) * 128, :] if i < 2 else self.din["x"][(i - 2) * 128:(i - 1) * 128, :]
            xi = xin[slot]
            self.dma(xi, src)
            yield
            for half in range(2):
                ps = self.PS[2 * slot + half]
                for k4 in range(4):
                    kt = half * 4 + k4
                    self.tr(ps[:, k4 * 128:(k4 + 1) * 128], xi[:, kt * 128:(kt + 1) * 128], self.ident)
                dst = xo[slot][:, half * 4:(half + 1) * 4, :]
                self.copy(dst, T(ps.ap.rearrange("p (k t) -> p k t", t=128), ps.s), e=("act" if half else "dve"))
                yield
            self.dma(self.xT.t(self.xT.ap[:, :, i * 128:(i + 1) * 128], i * 128, 128), xo[slot])
            yield
        self.run_pipe((item(i, i % K) for i in range(NCH)), K, gap=1)
        self.barrier()


def norm_phase2(self, st, l, w_, hT, tiles):
    sec_sh = 0 if w_ == 0 else 3
    K = 3
    with ExitStack() as ph:
        xs = [self.sb(ph, "nxs%d" % k, [128, KT, 512], F32) for k in range(K)]
        sq = [self.sb(ph, "nsq%d" % k, [128, 2, 512], BF16) for k in range(K)]
        rs = [self.sb(ph, "nrs%d" % k, [128, 512], F32) for k in range(K)]
        tm = [self.sb(ph, "ntm%d" % k, [128, 2, 512], F32) for k in range(K)]

        def item(p0, n, j, slot):
            x_ = xs[slot]
            self.dma(x_[:, :, :n], self.xT.t(self.xT.ap[:, :, p0:p0 + n], p0, n))
            yield
            ps = self.PS[2 * slot]
            for kt in range(KT):
                s_ = sq[slot][:, kt % 2, :n]
                if kt % 2:
                    self.tt(s_, x_[:, kt, :n], x_[:, kt, :n], ALU.mult, e="pool")
                else:
                    self.act(s_, x_[:, kt, :n], AF.Square)
                self.mm(ps[:, :n], self.onesb, s_, start=(kt == 0), stop=(kt == KT - 1))
                yield
            r_ = rs[slot]
            self.act(r_[:, :n], ps[:, :n], AF.Sqrt, bias=self.epsc, scale=1.0 / D)
            self.recip(r_[:, :n], r_[:, :n])
            yield
            for kt in range(KT):
                t_ = tm[slot][:, kt % 2, :n]
                self.stt(t_, x_[:, kt, :n], self.gs[l][w_][:, kt, j:j + 1], r_[:, :n], ALU.mult, ALU.mult)
                self.act(hT[:, kt, p0:p0 + n], t_, AF.Identity,
                         bias=self.modv[l][:, sec_sh * 8 + kt, j:j + 1], scale=1.0)
                yield
        self.run_pipe((item(p0, n, j, i % K) for i, (p0, n, j) in enumerate(tiles)), K, gap=6)
        self.barrier()


def outproj_phase2(self, l, wname, widx, nkt, src, gsec, tiles):
    K = 2
    with ExitStack() as ph:
        wsb = self.sb(ph, "opw", [128, nkt, D], BF16)
        wd = self.din[wname] if widx is None else self.din[wname][widx]
        for kt in range(nkt):
            self.dma(wsb[:, kt, :], wd[:, kt, :], q="pool")
        ub = [self.sb(ph, "opu%d" % k, [128, nkt, 512], BF16) for k in range(K)]
        xs = [self.sb(ph, "opx%d" % k, [128, KT, 512], F32) for k in range(K)]

        def item(p0, n, j, slot):
            u = ub[slot]
            x_ = xs[slot]
            self.dma(u[:, :, :n], src.t(src.ap[:, :, p0:p0 + n], p0, n))
            self.dma(x_[:, :, :n], self.xT.t(self.xT.ap[:, :, p0:p0 + n], p0, n))
            yield
            for nt in range(KT):
                ps = self.PS[4 * slot + nt % 4]
                for kt in range(nkt):
                    self.mm(ps[:, :n], wsb[:, kt, nt * 128:(nt + 1) * 128], u[:, kt, :n],
                            start=(kt == 0), stop=(kt == nkt - 1))
                self.stt(x_[:, nt, :n], ps[:, :n], self.modv[l][:, gsec * 8 + nt, j:j + 1], x_[:, nt, :n],
                         ALU.mult, ALU.add)
                yield
            self.dma(self.xT.t(self.xT.ap[:, :, p0:p0 + n], p0, n), x_[:, :, :n])
            yield
        self.run_pipe((item(p0, n, j, i % K) for i, (p0, n, j) in enumerate(tiles)), K, gap=5)
        self.barrier()


def final_phase2(self):
    K = 4
    with ExitStack() as ph:
        fg = self.sb(ph, "fng", [128, D], F32)
        self.dma(fg, self.din["fng"])
        xs = [self.sb(ph, "fx%d" % k, [128, KT, 128], F32) for k in range(K)]
        xt = [self.sb(ph, "fxt%d" % k, [128, D], F32) for k in range(K)]
        junk = self.sb(ph, "fjunk", [128, D], F32)
        ss = [self.sb(ph, "fss%d" % k, [128, 1], F32) for k in range(K)]
        outs = [T(self.out.ap[i * 128:(i + 1) * 128, :]) for i in range(TX // 128)]
        self.out_tiles = outs

        def item(i, slot):
            p0 = TC + i * 128
            x_ = xs[slot]
            self.dma(x_, self.xT.t(self.xT.ap[:, :, p0:p0 + 128], p0, 128))
            yield
            o_ = xt[slot]
            for half in range(2):
                ps = self.PS[2 * slot + half]
                for k4 in range(4):
                    self.tr(ps[:, k4 * 128:(k4 + 1) * 128], x_[:, half * 4 + k4, :], self.ident)
                self.copy(o_[:, half * 512:(half + 1) * 512], ps, e=("act" if half else "dve"))
                yield
            s_ = ss[slot]
            self.act(junk, o_, AF.Square, accum=s_)
            yield
            self.act(s_, s_, AF.Sqrt, bias=self.epsc, scale=1.0 / D)
            self.recip(s_, s_)
            yield
            self.stt(o_, o_, s_, fg, ALU.mult, ALU.mult)
            yield
            self.dma(outs[i], o_)
            yield
        self.run_pipe((item(i, i % K) for i in range(TX // 128)), K, gap=2)
        self.barrier()


def merge_phase2(self):
    K = 4
    H = [slice(h * 128, (h + 1) * 128) for h in range(4)]
    with ExitStack() as ph:
        gn = [self.sb(ph, "mgn%d" % k, [128, 512], F32) for k in range(2)]
        self.dma(gn[0], self.din["hng"])
        self.dma(gn[1], self.din["gng"])
        o0 = [self.sb(ph, "mo0%d" % k, [128, 512], F32) for k in range(K)]
        o1 = [self.sb(ph, "mo1%d" % k, [128, 512], F32) for k in range(K)]
        ag = [self.sb(ph, "mag%d" % k, [128, 512], F32) for k in range(K)]
        junk = self.sb(ph, "mjunk", [128, 128], F32)
        ss = [self.sb(ph, "mss%d" % k, [128, 4], F32) for k in range(K)]
        ya = [self.sb(ph, "mya%d" % k, [128, 512], BF16) for k in range(K)]
        yaT = [self.sb(ph, "myaT%d" % k, [128, 4, 128], BF16) for k in range(K)]

        def item(ci, mx, slot):
            p0 = ci * 128
            src = self.OAd if mx == 0 else self.OBd
            gcol = 2048 if mx == 0 else 2560
            o_, g_, s_ = o0[slot], ag[slot], ss[slot]
            self.dma(o_, src[0].t(src[0].ap[p0:p0 + 128, :], p0, 128))
            self.dma(o1[slot], src[1].t(src[1].ap[p0:p0 + 128, :], p0, 128))
            self.dma(g_, self.Atm.t(self.Atm.ap[p0:p0 + 128, gcol:gcol + 512], p0, 128))
            yield
            self.tt(o_, o_, o1[slot], ALU.add, e="pool")
            self.tt(g_, g_, gn[mx], ALU.mult, e="pool")
            yield
            for h in range(4):
                self.act(junk, o_[:, H[h]], AF.Square, accum=s_[:, h:h + 1])
            yield
            self.act(s_, s_, AF.Sqrt, bias=self.epsc, scale=1.0 / 128)
            self.recip(s_, s_)
            yield
            for h in range(4):
                self.stt(ya[slot][:, H[h]], o_[:, H[h]], s_[:, h:h + 1], g_[:, H[h]], ALU.mult, ALU.mult)
            yield
            pst = self.PS[2 * slot]
            for h in range(4):
                self.tr(psb(pst)[:, H[h]], ya[slot][:, H[h]], self.identb)
            self.copy(T(yaT[slot].ap.rearrange("p h t -> p (h t)"), yaT[slot].s), psb(pst)[:, 0:512], e="act")
            yield
            self.dma(self.yT.t(self.yT.ap[:, 4 * mx:4 * mx + 4, p0:p0 + 128], p0, 128), yaT[slot])
            yield
        items = [(ci, mx) for ci in range(NCH) for mx in range(2)]
        self.run_pipe((item(ci, mx, i % K) for i, (ci, mx) in enumerate(items)), K, gap=2)
        self.barrier()


Prog.xinit_phase = xinit_phase2
Prog.norm_phase = norm_phase2
Prog.outproj_phase = outproj_phase2
Prog.final_phase = final_phase2
Prog.merge_phase = merge_phase2


def mixer_c_phases2(self, hT):
    K = 3
    xbT = DT(self.scr("xbT", [KT, 128, P], F32).ap.rearrange("k p t -> p k t"))
    gelT = DT(self.scr("gelT", [KT, 128, P], F32).ap.rearrange("k p t -> p k t"))
    win = self.din["c_w_in"]
    with ExitStack() as ph:
        pre = [self.sb(ph, "cpre%d" % k, [128, CPAD], BF16) for k in range(2)]
        for k in range(2):
            self.memset(pre[k], 0.0)
        ccw = self.sb(ph, "ccw", [128, KT, 4], F32)
        ccb = self.sb(ph, "ccb", [128, KT], F32)
        self.dma(ccw, self.din["ccw"])
        self.dma(ccb, self.din["ccb"])
        wg = [self.sb(ph, "cwg%d" % k, [128, KT, 128], BF16) for k in range(2)]
        wx = [self.sb(ph, "cwx%d" % k, [128, KT, 128], BF16) for k in range(2)]
        dw = [self.sb(ph, "cdw%d" % k, [128, 4, 128], BF16) for k in range(2)]
        gx = [self.sb(ph, "cgx%d" % k, [128, 512], F32) for k in range(K)]
        gt = [self.sb(ph, "cgt%d" % k, [128, 512], F32) for k in range(K)]
        xo = [self.sb(ph, "cxo%d" % k, [128, 512], F32) for k in range(K)]
        a_done = [0] * KT

        def W(c):
            self.dma(wg[c % 2], win[:, :, c * 128:(c + 1) * 128], q="pool")
            self.dma(wx[c % 2], win[:, :, D + c * 128:D + (c + 1) * 128], q="pool")
            for k in range(4):
                self.ts(dw[c % 2][:, k, :], self.identb, ccw[:, c, k:k + 1], ALU.mult)
            yield

        def A(c, p0, n, j, slot):
            ps = self.PS[2 * slot]
            for kt in range(KT):
                self.mm(ps[:, :n], wx[c % 2][:, kt, :], hT[:, kt, p0:p0 + n], start=(kt == 0), stop=(kt == KT - 1))
            yield
            off = CP0 + p0 if j == 1 else CL0 + (p0 - TC)
            self.copy(pre[c % 2][:, off:off + n], ps[:, :n], e="act")
            a_done[c] += 1
            yield

        def Bi(c, p0, n, j, slot):
            while a_done[c] < len(TILES):
                yield
            off = CP0 + p0 if j == 1 else CL0 + (p0 - TC)
            ps = self.PS[2 * slot]
            pr = pre[c % 2]
            for k in range(4):
                self.mm(ps[:, :n], dw[c % 2][:, k, :], pr[:, off + k - 2:off + k - 2 + n], start=(k == 0), stop=(k == 3))
            x_o = xo[slot]
            self.act(x_o[:, :n], ps[:, :n], AF.Identity, bias=ccb[:, c:c + 1], scale=1.0)
            yield
            self.dma(xbT.t(xbT.ap[:, c, p0:p0 + n], p0, n), x_o[:, :n])
            if j == 0:
                ps2 = self.PS[2 * slot + 1]
                for kt in range(KT):
                    self.mm(ps2[:, :n], wg[c % 2][:, kt, :], hT[:, kt, p0:p0 + n], start=(kt == 0), stop=(kt == KT - 1))
                g_x, g_t = gx[slot], gt[slot]
                self.copy(g_x[:, :n], ps2[:, :n], e="act")
                yield
                self.tt(g_t[:, :n], g_x[:, :n], g_x[:, :n], ALU.mult, e="pool")
                yield
                self.ts(g_t[:, :n], g_t[:, :n], 0.044715, ALU.mult, 1.0, ALU.add)
                yield
                self.tt(g_t[:, :n], g_t[:, :n], g_x[:, :n], ALU.mult, e="pool")
                yield
                self.act(g_t[:, :n], g_t[:, :n], AF.Sigmoid, scale=GELU_C)
                yield
                self.tt(g_x[:, :n], g_x[:, :n], g_t[:, :n], ALU.mult)
                yield
                self.dma(gelT.t(gelT.ap[:, c, p0:p0 + n], p0, n), g_x[:, :n])
            yield

        def items():
            i = 0
            for c in range(KT):
                yield W(c)
                for (p0, n, j) in TILES:
                    yield A(c, p0, n, j, i % K)
                    i += 1
                for (p0, n, j) in TILES:
                    yield Bi(c, p0, n, j, i % K)
                    i += 1
        self.run_pipe(items(), K, gap=1)
        self.barrier()
    return xbT, gelT


def mixer_c_scan2(self, xbT, gelT):
    nc = self.nc
    K = 4
    with ExitStack() as ph:
        cgw = self.sb(ph, "cgw", [128, 16, 512], BF16)
        for k in range(16):
            self.dma(cgw[:, k, :], self.din["cgw"][:, k, :], q="pool")
        cgb = self.sb(ph, "cgb", [128, 2, 4, 4], F32)
        lam = self.sb(ph, "clam", [128, 2, KT], F32)
        scl = self.sb(ph, "cscl", [128, 2, KT], F32)
        scl2 = self.sb(ph, "cscl2", [128, 2, KT], F32)
        self.dma(cgb, self.din["cgb"])
        self.dma(lam, self.din["clam"])
        ncgb = self.sb(ph, "ncgb", [128, 2, 4, 4], F32)
        self.ts(ncgb, cgb, -1.0, ALU.mult)
        self.act(scl, lam, AF.Exp, scale=-1.0)
        self.act(scl, scl, AF.Ln, bias=self.ones[:, 0:1], scale=1.0)
        self.ts(scl2, scl, -16.0, ALU.mult)
        self.ts(scl, scl, -8.0, ALU.mult)
        xbb = [self.sb(ph, "cxbb%d" % k, [128, 2, P], BF16) for k in range(2)]
        xbf = [self.sb(ph, "cxbf%d" % k, [128, P], F32) for k in range(2)]
        hs = [[self.sb(ph, "chs%d_%d" % (k, d), [128, P], F32) for d in range(2)] for k in range(2)]
        rb = [self.sb(ph, "crb%d" % k, [128, 512], F32) for k in range(K)]
        ib = [self.sb(ph, "cib%d" % k, [128, 512], F32) for k in range(K)]
        ab = [self.sb(ph, "cab%d" % k, [128, 512], F32) for k in range(K)]
        yo = [self.sb(ph, "cyo%d" % k, [128, 512], BF16) for k in range(K)]
        scans = [0] * KT
        allslots = tuple(xbT.slots)

        def L(c):
            h, half = c // 2, c % 2
            if half == 0:
                for k2 in range(2):
                    self.dma(xbb[h % 2][:, k2, :], T(xbT.ap[:, 2 * h + k2, :], allslots), q="pool")
            self.dma(xbf[c % 2], T(xbT.ap[:, c, :], allslots))
            yield

        def S(c, d, p0, n, init, slot):
            h, half = c // 2, c % 2
            psr, psi = self.PS[2 * slot], self.PS[2 * slot + 1]
            xb_ = xbb[h % 2]
            for k2 in range(2):
                wbase = cgw[:, (d * 4 + h) * 2 + k2, :]
                self.mm(psr[:, :n], wbase[:, half * 128:(half + 1) * 128], xb_[:, k2, p0:p0 + n],
                        start=(k2 == 0), stop=(k2 == 1))
            for k2 in range(2):
                wbase = cgw[:, (d * 4 + h) * 2 + k2, :]
                self.mm(psi[:, :n], wbase[:, 256 + half * 128:256 + (half + 1) * 128], xb_[:, k2, p0:p0 + n],
                        start=(k2 == 0), stop=(k2 == 1))
            yield
            r_, i_, a_ = rb[slot][:, :n], ib[slot][:, :n], ab[slot][:, :n]
            self.act(r_, psr[:, :n], AF.Sigmoid, bias=cgb[:, d, h, half:half + 1], scale=1.0)
            self.act(i_, psi[:, :n], AF.Sigmoid, bias=cgb[:, d, h, 2 + half:3 + half], scale=1.0)
            yield
            self.act(a_, r_, AF.Exp, scale=scl[:, d, c:c + 1])
            yield
            self.tt(r_, a_, a_, ALU.mult, e="pool")
            self.tt(i_, i_, xbf[c % 2][:, p0:p0 + n], ALU.mult, e="pool")
            yield
            self.ts(r_, r_, -1.0, ALU.mult, 1.0, ALU.add)
            yield
            self.act(r_, r_, AF.Sqrt)
            yield
            self.tt(r_, r_, i_, ALU.mult)
            yield
            h_ = hs[c % 2][d]
            if d == 0:
                o_ap, a_ap, u_ap = h_.ap[:, p0:p0 + n], a_.ap, r_.ap
            else:
                lo = p0 - 1 if p0 > 0 else None
                o_ap = h_.ap[:, p0 + n - 1:lo:-1]
                a_ap = a_.ap[:, ::-1]
                u_ap = r_.ap[:, ::-1]
            iv = 0.0 if init is None else h_.ap[:, init:init + 1]
            self.op("dve", lambda: nc.vector.tensor_tensor_scan(o_ap, a_ap, u_ap, iv, ALU.mult, ALU.add),
                    r=[a_, r_, h_], w=[h_])
            scans[c] += 1
            yield

        def O(c, p0, n, slot):
            while scans[c] < 2 * len(TILES):
                yield
            g_ = ab[slot][:, :n]
            self.dma(g_, gelT.t(gelT.ap[:, c, p0:p0 + n], p0, n))
            t_ = rb[slot][:, :n]
            self.tt(t_, hs[c % 2][0][:, p0:p0 + n], hs[c % 2][1][:, p0:p0 + n], ALU.add, e="pool")
            yield
            y_ = yo[slot][:, :n]
            self.tt(y_, g_, t_, ALU.mult)
            yield
            self.dma(self.yT.t(self.yT.ap[:, c, p0:p0 + n], p0, n), y_)
            yield

        def items():
            i = 0
            for c in range(KT):
                yield L(c)
                fw = []
                prev = None
                for (p0, n, j) in TILES:
                    fw.append((0, p0, n, prev))
                    prev = p0 + n - 1
                bw = [(1, 0, TC, None)]
                prev = 0
                for (p0, n, j) in reversed(LTILES):
                    bw.append((1, p0, n, prev))
                    prev = p0
                for f_, b_ in zip(fw, bw):
                    for (d, p0, n, init) in (f_, b_):
                        yield S(c, d, p0, n, init, i % K)
                        i += 1
                for (p0, n, j) in LTILES:
                    yield O(c, p0, n, i % K)
                    i += 1
        self.run_pipe(items(), K, gap=0)
        self.barrier()


Prog.mixer_c_phases = mixer_c_phases2
Prog.mixer_c_scan = mixer_c_scan2


def setup_streams(self):
    K = 3
    with ExitStack() as st:
        cv = self.sb(st, "cv", [128, KT, 2], F32)
        sv = self.sb(st, "sv", [128, KT, 2], F32)
        mb = [self.sb(st, "mb%d" % l, [128, 48], F32) for l in range(2)]
        tmp = self.sb(st, "mtmp", [128, KT, 2], F32)
        wb = [self.sb(st, "mw%d" % k, [128, KT, 128], F32) for k in range(3)]
        xin = [self.sb(st, "xin%d" % k, [128, D], F32) for k in range(K)]
        xo = [self.sb(st, "xo%d" % k, [128, KT, 128], F32) for k in range(K)]
        self.dma(cv, self.din["cvec"])
        for l in range(2):
            self.dma(mb[l], self.din["mod_b"][l])
        self.act(sv, cv, AF.Silu)
        psm = [self.PS[6], self.PS[7]]

        def mod_stream():
            q = 0
            for l in range(2):
                ps = psm[l]
                for n in range(48):
                    w = wb[q % 3]
                    q += 1
                    self.dma(w, self.din["mod_w"][l, :, :, n * 128:(n + 1) * 128])
                    for kt in range(KT):
                        self.mm(ps[:, 2 * n:2 * n + 2], w[:, kt, :], sv[:, kt, :], start=(kt == 0), stop=(kt == KT - 1))
                    yield
                pv = ps[:, 0:96]
                for j in range(2):
                    self.tt(self.modv[l][:, :, j], T(pv.ap.rearrange("p (n j) -> p n j", j=2)[:, :, j], pv.s), mb[l], ALU.add)
                for w_, sec in ((0, 1), (1, 4)):
                    self.ts(tmp, self.modv[l][:, sec * 8:(sec + 1) * 8, :], 1.0, ALU.add)
                    for j in range(2):
                        self.tt(self.gs[l][w_][:, :, j], tmp[:, :, j], self.ng[:, 2 * l + w_, :], ALU.mult)
                yield

        def xitem(i, slot):
            src = self.din["ctx"][i * 128:(i + 1) * 128, :] if i < 2 else self.din["x"][(i - 2) * 128:(i - 1) * 128, :]
            xi = xin[slot]
            self.dma(xi, src)
            yield
            for half in range(2):
                ps = self.PS[2 * slot + half]
                for k4 in range(4):
                    kt = half * 4 + k4
                    self.tr(ps[:, k4 * 128:(k4 + 1) * 128], xi[:, kt * 128:(kt + 1) * 128], self.ident)
                dst = xo[slot][:, half * 4:(half + 1) * 4, :]
                self.copy(dst, T(ps.ap.rearrange("p (k t) -> p k t", t=128), ps.s), e=("act" if half else "dve"))
                yield
            self.dma(self.xT.t(self.xT.ap[:, :, i * 128:(i + 1) * 128], i * 128, 128), xo[slot])
            yield

        def xstream():
            it = iter(xitem(i, i % K) for i in range(NCH))
            active = []
            more = True
            while more or active:
                if more and len(active) < K:
                    g = next(it, None)
                    if g is None:
                        more = False
                    else:
                        active.append(g)
                for g in list(active):
                    try:
                        next(g)
                    except StopIteration:
                        active.remove(g)
                yield
        self.run_streams([mod_stream(), xstream()])
        self.barrier()


Prog.setup_streams = setup_streams
```

```python
import numpy as np
from contextlib import ExitStack
import concourse.bass as bass
import concourse.mybir as mybir
from concourse.bass_utils import run_bass_kernel_spmd

F32 = mybir.dt.float32
BF16 = mybir.dt.bfloat16
F32R = mybir.dt.float32r
AF = mybir.ActivationFunctionType
ALU = mybir.AluOpType
AX = mybir.AxisListType

SAME_ENGINE_SYNC = True
USE_F32R = False
N_DMA_SEMS = 8


class Slot:
    __slots__ = ("w", "r")

    def __init__(self):
        self.w = None
        self.r = {}


class Em:
    ENG = ("pe", "dve", "act", "pool", "sp")

    def __init__(self, nc, stack):
        self.nc = nc
        self.eng = {"pe": nc.tensor, "dve": nc.vector, "act": nc.scalar, "pool": nc.gpsimd, "sp": nc.sync}
        self.prog = {e: [] for e in self.ENG}
        self.sem = {}
        self.cnt = {}
        for e in ("pe", "dve", "act", "pool"):
            self.sem[e] = stack.enter_context(nc.semaphore("s_" + e))
            self.cnt[e] = 0
        self.dq = {}
        for q in ("sp", "pool", "act"):
            lst = []
            for i in range(N_DMA_SEMS):
                k = "d_%s_%d" % (q, i)
                self.sem[k] = stack.enter_context(nc.semaphore(k))
                self.cnt[k] = 0
                lst.append(k)
            self.dq[q] = [lst, 0]
        self.seen = {e: {} for e in self.ENG}
        self.ninst = 0

    def slots(self, n):
        return [Slot() for _ in range(n)]

    def _need(self, e, reads, writes, is_pe_mm=False):
        need = {}

        def add(tok):
            if tok is None:
                return
            k, v = tok
            if k == e and not SAME_ENGINE_SYNC:
                return
            if need.get(k, 0) < v:
                need[k] = v
        for s in reads:
            add(s.w)
        for s in writes:
            if s.w is not None and s.w[0] != e:
                add(s.w)
            for k, v in s.r.items():
                if k == e:
                    continue
                add((k, v))
        return need

    def _emit_waits(self, e, need):
        seen = self.seen[e]
        for k, v in need.items():
            if seen.get(k, 0) >= v:
                continue
            seen[k] = v
            sem = self.sem[k]
            self.prog[e].append(lambda eng, sem=sem, v=v: eng.wait_ge(sem, v))

    def _mark(self, tok, reads, writes):
        for s in reads:
            if s.r.get(tok[0], 0) < tok[1]:
                s.r[tok[0]] = tok[1]
        for s in writes:
            s.w = tok
            s.r = {}

    def op(self, e, fn, reads=(), writes=(), mm=False):
        need = self._need(e, reads, writes, is_pe_mm=mm)
        self._emit_waits(e, need)
        self.cnt[e] += 1
        n = self.cnt[e]
        sem = self.sem[e]
        self.prog[e].append(lambda eng, fn=fn, sem=sem: fn().then_inc(sem, 1))
        self._mark((e, n), reads, writes)
        self.ninst += 1

    def dma(self, out, in_, reads=(), writes=(), q="sp", **kw):
        lst, idx = self.dq[q]
        k = lst[idx % len(lst)]
        self.dq[q][1] = idx + 1
        need = self._need(q, reads, writes)
        if self.cnt[k] > 0:
            need[k] = max(need.get(k, 0), self.cnt[k])
        self._emit_waits(q, need)
        self.cnt[k] += 16
        v = self.cnt[k]
        sem = self.sem[k]
        eng = self.eng[q]
        self.prog[q].append(lambda _e, eng=eng, out=out, in_=in_, sem=sem, kw=kw:
                            eng.dma_start(out=out, in_=in_, **kw).then_inc(sem, 16))
        self._mark((k, v), reads, writes)
        self.ninst += 1

    def wait_all(self, e, slots):
        need = {}
        for s in slots:
            if s.w is not None:
                need[s.w[0]] = max(need.get(s.w[0], 0), s.w[1])
        self._emit_waits(e, need)

    def finish(self, out_slots):
        self.wait_all("sp", out_slots)
        nc = self.nc
        with nc.Block() as block:
            @block.sync
            def _(eng):
                for f in self.prog["sp"]:
                    f(eng)

            @block.tensor
            def _(eng):
                for f in self.prog["pe"]:
                    f(eng)

            @block.vector
            def _(eng):
                for f in self.prog["dve"]:
                    f(eng)

            @block.scalar
            def _(eng):
                for f in self.prog["act"]:
                    f(eng)

            @block.gpsimd
            def _(eng):
                for f in self.prog["pool"]:
                    f(eng)


class T:
    __slots__ = ("ap", "s")

    def __init__(self, ap, slot=None):
        self.ap = ap
        self.s = slot if slot is not None else Slot()

    def __getitem__(self, key):
        return T(self.ap[key], self.s)


def _sl(lst):
    out = []
    for x in lst:
        s = getattr(x, "s", x)
        if isinstance(s, (list, tuple)):
            out.extend(s)
        else:
            out.append(s)
    return out


class DT:
    def __init__(self, ap, n=34):
        self.ap = ap
        self.slots = [Slot() for _ in range(n)]

    def t(self, ap, p0, n):
        a, b = p0 // 128, (p0 + n + 127) // 128
        return T(ap, tuple(self.slots[a:b]))


D = 1024
KT = 8
TC = 256
TX = 4096
P = TC + TX
NCH = P // 128
FF = 2816
FT = 22
ABN = 4624
EPS = 1e-6
TILES = [(0, 256, 1)] + [(TC + 512 * i, 512, 0) for i in range(8)]
LTILES = TILES[1:]


class Prog:
    def __init__(self, debug=()):
        self.debug = set(debug)
        nc = bass.Bass("TRN2", target_bir_lowering=False)
        self.nc = nc
        self.stack = ExitStack()
        self.em = Em(nc, self.stack)
        self.din = {}
        self.dscr = {}
        self.outs = []
        self.psi = 0

    def inp(self, name, shape, dt=F32):
        t = self.nc.dram_tensor(name, list(shape), dt, kind="ExternalInput").ap()
        self.din[name] = T(t)
        return self.din[name]

    def scr(self, name, shape, dt):
        kind = "ExternalOutput" if name in self.debug else "Internal"
        t = T(self.nc.dram_tensor(name, list(shape), dt, kind=kind).ap())
        self.dscr[name] = t
        if name in self.debug:
            self.outs.append(t)
        return t

    def sb(self, st, name, shape, dt):
        self.uid = getattr(self, "uid", 0) + 1
        return T(st.enter_context(self.nc.sbuf_tensor("s%d_%s" % (self.uid, name), list(shape), dt)).ap())

    def op(self, e, fn, r=(), w=(), mm=False):
        self.em.op(e, fn, _sl(r), _sl(w), mm=mm)

    def dma(self, out, in_, q="sp", **kw):
        self.em.dma(out.ap, in_.ap, reads=_sl([in_]), writes=_sl([out]), q=q, **kw)

    def mm(self, out, lhsT, rhs, start=True, stop=True):
        nc = self.nc
        self.op("pe", lambda: nc.tensor.matmul(out.ap, lhsT.ap, rhs.ap, start=start, stop=stop),
                r=[lhsT, rhs], w=[out], mm=True)

    def tr(self, out, in_, ident):
        nc = self.nc
        self.op("pe", lambda: nc.tensor.transpose(out.ap, in_.ap, ident.ap), r=[in_, ident], w=[out], mm=True)

    def act(self, out, in_, func, bias=None, scale=None, accum=None, extra_r=()):
        nc = self.nc
        kw = {}
        r = [in_] + list(extra_r)
        if bias is not None:
            kw["bias"] = bias.ap if isinstance(bias, T) else bias
            if isinstance(bias, T):
                r.append(bias)
        if scale is not None:
            kw["scale"] = scale.ap if isinstance(scale, T) else scale
            if isinstance(scale, T):
                r.append(scale)
        w = [out]
        if accum is not None:
            kw["accum_out"] = accum.ap
            w.append(accum)
        self.op("act", lambda: nc.scalar.activation(out.ap, in_.ap, func, **kw), r=r, w=w)

    def tt(self, out, a, b, op, e="dve"):
        eng = self.nc.vector if e == "dve" else self.nc.gpsimd
        self.op(e, lambda: eng.tensor_tensor(out.ap, a.ap, b.ap, op), r=[a, b], w=[out])

    def ts(self, out, a, s1, op0, s2=None, op1=None, e="dve"):
        eng = self.nc.vector if e == "dve" else self.nc.gpsimd
        r = [a]
        v1 = s1.ap if isinstance(s1, T) else s1
        v2 = s2.ap if isinstance(s2, T) else s2
        if isinstance(s1, T):
            r.append(s1)
        if isinstance(s2, T):
            r.append(s2)
        if op1 is None and e == "pool" and op0 in (ALU.mult, ALU.add):
            o1, c1 = (ALU.add, 0.0) if op0 == ALU.mult else (ALU.mult, 1.0)
            self.op(e, lambda: eng.tensor_scalar(out.ap, a.ap, v1, c1, op0, o1), r=r, w=[out])
        elif op1 is None:
            self.op(e, lambda: eng.tensor_scalar(out.ap, a.ap, v1, None, op0), r=r, w=[out])
        else:
            self.op(e, lambda: eng.tensor_scalar(out.ap, a.ap, v1, v2, op0, op1), r=r, w=[out])

    def stt(self, out, a, s, b, op0, op1):
        nc = self.nc
        r = [a, b]
        v = s.ap if isinstance(s, T) else s
        if isinstance(s, T):
            r.append(s)
        self.op("dve", lambda: nc.vector.scalar_tensor_tensor(out.ap, a.ap, v, b.ap, op0, op1), r=r, w=[out])

    def copy(self, out, in_, e="dve"):
        nc = self.nc
        if e == "act":
            self.op("act", lambda: nc.scalar.copy(out.ap, in_.ap), r=[in_], w=[out])
        elif e == "pool":
            self.op("pool", lambda: nc.gpsimd.tensor_copy(out.ap, in_.ap), r=[in_], w=[out])
        else:
            self.op("dve", lambda: nc.vector.tensor_copy(out.ap, in_.ap), r=[in_], w=[out])

    def memset(self, out, val, e="pool"):
        eng = self.nc.gpsimd if e == "pool" else self.nc.vector
        self.op(e, lambda: eng.memset(out.ap, val), w=[out])

    def recip(self, out, in_):
        nc = self.nc
        self.op("dve", lambda: nc.vector.reciprocal(out.ap, in_.ap), r=[in_], w=[out])

    def nextps(self, pool=None):
        if pool is None:
            t = self.PS[self.psi % len(self.PS)]
            self.psi += 1
            return t
        self.psp = getattr(self, "psp", {})
        k = self.psp.get(pool, 0)
        self.psp[pool] = k + 1
        return self.PS[4 * pool + (k % 4)]

    def run_pipe(self, gens, K, gap=1):
        it = iter(gens)
        active = []
        more = True
        rnd = 0
        while True:
            if gap == 0:
                if more and not active:
                    while len(active) < K:
                        g = next(it, None)
                        if g is None:
                            more = False
                            break
                        active.append(g)
            elif more and len(active) < K and rnd % gap == 0:
                g = next(it, None)
                if g is None:
                    more = False
                else:
                    active.append(g)
            rnd += 1
            if not active:
                if not more:
                    break
                continue
            for g in list(active):
                try:
                    next(g)
                except StopIteration:
                    active.remove(g)

    def run_streams(self, gens):
        gens = list(gens)
        while gens:
            for g in list(gens):
                try:
                    next(g)
                except StopIteration:
                    gens.remove(g)

    def barrier(self):
        em = self.em
        need = {k: v for k, v in em.cnt.items() if v > 0}
        for e in Em.ENG:
            em._emit_waits(e, dict(need))

    def declare(self):
        i = self.inp
        i("x", [TX, D]); i("ctx", [TC, D]); i("cvec", [128, KT, 2])
        i("mod_w", [2, 128, KT, 6 * D]); i("mod_b", [2, 128, 48]); i("ng", [128, 4, KT]); i("fng", [128, D])
        i("ab_w_in", [128, KT, ABN]); i("hlb", [128, 2, 3, 512]); i("hng", [128, 512]); i("gng", [128, 512])
        i("gcw", [128, 12, 4]); i("galog", [128, 8]); i("gdtb", [128, 8]); i("ab_w_out", [128, KT, D])
        i("c_w_in", [128, KT, 2 * D]); i("ccw", [128, KT, 4]); i("ccb", [128, KT]); i("cgw", [128, 16, 512])
        i("cgb", [128, 2, 4, 4]); i("clam", [128, 2, KT]); i("c_w_out", [128, KT, D])
        i("ffn_w_up", [2, 128, KT, 2 * FF]); i("fcw", [2, 128, FT, 9]); i("fcb", [2, 128, FT])
        i("ffn_w_down", [2, 128, FT, D])
        i("cst", [128, 10, 128]); i("sel", [128, 2, 3]); i("m4", [128, 2, 512])
        nc = self.nc
        self.out = T(nc.dram_tensor("out", [TX, D], F32, kind="ExternalOutput").ap())
        self.outs.append(self.out)
        self.xT = DT(self.scr("xT", [KT, 128, P], F32).ap.rearrange("k p t -> p k t"))
        self.yT = DT(self.scr("yT", [KT, 128, P], BF16).ap.rearrange("k p t -> p k t"))
        self.uT = DT(self.scr("uT", [FT, 128, P], BF16).ap.rearrange("k p t -> p k t"))

    def setup(self):
        st = self.stack
        nc = self.nc
        self.PS = [T(st.enter_context(nc.psum_tensor("ps%d" % k, [128, 512], F32)).ap()) for k in range(8)]
        self.cst = self.sb(st, "cst", [128, 10, 128], F32)
        self.dma(self.cst, self.din["cst"])
        self.ident = self.cst[:, 0, :]
        self.ones = self.cst[:, 1, :]
        self.identb = self.sb(st, "identb", [128, 128], BF16)
        self.copy(self.identb, self.ident)
        self.onesb = self.sb(st, "onesb", [128, 128], BF16)
        self.copy(self.onesb, self.ones)
        self.ng = self.sb(st, "ng", [128, 4, KT], F32)
        self.dma(self.ng, self.din["ng"])
        self.modv = [self.sb(st, "modv%d" % l, [128, 48, 2], F32) for l in range(2)]
        self.gs = [[self.sb(st, "gs%d_%d" % (l, w), [128, KT, 2], F32) for w in range(2)] for l in range(2)]
        self.epsc = self.sb(st, "epsc", [128, 1], F32)
        self.memset(self.epsc, EPS)

    def mod_phase(self, l):
        nc = self.nc
        with ExitStack() as st:
            cv = self.sb(st, "cv", [128, KT, 2], F32)
            sv = self.sb(st, "sv", [128, KT, 2], F32)
            mb = self.sb(st, "mb", [128, 48], F32)
            tmp = self.sb(st, "mtmp", [128, KT, 2], F32)
            wb = [self.sb(st, "mw%d" % k, [128, KT, 128], F32) for k in range(2)]
            self.dma(cv, self.din["cvec"])
            self.dma(mb, self.din["mod_b"][l])
            self.act(sv, cv, AF.Silu)
            ps = self.nextps()
            for n in range(48):
                w = wb[n % 2]
                self.dma(w, self.din["mod_w"][l, :, :, n * 128:(n + 1) * 128])
                for kt in range(KT):
                    self.mm(ps[:, 2 * n:2 * n + 2], w[:, kt, :], sv[:, kt, :], start=(kt == 0), stop=(kt == KT - 1))
            pv = ps[:, 0:96]
            for j in range(2):
                self.tt(self.modv[l][:, :, j], T(pv.ap.rearrange("p (n j) -> p n j", j=2)[:, :, j], pv.s), mb, ALU.add)
            for w_, sec in ((0, 1), (1, 4)):
                self.ts(tmp, self.modv[l][:, sec * 8:(sec + 1) * 8, :], 1.0, ALU.add)
                for j in range(2):
                    self.tt(self.gs[l][w_][:, :, j], tmp[:, :, j], self.ng[:, 2 * l + w_, :], ALU.mult)
            self.barrier()

    def xinit_phase(self):
        with ExitStack() as st:
            xin = [self.sb(st, "xin%d" % k, [128, D], F32) for k in range(2)]
            xo = [self.sb(st, "xo%d" % k, [128, KT, 128], F32) for k in range(2)]
            for i in range(NCH):
                src = self.din["ctx"][i * 128:(i + 1) * 128, :] if i < 2 else self.din["x"][(i - 2) * 128:(i - 1) * 128, :]
                xi = xin[i % 2]
                self.dma(xi, src)
                for half in range(2):
                    ps = self.nextps()
                    for k4 in range(4):
                        kt = half * 4 + k4
                        self.tr(ps[:, k4 * 128:(k4 + 1) * 128], xi[:, kt * 128:(kt + 1) * 128], self.ident)
                    dst = xo[i % 2][:, half * 4:(half + 1) * 4, :]
                    src_ps = T(ps.ap.rearrange("p (k t) -> p k t", t=128), ps.s)
                    self.copy(dst, src_ps, e=("act" if half else "dve"))
                self.dma(self.xT.t(self.xT.ap[:, :, i * 128:(i + 1) * 128], i * 128, 128), xo[i % 2])
            self.barrier()

    def norm_phase(self, st, l, w_, hT, tiles):
        sec_sh = 0 if w_ == 0 else 3
        with ExitStack() as ph:
            xs = [self.sb(ph, "nxs%d" % k, [128, KT, 512], F32) for k in range(2)]
            sq = [self.sb(ph, "nsq%d" % k, [128, KT, 512], F32) for k in range(2)]
            rs = [self.sb(ph, "nrs%d" % k, [128, 512], F32) for k in range(2)]
            tm = [self.sb(ph, "ntm%d" % k, [128, 512], F32) for k in range(2)]
            for i, (p0, n, j) in enumerate(tiles):
                x_ = xs[i % 2]
                self.dma(x_[:, :, :n], self.xT.t(self.xT.ap[:, :, p0:p0 + n], p0, n))
                self.act(sq[i % 2][:, :, :n], x_[:, :, :n], AF.Square)
                ps = self.nextps()
                for kt in range(KT):
                    self.mm(ps[:, :n], self.ones, sq[i % 2][:, kt, :n], start=(kt == 0), stop=(kt == KT - 1))
                r_ = rs[i % 2]
                self.act(r_[:, :n], ps[:, :n], AF.Sqrt, bias=self.epsc, scale=1.0 / D)
                self.recip(r_[:, :n], r_[:, :n])
                for kt in range(KT):
                    t_ = tm[kt % 2]
                    self.stt(t_[:, :n], x_[:, kt, :n], self.gs[l][w_][:, kt, j:j + 1], r_[:, :n], ALU.mult, ALU.mult)
                    self.act(hT[:, kt, p0:p0 + n], t_[:, :n], AF.Identity,
                             bias=self.modv[l][:, sec_sh * 8 + kt, j:j + 1], scale=1.0)
            self.barrier()

    def outproj_phase(self, l, wname, widx, nkt, src, gsec, tiles):
        with ExitStack() as ph:
            wsb = self.sb(ph, "opw", [128, nkt, D], BF16)
            wd = self.din[wname] if widx is None else self.din[wname][widx]
            for kt in range(nkt):
                self.dma(wsb[:, kt, :], wd[:, kt, :], q="pool")
            ub = [self.sb(ph, "opu%d" % k, [128, nkt, 512], BF16) for k in range(2)]
            xs = [self.sb(ph, "opx%d" % k, [128, KT, 512], F32) for k in range(2)]
            for i, (p0, n, j) in enumerate(tiles):
                u = ub[i % 2]
                x_ = xs[i % 2]
                self.dma(u[:, :, :n], src.t(src.ap[:, :, p0:p0 + n], p0, n))
                self.dma(x_[:, :, :n], self.xT.t(self.xT.ap[:, :, p0:p0 + n], p0, n))
                for nt in range(KT):
                    ps = self.nextps()
                    for kt in range(nkt):
                        self.mm(ps[:, :n], wsb[:, kt, nt * 128:(nt + 1) * 128], u[:, kt, :n],
                                start=(kt == 0), stop=(kt == nkt - 1))
                    self.stt(x_[:, nt, :n], ps[:, :n], self.modv[l][:, gsec * 8 + nt, j:j + 1], x_[:, nt, :n],
                             ALU.mult, ALU.add)
                self.dma(self.xT.t(self.xT.ap[:, :, p0:p0 + n], p0, n), x_[:, :, :n])
            self.barrier()

    def ffn_phase(self, l, hT, tiles):
        nc = self.nc
        L0 = 323
        PADLEN = L0 + TX + 65
        with ExitStack() as ph:
            aT = self.sb(ph, "faT", [128, PADLEN], BF16)
            self.memset(aT, 0.0)
            cw = self.sb(ph, "fcw", [128, FT, 9], F32)
            cb = self.sb(ph, "fcb", [128, FT], F32)
            self.dma(cw, self.din["fcw"][l])
            self.dma(cb, self.din["fcb"][l])
            wa = [self.sb(ph, "fwa%d" % k, [128, KT, 128], BF16) for k in range(2)]
            wv = [self.sb(ph, "fwv%d" % k, [128, KT, 128], BF16) for k in range(2)]
            dw = [self.sb(ph, "fdw%d" % k, [128, 9, 128], BF16) for k in range(2)]
            sa = [self.sb(ph, "fsa%d" % k, [128, 512], F32) for k in range(2)]
            uo = [self.sb(ph, "fuo%d" % k, [128, 512], BF16) for k in range(2)]
            wup = self.din["ffn_w_up"][l]
            it = 0
            for c in range(FT):
                a_w, v_w, d_w = wa[c % 2], wv[c % 2], dw[c % 2]
                self.dma(a_w, wup[:, :, c * 128:(c + 1) * 128], q="pool")
                self.dma(v_w, wup[:, :, FF + c * 128:FF + (c + 1) * 128], q="pool")
                for tap in range(9):
                    self.ts(d_w[:, tap, :], self.identb, cw[:, c, tap:tap + 1], ALU.mult)
                for (p0, n, j) in tiles:
                    ps = self.nextps()
                    for kt in range(KT):
                        self.mm(ps[:, :n], a_w[:, kt, :], hT[:, kt, p0:p0 + n], start=(kt == 0), stop=(kt == KT - 1))
                    off = 1 + p0 if j == 1 else L0 + (p0 - TC)
                    self.copy(aT[:, off:off + n], ps[:, :n], e="act")
                for (p0, n, j) in tiles:
                    psv = self.nextps()
                    for kt in range(KT):
                        self.mm(psv[:, :n], v_w[:, kt, :], hT[:, kt, p0:p0 + n], start=(kt == 0), stop=(kt == KT - 1))
                    psc = self.nextps()
                    if j == 1:
                        for q_, kw in enumerate((1, 0, 2)):
                            o = 1 + p0 + (kw - 1)
                            self.mm(psc[:, :n], d_w[:, 3 + kw, :], aT[:, o:o + n], start=(q_ == 0), stop=(q_ == 2))
                    else:
                        t0 = L0 + (p0 - TC)
                        order = [(1, 1)] + [(kh, kw) for kh in range(3) for kw in range(3) if (kh, kw) != (1, 1)]
                        pv3 = T(psc.ap.rearrange("p (r c) -> p r c", c=64), psc.s)
                        for q_, (kh, kw) in enumerate(order):
                            o = t0 + 64 * (kh - 1)
                            src3 = T(aT.ap[:, o:o + 512].rearrange("p (r c) -> p r c", c=64), aT.s)
                            if kw == 1:
                                dst_, s_ = pv3, src3
                            elif kw == 2:
                                dst_, s_ = pv3[:, :, 0:63], src3[:, :, 1:64]
                            else:
                                dst_, s_ = pv3[:, :, 1:64], src3[:, :, 0:63]
                            self.mm(dst_, d_w[:, kh * 3 + kw, :], s_, start=(q_ == 0), stop=(q_ == 8))
                    s_a = sa[it % 2]
                    u_o = uo[it % 2]
                    it += 1
                    self.act(s_a[:, :n], psc[:, :n], AF.Silu, bias=cb[:, c:c + 1], scale=1.0)
                    self.tt(u_o[:, :n], s_a[:, :n], psv[:, :n], ALU.mult)
                    self.dma(self.uT.t(self.uT.ap[:, c, p0:p0 + n], p0, n), u_o[:, :n])
            self.barrier()

    def final_phase(self):
        with ExitStack() as ph:
            fg = self.sb(ph, "fng", [128, D], F32)
            self.dma(fg, self.din["fng"])
            xs = [self.sb(ph, "fx%d" % k, [128, KT, 128], F32) for k in range(2)]
            xt = [self.sb(ph, "fxt%d" % k, [128, D], F32) for k in range(2)]
            junk = self.sb(ph, "fjunk", [128, D], F32)
            ss = [self.sb(ph, "fss%d" % k, [128, 1], F32) for k in range(2)]
            for i in range(TX // 128):
                p0 = TC + i * 128
                x_ = xs[i % 2]
                self.dma(x_, self.xT.t(self.xT.ap[:, :, p0:p0 + 128], p0, 128))
                o_ = xt[i % 2]
                for half in range(2):
                    ps = self.nextps()
                    for k4 in range(4):
                        self.tr(ps[:, k4 * 128:(k4 + 1) * 128], x_[:, half * 4 + k4, :], self.ident)
                    self.copy(o_[:, half * 512:(half + 1) * 512], ps, e=("act" if half else "dve"))
                s_ = ss[i % 2]
                self.act(junk, o_, AF.Square, accum=s_)
                self.act(s_, s_, AF.Sqrt, bias=self.epsc, scale=1.0 / D)
                self.recip(s_, s_)
                self.stt(o_, o_, s_, fg, ALU.mult, ALU.mult)
                self.dma(self.out[i * 128:(i + 1) * 128, :], o_)
            self.barrier()


def _ktm(w, kt):
    return np.ascontiguousarray(w.reshape(kt, 128, -1).transpose(1, 0, 2))


def _col(v, kt):
    return np.ascontiguousarray(v.reshape(kt, 128).T)


def _consts():
    j = np.arange(128)[:, None]
    i = np.arange(128)[None, :]
    c = np.zeros((128, 10, 128), np.float32)
    c[:, 0] = (j == i)
    c[:, 1] = 1.0
    c[:, 2] = (j <= i)
    c[:, 3] = (j >= i)
    c[:, 4] = 1.0 * ((j >= 64) & (j <= i)) - 1.0 * ((j > i) & (j <= 63))
    c[:, 5] = 1.0 * ((j >= i) & (j <= 63)) - 1.0 * ((j >= 64) & (j < i))
    c[:, 6] = np.where(j <= i, 0.0, -30000.0)
    c[:, 7] = np.where(j >= i, 0.0, -30000.0)
    c[:, 8] = np.where(i < j, 0.0, 30000.0)
    c[:, 9] = np.where(i > j, 0.0, 30000.0)
    t = np.arange(128)
    sel = np.zeros((128, 2, 3), np.float32)
    sel[:, 0, 0] = (t <= 63); sel[:, 0, 1] = 1.0; sel[:, 0, 2] = (t > 63)
    sel[:, 1, 0] = (t >= 64); sel[:, 1, 1] = 1.0; sel[:, 1, 2] = (t < 64)
    m4 = np.zeros((128, 2, 512), np.float32)
    m4[:, 0] = np.tile(c[:, 2], (1, 4))
    m4[:, 1] = np.tile(c[:, 3], (1, 4))
    return c, sel, m4


def host_prep(inp, b):
    f = lambda a: np.ascontiguousarray(np.asarray(a, dtype=np.float32))
    m = {}
    m["x"] = f(inp["x"][b]); m["ctx"] = f(inp["ctx"][b])
    cv = np.stack([inp["c"][b], inp["c_ctx"]], axis=-1)
    m["cvec"] = f(cv.reshape(KT, 128, 2).transpose(1, 0, 2))
    m["mod_w"] = f(np.stack([_ktm(inp["mod_w"][l], KT) for l in range(2)]))
    m["mod_b"] = f(np.stack([_col(inp["mod_b"][l], 48) for l in range(2)]))
    m["ng"] = f(np.stack([_col(inp["norm_mix_g"][0], KT), _col(inp["norm_ffn_g"][0], KT),
                          _col(inp["norm_mix_g"][1], KT), _col(inp["norm_ffn_g"][1], KT)], axis=1))
    m["fng"] = f(np.tile(inp["final_norm_g"][None, :], (128, 1)))
    m["ab_w_in"] = f(_ktm(inp["ab_w_in"][0], KT))
    m["hlb"] = f(np.tile(inp["hgrn_lb"][None], (128, 1, 1, 1)))
    m["hng"] = f(np.tile(np.tile(inp["hgrn_norm_g"][0], 4)[None, :], (128, 1)))
    m["gng"] = f(np.tile(np.tile(inp["gdn_norm_g"][0], 4)[None, :], (128, 1)))
    m["gcw"] = f(inp["gdn_conv_w"][0].reshape(4, 12, 128).transpose(2, 1, 0))
    m["galog"] = f(np.tile(inp["gdn_a_log"][0].reshape(1, 8), (128, 1)))
    m["gdtb"] = f(np.tile(inp["gdn_dt_bias"][0].reshape(1, 8), (128, 1)))
    m["ab_w_out"] = f(_ktm(inp["ab_w_out"][0], KT))
    m["c_w_in"] = f(_ktm(inp["c_w_in"][0], KT))
    m["ccw"] = f(inp["c_conv_w"][0].reshape(4, KT, 128).transpose(2, 1, 0))
    m["ccb"] = f(_col(inp["c_conv_b"][0], KT))
    gw = inp["c_gate_w"][0]
    m["cgw"] = f(gw.reshape(2, 4, 2, 128, 512).transpose(3, 0, 1, 2, 4).reshape(128, 16, 512))
    m["cgb"] = f(inp["c_gate_b"][0].reshape(2, 4, 4, 128).transpose(3, 0, 1, 2))
    m["clam"] = f(inp["c_lambda"][0].reshape(2, KT, 128).transpose(2, 0, 1))
    m["c_w_out"] = f(_ktm(inp["c_w_out"][0], KT))
    m["ffn_w_up"] = f(np.stack([_ktm(inp["ffn_w_up"][l], KT) for l in range(2)]))
    m["fcw"] = f(np.stack([inp["ffn_conv_w"][l].reshape(9, FT, 128).transpose(2, 1, 0) for l in range(2)]))
    m["fcb"] = f(np.stack([_col(inp["ffn_conv_b"][l], FT) for l in range(2)]))
    m["ffn_w_down"] = f(np.stack([_ktm(inp["ffn_w_down"][l], FT) for l in range(2)]))
    c, sel, m4 = _consts()
    m["cst"] = c; m["sel"] = sel; m["m4"] = m4
    return m


GELU_C = 1.5957691216057308


def mixer_c_phases(self, hT):
    nc = self.nc
    CP0, CL0 = 2, 261
    CPAD = 4358
    xbT = DT(self.scr("xbT", [KT, 128, P], F32).ap.rearrange("k p t -> p k t"))
    gelT = DT(self.scr("gelT", [KT, 128, P], F32).ap.rearrange("k p t -> p k t"))
    win = self.din["c_w_in"]
    with ExitStack() as ph:
        pre = self.sb(ph, "cpre", [128, CPAD], BF16)
        self.memset(pre, 0.0)
        ccw = self.sb(ph, "ccw", [128, KT, 4], F32)
        ccb = self.sb(ph, "ccb", [128, KT], F32)
        self.dma(ccw, self.din["ccw"])
        self.dma(ccb, self.din["ccb"])
        wg = [self.sb(ph, "cwg%d" % k, [128, KT, 128], BF16) for k in range(2)]
        wx = [self.sb(ph, "cwx%d" % k, [128, KT, 128], BF16) for k in range(2)]
        dw = [self.sb(ph, "cdw%d" % k, [128, 4, 128], BF16) for k in range(2)]
        gx = [self.sb(ph, "cgx%d" % k, [128, 512], F32) for k in range(2)]
        gt = [self.sb(ph, "cgt%d" % k, [128, 512], F32) for k in range(2)]
        xo = [self.sb(ph, "cxo%d" % k, [128, 512], F32) for k in range(2)]
        it = 0
        for c in range(KT):
            g_w, x_w, d_w = wg[c % 2], wx[c % 2], dw[c % 2]
            self.dma(g_w, win[:, :, c * 128:(c + 1) * 128], q="pool")
            self.dma(x_w, win[:, :, D + c * 128:D + (c + 1) * 128], q="pool")
            for k in range(4):
                self.ts(d_w[:, k, :], self.identb, ccw[:, c, k:k + 1], ALU.mult)
            for (p0, n, j) in TILES:
                ps = self.nextps()
                for kt in range(KT):
                    self.mm(ps[:, :n], x_w[:, kt, :], hT[:, kt, p0:p0 + n], start=(kt == 0), stop=(kt == KT - 1))
                off = CP0 + p0 if j == 1 else CL0 + (p0 - TC)
                self.copy(pre[:, off:off + n], ps[:, :n], e="act")
            for (p0, n, j) in TILES:
                off = CP0 + p0 if j == 1 else CL0 + (p0 - TC)
                ps = self.nextps()
                for k in range(4):
                    self.mm(ps[:, :n], d_w[:, k, :], pre[:, off + k - 2:off + k - 2 + n], start=(k == 0), stop=(k == 3))
                x_o = xo[it % 2]
                self.act(x_o[:, :n], ps[:, :n], AF.Identity, bias=ccb[:, c:c + 1], scale=1.0)
                self.dma(xbT.t(xbT.ap[:, c, p0:p0 + n], p0, n), x_o[:, :n])
                if j == 0:
                    ps2 = self.nextps()
                    for kt in range(KT):
                        self.mm(ps2[:, :n], g_w[:, kt, :], hT[:, kt, p0:p0 + n], start=(kt == 0), stop=(kt == KT - 1))
                    g_x, g_t = gx[it % 2], gt[it % 2]
                    self.copy(g_x[:, :n], ps2[:, :n], e="act")
                    self.tt(g_t[:, :n], g_x[:, :n], g_x[:, :n], ALU.mult, e="pool")
                    self.ts(g_t[:, :n], g_t[:, :n], 0.044715, ALU.mult, 1.0, ALU.add)
                    self.tt(g_t[:, :n], g_t[:, :n], g_x[:, :n], ALU.mult, e="pool")
                    self.act(g_t[:, :n], g_t[:, :n], AF.Sigmoid, scale=GELU_C)
                    self.tt(g_x[:, :n], g_x[:, :n], g_t[:, :n], ALU.mult)
                    self.dma(gelT.t(gelT.ap[:, c, p0:p0 + n], p0, n), g_x[:, :n])
                it += 1
        self.barrier()
    return xbT, gelT


def mixer_c_scan(self, xbT, gelT):
    nc = self.nc
    with ExitStack() as ph:
        cgw = self.sb(ph, "cgw", [128, 16, 512], BF16)
        for k in range(16):
            self.dma(cgw[:, k, :], self.din["cgw"][:, k, :], q="pool")
        cgb = self.sb(ph, "cgb", [128, 2, 4, 4], F32)
        lam = self.sb(ph, "clam", [128, 2, KT], F32)
        scl = self.sb(ph, "cscl", [128, 2, KT], F32)
        scl2 = self.sb(ph, "cscl2", [128, 2, KT], F32)
        self.dma(cgb, self.din["cgb"])
        self.dma(lam, self.din["clam"])
        one = self.ones[:, 0:1]
        self.act(scl, lam, AF.Exp, scale=-1.0)
        self.act(scl, scl, AF.Ln, bias=one, scale=1.0)
        self.ts(scl2, scl, -16.0, ALU.mult)
        self.ts(scl, scl, -8.0, ALU.mult)
        xbf = self.sb(ph, "cxbf", [128, 2, P], F32)
        xbb = self.sb(ph, "cxbb", [128, 2, P], BF16)
        rb = self.sb(ph, "crb", [128, P], F32)
        ib = self.sb(ph, "cib", [128, P], F32)
        ab = self.sb(ph, "cab", [128, P], F32)
        hs = [self.sb(ph, "chs%d" % k, [128, P], F32) for k in range(2)]
        gl = [self.sb(ph, "cgl%d" % k, [128, 512], F32) for k in range(2)]
        yo = [self.sb(ph, "cyo%d" % k, [128, 512], BF16) for k in range(2)]
        it = 0
        for h in range(4):
            for half in range(2):
                self.dma(xbf[:, half, :], T(xbT.ap[:, 2 * h + half, :], tuple(xbT.slots)))
                self.copy(xbb[:, half, :], xbf[:, half, :], e="pool")
            for half in range(2):
                c = 2 * h + half
                for d in range(2):
                    for (p0, n, j) in TILES:
                        psr = self.nextps()
                        psi = self.nextps()
                        for k2 in range(2):
                            wbase = cgw[:, (d * 4 + h) * 2 + k2, :]
                            self.mm(psr[:, :n], wbase[:, half * 128:(half + 1) * 128], xbb[:, k2, p0:p0 + n],
                                    start=(k2 == 0), stop=(k2 == 1))
                        for k2 in range(2):
                            wbase = cgw[:, (d * 4 + h) * 2 + k2, :]
                            self.mm(psi[:, :n], wbase[:, 256 + half * 128:256 + (half + 1) * 128], xbb[:, k2, p0:p0 + n],
                                    start=(k2 == 0), stop=(k2 == 1))
                        self.act(rb[:, p0:p0 + n], psr[:, :n], AF.Sigmoid, bias=cgb[:, d, h, half:half + 1], scale=1.0)
                        self.act(ib[:, p0:p0 + n], psi[:, :n], AF.Sigmoid, bias=cgb[:, d, h, 2 + half:3 + half], scale=1.0)
                    self.act(ab, rb, AF.Exp, scale=scl[:, d, c:c + 1])
                    self.act(rb, rb, AF.Exp, scale=scl2[:, d, c:c + 1])
                    self.ts(rb, rb, -1.0, ALU.mult, 1.0, ALU.add)
                    self.act(rb, rb, AF.Sqrt)
                    self.tt(ib, ib, xbf[:, half, :], ALU.mult, e="pool")
                    self.tt(rb, rb, ib, ALU.mult)
                    h_ = hs[d]
                    if d == 0:
                        self.op("dve", lambda h_=h_: nc.vector.tensor_tensor_scan(h_.ap, ab.ap, rb.ap, 0.0, ALU.mult, ALU.add),
                                r=[ab, rb], w=[h_])
                    else:
                        self.op("dve", lambda h_=h_: nc.vector.tensor_tensor_scan(
                            h_.ap[:, TC - 1::-1], ab.ap[:, TC - 1::-1], rb.ap[:, TC - 1::-1], 0.0, ALU.mult, ALU.add),
                            r=[ab, rb], w=[h_])
                        self.op("dve", lambda h_=h_: nc.vector.tensor_tensor_scan(
                            h_.ap[:, P - 1:TC - 1:-1], ab.ap[:, P - 1:TC - 1:-1], rb.ap[:, P - 1:TC - 1:-1],
                            h_.ap[:, 0:1], ALU.mult, ALU.add), r=[ab, rb, h_], w=[h_])
                self.tt(hs[0], hs[0], hs[1], ALU.add, e="pool")
                for (p0, n, j) in LTILES:
                    g_ = gl[it % 2]
                    y_ = yo[it % 2]
                    it += 1
                    self.dma(g_[:, :n], gelT.t(gelT.ap[:, c, p0:p0 + n], p0, n))
                    self.tt(y_[:, :n], g_[:, :n], hs[0][:, p0:p0 + n], ALU.mult)
                    self.dma(self.yT.t(self.yT.ap[:, c, p0:p0 + n], p0, n), y_[:, :n])
        self.barrier()


Prog.mixer_c_phases = mixer_c_phases
Prog.mixer_c_scan = mixer_c_scan


NTM = 3088
CP0, CL0, CPAD = 2, 261, 4358


def psb(ps):
    return T(ps.ap.bitcast(BF16), ps.s)


def ab_proj_phase(self, hT):
    self.Atm = DT(self.scr("Atm", [P, NTM], F32).ap)
    self.preT = self.scr("preT", [12, 128, CPAD], BF16)
    win = self.din["ab_w_in"]
    with ExitStack() as ph:
        wb = [self.sb(ph, "abw%d" % k, [128, KT, 512], BF16) for k in range(2)]
        ob = [self.sb(ph, "abo%d" % k, [128, 512], F32) for k in range(3)]
        blocks = [(0, 0, 512, "silu"), (512, 512, 512, "c"), (1024, 1024, 512, "c"), (1536, 1536, 512, "c"),
                  (2048, 2048, 512, "silu"), (4096, 2560, 512, "silu"), (4608, 3072, 16, "c")]
        it = 0
        for bi, (wc, oc, ncol, kind) in enumerate(blocks):
            w = wb[bi % 2]
            self.dma(w[:, :, :ncol], win[:, :, wc:wc + ncol], q="pool")
            for i in range(NCH):
                ps = self.nextps()
                for kt in range(KT):
                    self.mm(ps[:, :ncol], hT[:, kt, i * 128:(i + 1) * 128], w[:, kt, :ncol],
                            start=(kt == 0), stop=(kt == KT - 1))
                o = ob[it % 3]
                it += 1
                if kind == "silu":
                    self.act(o[:, :ncol], ps[:, :ncol], AF.Silu)
                else:
                    self.copy(o[:, :ncol], ps[:, :ncol], e="dve")
                self.dma(self.Atm.t(self.Atm.ap[i * 128:(i + 1) * 128, oc:oc + ncol], i * 128, 128), o[:, :ncol])
        pre = [self.sb(ph, "abpre%d" % k, [128, CPAD], BF16) for k in range(2)]
        wq = [self.sb(ph, "abwq%d" % k, [128, KT, 128], BF16) for k in range(2)]
        gcw = self.sb(ph, "abgcw", [128, 12, 4], F32)
        self.dma(gcw, self.din["gcw"])
        dwq = [self.sb(ph, "abdw%d" % k, [128, 4, 128], BF16) for k in range(2)]
        so = [self.sb(ph, "abso%d" % k, [128, 512], BF16) for k in range(3)]
        self.qkvT = DT(self.scr("qkvT", [12, 128, P], BF16).ap.rearrange("c p t -> p c t"))
        for k in range(2):
            self.memset(pre[k], 0.0)
        it = 0
        for c in range(12):
            w = wq[c % 2]
            pr = pre[c % 2]
            self.dma(w, win[:, :, 2560 + c * 128:2560 + (c + 1) * 128], q="pool")
            for k in range(4):
                self.ts(dwq[c % 2][:, k, :], self.identb, gcw[:, c, k:k + 1], ALU.mult)
            for (p0, n, j) in TILES:
                ps = self.nextps()
                for kt in range(KT):
                    self.mm(ps[:, :n], w[:, kt, :], hT[:, kt, p0:p0 + n], start=(kt == 0), stop=(kt == KT - 1))
                off = CP0 + p0 if j == 1 else CL0 + (p0 - TC)
                self.copy(pr[:, off:off + n], ps[:, :n], e=("act" if (p0 // 512) % 2 else "dve"))
            for (p0, n, j) in TILES:
                off = CP0 + p0 if j == 1 else CL0 + (p0 - TC)
                ps = self.nextps()
                for k in range(4):
                    self.mm(ps[:, :n], dwq[c % 2][:, k, :], pr[:, off + k - 2:off + k - 2 + n], start=(k == 0), stop=(k == 3))
                o = so[it % 3]
                it += 1
                self.act(o[:, :n], ps[:, :n], AF.Silu)
                self.dma(self.qkvT.t(self.qkvT.ap[:, c, p0:p0 + n], p0, n), o[:, :n])
        self.barrier()


def chunk_order(d):
    return list(range(NCH)) if d == 0 else [1, 0] + list(range(NCH - 1, 1, -1))


def hgrn_pass(self, d, st_p):
    nc = self.nc
    sc = 128.0 ** -0.5
    with ExitStack() as ph:
        S = self.sb(ph, "hS", [128, 4, 128], F32)
        self.memset(S, 0.0)
        Mrel = self.cst[:, 4 + d, :]
        sel = self.selc[:, d, :]
        m4 = self.m4c[:, d, :]
        lbB = self.lbB[:, d, :]
        omlb = self.omlb[:, d, :]
        inb = [self.sb(ph, "hin%d" % k, [128, 2048], F32) for k in range(2)]
        sg = self.sb(ph, "hsg", [128, 512], F32)
        lf = [self.sb(ph, "hlf%d" % k, [128, 512], F32) for k in range(2)]
        kk = self.sb(ph, "hkk", [128, 512], F32)
        E1 = self.sb(ph, "hE1", [128, 512], F32)
        E2 = self.sb(ph, "hE2", [128, 512], F32)
        Ee = [self.sb(ph, "hEe%d" % k, [128, 12], F32) for k in range(2)]
        qin = [self.sb(ph, "hqin%d" % k, [128, 512], BF16) for k in range(2)]
        kin = [self.sb(ph, "hkin%d" % k, [128, 512], BF16) for k in range(2)]
        vbf = [self.sb(ph, "hvbf%d" % k, [128, 512], BF16) for k in range(2)]
        qT = [self.sb(ph, "hqT%d" % k, [128, 4, 128], BF16) for k in range(2)]
        kT = [self.sb(ph, "hkT%d" % k, [128, 4, 128], BF16) for k in range(2)]
        at = [self.sb(ph, "hat%d" % k, [128, 512], BF16) for k in range(2)]
        St = [self.sb(ph, "hSt%d" % k, [128, 4, 128], BF16) for k in range(2)]
        tmpS = self.sb(ph, "htmpS", [128, 4, 128], F32)
        ob = [self.sb(ph, "hob%d" % k, [128, 512], F32) for k in range(2)]
        if d == 1:
            oa = [self.sb(ph, "hoa%d" % k, [128, 512], F32) for k in range(2)]
            ag = [self.sb(ph, "hag%d" % k, [128, 512], F32) for k in range(2)]
            hng = self.sb(ph, "hng", [128, 512], F32)
            self.dma(hng, self.din["hng"])
            junk = self.sb(ph, "hjunk", [128, 128], F32)
            ss = [self.sb(ph, "hss%d" % k, [128, 4], F32) for k in range(2)]
            ya = [self.sb(ph, "hya%d" % k, [128, 512], BF16) for k in range(2)]
            yaT = [self.sb(ph, "hyaT%d" % k, [128, 4, 128], BF16) for k in range(2)]
        for it, ci in enumerate(chunk_order(d)):
            b2 = it % 2
            p0 = ci * 128
            x_ = inb[b2]
            self.dma(x_, self.Atm.t(self.Atm.ap[p0:p0 + 128, 0:2048], p0, 128))
            af = x_[:, 512 * (1 + d):512 * (2 + d)]
            self.act(sg, af, AF.Sigmoid)
            self.tt(sg, sg, omlb, ALU.mult)
            self.tt(sg, sg, lbB, ALU.add, e="pool")
            l_ = lf[b2]
            self.act(l_, sg, AF.Ln)
            self.ts(kk, sg, -1.0, ALU.mult, 1.0, ALU.add, e="pool")
            psb_ = self.nextps()
            self.mm(psb_, Mrel, l_)
            pse = self.nextps()
            for h in range(4):
                self.mm(pse[:, 3 * h:3 * h + 3], l_[:, h * 128:(h + 1) * 128], sel)
            self.act(E1, psb_, AF.Exp)
            self.act(E2, psb_, AF.Exp, scale=-1.0)
            e_ = Ee[b2]
            self.act(e_, pse[:, 0:12], AF.Exp)
            q_, k_, v_ = qin[b2], kin[b2], vbf[b2]
            self.stt(q_, x_[:, 0:512], sc, E1, ALU.mult, ALU.mult)
            self.tt(k_, kk, E2, ALU.mult, e="pool")
            self.copy(v_, x_[:, 1536:2048], e="pool")
            psq = self.nextps()
            psk = self.nextps()
            for h in range(4):
                self.tr(psb(psq)[:, h * 128:(h + 1) * 128], q_[:, h * 128:(h + 1) * 128], self.identb)
            for h in range(4):
                self.tr(psb(psk)[:, h * 128:(h + 1) * 128], k_[:, h * 128:(h + 1) * 128], self.identb)
            qT_, kT_ = qT[b2], kT[b2]
            self.copy(T(qT_.ap.rearrange("p h t -> p (h t)"), qT_.s), psb(psq)[:, 0:512], e="dve")
            self.copy(T(kT_.ap.rearrange("p h t -> p (h t)"), kT_.s), psb(psk)[:, 0:512], e="act")
            psa = self.nextps()
            for h in range(4):
                self.mm(psa[:, h * 128:(h + 1) * 128], kT_[:, h, :], qT_[:, h, :])
            a_ = at[b2]
            self.ts(E1, psa, 1e30, ALU.min, -1e30, ALU.max)
            self.tt(a_, E1, m4, ALU.mult, e="pool")
            St_ = St[b2]
            for h in range(4):
                self.act(St_[:, h, :], S[:, h, :], AF.Copy, scale=e_[:, 3 * h:3 * h + 1])
            pso = self.nextps()
            for h in range(4):
                hs = slice(h * 128, (h + 1) * 128)
                self.mm(pso[:, hs], a_[:, hs], v_[:, hs], start=True, stop=False)
                self.mm(pso[:, hs], qT_[:, h, :], St_[:, h, :], start=False, stop=True)
            psd = self.nextps()
            for h in range(4):
                hs = slice(h * 128, (h + 1) * 128)
                self.mm(psd[:, hs], k_[:, hs], v_[:, hs])
            for h in range(4):
                hs = slice(h * 128, (h + 1) * 128)
                self.act(tmpS[:, h, :], psd[:, hs], AF.Copy, scale=e_[:, 3 * h + 2:3 * h + 3])
                self.stt(S[:, h, :], S[:, h, :], e_[:, 3 * h + 1:3 * h + 2], tmpS[:, h, :], ALU.mult, ALU.add)
            o_ = ob[b2]
            if d == 0:
                self.copy(o_, pso, e="dve")
                self.dma(self.OA.t(self.OA.ap[p0:p0 + 128, :], p0, 128), o_)
            else:
                oa_, ag_ = oa[b2], ag[b2]
                self.dma(oa_, self.OA.t(self.OA.ap[p0:p0 + 128, :], p0, 128))
                self.dma(ag_, self.Atm.t(self.Atm.ap[p0:p0 + 128, 2048:2560], p0, 128))
                self.tt(o_, pso, oa_, ALU.add)
                self.merge(o_, ag_, hng, junk, ss[b2], ya[b2], yaT[b2], 0, p0)
        self.barrier()


def merge(self, o_, gate_, gn, junk, ss_, ya_, yaT_, kt0, p0):
    for h in range(4):
        self.act(junk, o_[:, h * 128:(h + 1) * 128], AF.Square, accum=ss_[:, h:h + 1])
    self.act(ss_, ss_, AF.Sqrt, bias=self.epsc, scale=1.0 / 128)
    self.recip(ss_, ss_)
    self.tt(gate_, gate_, gn, ALU.mult, e="pool")
    for h in range(4):
        hs = slice(h * 128, (h + 1) * 128)
        self.stt(ya_[:, hs], o_[:, hs], ss_[:, h:h + 1], gate_[:, hs], ALU.mult, ALU.mult)
    pst = self.nextps()
    for h in range(4):
        self.tr(psb(pst)[:, h * 128:(h + 1) * 128], ya_[:, h * 128:(h + 1) * 128], self.identb)
    self.copy(T(yaT_.ap.rearrange("p h t -> p (h t)"), yaT_.s), psb(pst)[:, 0:512], e="act")
    self.dma(self.yT.t(self.yT.ap[:, kt0:kt0 + 4, p0:p0 + 128], p0, 128), yaT_)


def ab_setup(self, st):
    self.selc = self.sb(st, "selc", [128, 2, 3], F32)
    self.m4c = self.sb(st, "m4c", [128, 2, 512], F32)
    self.dma(self.selc, self.din["sel"])
    self.dma(self.m4c, self.din["m4"])
    hl = self.sb(st, "hlb", [128, 2, 3, 512], F32)
    self.dma(hl, self.din["hlb"])
    self.lbB = self.sb(st, "lbB", [128, 2, 512], F32)
    self.omlb = self.sb(st, "omlb", [128, 2, 512], F32)
    self.act(hl, hl, AF.Exp)
    self.tt(self.omlb, hl[:, :, 0, :], hl[:, :, 1, :], ALU.add)
    self.tt(self.omlb, self.omlb, hl[:, :, 2, :], ALU.add)
    self.recip(self.omlb, self.omlb)
    self.tt(self.lbB, hl[:, :, 0, :], self.omlb, ALU.mult)
    self.ts(self.omlb, self.lbB, -1.0, ALU.mult, 1.0, ALU.add)
    self.OA = DT(self.scr("OA", [P, 512], F32).ap)
    self.OB = DT(self.scr("OB", [P, 512], F32).ap)


Prog.ab_proj_phase = ab_proj_phase
Prog.hgrn_pass = hgrn_pass
Prog.merge = merge
Prog.ab_setup = ab_setup


def gdn_setup(self, st):
    self.gG = self.sb(st, "gG", [128, NCH, 8], F32)
    self.gB = self.sb(st, "gB", [128, NCH, 8], F32)
    self.gNB = self.sb(st, "gNB", [128, NCH, 8], F32)
    with ExitStack() as ph:
        gin = self.sb(ph, "gin", [128, NCH, 16], F32)
        al = self.sb(ph, "galog", [128, 8], F32)
        db = self.sb(ph, "gdtb", [128, 8], F32)
        self.dma(al, self.din["galog"])
        self.dma(db, self.din["gdtb"])
        for i in range(NCH):
            self.dma(gin[:, i, :], self.Atm.t(self.Atm.ap[i * 128:(i + 1) * 128, 3072:3088], i * 128, 128))
        for c in range(8):
            self.ts(self.gG[:, :, c], gin[:, :, c], db[:, c:c + 1], ALU.add)
        self.act(self.gG, self.gG, AF.Exp)
        self.act(self.gG, self.gG, AF.Ln, bias=self.ones[:, 0:1], scale=1.0)
        self.act(al, al, AF.Exp)
        for c in range(8):
            self.ts(self.gG[:, :, c], self.gG[:, :, c], al[:, c:c + 1], ALU.mult, -1.0, ALU.mult)
        self.act(self.gB, gin[:, :, 8:16], AF.Sigmoid)
        self.ts(self.gNB, self.gB, -1.0, ALU.mult)
        self.barrier()


def gdn_pass(self, d):
    nc = self.nc
    sc = 128.0 ** -0.5
    with ExitStack() as ph:
        S = self.sb(ph, "gS", [128, 4, 128], F32)
        Sb = self.sb(ph, "gSb", [128, 4, 128], BF16)
        self.memset(S, 0.0)
        self.memset(Sb, 0.0)
        TRI = self.cst[:, 2 + d, :]
        NEGT = self.cst[:, 6 + d, :]
        POSA = self.cst[:, 8 + d, :]
        gcw = self.sb(ph, "gcw", [128, 12, 4], F32)
        self.dma(gcw, self.din["gcw"])
        dwg = self.sb(ph, "gdw", [128, 12, 4, 128], BF16)
        for c in range(12):
            for k in range(4):
                self.ts(dwg[:, c, k, :], self.identb, gcw[:, c, k:k + 1], ALU.mult)
        I4 = self.sb(ph, "gI4", [128, 4, 128], F32)
        for h in range(4):
            self.copy(I4[:, h, :], self.ident, e="pool")
        pw = [self.sb(ph, "gpw%d" % k, [128, 12, 131], BF16) for k in range(2)]
        qs = self.sb(ph, "gqs", [128, 512], F32)
        ks = self.sb(ph, "gks", [128, 512], F32)
        vs = self.sb(ph, "gvs", [128, 512], F32)
        junk = self.sb(ph, "gjunk", [128, 128], F32)
        rn = self.sb(ph, "grn", [128, 8], F32)
        rnq = self.sb(ph, "grnq", [128, 4], F32)
        gcs = self.sb(ph, "ggcs", [128, 8], F32)
        ngc = self.sb(ph, "gngc", [128, 4], F32)
        egc = self.sb(ph, "gegc", [128, 4], F32)
        ekd = self.sb(ph, "gekd", [128, 4], F32)
        egl = self.sb(ph, "gegl", [128, 4], F32)
        bw = self.sb(ph, "gbw", [128, 4], F32)
        gbc = self.sb(ph, "ggbc", [128, 4, 128], F32)
        decT = self.sb(ph, "gdecT", [128, 512], F32)
        decA = self.sb(ph, "gdecA", [128, 4, 128], F32)
        khb = self.sb(ph, "gkhb", [128, 512], BF16)
        qhb = self.sb(ph, "gqhb", [128, 512], BF16)
        khT = self.sb(ph, "gkhT", [128, 4, 128], BF16)
        qhT = self.sb(ph, "gqhT", [128, 4, 128], BF16)
        X = [self.sb(ph, "gX%d" % k, [128, 4, 128], F32) for k in range(2)]
        XT = [self.sb(ph, "gXT%d" % k, [128, 4, 128], F32) for k in range(2)]
        PT = self.sb(ph, "gPT", [128, 4, 128], F32)
        RU = self.sb(ph, "gRU", [128, 4, 128], F32)
        RW = self.sb(ph, "gRW", [128, 4, 128], F32)
        u = self.sb(ph, "gu", [128, 512], F32)
        wT = self.sb(ph, "gwT", [128, 4, 128], BF16)
        atT = self.sb(ph, "gatT", [128, 512], BF16)
        qg = self.sb(ph, "gqg", [128, 512], BF16)
        qgT = self.sb(ph, "gqgT", [128, 4, 128], BF16)
        kd = self.sb(ph, "gkd", [128, 512], BF16)
        vn = self.sb(ph, "gvn", [128, 512], BF16)
        ob = [self.sb(ph, "gob%d" % k, [128, 512], F32) for k in range(2)]
        if d == 1:
            oa = [self.sb(ph, "goa%d" % k, [128, 512], F32) for k in range(2)]
            ag = [self.sb(ph, "gag%d" % k, [128, 512], F32) for k in range(2)]
            gng = self.sb(ph, "gng", [128, 512], F32)
            self.dma(gng, self.din["gng"])
            ss = [self.sb(ph, "gss%d" % k, [128, 4], F32) for k in range(2)]
            ya = [self.sb(ph, "gya%d" % k, [128, 512], BF16) for k in range(2)]
            yaT = [self.sb(ph, "gyaT%d" % k, [128, 4, 128], BF16) for k in range(2)]
        f2 = lambda t: T(t.ap.rearrange("p h t -> p (h t)"), t.s)
        H = [slice(h * 128, (h + 1) * 128) for h in range(4)]
        for it, ci in enumerate(chunk_order(d)):
            b2 = it % 2
            p0 = ci * 128
            off = CP0 + p0 if ci < 2 else CL0 + (p0 - TC)
            w_ = pw[b2]
            self.dma(w_, T(self.preT.ap[:, :, off - 2:off + 129].rearrange("c p t -> p c t"), self.preT.s))
            pss = [self.nextps() for _ in range(3)]
            for c in range(12):
                for k in range(4):
                    self.mm(pss[c // 4][:, H[c % 4]], w_[:, c, k:k + 128], dwg[:, c, k, :], start=(k == 0), stop=(k == 3))
            self.act(qs, pss[0], AF.Silu)
            self.act(ks, pss[1], AF.Silu)
            self.act(vs, pss[2], AF.Silu)
            for h in range(4):
                self.act(junk, qs[:, H[h]], AF.Square, accum=rn[:, h:h + 1])
                self.act(junk, ks[:, H[h]], AF.Square, accum=rn[:, 4 + h:5 + h])
            self.act(rn, rn, AF.Sqrt, bias=self.epsc, scale=1.0)
            self.recip(rn, rn)
            self.ts(rnq, rn[:, 0:4], sc, ALU.mult)
            g4 = self.gG[:, ci, 4 * d:4 * d + 4]
            b4 = self.gB[:, ci, 4 * d:4 * d + 4]
            nb4 = self.gNB[:, ci, 4 * d:4 * d + 4]
            psg = self.nextps()
            self.mm(psg[:, 0:4], TRI, g4)
            self.mm(psg[:, 4:8], self.ones, g4)
            self.copy(gcs, psg[:, 0:8], e="dve")
            self.ts(ngc, gcs[:, 0:4], -1.0, ALU.mult)
            self.act(egc, gcs[:, 0:4], AF.Exp)
            self.act(egl, gcs[:, 4:8], AF.Exp)
            self.tt(ekd, gcs[:, 4:8], gcs[:, 0:4], ALU.subtract)
            self.act(ekd, ekd, AF.Exp)
            self.tt(bw, b4, egc, ALU.mult)
            for h in range(4):
                self.ts(gbc[:, h, :], self.ones, g4[:, h:h + 1], ALU.mult, e="pool")
            psD = self.nextps()
            psA = self.nextps()
            for h in range(4):
                self.mm(psD[:, H[h]], gbc[:, h, :], TRI, start=True, stop=False)
                self.mm(psD[:, H[h]], self.ident, NEGT, start=False, stop=True)
            for h in range(4):
                self.mm(psA[:, H[h]], gbc[:, h, :], TRI, start=True, stop=False)
                self.mm(psA[:, H[h]], self.ident, POSA, start=False, stop=True)
            for h in range(4):
                self.act(decT[:, H[h]], psD[:, H[h]], AF.Exp, bias=ngc[:, h:h + 1], scale=1.0)
                self.act(decA[:, h, :], psA[:, H[h]], AF.Exp, bias=gcs[:, h:h + 1], scale=-1.0)
            for h in range(4):
                self.act(khb[:, H[h]], ks[:, H[h]], AF.Copy, scale=rn[:, 4 + h:5 + h])
                self.act(qhb[:, H[h]], qs[:, H[h]], AF.Copy, scale=rnq[:, h:h + 1])
            pst = self.nextps()
            pst2 = self.nextps()
            for h in range(4):
                self.tr(psb(pst)[:, H[h]], khb[:, H[h]], self.identb)
            for h in range(4):
                self.tr(psb(pst2)[:, H[h]], qhb[:, H[h]], self.identb)
            self.copy(f2(khT), psb(pst)[:, 0:512], e="dve")
            self.copy(f2(qhT), psb(pst2)[:, 0:512], e="act")
            psK = self.nextps()
            for h in range(4):
                self.mm(psK[:, H[h]], khT[:, h, :], khT[:, h, :])
            X0, XT0 = X[0], XT[0]
            for h in range(4):
                self.stt(X0[:, h, :], psK[:, H[h]], nb4[:, h:h + 1], decA[:, h, :], ALU.mult, ALU.mult)
            psn = self.nextps()
            for h in range(4):
                self.tr(psn[:, H[h]], X0[:, h, :], self.ident)
            self.copy(f2(XT0), psn, e="act")
            self.tt(f2(PT), f2(XT0), f2(I4), ALU.add, e="pool")
            cur = 0
            for m in range(6):
                Xc, XTc, Xn, XTn = X[cur], XT[cur], X[1 - cur], XT[1 - cur]
                ps1 = self.nextps()
                for h in range(4):
                    self.mm(ps1[:, H[h]], XTc[:, h, :], Xc[:, h, :])
                self.copy(f2(Xn), ps1, e="act")
                if m < 5:
                    ps2 = self.nextps()
                    for h in range(4):
                        self.mm(ps2[:, H[h]], Xc[:, h, :], XTc[:, h, :])
                    self.copy(f2(XTn), ps2, e="dve")
                ps3 = self.nextps()
                for h in range(4):
                    self.mm(ps3[:, H[h]], Xn[:, h, :], PT[:, h, :])
                self.tt(f2(PT), f2(PT), ps3, ALU.add)
                cur = 1 - cur
            for h in range(4):
                self.ts(RU[:, h, :], vs[:, H[h]], b4[:, h:h + 1], ALU.mult, e="pool")
                self.ts(RW[:, h, :], ks[:, H[h]], rn[:, 4 + h:5 + h], ALU.mult, bw[:, h:h + 1], ALU.mult, e="pool")
            psu = self.nextps()
            psw = self.nextps()
            for h in range(4):
                self.mm(psu[:, H[h]], PT[:, h, :], RU[:, h, :])
            for h in range(4):
                self.mm(psw[:, H[h]], RW[:, h, :], PT[:, h, :])
            self.copy(u, psu, e="act")
            self.copy(f2(wT), psw, e="dve")
            psq = self.nextps()
            for h in range(4):
                self.mm(psq[:, H[h]], khT[:, h, :], qhT[:, h, :])
            self.tt(atT, psq, decT, ALU.mult)
            for h in range(4):
                self.ts(qg[:, H[h]], qhb[:, H[h]], egc[:, h:h + 1], ALU.mult, e="pool")
                self.ts(kd[:, H[h]], khb[:, H[h]], ekd[:, h:h + 1], ALU.mult, e="pool")
            pst3 = self.nextps()
            for h in range(4):
                self.tr(psb(pst3)[:, H[h]], qg[:, H[h]], self.identb)
            self.copy(f2(qgT), psb(pst3)[:, 0:512], e="act")
            psws = self.nextps()
            for h in range(4):
                self.mm(psws[:, H[h]], wT[:, h, :], Sb[:, h, :])
            self.tt(vn, u, psws, ALU.subtract)
            pso = self.nextps()
            for h in range(4):
                self.mm(pso[:, H[h]], qgT[:, h, :], Sb[:, h, :], start=True, stop=False)
                self.mm(pso[:, H[h]], atT[:, H[h]], vn[:, H[h]], start=False, stop=True)
            psd = self.nextps()
            for h in range(4):
                self.mm(psd[:, H[h]], kd[:, H[h]], vn[:, H[h]])
            for h in range(4):
                self.stt(S[:, h, :], S[:, h, :], egl[:, h:h + 1], psd[:, H[h]], ALU.mult, ALU.add)
            self.copy(f2(Sb), f2(S), e="act")
            o_ = ob[b2]
            if d == 0:
                self.copy(o_, pso, e="dve")
                self.dma(self.OB.t(self.OB.ap[p0:p0 + 128, :], p0, 128), o_)
            else:
                oa_, ag_ = oa[b2], ag[b2]
                self.dma(oa_, self.OB.t(self.OB.ap[p0:p0 + 128, :], p0, 128))
                self.dma(ag_, self.Atm.t(self.Atm.ap[p0:p0 + 128, 2560:3072], p0, 128))
                self.tt(o_, pso, oa_, ALU.add)
                self.merge(o_, ag_, gng, junk, ss[b2], ya[b2], yaT[b2], 4, p0)
        self.barrier()


Prog.gdn_setup = gdn_setup
Prog.gdn_pass = gdn_pass


def build_full(debug=()):
    pg = Prog(debug=debug)
    pg.declare()
    pg.setup()
    pg.setup_streams()
    with ExitStack() as st:
        hT = pg.sb(st, "hT", [128, KT, P], BF16)
        pg.norm_phase(st, 0, 0, hT, TILES)
        pg.ab_proj_phase(hT)
        pg.barrier()
    with ExitStack() as st:
        pg.ab_setup2(st)
        pg.gdn_setup(st)
        pg.hgrn_pipe()
        pg.gdn_pipe()
    with ExitStack() as st:
        hT = pg.sb(st, "hT", [128, KT, P], BF16)
        with ExitStack() as ph:
            merged = set()
            ready = lambda p0, n: all((ci, mx) in merged for ci in range(p0 // 128, (p0 + n) // 128) for mx in range(2))
            mi, mk = pg.merge_items(ph, merged)
            oi, ok_ = pg.outproj_norm_items(ph, 0, "ab_w_out", None, KT, pg.yT, 2, TILES, hT, 0, 1, ready, (2, 6))
            pg.run_streams([pipe_gen(mi, mk, 2), pipe_gen(oi, ok_, 1)])
            pg.barrier()
        pg.ffn_phase(0, hT, TILES)
        pg.barrier()
    pg.outproj_phase(0, "ffn_w_down", 0, FT, pg.uT, 5, TILES)
    with ExitStack() as st:
        hT = pg.sb(st, "hT", [128, KT, P], BF16)
        pg.norm_phase(st, 1, 0, hT, TILES)
        xbT, gelT = pg.mixer_c_phases(hT)
        pg.barrier()
    pg.mixer_c_scan(xbT, gelT)
    with ExitStack() as st:
        hT = pg.sb(st, "hT", [128, KT, P], BF16)
        with ExitStack() as ph:
            oi, ok_ = pg.outproj_norm_items(ph, 1, "c_w_out", None, KT, pg.yT, 2, LTILES, hT, 1, 1,
                                            (lambda p0, n: True), (0, 6))
            pg.run_streams([pipe_gen(oi, ok_, 1)])
            pg.barrier()
        pg.ffn_phase(1, hT, LTILES)
        pg.barrier()
    pg.outproj_phase(1, "ffn_w_down", 1, FT, pg.uT, 5, LTILES)
    pg.final_phase()
    pg.em.finish(_sl(pg.outs) + _sl(getattr(pg, "out_tiles", [])))
    return pg


def kernel(**inputs):
    inp = {k: np.asarray(v) for k, v in inputs.items()}
    B = inp["x"].shape[0]
    pg = build_full()
    shared = host_prep(inp, 0)
    in_maps = []
    for b in range(B):
        m = dict(shared)
        m["x"] = np.ascontiguousarray(inp["x"][b], dtype=np.float32)
        m["ctx"] = np.ascontiguousarray(inp["ctx"][b], dtype=np.float32)
        cv = np.stack([inp["c"][b], inp["c_ctx"]], axis=-1)
        m["cvec"] = np.ascontiguousarray(cv.reshape(KT, 128, 2).transpose(1, 0, 2), dtype=np.float32)
        in_maps.append({k: v for k, v in m.items() if k in pg.din})
    res = run_bass_kernel_spmd(pg.nc, in_maps, core_ids=list(range(B)))
    out = np.stack([np.asarray(r["out"], dtype=np.float32) for r in res.results], axis=0)
    return out


def hgrn_stream(self, d, pool, ph):
    nc = self.nc
    sc = 128.0 ** -0.5
    NP = lambda: self.nextps(pool)
    if True:
        S = self.sb(ph, "hS", [128, 4, 128], F32)
        self.memset(S, 0.0)
        Mrel = self.cst[:, 4 + d, :]
        sel = self.selc[:, d, :]
        m4 = self.m4c[:, d, :]
        lbB = self.lbB[:, d, :]
        omlb = self.omlb[:, d, :]
        inb = [self.sb(ph, "hin%d" % k, [128, 2048], F32) for k in range(2)]
        sg = [self.sb(ph, "hsg%d" % k, [128, 512], F32) for k in range(2)]
        lf = [self.sb(ph, "hlf%d" % k, [128, 512], F32) for k in range(2)]
        kk = [self.sb(ph, "hkk%d" % k, [128, 512], F32) for k in range(2)]
        E1 = [self.sb(ph, "hE1%d" % k, [128, 512], F32) for k in range(2)]
        E2 = [self.sb(ph, "hE2%d" % k, [128, 512], F32) for k in range(2)]
        cl = self.sb(ph, "hcl", [128, 512], F32)
        Ee = [self.sb(ph, "hEe%d" % k, [128, 12], F32) for k in range(2)]
        qin = [self.sb(ph, "hqin%d" % k, [128, 512], BF16) for k in range(2)]
        kin = [self.sb(ph, "hkin%d" % k, [128, 512], BF16) for k in range(2)]
        vbf = [self.sb(ph, "hvbf%d" % k, [128, 512], BF16) for k in range(2)]
        qT = [self.sb(ph, "hqT%d" % k, [128, 4, 128], BF16) for k in range(2)]
        kT = [self.sb(ph, "hkT%d" % k, [128, 4, 128], BF16) for k in range(2)]
        at = [self.sb(ph, "hat%d" % k, [128, 512], BF16) for k in range(2)]
        St = [self.sb(ph, "hSt%d" % k, [128, 4, 128], BF16) for k in range(2)]
        tmpS = self.sb(ph, "htmpS", [128, 4, 128], F32)
        ob = [self.sb(ph, "hob%d" % k, [128, 512], F32) for k in range(2)]
        f2 = lambda t: T(t.ap.rearrange("p h t -> p (h t)"), t.s)
        H = [slice(h * 128, (h + 1) * 128) for h in range(4)]
        OAd = self.OAd[d]
        for it, ci in enumerate(chunk_order(d)):
            b2 = it % 2
            p0 = ci * 128
            x_ = inb[b2]
            self.dma(x_, self.Atm.t(self.Atm.ap[p0:p0 + 128, 0:2048], p0, 128))
            af = x_[:, 512 * (1 + d):512 * (2 + d)]
            s_ = sg[b2]
            self.act(s_, af, AF.Sigmoid)
            self.tt(s_, s_, omlb, ALU.mult)
            self.tt(s_, s_, lbB, ALU.add, e="pool")
            yield
            l_ = lf[b2]
            self.act(l_, s_, AF.Ln)
            self.ts(kk[b2], s_, -1.0, ALU.mult, 1.0, ALU.add, e="pool")
            psb_ = NP()
            self.mm(psb_, Mrel, l_)
            pse = NP()
            for h in range(4):
                self.mm(pse[:, 3 * h:3 * h + 3], l_[:, H[h]], sel)
            yield
            self.act(E1[b2], psb_, AF.Exp)
            self.act(E2[b2], psb_, AF.Exp, scale=-1.0)
            e_ = Ee[b2]
            self.act(e_, pse[:, 0:12], AF.Exp)
            yield
            q_, k_, v_ = qin[b2], kin[b2], vbf[b2]
            self.stt(q_, x_[:, 0:512], sc, E1[b2], ALU.mult, ALU.mult)
            self.tt(k_, kk[b2], E2[b2], ALU.mult, e="pool")
            self.copy(v_, x_[:, 1536:2048], e="pool")
            yield
            psq = NP()
            psk = NP()
            for h in range(4):
                self.tr(psb(psq)[:, H[h]], q_[:, H[h]], self.identb)
            for h in range(4):
                self.tr(psb(psk)[:, H[h]], k_[:, H[h]], self.identb)
            qT_, kT_ = qT[b2], kT[b2]
            self.copy(f2(qT_), psb(psq)[:, 0:512], e="dve")
            self.copy(f2(kT_), psb(psk)[:, 0:512], e="act")
            yield
            psa = NP()
            for h in range(4):
                self.mm(psa[:, H[h]], kT_[:, h, :], qT_[:, h, :])
            a_ = at[b2]
            self.ts(cl, psa, 1e30, ALU.min, -1e30, ALU.max)
            self.tt(a_, cl, m4, ALU.mult, e="pool")
            yield
            St_ = St[b2]
            for h in range(4):
                self.act(St_[:, h, :], S[:, h, :], AF.Copy, scale=e_[:, 3 * h:3 * h + 1])
            yield
            pso = NP()
            for h in range(4):
                self.mm(pso[:, H[h]], a_[:, H[h]], v_[:, H[h]], start=True, stop=False)
                self.mm(pso[:, H[h]], qT_[:, h, :], St_[:, h, :], start=False, stop=True)
            psd = NP()
            for h in range(4):
                self.mm(psd[:, H[h]], k_[:, H[h]], v_[:, H[h]])
            yield
            for h in range(4):
                self.act(tmpS[:, h, :], psd[:, H[h]], AF.Copy, scale=e_[:, 3 * h + 2:3 * h + 3])
                self.stt(S[:, h, :], S[:, h, :], e_[:, 3 * h + 1:3 * h + 2], tmpS[:, h, :], ALU.mult, ALU.add)
            o_ = ob[b2]
            self.copy(o_, pso, e="dve")
            self.dma(OAd.t(OAd.ap[p0:p0 + 128, :], p0, 128), o_)
            yield


def gdn_shared(self, st):
    gcw = self.sb(st, "gcw", [128, 12, 4], F32)
    self.dma(gcw, self.din["gcw"])
    self.dwg = self.sb(st, "gdw", [128, 12, 4, 128], BF16)
    for c in range(12):
        for k in range(4):
            self.ts(self.dwg[:, c, k, :], self.identb, gcw[:, c, k:k + 1], ALU.mult)
    self.I4 = self.sb(st, "gI4", [128, 4, 128], F32)
    for h in range(4):
        self.copy(self.I4[:, h, :], self.ident, e="pool")


def gdn_stream(self, d, pool, ph):
    nc = self.nc
    sc = 128.0 ** -0.5
    NP = lambda: self.nextps(pool)
    if True:
        S = self.sb(ph, "gS", [128, 4, 128], F32)
        Sb = self.sb(ph, "gSb", [128, 4, 128], BF16)
        self.memset(S, 0.0)
        self.memset(Sb, 0.0)
        TRI = self.cst[:, 2 + d, :]
        NEGT = self.cst[:, 6 + d, :]
        POSA = self.cst[:, 8 + d, :]
        dwg, I4 = self.dwg, self.I4
        pw = [self.sb(ph, "gpw%d" % k, [128, 12, 131], BF16) for k in range(2)]
        qs = self.sb(ph, "gqs", [128, 512], F32)
        ks = self.sb(ph, "gks", [128, 512], F32)
        vs = self.sb(ph, "gvs", [128, 512], F32)
        junk = self.sb(ph, "gjunk", [128, 128], F32)
        rn = self.sb(ph, "grn", [128, 8], F32)
        rnq = self.sb(ph, "grnq", [128, 4], F32)
        gcs = self.sb(ph, "ggcs", [128, 8], F32)
        ngc = self.sb(ph, "gngc", [128, 4], F32)
        egc = self.sb(ph, "gegc", [128, 4], F32)
        ekd = self.sb(ph, "gekd", [128, 4], F32)
        egl = self.sb(ph, "gegl", [128, 4], F32)
        bw = self.sb(ph, "gbw", [128, 4], F32)
        gbc = self.sb(ph, "ggbc", [128, 4, 128], F32)
        decT = self.sb(ph, "gdecT", [128, 512], F32)
        decA = self.sb(ph, "gdecA", [128, 4, 128], F32)
        khb = self.sb(ph, "gkhb", [128, 512], BF16)
        qhb = self.sb(ph, "gqhb", [128, 512], BF16)
        khT = self.sb(ph, "gkhT", [128, 4, 128], BF16)
        qhT = self.sb(ph, "gqhT", [128, 4, 128], BF16)
        X = [self.sb(ph, "gX%d" % k, [128, 4, 128], F32) for k in range(2)]
        XT = [self.sb(ph, "gXT%d" % k, [128, 4, 128], F32) for k in range(2)]
        PT = self.sb(ph, "gPT", [128, 4, 128], F32)
        RU = self.sb(ph, "gRU", [128, 4, 128], F32)
        RW = self.sb(ph, "gRW", [128, 4, 128], F32)
        u = self.sb(ph, "gu", [128, 512], F32)
        wT = self.sb(ph, "gwT", [128, 4, 128], BF16)
        atT = self.sb(ph, "gatT", [128, 512], BF16)
        qg = self.sb(ph, "gqg", [128, 512], BF16)
        qgT = self.sb(ph, "gqgT", [128, 4, 128], BF16)
        kd = self.sb(ph, "gkd", [128, 512], BF16)
        vn = self.sb(ph, "gvn", [128, 512], BF16)
        ob = [self.sb(ph, "gob%d" % k, [128, 512], F32) for k in range(2)]
        f2 = lambda t: T(t.ap.rearrange("p h t -> p (h t)"), t.s)
        R = (lambda t: T(t.ap.bitcast(F32R), t.s)) if USE_F32R else (lambda t: t)
        rb = self.sb(ph, "grb", [128, 4], F32)
        H = [slice(h * 128, (h + 1) * 128) for h in range(4)]
        OBd = self.OBd[d]
        for it, ci in enumerate(chunk_order(d)):
            b2 = it % 2
            p0 = ci * 128
            off = CP0 + p0 if ci < 2 else CL0 + (p0 - TC)
            w_ = pw[b2]
            self.dma(w_, T(self.preT.ap[:, :, off - 2:off + 129].rearrange("c p t -> p c t"), self.preT.s))
            pss = [NP() for _ in range(3)]
            for c in range(12):
                for k in range(4):
                    self.mm(pss[c // 4][:, H[c % 4]], w_[:, c, k:k + 128], dwg[:, c, k, :], start=(k == 0), stop=(k == 3))
                if c % 4 == 3:
                    yield
            self.act(qs, pss[0], AF.Silu)
            self.act(ks, pss[1], AF.Silu)
            self.act(vs, pss[2], AF.Silu)
            yield
            for h in range(4):
                self.act(junk, qs[:, H[h]], AF.Square, accum=rn[:, h:h + 1])
                self.act(junk, ks[:, H[h]], AF.Square, accum=rn[:, 4 + h:5 + h])
            self.act(rn, rn, AF.Sqrt, bias=self.epsc, scale=1.0)
            self.recip(rn, rn)
            self.ts(rnq, rn[:, 0:4], sc, ALU.mult)
            yield
            g4 = self.gG[:, ci, 4 * d:4 * d + 4]
            b4 = self.gB[:, ci, 4 * d:4 * d + 4]
            nb4 = self.gNB[:, ci, 4 * d:4 * d + 4]
            psg = NP()
            self.mm(psg[:, 0:4], TRI, g4)
            self.mm(psg[:, 4:8], self.ones, g4)
            self.copy(gcs, psg[:, 0:8], e="dve")
            self.ts(ngc, gcs[:, 0:4], -1.0, ALU.mult)
            self.act(egc, gcs[:, 0:4], AF.Exp)
            self.act(egl, gcs[:, 4:8], AF.Exp)
            self.tt(ekd, gcs[:, 4:8], gcs[:, 0:4], ALU.subtract)
            self.act(ekd, ekd, AF.Exp)
            self.tt(bw, b4, egc, ALU.mult)
            yield
            for h in range(4):
                self.ts(gbc[:, h, :], self.ones, g4[:, h:h + 1], ALU.mult, e=("pool" if h % 2 else "dve"))
            psD = NP()
            psA = NP()
            for h in range(4):
                self.mm(psD[:, H[h]], gbc[:, h, :], TRI, start=True, stop=False)
                self.mm(psD[:, H[h]], self.ident, NEGT, start=False, stop=True)
            yield
            for h in range(4):
                self.mm(psA[:, H[h]], gbc[:, h, :], TRI, start=True, stop=False)
                self.mm(psA[:, H[h]], self.ident, POSA, start=False, stop=True)
            yield
            for h in range(4):
                self.act(decT[:, H[h]], psD[:, H[h]], AF.Exp, bias=ngc[:, h:h + 1], scale=1.0)
                self.act(decA[:, h, :], psA[:, H[h]], AF.Exp, bias=gcs[:, h:h + 1], scale=-1.0)
            yield
            for h in range(4):
                self.act(khb[:, H[h]], ks[:, H[h]], AF.Copy, scale=rn[:, 4 + h:5 + h])
                self.ts(qhb[:, H[h]], qs[:, H[h]], rnq[:, h:h + 1], ALU.mult)
            yield
            pst = NP()
            pst2 = NP()
            for h in range(4):
                self.tr(psb(pst)[:, H[h]], khb[:, H[h]], self.identb)
            for h in range(4):
                self.tr(psb(pst2)[:, H[h]], qhb[:, H[h]], self.identb)
            self.copy(f2(khT), psb(pst)[:, 0:512], e="dve")
            self.copy(f2(qhT), psb(pst2)[:, 0:512], e="act")
            yield
            psK = NP()
            for h in range(4):
                self.mm(psK[:, H[h]], khT[:, h, :], khT[:, h, :])
            X0, XT0 = X[0], XT[0]
            for h in range(4):
                self.stt(R(X0[:, h, :]), psK[:, H[h]], nb4[:, h:h + 1], decA[:, h, :], ALU.mult, ALU.mult)
            yield
            psn = NP()
            for h in range(4):
                self.tr(psn[:, H[h]], X0[:, h, :], self.ident)
            self.copy(R(f2(XT0)), psn, e="act")
            self.tt(R(f2(PT)), f2(XT0), f2(I4), ALU.add)
            yield
            cur = 0
            for m in range(6):
                Xc, XTc, Xn, XTn = X[cur], XT[cur], X[1 - cur], XT[1 - cur]
                ps1 = NP()
                for h in range(4):
                    self.mm(ps1[:, H[h]], R(XTc[:, h, :]), R(Xc[:, h, :]))
                self.copy(R(f2(Xn)), ps1, e="act")
                yield
                if m < 5:
                    ps2 = NP()
                    for h in range(4):
                        self.mm(ps2[:, H[h]], R(Xc[:, h, :]), R(XTc[:, h, :]))
                    self.copy(R(f2(XTn)), ps2, e="dve")
                    yield
                ps3 = NP()
                for h in range(4):
                    self.mm(ps3[:, H[h]], R(Xn[:, h, :]), R(PT[:, h, :]))
                self.tt(R(f2(PT)), f2(PT), ps3, ALU.add)
                cur = 1 - cur
                yield
            for h in range(4):
                self.ts(R(RU[:, h, :]), vs[:, H[h]], b4[:, h:h + 1], ALU.mult)
                self.ts(R(RW[:, h, :]), ks[:, H[h]], rn[:, 4 + h:5 + h], ALU.mult, bw[:, h:h + 1], ALU.mult)
            yield
            psu = NP()
            psw = NP()
            for h in range(4):
                self.mm(psu[:, H[h]], R(PT[:, h, :]), R(RU[:, h, :]))
            for h in range(4):
                self.mm(psw[:, H[h]], R(RW[:, h, :]), R(PT[:, h, :]))
            self.copy(u, psu, e="act")
            self.copy(f2(wT), psw, e="dve")
            yield
            psq = NP()
            for h in range(4):
                self.mm(psq[:, H[h]], khT[:, h, :], qhT[:, h, :])
            self.tt(atT, psq, decT, ALU.mult)
            for h in range(4):
                self.ts(qg[:, H[h]], qhb[:, H[h]], egc[:, h:h + 1], ALU.mult, e="pool")
                self.act(kd[:, H[h]], khb[:, H[h]], AF.Copy, scale=ekd[:, h:h + 1])
            yield
            pst3 = NP()
            for h in range(4):
                self.tr(psb(pst3)[:, H[h]], qg[:, H[h]], self.identb)
            self.copy(f2(qgT), psb(pst3)[:, 0:512], e="act")
            yield
            psws = NP()
            for h in range(4):
                self.mm(psws[:, H[h]], wT[:, h, :], Sb[:, h, :])
            self.tt(vn, u, psws, ALU.subtract)
            yield
            pso = NP()
            for h in range(4):
                self.mm(pso[:, H[h]], qgT[:, h, :], Sb[:, h, :], start=True, stop=False)
                self.mm(pso[:, H[h]], atT[:, H[h]], vn[:, H[h]], start=False, stop=True)
            psd = NP()
            for h in range(4):
                self.mm(psd[:, H[h]], kd[:, H[h]], vn[:, H[h]])
            yield
            for h in range(4):
                self.stt(S[:, h, :], S[:, h, :], egl[:, h:h + 1], psd[:, H[h]], ALU.mult, ALU.add)
            self.copy(f2(Sb), f2(S), e="act")
            o_ = ob[b2]
            self.copy(o_, pso, e="dve")
            self.dma(OBd.t(OBd.ap[p0:p0 + 128, :], p0, 128), o_)
            yield


def merge_phase(self):
    with ExitStack() as ph:
        gn = [self.sb(ph, "mgn%d" % k, [128, 512], F32) for k in range(2)]
        self.dma(gn[0], self.din["hng"])
        self.dma(gn[1], self.din["gng"])
        o0 = [self.sb(ph, "mo0%d" % k, [128, 512], F32) for k in range(4)]
        o1 = [self.sb(ph, "mo1%d" % k, [128, 512], F32) for k in range(4)]
        ag = [self.sb(ph, "mag%d" % k, [128, 512], F32) for k in range(4)]
        junk = self.sb(ph, "mjunk", [128, 128], F32)
        ss = [self.sb(ph, "mss%d" % k, [128, 4], F32) for k in range(4)]
        ya = [self.sb(ph, "mya%d" % k, [128, 512], BF16) for k in range(4)]
        yaT = [self.sb(ph, "myaT%d" % k, [128, 4, 128], BF16) for k in range(4)]
        it = 0
        for ci in range(NCH):
            p0 = ci * 128
            for mx in range(2):
                b = it % 4
                it += 1
                src = self.OAd if mx == 0 else self.OBd
                gcol = 2048 if mx == 0 else 2560
                self.dma(o0[b], src[0].t(src[0].ap[p0:p0 + 128, :], p0, 128))
                self.dma(o1[b], src[1].t(src[1].ap[p0:p0 + 128, :], p0, 128))
                self.dma(ag[b], self.Atm.t(self.Atm.ap[p0:p0 + 128, gcol:gcol + 512], p0, 128))
                self.tt(o0[b], o0[b], o1[b], ALU.add, e="pool")
                self.merge(o0[b], ag[b], gn[mx], junk, ss[b], ya[b], yaT[b], 4 * mx, p0)
        self.barrier()


def ab_setup2(self, st):
    self.ab_setup(st)
    self.OAd = [DT(self.scr("OA%d" % k, [P, 512], F32).ap) for k in range(2)]
    self.OBd = [DT(self.scr("OB%d" % k, [P, 512], F32).ap) for k in range(2)]


Prog.hgrn_stream = hgrn_stream
Prog.gdn_shared = gdn_shared
Prog.gdn_stream = gdn_stream
Prog.merge_phase = merge_phase
Prog.ab_setup2 = ab_setup2


def ab_items_order():
    o0, o1 = chunk_order(0), chunk_order(1)
    out = []
    pa = pb = None
    for a, b in zip(o0, o1):
        out.append((0, a, pa))
        out.append((1, b, pb))
        pa, pb = (0, a), (1, b)
    return out


def hgrn_pipe(self):
    sc = 128.0 ** -0.5
    K = 4
    H = [slice(h * 128, (h + 1) * 128) for h in range(4)]
    f2 = lambda t: T(t.ap.rearrange("p h t -> p (h t)"), t.s)
    with ExitStack() as ph:
        S = [self.sb(ph, "hS%d" % d, [128, 4, 128], F32) for d in range(2)]
        for d in range(2):
            self.memset(S[d], 0.0)
        B = []
        for k in range(K):
            b = {}
            for nm, shp, dt in (("x", [128, 2048], F32), ("sg", [128, 512], F32), ("lf", [128, 512], F32),
                                ("kk", [128, 512], F32), ("E1", [128, 512], F32), ("E2", [128, 512], F32),
                                ("cl", [128, 512], F32), ("Ee", [128, 12], F32), ("q", [128, 512], BF16),
                                ("k", [128, 512], BF16), ("v", [128, 512], BF16), ("qT", [128, 4, 128], BF16),
                                ("kT", [128, 4, 128], BF16), ("at", [128, 512], BF16), ("St", [128, 4, 128], BF16),
                                ("tS", [128, 4, 128], F32), ("o", [128, 512], F32)):
                b[nm] = self.sb(ph, "h%s%d" % (nm, k), shp, dt)
            B.append(b)

        done = set()

        def item(d, ci, slot, prev):
            b = B[slot]
            cnt = [0]

            def NP():
                cnt[0] += 1
                return self.PS[2 * slot + (cnt[0] % 2)]
            p0 = ci * 128
            x_ = b["x"]
            self.dma(x_, self.Atm.t(self.Atm.ap[p0:p0 + 128, 0:2048], p0, 128))
            af = x_[:, 512 * (1 + d):512 * (2 + d)]
            s_ = b["sg"]
            self.act(s_, af, AF.Exp, scale=-1.0)
            yield
            self.ts(s_, s_, 1.0, ALU.add, e="pool")
            self.recip(s_, s_)
            yield
            self.tt(s_, s_, self.omlb[:, d, :], ALU.mult)
            self.tt(s_, s_, self.lbB[:, d, :], ALU.add, e="pool")
            yield
            l_ = b["lf"]
            self.act(l_, s_, AF.Ln)
            self.ts(b["kk"], s_, -1.0, ALU.mult, 1.0, ALU.add, e="pool")
            yield
            psb_ = NP()
            self.mm(psb_, self.cst[:, 4 + d, :], l_)
            pse = NP()
            for h in range(4):
                self.mm(pse[:, 3 * h:3 * h + 3], l_[:, H[h]], self.selc[:, d, :])
            yield
            self.act(b["E1"], psb_, AF.Exp)
            self.act(b["E2"], psb_, AF.Exp, scale=-1.0)
            e_ = b["Ee"]
            self.act(e_, pse[:, 0:12], AF.Exp)
            yield
            q_, k_, v_ = b["q"], b["k"], b["v"]
            self.stt(q_, x_[:, 0:512], sc, b["E1"], ALU.mult, ALU.mult)
            self.tt(k_, b["kk"], b["E2"], ALU.mult, e="pool")
            self.copy(v_, x_[:, 1536:2048], e="pool")
            yield
            psq = NP()
            psk = NP()
            for h in range(4):
                self.tr(psb(psq)[:, H[h]], q_[:, H[h]], self.identb)
            for h in range(4):
                self.tr(psb(psk)[:, H[h]], k_[:, H[h]], self.identb)
            qT_, kT_ = b["qT"], b["kT"]
            self.copy(f2(qT_), psb(psq)[:, 0:512], e="dve")
            self.copy(f2(kT_), psb(psk)[:, 0:512], e="act")
            yield
            psa = NP()
            for h in range(4):
                self.mm(psa[:, H[h]], kT_[:, h, :], qT_[:, h, :])
            a_ = b["at"]
            self.ts(b["cl"], psa, 1e30, ALU.min, -1e30, ALU.max)
            yield
            self.tt(a_, b["cl"], self.m4c[:, d, :], ALU.mult, e="pool")
            yield
            St_ = b["St"]
            Sd = S[d]
            while prev is not None and prev not in done:
                yield
            for h in range(4):
                self.ts(St_[:, h, :], Sd[:, h, :], e_[:, 3 * h:3 * h + 1], ALU.mult, e="pool")
            for h in range(4):
                self.ts(Sd[:, h, :], Sd[:, h, :], e_[:, 3 * h + 1:3 * h + 2], ALU.mult, e="pool")
            yield
            pso = NP()
            for h in range(4):
                self.mm(pso[:, H[h]], a_[:, H[h]], v_[:, H[h]], start=True, stop=False)
                self.mm(pso[:, H[h]], qT_[:, h, :], St_[:, h, :], start=False, stop=True)
            psd = NP()
            for h in range(4):
                self.mm(psd[:, H[h]], k_[:, H[h]], v_[:, H[h]])
            yield
            for h in range(4):
                self.stt(Sd[:, h, :], psd[:, H[h]], e_[:, 3 * h + 2:3 * h + 3], Sd[:, h, :], ALU.mult, ALU.add)
            done.add((d, ci))
            yield
            o_ = b["o"]
            self.copy(o_, pso, e="dve")
            OAd = self.OAd[d]
            self.dma(OAd.t(OAd.ap[p0:p0 + 128, :], p0, 128), o_)
            yield

        self.run_pipe((item(d, ci, i % K, pv) for i, (d, ci, pv) in enumerate(ab_items_order())), K, gap=4)
        self.barrier()


def gdn_pipe(self):
    sc = 128.0 ** -0.5
    K = 3
    H = [slice(h * 128, (h + 1) * 128) for h in range(4)]
    f2 = lambda t: T(t.ap.rearrange("p h t -> p (h t)"), t.s)
    with ExitStack() as ph:
        self.I4 = self.sb(ph, "gI4", [128, 4, 128], F32)
        for h in range(4):
            self.copy(self.I4[:, h, :], self.ident, e="pool")
        I4 = self.I4
        S = [self.sb(ph, "gS%d" % d, [128, 4, 128], F32) for d in range(2)]
        Sb = [self.sb(ph, "gSb%d" % d, [128, 4, 128], BF16) for d in range(2)]
        junk = self.sb(ph, "gjunk", [128, 128], F32)
        for d in range(2):
            self.memset(S[d], 0.0)
            self.memset(Sb[d], 0.0)
        B = []
        for k in range(K):
            b = {}
            for nm, shp, dt in (("pw", [128, 12, 131], BF16), ("qs", [128, 512], F32), ("ks", [128, 512], F32),
                                ("vs", [128, 512], F32), ("rn", [128, 8], F32), ("rnq", [128, 4], F32),
                                ("gcs", [128, 8], F32), ("ngc", [128, 4], F32), ("egc", [128, 4], F32),
                                ("ekd", [128, 4], F32), ("egl", [128, 4], F32), ("bw", [128, 4], F32),
                                ("gbc", [128, 4, 128], F32), ("decT", [128, 512], F32), ("decA", [128, 4, 128], F32),
                                ("khb", [128, 512], BF16), ("qhb", [128, 512], BF16), ("khT", [128, 4, 128], BF16),
                                ("qhT", [128, 4, 128], BF16), ("X0", [128, 4, 128], F32), ("X1", [128, 4, 128], F32),
                                ("XT0", [128, 4, 128], F32), ("XT1", [128, 4, 128], F32), ("PT", [128, 4, 128], F32),
                                ("RU", [128, 4, 128], F32), ("RW", [128, 4, 128], F32), ("u", [128, 512], F32),
                                ("wT", [128, 4, 128], BF16), ("atT", [128, 512], BF16), ("qg", [128, 512], BF16),
                                ("qgT", [128, 4, 128], BF16), ("kd", [128, 512], BF16), ("vn", [128, 512], BF16),
                                ("o", [128, 512], F32)):
                b[nm] = self.sb(ph, "g%s%d" % (nm, k), shp, dt)
            B.append(b)
        banks = [(0, 1, 2), (3, 4, 5), (6, 7)]

        done = set()

        def item(d, ci, slot, prev):
            b = B[slot]
            cnt = [0]
            bk = banks[slot]

            def NP():
                cnt[0] += 1
                return self.PS[bk[cnt[0] % len(bk)]]
            TRI = self.cst[:, 2 + d, :]
            NEGT = self.cst[:, 6 + d, :]
            POSA = self.cst[:, 8 + d, :]
            p0 = ci * 128
            off = CP0 + p0 if ci < 2 else CL0 + (p0 - TC)
            w_ = b["pw"]
            qs, ks, vs, rn, rnq, gcs, ngc = b["qs"], b["ks"], b["vs"], b["rn"], b["rnq"], b["gcs"], b["ngc"]
            egc, ekd, egl, bw, gbc, decT, decA = b["egc"], b["ekd"], b["egl"], b["bw"], b["gbc"], b["decT"], b["decA"]
            khb, qhb, khT, qhT, PT, RU, RW = b["khb"], b["qhb"], b["khT"], b["qhT"], b["PT"], b["RU"], b["RW"]
            u, wT, atT, qg, qgT, kd, vn = b["u"], b["wT"], b["atT"], b["qg"], b["qgT"], b["kd"], b["vn"]
            X = [b["X0"], b["X1"]]
            XT = [b["XT0"], b["XT1"]]
            self.dma(w_[:, :, 0:128], self.qkvT.t(self.qkvT.ap[:, :, p0:p0 + 128], p0, 128))
            for grp, dst in enumerate((qs, ks, vs)):
                ps = NP()
                for c4 in range(4):
                    self.tr(psb(ps)[:, H[c4]], w_[:, grp * 4 + c4, 0:128], self.identb)
                self.copy(dst, psb(ps)[:, 0:512], e=("dve" if grp == 1 else "act"))
                yield
            for h in range(4):
                self.act(junk, qs[:, H[h]], AF.Square, accum=rn[:, h:h + 1])
                self.act(junk, ks[:, H[h]], AF.Square, accum=rn[:, 4 + h:5 + h])
            yield
            self.act(rn, rn, AF.Sqrt, bias=self.epsc, scale=1.0)
            self.recip(rn, rn)
            self.ts(rnq, rn[:, 0:4], sc, ALU.mult)
            yield
            g4 = self.gG[:, ci, 4 * d:4 * d + 4]
            b4 = self.gB[:, ci, 4 * d:4 * d + 4]
            nb4 = self.gNB[:, ci, 4 * d:4 * d + 4]
            psg = NP()
            self.mm(psg[:, 0:4], TRI, g4)
            self.mm(psg[:, 4:8], self.ones, g4)
            self.copy(gcs, psg[:, 0:8], e="dve")
            yield
            self.ts(ngc, gcs[:, 0:4], -1.0, ALU.mult)
            self.act(egc, gcs[:, 0:4], AF.Exp)
            self.act(egl, gcs[:, 4:8], AF.Exp)
            self.tt(ekd, gcs[:, 4:8], gcs[:, 0:4], ALU.subtract)
            yield
            self.act(ekd, ekd, AF.Exp)
            self.tt(bw, b4, egc, ALU.mult)
            for h in range(4):
                self.ts(gbc[:, h, :], self.ones, g4[:, h:h + 1], ALU.mult, e=("pool" if h % 2 else "dve"))
            yield
            psD = NP()
            for h in range(4):
                self.mm(psD[:, H[h]], gbc[:, h, :], TRI, start=True, stop=False)
                self.mm(psD[:, H[h]], self.ident, NEGT, start=False, stop=True)
            for h in range(4):
                self.act(decT[:, H[h]], psD[:, H[h]], AF.Exp, bias=ngc[:, h:h + 1], scale=1.0)
            yield
            psA = NP()
            for h in range(4):
                self.mm(psA[:, H[h]], gbc[:, h, :], TRI, start=True, stop=False)
                self.mm(psA[:, H[h]], self.ident, POSA, start=False, stop=True)
            for h in range(4):
                self.act(decA[:, h, :], psA[:, H[h]], AF.Exp, bias=gcs[:, h:h + 1], scale=-1.0)
            yield
            for h in range(4):
                self.act(khb[:, H[h]], ks[:, H[h]], AF.Copy, scale=rn[:, 4 + h:5 + h])
                self.ts(qhb[:, H[h]], qs[:, H[h]], rnq[:, h:h + 1], ALU.mult)
            yield
            pst = NP()
            for h in range(4):
                self.tr(psb(pst)[:, H[h]], khb[:, H[h]], self.identb)
            self.copy(f2(khT), psb(pst)[:, 0:512], e="dve")
            pst2 = NP()
            for h in range(4):
                self.tr(psb(pst2)[:, H[h]], qhb[:, H[h]], self.identb)
            self.copy(f2(qhT), psb(pst2)[:, 0:512], e="act")
            yield
            psK = NP()
            for h in range(4):
                self.mm(psK[:, H[h]], khT[:, h, :], khT[:, h, :])
            for h in range(4):
                self.stt(X[0][:, h, :], psK[:, H[h]], nb4[:, h:h + 1], decA[:, h, :], ALU.mult, ALU.mult)
            yield
            psn = NP()
            for h in range(4):
                self.tr(psn[:, H[h]], X[0][:, h, :], self.ident)
            self.copy(f2(XT[0]), psn, e="act")
            yield
            self.tt(f2(PT), f2(XT[0]), f2(I4), ALU.add, e="pool")
            yield
            cur = 0
            for m in range(6):
                Xc, XTc, Xn, XTn = X[cur], XT[cur], X[1 - cur], XT[1 - cur]
                ps1 = NP()
                for h in range(4):
                    self.mm(ps1[:, H[h]], XTc[:, h, :], Xc[:, h, :])
                self.copy(f2(Xn), ps1, e="act")
                yield
                if m < 5:
                    ps2 = NP()
                    for h in range(4):
                        self.mm(ps2[:, H[h]], Xc[:, h, :], XTc[:, h, :])
                    self.copy(f2(XTn), ps2, e="dve")
                    yield
                ps3 = NP()
                for h in range(4):
                    self.mm(ps3[:, H[h]], Xn[:, h, :], PT[:, h, :])
                self.tt(f2(PT), f2(PT), ps3, ALU.add)
                cur = 1 - cur
                yield
            for h in range(4):
                self.ts(RU[:, h, :], vs[:, H[h]], b4[:, h:h + 1], ALU.mult, e=("pool" if h % 2 else "dve"))
                self.ts(RW[:, h, :], ks[:, H[h]], rn[:, 4 + h:5 + h], ALU.mult, bw[:, h:h + 1], ALU.mult,
                        e=("dve" if h % 2 else "pool"))
            yield
            psu = NP()
            for h in range(4):
                self.mm(psu[:, H[h]], PT[:, h, :], RU[:, h, :])
            self.copy(u, psu, e="act")
            psw = NP()
            for h in range(4):
                self.mm(psw[:, H[h]], RW[:, h, :], PT[:, h, :])
            self.copy(f2(wT), psw, e="dve")
            yield
            psq = NP()
            for h in range(4):
                self.mm(psq[:, H[h]], khT[:, h, :], qhT[:, h, :])
            self.tt(atT, psq, decT, ALU.mult)
            yield
            for h in range(4):
                self.ts(qg[:, H[h]], qhb[:, H[h]], egc[:, h:h + 1], ALU.mult, e="pool")
                self.act(kd[:, H[h]], khb[:, H[h]], AF.Copy, scale=ekd[:, h:h + 1])
            yield
            pst3 = NP()
            for h in range(4):
                self.tr(psb(pst3)[:, H[h]], qg[:, H[h]], self.identb)
            self.copy(f2(qgT), psb(pst3)[:, 0:512], e="act")
            yield
            Sd, Sbd = S[d], Sb[d]
            while prev is not None and prev not in done:
                yield
            psws = NP()
            for h in range(4):
                self.mm(psws[:, H[h]], wT[:, h, :], Sbd[:, h, :])
            self.tt(vn, u, psws, ALU.subtract)
            yield
            pso = NP()
            for h in range(4):
                self.mm(pso[:, H[h]], qgT[:, h, :], Sbd[:, h, :], start=True, stop=False)
                self.mm(pso[:, H[h]], atT[:, H[h]], vn[:, H[h]], start=False, stop=True)
            o_ = b["o"]
            self.copy(o_, pso, e="act")
            psd = NP()
            for h in range(4):
                self.mm(psd[:, H[h]], kd[:, H[h]], vn[:, H[h]])
            for h in range(4):
                self.stt(Sd[:, h, :], Sd[:, h, :], egl[:, h:h + 1], psd[:, H[h]], ALU.mult, ALU.add)
            yield
            self.copy(f2(Sbd), f2(Sd), e="act")
            done.add((d, ci))
            OBd = self.OBd[d]
            self.dma(OBd.t(OBd.ap[p0:p0 + 128, :], p0, 128), o_)
            yield

        self.run_pipe((item(d, ci, i % K, pv) for i, (d, ci, pv) in enumerate(ab_items_order())), K, gap=14)
        self.barrier()


Prog.hgrn_pipe = hgrn_pipe
Prog.gdn_pipe = gdn_pipe


def xinit_phase2(self):
    K = 4
    with ExitStack() as st:
        xin = [self.sb(st, "xin%d" % k, [128, D], F32) for k in range(K)]
        xo = [self.sb(st, "xo%d" % k, [128, KT, 128], F32) for k in range(K)]

        def item(i, slot):
            src = self.din["ctx"][i * 128:(i + 1) * 128, :] if i < 2 else self.din["x"][(i - 2) * 128:(i - 1) * 128, :]
            xi = xin[slot]
            self.dma(xi, src)
            yield
            for half in range(2):
                ps = self.PS[2 * slot + half]
                for k4 in range(4):
                    kt = half * 4 + k4
                    self.tr(ps[:, k4 * 128:(k4 + 1) * 128], xi[:, kt * 128:(kt + 1) * 128], self.ident)
                dst = xo[slot][:, half * 4:(half + 1) * 4, :]
                self.copy(dst, T(ps.ap.rearrange("p (k t) -> p k t", t=128), ps.s), e=("act" if half else "dve"))
                yield
            self.dma(self.xT.t(self.xT.ap[:, :, i * 128:(i + 1) * 128], i * 128, 128), xo[slot])
            yield
        self.run_pipe((item(i, i % K) for i in range(NCH)), K, gap=1)
        self.barrier()


def norm_phase2(self, st, l, w_, hT, tiles):
    sec_sh = 0 if w_ == 0 else 3
    K = 3
    with ExitStack() as ph:
        xs = [self.sb(ph, "nxs%d" % k, [128, KT, 512], F32) for k in range(K)]
        sq = [self.sb(ph, "nsq%d" % k, [128, 2, 512], BF16) for k in range(K)]
        rs = [self.sb(ph, "nrs%d" % k, [128, 512], F32) for k in range(K)]
        tm = [self.sb(ph, "ntm%d" % k, [128, 2, 512], F32) for k in range(K)]

        def item(p0, n, j, slot):
            x_ = xs[slot]
            self.dma(x_[:, :, :n], self.xT.t(self.xT.ap[:, :, p0:p0 + n], p0, n))
            yield
            ps = self.PS[2 * slot]
            for kt in range(KT):
                s_ = sq[slot][:, kt % 2, :n]
                if kt % 2:
                    self.tt(s_, x_[:, kt, :n], x_[:, kt, :n], ALU.mult, e="pool")
                else:
                    self.act(s_, x_[:, kt, :n], AF.Square)
                self.mm(ps[:, :n], self.onesb, s_, start=(kt == 0), stop=(kt == KT - 1))
                yield
            r_ = rs[slot]
            self.act(r_[:, :n], ps[:, :n], AF.Sqrt, bias=self.epsc, scale=1.0 / D)
            self.recip(r_[:, :n], r_[:, :n])
            yield
            for kt in range(KT):
                t_ = tm[slot][:, kt % 2, :n]
                self.stt(t_, x_[:, kt, :n], self.gs[l][w_][:, kt, j:j + 1], r_[:, :n], ALU.mult, ALU.mult)
                self.act(hT[:, kt, p0:p0 + n], t_, AF.Identity,
                         bias=self.modv[l][:, sec_sh * 8 + kt, j:j + 1], scale=1.0)
                yield
        self.run_pipe((item(p0, n, j, i % K) for i, (p0, n, j) in enumerate(tiles)), K, gap=6)
        self.barrier()


def outproj_phase2(self, l, wname, widx, nkt, src, gsec, tiles):
    K = 2
    with ExitStack() as ph:
        wsb = self.sb(ph, "opw", [128, nkt, D], BF16)
        wd = self.din[wname] if widx is None else self.din[wname][widx]
        for kt in range(nkt):
            self.dma(wsb[:, kt, :], wd[:, kt, :], q="pool")
        ub = [self.sb(ph, "opu%d" % k, [128, nkt, 512], BF16) for k in range(K)]
        xs = [self.sb(ph, "opx%d" % k, [128, KT, 512], F32) for k in range(K)]

        def item(p0, n, j, slot):
            u = ub[slot]
            x_ = xs[slot]
            self.dma(u[:, :, :n], src.t(src.ap[:, :, p0:p0 + n], p0, n))
            self.dma(x_[:, :, :n], self.xT.t(self.xT.ap[:, :, p0:p0 + n], p0, n))
            yield
            for nt in range(KT):
                ps = self.PS[4 * slot + nt % 4]
                for kt in range(nkt):
                    self.mm(ps[:, :n], wsb[:, kt, nt * 128:(nt + 1) * 128], u[:, kt, :n],
                            start=(kt == 0), stop=(kt == nkt - 1))
                self.stt(x_[:, nt, :n], ps[:, :n], self.modv[l][:, gsec * 8 + nt, j:j + 1], x_[:, nt, :n],
                         ALU.mult, ALU.add)
                yield
            self.dma(self.xT.t(self.xT.ap[:, :, p0:p0 + n], p0, n), x_[:, :, :n])
            yield
        self.run_pipe((item(p0, n, j, i % K) for i, (p0, n, j) in enumerate(tiles)), K, gap=5)
        self.barrier()


def final_phase2(self):
    K = 4
    with ExitStack() as ph:
        fg = self.sb(ph, "fng", [128, D], F32)
        self.dma(fg, self.din["fng"])
        xs = [self.sb(ph, "fx%d" % k, [128, KT, 128], F32) for k in range(K)]
        xt = [self.sb(ph, "fxt%d" % k, [128, D], F32) for k in range(K)]
        junk = self.sb(ph, "fjunk", [128, D], F32)
        ss = [self.sb(ph, "fss%d" % k, [128, 1], F32) for k in range(K)]
        outs = [T(self.out.ap[i * 128:(i + 1) * 128, :]) for i in range(TX // 128)]
        self.out_tiles = outs

        def item(i, slot):
            p0 = TC + i * 128
            x_ = xs[slot]
            self.dma(x_, self.xT.t(self.xT.ap[:, :, p0:p0 + 128], p0, 128))
            yield
            o_ = xt[slot]
            for half in range(2):
                ps = self.PS[2 * slot + half]
                for k4 in range(4):
                    self.tr(ps[:, k4 * 128:(k4 + 1) * 128], x_[:, half * 4 + k4, :], self.ident)
                self.copy(o_[:, half * 512:(half + 1) * 512], ps, e=("act" if half else "dve"))
                yield
            s_ = ss[slot]
            self.act(junk, o_, AF.Square, accum=s_)
            yield
            self.act(s_, s_, AF.Sqrt, bias=self.epsc, scale=1.0 / D)
            self.recip(s_, s_)
            yield
            self.stt(o_, o_, s_, fg, ALU.mult, ALU.mult)
            yield
            self.dma(outs[i], o_)
            yield
        self.run_pipe((item(i, i % K) for i in range(TX // 128)), K, gap=2)
        self.barrier()


def merge_phase2(self):
    K = 4
    H = [slice(h * 128, (h + 1) * 128) for h in range(4)]
    with ExitStack() as ph:
        gn = [self.sb(ph, "mgn%d" % k, [128, 512], F32) for k in range(2)]
        self.dma(gn[0], self.din["hng"])
        self.dma(gn[1], self.din["gng"])
        o0 = [self.sb(ph, "mo0%d" % k, [128, 512], F32) for k in range(K)]
        o1 = [self.sb(ph, "mo1%d" % k, [128, 512], F32) for k in range(K)]
        ag = [self.sb(ph, "mag%d" % k, [128, 512], F32) for k in range(K)]
        junk = self.sb(ph, "mjunk", [128, 128], F32)
        ss = [self.sb(ph, "mss%d" % k, [128, 4], F32) for k in range(K)]
        ya = [self.sb(ph, "mya%d" % k, [128, 512], BF16) for k in range(K)]
        yaT = [self.sb(ph, "myaT%d" % k, [128, 4, 128], BF16) for k in range(K)]

        def item(ci, mx, slot):
            p0 = ci * 128
            src = self.OAd if mx == 0 else self.OBd
            gcol = 2048 if mx == 0 else 2560
            o_, g_, s_ = o0[slot], ag[slot], ss[slot]
            self.dma(o_, src[0].t(src[0].ap[p0:p0 + 128, :], p0, 128))
            self.dma(o1[slot], src[1].t(src[1].ap[p0:p0 + 128, :], p0, 128))
            self.dma(g_, self.Atm.t(self.Atm.ap[p0:p0 + 128, gcol:gcol + 512], p0, 128))
            yield
            self.tt(o_, o_, o1[slot], ALU.add, e="pool")
            self.tt(g_, g_, gn[mx], ALU.mult, e="pool")
            yield
            for h in range(4):
                self.act(junk, o_[:, H[h]], AF.Square, accum=s_[:, h:h + 1])
            yield
            self.act(s_, s_, AF.Sqrt, bias=self.epsc, scale=1.0 / 128)
            self.recip(s_, s_)
            yield
            for h in range(4):
                self.stt(ya[slot][:, H[h]], o_[:, H[h]], s_[:, h:h + 1], g_[:, H[h]], ALU.mult, ALU.mult)
            yield
            pst = self.PS[2 * slot]
            for h in range(4):
                self.tr(psb(pst)[:, H[h]], ya[slot][:, H[h]], self.identb)
            self.copy(T(yaT[slot].ap.rearrange("p h t -> p (h t)"), yaT[slot].s), psb(pst)[:, 0:512], e="act")
            yield
            self.dma(self.yT.t(self.yT.ap[:, 4 * mx:4 * mx + 4, p0:p0 + 128], p0, 128), yaT[slot])
            yield
        items = [(ci, mx) for ci in range(NCH) for mx in range(2)]
        self.run_pipe((item(ci, mx, i % K) for i, (ci, mx) in enumerate(items)), K, gap=2)
        self.barrier()


Prog.xinit_phase = xinit_phase2
Prog.norm_phase = norm_phase2
Prog.outproj_phase = outproj_phase2
Prog.final_phase = final_phase2
Prog.merge_phase = merge_phase2


def mixer_c_phases2(self, hT):
    K = 3
    xbT = DT(self.scr("xbT", [KT, 128, P], F32).ap.rearrange("k p t -> p k t"))
    gelT = DT(self.scr("gelT", [KT, 128, P], F32).ap.rearrange("k p t -> p k t"))
    win = self.din["c_w_in"]
    with ExitStack() as ph:
        pre = [self.sb(ph, "cpre%d" % k, [128, CPAD], BF16) for k in range(2)]
        for k in range(2):
            self.memset(pre[k], 0.0)
        ccw = self.sb(ph, "ccw", [128, KT, 4], F32)
        ccb = self.sb(ph, "ccb", [128, KT], F32)
        self.dma(ccw, self.din["ccw"])
        self.dma(ccb, self.din["ccb"])
        wg = [self.sb(ph, "cwg%d" % k, [128, KT, 128], BF16) for k in range(2)]
        wx = [self.sb(ph, "cwx%d" % k, [128, KT, 128], BF16) for k in range(2)]
        dw = [self.sb(ph, "cdw%d" % k, [128, 4, 128], BF16) for k in range(2)]
        gx = [self.sb(ph, "cgx%d" % k, [128, 512], F32) for k in range(K)]
        gt = [self.sb(ph, "cgt%d" % k, [128, 512], F32) for k in range(K)]
        xo = [self.sb(ph, "cxo%d" % k, [128, 512], F32) for k in range(K)]
        a_done = [0] * KT

        def W(c):
            self.dma(wg[c % 2], win[:, :, c * 128:(c + 1) * 128], q="pool")
            self.dma(wx[c % 2], win[:, :, D + c * 128:D + (c + 1) * 128], q="pool")
            for k in range(4):
                self.ts(dw[c % 2][:, k, :], self.identb, ccw[:, c, k:k + 1], ALU.mult)
            yield

        def A(c, p0, n, j, slot):
            ps = self.PS[2 * slot]
            for kt in range(KT):
                self.mm(ps[:, :n], wx[c % 2][:, kt, :], hT[:, kt, p0:p0 + n], start=(kt == 0), stop=(kt == KT - 1))
            yield
            off = CP0 + p0 if j == 1 else CL0 + (p0 - TC)
            self.copy(pre[c % 2][:, off:off + n], ps[:, :n], e="act")
            a_done[c] += 1
            yield

        def Bi(c, p0, n, j, slot):
            while a_done[c] < len(TILES):
                yield
            off = CP0 + p0 if j == 1 else CL0 + (p0 - TC)
            ps = self.PS[2 * slot]
            pr = pre[c % 2]
            for k in range(4):
                self.mm(ps[:, :n], dw[c % 2][:, k, :], pr[:, off + k - 2:off + k - 2 + n], start=(k == 0), stop=(k == 3))
            x_o = xo[slot]
            self.act(x_o[:, :n], ps[:, :n], AF.Identity, bias=ccb[:, c:c + 1], scale=1.0)
            yield
            self.dma(xbT.t(xbT.ap[:, c, p0:p0 + n], p0, n), x_o[:, :n])
            if j == 0:
                ps2 = self.PS[2 * slot + 1]
                for kt in range(KT):
                    self.mm(ps2[:, :n], wg[c % 2][:, kt, :], hT[:, kt, p0:p0 + n], start=(kt == 0), stop=(kt == KT - 1))
                g_x, g_t = gx[slot], gt[slot]
                self.copy(g_x[:, :n], ps2[:, :n], e="act")
                yield
                self.tt(g_t[:, :n], g_x[:, :n], g_x[:, :n], ALU.mult, e="pool")
                yield
                self.ts(g_t[:, :n], g_t[:, :n], 0.044715, ALU.mult, 1.0, ALU.add)
                yield
                self.tt(g_t[:, :n], g_t[:, :n], g_x[:, :n], ALU.mult, e="pool")
                yield
                self.act(g_t[:, :n], g_t[:, :n], AF.Sigmoid, scale=GELU_C)
                yield
                self.tt(g_x[:, :n], g_x[:, :n], g_t[:, :n], ALU.mult)
                yield
                self.dma(gelT.t(gelT.ap[:, c, p0:p0 + n], p0, n), g_x[:, :n])
            yield

        def items():
            i = 0
            for c in range(KT):
                yield W(c)
                for (p0, n, j) in TILES:
                    yield A(c, p0, n, j, i % K)
                    i += 1
                for (p0, n, j) in TILES:
                    yield Bi(c, p0, n, j, i % K)
                    i += 1
        self.run_pipe(items(), K, gap=1)
        self.barrier()
    return xbT, gelT


def mixer_c_scan2(self, xbT, gelT):
    nc = self.nc
    K = 4
    with ExitStack() as ph:
        cgw = self.sb(ph, "cgw", [128, 16, 512], BF16)
        for k in range(16):
            self.dma(cgw[:, k, :], self.din["cgw"][:, k, :], q="pool")
        cgb = self.sb(ph, "cgb", [128, 2, 4, 4], F32)
        lam = self.sb(ph, "clam", [128, 2, KT], F32)
        scl = self.sb(ph, "cscl", [128, 2, KT], F32)
        scl2 = self.sb(ph, "cscl2", [128, 2, KT], F32)
        self.dma(cgb, self.din["cgb"])
        self.dma(lam, self.din["clam"])
        ncgb = self.sb(ph, "ncgb", [128, 2, 4, 4], F32)
        self.ts(ncgb, cgb, -1.0, ALU.mult)
        self.act(scl, lam, AF.Exp, scale=-1.0)
        self.act(scl, scl, AF.Ln, bias=self.ones[:, 0:1], scale=1.0)
        self.ts(scl2, scl, -16.0, ALU.mult)
        self.ts(scl, scl, -8.0, ALU.mult)
        xbb = [self.sb(ph, "cxbb%d" % k, [128, 2, P], BF16) for k in range(2)]
        xbf = [self.sb(ph, "cxbf%d" % k, [128, P], F32) for k in range(2)]
        hs = [[self.sb(ph, "chs%d_%d" % (k, d), [128, P], F32) for d in range(2)] for k in range(2)]
        rb = [self.sb(ph, "crb%d" % k, [128, 512], F32) for k in range(K)]
        ib = [self.sb(ph, "cib%d" % k, [128, 512], F32) for k in range(K)]
        ab = [self.sb(ph, "cab%d" % k, [128, 512], F32) for k in range(K)]
        yo = [self.sb(ph, "cyo%d" % k, [128, 512], BF16) for k in range(K)]
        scans = [0] * KT
        allslots = tuple(xbT.slots)

        def L(c):
            h, half = c // 2, c % 2
            if half == 0:
                for k2 in range(2):
                    self.dma(xbb[h % 2][:, k2, :], T(xbT.ap[:, 2 * h + k2, :], allslots), q="pool")
            self.dma(xbf[c % 2], T(xbT.ap[:, c, :], allslots))
            yield

        def S(c, d, p0, n, init, slot):
            h, half = c // 2, c % 2
            psr, psi = self.PS[2 * slot], self.PS[2 * slot + 1]
            xb_ = xbb[h % 2]
            for k2 in range(2):
                wbase = cgw[:, (d * 4 + h) * 2 + k2, :]
                self.mm(psr[:, :n], wbase[:, half * 128:(half + 1) * 128], xb_[:, k2, p0:p0 + n],
                        start=(k2 == 0), stop=(k2 == 1))
            for k2 in range(2):
                wbase = cgw[:, (d * 4 + h) * 2 + k2, :]
                self.mm(psi[:, :n], wbase[:, 256 + half * 128:256 + (half + 1) * 128], xb_[:, k2, p0:p0 + n],
                        start=(k2 == 0), stop=(k2 == 1))
            yield
            r_, i_, a_ = rb[slot][:, :n], ib[slot][:, :n], ab[slot][:, :n]
            self.act(r_, psr[:, :n], AF.Sigmoid, bias=cgb[:, d, h, half:half + 1], scale=1.0)
            self.act(i_, psi[:, :n], AF.Sigmoid, bias=cgb[:, d, h, 2 + half:3 + half], scale=1.0)
            yield
            self.act(a_, r_, AF.Exp, scale=scl[:, d, c:c + 1])
            yield
            self.tt(r_, a_, a_, ALU.mult, e="pool")
            self.tt(i_, i_, xbf[c % 2][:, p0:p0 + n], ALU.mult, e="pool")
            yield
            self.ts(r_, r_, -1.0, ALU.mult, 1.0, ALU.add)
            yield
            self.act(r_, r_, AF.Sqrt)
            yield
            self.tt(r_, r_, i_, ALU.mult)
            yield
            h_ = hs[c % 2][d]
            if d == 0:
                o_ap, a_ap, u_ap = h_.ap[:, p0:p0 + n], a_.ap, r_.ap
            else:
                lo = p0 - 1 if p0 > 0 else None
                o_ap = h_.ap[:, p0 + n - 1:lo:-1]
                a_ap = a_.ap[:, ::-1]
                u_ap = r_.ap[:, ::-1]
            iv = 0.0 if init is None else h_.ap[:, init:init + 1]
            self.op("dve", lambda: nc.vector.tensor_tensor_scan(o_ap, a_ap, u_ap, iv, ALU.mult, ALU.add),
                    r=[a_, r_, h_], w=[h_])
            scans[c] += 1
            yield

        def O(c, p0, n, slot):
            while scans[c] < 2 * len(TILES):
                yield
            g_ = ab[slot][:, :n]
            self.dma(g_, gelT.t(gelT.ap[:, c, p0:p0 + n], p0, n))
            t_ = rb[slot][:, :n]
            self.tt(t_, hs[c % 2][0][:, p0:p0 + n], hs[c % 2][1][:, p0:p0 + n], ALU.add, e="pool")
            yield
            y_ = yo[slot][:, :n]
            self.tt(y_, g_, t_, ALU.mult)
            yield
            self.dma(self.yT.t(self.yT.ap[:, c, p0:p0 + n], p0, n), y_)
            yield

        def items():
            i = 0
            for c in range(KT):
                yield L(c)
                fw = []
                prev = None
                for (p0, n, j) in TILES:
                    fw.append((0, p0, n, prev))
                    prev = p0 + n - 1
                bw = [(1, 0, TC, None)]
                prev = 0
                for (p0, n, j) in reversed(LTILES):
                    bw.append((1, p0, n, prev))
                    prev = p0
                for f_, b_ in zip(fw, bw):
                    for (d, p0, n, init) in (f_, b_):
                        yield S(c, d, p0, n, init, i % K)
                        i += 1
                for (p0, n, j) in LTILES:
                    yield O(c, p0, n, i % K)
                    i += 1
        self.run_pipe(items(), K, gap=0)
        self.barrier()


Prog.mixer_c_phases = mixer_c_phases2
Prog.mixer_c_scan = mixer_c_scan2


def setup_streams(self):
    K = 3
    with ExitStack() as st:
        cv = self.sb(st, "cv", [128, KT, 2], F32)
        sv = self.sb(st, "sv", [128, KT, 2], F32)
        mb = [self.sb(st, "mb%d" % l, [128, 48], F32) for l in range(2)]
        tmp = self.sb(st, "mtmp", [128, KT, 2], F32)
        wb = [self.sb(st, "mw%d" % k, [128, KT, 128], F32) for k in range(3)]
        xin = [self.sb(st, "xin%d" % k, [128, D], F32) for k in range(K)]
        xo = [self.sb(st, "xo%d" % k, [128, KT, 128], F32) for k in range(K)]
        self.dma(cv, self.din["cvec"])
        for l in range(2):
            self.dma(mb[l], self.din["mod_b"][l])
        self.act(sv, cv, AF.Silu)
        psm = [self.PS[6], self.PS[7]]

        def mod_stream():
            q = 0
            for l in range(2):
                ps = psm[l]
                for n in range(48):
                    w = wb[q % 3]
                    q += 1
                    self.dma(w, self.din["mod_w"][l, :, :, n * 128:(n + 1) * 128])
                    for kt in range(KT):
                        self.mm(ps[:, 2 * n:2 * n + 2], w[:, kt, :], sv[:, kt, :], start=(kt == 0), stop=(kt == KT - 1))
                    yield
                pv = ps[:, 0:96]
                for j in range(2):
                    self.tt(self.modv[l][:, :, j], T(pv.ap.rearrange("p (n j) -> p n j", j=2)[:, :, j], pv.s), mb[l], ALU.add)
                for w_, sec in ((0, 1), (1, 4)):
                    self.ts(tmp, self.modv[l][:, sec * 8:(sec + 1) * 8, :], 1.0, ALU.add)
                    for j in range(2):
                        self.tt(self.gs[l][w_][:, :, j], tmp[:, :, j], self.ng[:, 2 * l + w_, :], ALU.mult)
                yield

        def xitem(i, slot):
            src = self.din["ctx"][i * 128:(i + 1) * 128, :] if i < 2 else self.din["x"][(i - 2) * 128:(i - 1) * 128, :]
            xi = xin[slot]
            self.dma(xi, src)
            yield
            for half in range(2):
                ps = self.PS[2 * slot + half]
                for k4 in range(4):
                    kt = half * 4 + k4
                    self.tr(ps[:, k4 * 128:(k4 + 1) * 128], xi[:, kt * 128:(kt + 1) * 128], self.ident)
                dst = xo[slot][:, half * 4:(half + 1) * 4, :]
                self.copy(dst, T(ps.ap.rearrange("p (k t) -> p k t", t=128), ps.s), e=("act" if half else "dve"))
                yield
            self.dma(self.xT.t(self.xT.ap[:, :, i * 128:(i + 1) * 128], i * 128, 128), xo[slot])
            yield

        def xstream():
            it = iter(xitem(i, i % K) for i in range(NCH))
            active = []
            more = True
            while more or active:
                if more and len(active) < K:
                    g = next(it, None)
                    if g is None:
                        more = False
                    else:
                        active.append(g)
                for g in list(active):
                    try:
                        next(g)
                    except StopIteration:
                        active.remove(g)
                yield
        self.run_streams([mod_stream(), xstream()])
        self.barrier()


Prog.setup_streams = setup_streams


def pipe_gen(gens, K, gap=1):
    it = iter(gens)
    active = []
    more = True
    rnd = 0
    while more or active:
        if more and len(active) < K and (gap <= 1 or rnd % gap == 0):
            g = next(it, None)
            if g is None:
                more = False
            else:
                active.append(g)
        rnd += 1
        for g in list(active):
            try:
                next(g)
            except StopIteration:
                active.remove(g)
        yield


def outproj_norm_items(self, ph, l, wname, widx, nkt, src, gsec, tiles, hT, nl, nw, ready, banks):
    K = 2
    wsb = self.sb(ph, "opw", [128, nkt, D], BF16)
    wd = self.din[wname] if widx is None else self.din[wname][widx]
    for kt in range(nkt):
        self.dma(wsb[:, kt, :], wd[:, kt, :], q="pool")
    ub = [self.sb(ph, "opu%d" % k, [128, nkt, 512], BF16) for k in range(K)]
    xs = [self.sb(ph, "opx%d" % k, [128, KT, 512], F32) for k in range(K)]
    sq = [self.sb(ph, "opsq%d" % k, [128, 2, 512], BF16) for k in range(K)]
    rs = [self.sb(ph, "oprs%d" % k, [128, 512], F32) for k in range(K)]
    tm = [self.sb(ph, "optm%d" % k, [128, 2, 512], F32) for k in range(K)]
    sec_sh = 0 if nw == 0 else 3

    def item(p0, n, j, slot):
        while not ready(p0, n):
            yield
        u = ub[slot]
        x_ = xs[slot]
        self.dma(u[:, :, :n], src.t(src.ap[:, :, p0:p0 + n], p0, n))
        self.dma(x_[:, :, :n], self.xT.t(self.xT.ap[:, :, p0:p0 + n], p0, n))
        yield
        for nt in range(KT):
            ps = self.PS[banks[0] + 2 * slot + nt % 2]
            for kt in range(nkt):
                self.mm(ps[:, :n], wsb[:, kt, nt * 128:(nt + 1) * 128], u[:, kt, :n],
                        start=(kt == 0), stop=(kt == nkt - 1))
            self.stt(x_[:, nt, :n], ps[:, :n], self.modv[l][:, gsec * 8 + nt, j:j + 1], x_[:, nt, :n],
                     ALU.mult, ALU.add)
            yield
        self.dma(self.xT.t(self.xT.ap[:, :, p0:p0 + n], p0, n), x_[:, :, :n])
        yield
        ps = self.PS[banks[1] + slot]
        for kt in range(KT):
            s_ = sq[slot][:, kt % 2, :n]
            if kt % 2:
                self.tt(s_, x_[:, kt, :n], x_[:, kt, :n], ALU.mult, e="pool")
            else:
                self.act(s_, x_[:, kt, :n], AF.Square)
            self.mm(ps[:, :n], self.onesb, s_, start=(kt == 0), stop=(kt == KT - 1))
        r_ = rs[slot]
        self.act(r_[:, :n], ps[:, :n], AF.Sqrt, bias=self.epsc, scale=1.0 / D)
        yield
        self.recip(r_[:, :n], r_[:, :n])
        yield
        for kt in range(KT):
            t_ = tm[slot][:, kt % 2, :n]
            self.stt(t_, x_[:, kt, :n], self.gs[nl][nw][:, kt, j:j + 1], r_[:, :n], ALU.mult, ALU.mult)
            self.act(hT[:, kt, p0:p0 + n], t_, AF.Identity,
                     bias=self.modv[nl][:, sec_sh * 8 + kt, j:j + 1], scale=1.0)
            yield
    return (item(p0, n, j, i % K) for i, (p0, n, j) in enumerate(tiles)), K


def merge_items(self, ph, merged):
    K = 2
    H = [slice(h * 128, (h + 1) * 128) for h in range(4)]
    gn = [self.sb(ph, "mgn%d" % k, [128, 512], F32) for k in range(2)]
    self.dma(gn[0], self.din["hng"])
    self.dma(gn[1], self.din["gng"])
    o0 = [self.sb(ph, "mo0%d" % k, [128, 512], F32) for k in range(K)]
    o1 = [self.sb(ph, "mo1%d" % k, [128, 512], F32) for k in range(K)]
    ag = [self.sb(ph, "mag%d" % k, [128, 512], F32) for k in range(K)]
    junk = self.sb(ph, "mjunk", [128, 128], F32)
    ss = [self.sb(ph, "mss%d" % k, [128, 4], F32) for k in range(K)]
    ya = [self.sb(ph, "mya%d" % k, [128, 512], BF16) for k in range(K)]
    yaT = [self.sb(ph, "myaT%d" % k, [128, 4, 128], BF16) for k in range(K)]

    def item(ci, mx, slot):
        p0 = ci * 128
        src = self.OAd if mx == 0 else self.OBd
        gcol = 2048 if mx == 0 else 2560
        o_, g_, s_ = o0[slot], ag[slot], ss[slot]
        self.dma(o_, src[0].t(src[0].ap[p0:p0 + 128, :], p0, 128))
        self.dma(o1[slot], src[1].t(src[1].ap[p0:p0 + 128, :], p0, 128))
        self.dma(g_, self.Atm.t(self.Atm.ap[p0:p0 + 128, gcol:gcol + 512], p0, 128))
        yield
        self.tt(o_, o_, o1[slot], ALU.add, e="pool")
        self.tt(g_, g_, gn[mx], ALU.mult, e="pool")
        yield
        for h in range(4):
            self.act(junk, o_[:, H[h]], AF.Square, accum=s_[:, h:h + 1])
        yield
        self.act(s_, s_, AF.Sqrt, bias=self.epsc, scale=1.0 / 128)
        self.recip(s_, s_)
        yield
        for h in range(4):
            self.stt(ya[slot][:, H[h]], o_[:, H[h]], s_[:, h:h + 1], g_[:, H[h]], ALU.mult, ALU.mult)
        yield
        pst = self.PS[slot]
        for h in range(4):
            self.tr(psb(pst)[:, H[h]], ya[slot][:, H[h]], self.identb)
        self.copy(T(yaT[slot].ap.rearrange("p h t -> p (h t)"), yaT[slot].s), psb(pst)[:, 0:512], e="act")
        yield
        self.dma(self.yT.t(self.yT.ap[:, 4 * mx:4 * mx + 4, p0:p0 + 128], p0, 128), yaT[slot])
        merged.add((ci, mx))
        yield
    items = [(ci, mx) for ci in range(NCH) for mx in range(2)]
    return (item(ci, mx, i % K) for i, (ci, mx) in enumerate(items)), K


Prog.outproj_norm_items = outproj_norm_items
Prog.merge_items = merge_items


def ab_proj_phase2(self, hT):
    K = 4
    self.Atm = DT(self.scr("Atm", [P, NTM], F32).ap)
    self.qkvT = DT(self.scr("qkvT", [12, 128, P], BF16).ap.rearrange("c p t -> p c t"))
    win = self.din["ab_w_in"]
    with ExitStack() as ph:
        wb = [self.sb(ph, "abw%d" % k, [128, KT, 512], BF16) for k in range(2)]
        ob = [self.sb(ph, "abo%d" % k, [128, 512], F32) for k in range(K)]
        blocks = [(0, 0, 512, "silu"), (512, 512, 512, "c"), (1024, 1024, 512, "c"), (1536, 1536, 512, "c"),
                  (2048, 2048, 512, "silu"), (4096, 2560, 512, "silu"), (4608, 3072, 16, "c")]
        pre = [self.sb(ph, "abpre%d" % k, [128, CPAD], BF16) for k in range(2)]
        wq = [self.sb(ph, "abwq%d" % k, [128, KT, 128], BF16) for k in range(2)]
        gcw = self.sb(ph, "abgcw", [128, 12, 4], F32)
        self.dma(gcw, self.din["gcw"])
        dwq = [self.sb(ph, "abdw%d" % k, [128, 4, 128], BF16) for k in range(2)]
        so = [self.sb(ph, "abso%d" % k, [128, 512], BF16) for k in range(K)]
        for k in range(2):
            self.memset(pre[k], 0.0)
        a_done = [0] * 12

        def WT(bi):
            wc, oc, ncol, kind = blocks[bi]
            self.dma(wb[bi % 2][:, :, :ncol], win[:, :, wc:wc + ncol], q="pool")
            yield

        def TMi(bi, i, slot):
            wc, oc, ncol, kind = blocks[bi]
            w = wb[bi % 2]
            ps = self.PS[2 * slot]
            for kt in range(KT):
                self.mm(ps[:, :ncol], hT[:, kt, i * 128:(i + 1) * 128], w[:, kt, :ncol],
                        start=(kt == 0), stop=(kt == KT - 1))
            yield
            o = ob[slot]
            if kind == "silu":
                self.act(o[:, :ncol], ps[:, :ncol], AF.Silu)
            else:
                self.copy(o[:, :ncol], ps[:, :ncol], e="dve")
            yield
            self.dma(self.Atm.t(self.Atm.ap[i * 128:(i + 1) * 128, oc:oc + ncol], i * 128, 128), o[:, :ncol])
            yield

        def WF(c):
            self.dma(wq[c % 2], win[:, :, 2560 + c * 128:2560 + (c + 1) * 128], q="pool")
            for k in range(4):
                self.ts(dwq[c % 2][:, k, :], self.identb, gcw[:, c, k:k + 1], ALU.mult)
            yield

        def A(c, p0, n, j, slot):
            ps = self.PS[2 * slot]
            for kt in range(KT):
                self.mm(ps[:, :n], wq[c % 2][:, kt, :], hT[:, kt, p0:p0 + n], start=(kt == 0), stop=(kt == KT - 1))
            yield
            off = CP0 + p0 if j == 1 else CL0 + (p0 - TC)
            self.copy(pre[c % 2][:, off:off + n], ps[:, :n], e=("act" if slot % 2 else "dve"))
            a_done[c] += 1
            yield

        def Bc(c, p0, n, j, slot):
            while a_done[c] < len(TILES):
                yield
            off = CP0 + p0 if j == 1 else CL0 + (p0 - TC)
            ps = self.PS[2 * slot + 1]
            pr = pre[c % 2]
            for k in range(4):
                self.mm(ps[:, :n], dwq[c % 2][:, k, :], pr[:, off + k - 2:off + k - 2 + n], start=(k == 0), stop=(k == 3))
            yield
            o = so[slot]
            self.act(o[:, :n], ps[:, :n], AF.Silu)
            yield
            self.dma(self.qkvT.t(self.qkvT.ap[:, c, p0:p0 + n], p0, n), o[:, :n])
            yield

        def items():
            i_ = 0
            for bi in range(len(blocks)):
                yield WT(bi)
                for i in range(NCH):
                    yield TMi(bi, i, i_ % K)
                    i_ += 1
            for c in range(12):
                yield WF(c)
                for (p0, n, j) in TILES:
                    yield A(c, p0, n, j, i_ % K)
                    i_ += 1
                for (p0, n, j) in TILES:
                    yield Bc(c, p0, n, j, i_ % K)
                    i_ += 1
        self.run_pipe(items(), K, gap=1)
        self.barrier()


Prog.ab_proj_phase = ab_proj_phase2
```
